# Optimizing a Trainium2 kernel written in Bass

```python
import jax, jax.numpy as jnp
from jax import lax
import numpy as np

D_MODEL = 1024
BATCH = 1
SEQ = 16384
DEPTH = 1
DEC_BATCH = 4
DEC_SEQ = 4096
PAST_LEN = 128

HEAD_DIM = 64
ATT_GROUPS = ((128, 1), (512, 4), (2048, 16))
N_GROUPS = 3
HEADS_PER_GROUP = 4
N_ATT_HEADS = N_GROUPS * HEADS_PER_GROUP
ATT_QKV_WIDTH = N_ATT_HEADS * HEAD_DIM
ATT_OUT_WIDTH = HEADS_PER_GROUP * HEAD_DIM
Q_BLOCK = 128
ATT_SCALE = HEAD_DIM ** -0.5
NEG_INF = -1e30
N_BUCKETS = 32
MAX_DISTANCE = 1024
CHUNK = 128
SGU_GROUPS = 4
SGU_GROUP_DIM = 128
SGU_WIDTH = SGU_GROUPS * SGU_GROUP_DIM
D_FF = 2816
EPS = 1e-6
IN_COLS = 3 * ATT_QKV_WIDTH + 2 * SGU_WIDTH + 2 * D_MODEL

kernel_name = "hybrid_dilated_attn_gmlp_encoder"


def _rmsnorm(x, g):
    xf = x.astype(jnp.float32)
    y = xf * lax.rsqrt(jnp.mean(xf * xf, axis=-1, keepdims=True) + EPS)
    return (y * g.astype(jnp.float32)).astype(x.dtype)


def _layernorm(x, g, b):
    xf = x.astype(jnp.float32)
    mu = jnp.mean(xf, axis=-1, keepdims=True)
    var = jnp.mean(jnp.square(xf - mu), axis=-1, keepdims=True)
    y = (xf - mu) * lax.rsqrt(var + EPS)
    return (y * g.astype(jnp.float32) + b.astype(jnp.float32)).astype(x.dtype)


def _swiglu(x, w_gate, w_up, w_down):
    return (jax.nn.silu(x @ w_gate) * (x @ w_up)) @ w_down


def _t5_bucket(rel):
    nb = N_BUCKETS // 2
    max_exact = nb // 2
    ret = (rel > 0).astype(np.int32) * nb
    n = np.abs(rel)
    large = max_exact + (np.log(np.maximum(n, 1) / max_exact) / np.log(MAX_DISTANCE / max_exact)
                         * (nb - max_exact)).astype(np.int32)
    large = np.minimum(large, nb - 1)
    return ret + np.where(n < max_exact, n, large)


def _dilated_attention(q, k, v, rel_bias):
    B, S = q.shape[0], q.shape[1]
    qs = [q[:, :, g] for g in range(N_GROUPS)]
    ks = [k[:, :, g] for g in range(N_GROUPS)]
    vs = [v[:, :, g] for g in range(N_GROUPS)]
    offs, biases = [], []
    for g, (window, dil) in enumerate(ATT_GROUPS):
        half = (window // 2) // dil
        off = np.arange(-half, half + 1, dtype=np.int32) * dil
        offs.append(jnp.asarray(off))
        hs = slice(g * HEADS_PER_GROUP, (g + 1) * HEADS_PER_GROUP)
        biases.append(rel_bias[_t5_bucket(off)][:, hs].T.astype(jnp.float32))
    q_idx = jnp.arange(Q_BLOCK, dtype=jnp.int32)

    def block(start):
        outs, lses = [], []
        for g in range(N_GROUPS):
            qg = lax.dynamic_slice_in_dim(qs[g], start, Q_BLOCK, axis=1)
            pos = start + q_idx[:, None] + offs[g][None, :]
            valid = (pos >= 0) & (pos < S)
            pc = jnp.clip(pos, 0, S - 1)
            kg = ks[g][:, pc]
            vg = vs[g][:, pc]
            logits = (jnp.einsum('bqhd,bqjhd->bhqj', qg, kg).astype(jnp.float32) * ATT_SCALE
                      + biases[g][None, :, None, :])
            logits = jnp.where(valid[None, None], logits, NEG_INF)
            m = jnp.max(logits, axis=-1, keepdims=True)
            e = jnp.exp(logits - m)
            s = jnp.sum(e, axis=-1)
            o = jnp.einsum('bhqj,bqjhd->bqhd', e, vg.astype(jnp.float32)) / jnp.swapaxes(s, 1, 2)[..., None]
            outs.append(o)
            lses.append(m[..., 0] + jnp.log(s))
        alpha = jax.nn.softmax(jnp.stack(lses), axis=0)
        alpha = jnp.swapaxes(alpha, 2, 3)[..., None]
        return jnp.sum(alpha * jnp.stack(outs), axis=0).astype(q.dtype)

    starts = jnp.arange(S // Q_BLOCK, dtype=jnp.int32) * Q_BLOCK
    out = lax.map(block, starts)
    return jnp.moveaxis(out, 0, 1).reshape(B, S, ATT_OUT_WIDTH)


def _spatial_gating(z, ln_g, ln_b, w_s, b_s):
    B, S = z.shape[0], z.shape[1]
    u, vv = jnp.split(z, 2, axis=-1)
    vv = _layernorm(vv, ln_g, ln_b)
    vc = vv.reshape(B, S // CHUNK, CHUNK, SGU_GROUPS, SGU_GROUP_DIM)
    mixed = jnp.einsum('gpq,bcqgd->bcpgd', w_s, vc) + b_s.T[None, None, :, :, None]
    return u * mixed.reshape(B, S, SGU_WIDTH)


def _layer(x, rel_bias, ffn1_pre_g, ffn1_post_g, ffn1_w_gate, ffn1_w_up, ffn1_w_down,
           mix_pre_g, w_in, sgu_ln_g, sgu_ln_b, sgu_w_s, sgu_b_s, w_att, w_sgu, w_out, mix_post_g,
           ffn2_pre_g, ffn2_post_g, ffn2_w_gate, ffn2_w_up, ffn2_w_down, final_g):
    B, S = x.shape[0], x.shape[1]
    h = _swiglu(_rmsnorm(x, ffn1_pre_g), ffn1_w_gate, ffn1_w_up, ffn1_w_down)
    x = x + 0.5 * _rmsnorm(h, ffn1_post_g)
    h = _rmsnorm(x, mix_pre_g)
    proj = h @ w_in
    q, k, v, z, g_a, g_b = jnp.split(proj, np.cumsum([ATT_QKV_WIDTH, ATT_QKV_WIDTH, ATT_QKV_WIDTH,
                                                     2 * SGU_WIDTH, D_MODEL]).tolist(), axis=-1)
    hshape = (B, S, N_GROUPS, HEADS_PER_GROUP, HEAD_DIM)
    att = _dilated_attention(q.reshape(hshape), k.reshape(hshape), v.reshape(hshape), rel_bias)
    sgu = _spatial_gating(jax.nn.gelu(z, approximate=False), sgu_ln_g, sgu_ln_b, sgu_w_s, sgu_b_s)
    merged = jax.nn.sigmoid(g_a) * (att @ w_att) + jax.nn.sigmoid(g_b) * (sgu @ w_sgu)
    x = x + _rmsnorm(merged @ w_out, mix_post_g)
    h = _swiglu(_rmsnorm(x, ffn2_pre_g), ffn2_w_gate, ffn2_w_up, ffn2_w_down)
    x = x + 0.5 * _rmsnorm(h, ffn2_post_g)
    return _rmsnorm(x, final_g)


def _trunk(x, rel_bias, ffn1_pre_g, ffn1_post_g, ffn1_w_gate, ffn1_w_up, ffn1_w_down,
           mix_pre_g, w_in, sgu_ln_g, sgu_ln_b, sgu_w_s, sgu_b_s, w_att, w_sgu, w_out, mix_post_g,
           ffn2_pre_g, ffn2_post_g, ffn2_w_gate, ffn2_w_up, ffn2_w_down, final_g):
    for l in range(DEPTH):
        x = _layer(x, rel_bias, ffn1_pre_g[l], ffn1_post_g[l], ffn1_w_gate[l], ffn1_w_up[l], ffn1_w_down[l],
                   mix_pre_g[l], w_in[l], sgu_ln_g[l], sgu_ln_b[l], sgu_w_s[l], sgu_b_s[l],
                   w_att[l], w_sgu[l], w_out[l], mix_post_g[l],
                   ffn2_pre_g[l], ffn2_post_g[l], ffn2_w_gate[l], ffn2_w_up[l], ffn2_w_down[l], final_g[l])
    return x


def setup_inputs(seed: int = 0) -> dict:
    key = jax.random.key(seed)
    ks = jax.random.split(key, 32)
    f32 = jnp.float32

    def nrm(k, shape, scale):
        return jax.random.normal(k, shape, f32) * scale

    def gain(k, n):
        return 1.0 + 0.02 * jax.random.normal(k, (DEPTH, n), f32)

    L = DEPTH
    return {
        "x_prompt": jax.random.normal(ks[0], (BATCH, SEQ, D_MODEL), f32),
        "x_sample": jax.random.normal(ks[1], (DEC_BATCH, DEC_SEQ, D_MODEL), f32),
        "rel_bias": nrm(ks[2], (N_BUCKETS, N_ATT_HEADS), 0.5),
        "ffn1_pre_g": gain(ks[3], D_MODEL),
        "ffn1_post_g": gain(ks[4], D_MODEL),
        "ffn1_w_gate": nrm(ks[5], (L, D_MODEL, D_FF), D_MODEL ** -0.5),
        "ffn1_w_up": nrm(ks[6], (L, D_MODEL, D_FF), D_MODEL ** -0.5),
        "ffn1_w_down": nrm(ks[7], (L, D_FF, D_MODEL), D_FF ** -0.5),
        "mix_pre_g": gain(ks[8], D_MODEL),
        "w_in": nrm(ks[9], (L, D_MODEL, IN_COLS), D_MODEL ** -0.5),
        "sgu_ln_g": gain(ks[10], SGU_WIDTH),
        "sgu_ln_b": nrm(ks[11], (L, SGU_WIDTH), 0.02),
        "sgu_w_s": nrm(ks[12], (L, SGU_GROUPS, CHUNK, CHUNK), CHUNK ** -0.5),
        "sgu_b_s": 1.0 + nrm(ks[13], (L, SGU_GROUPS, CHUNK), 0.02),
        "w_att": nrm(ks[14], (L, ATT_OUT_WIDTH, D_MODEL), ATT_OUT_WIDTH ** -0.5),
        "w_sgu": nrm(ks[15], (L, SGU_WIDTH, D_MODEL), SGU_WIDTH ** -0.5),
        "w_out": nrm(ks[16], (L, D_MODEL, D_MODEL), D_MODEL ** -0.5),
        "mix_post_g": gain(ks[17], D_MODEL),
        "ffn2_pre_g": gain(ks[18], D_MODEL),
        "ffn2_post_g": gain(ks[19], D_MODEL),
        "ffn2_w_gate": nrm(ks[20], (L, D_MODEL, D_FF), D_MODEL ** -0.5),
        "ffn2_w_up": nrm(ks[21], (L, D_MODEL, D_FF), D_MODEL ** -0.5),
        "ffn2_w_down": nrm(ks[22], (L, D_FF, D_MODEL), D_FF ** -0.5),
        "final_g": gain(ks[23], D_MODEL),
    }


def reference(x_prompt, x_sample, rel_bias, ffn1_pre_g, ffn1_post_g, ffn1_w_gate, ffn1_w_up, ffn1_w_down,
              mix_pre_g, w_in, sgu_ln_g, sgu_ln_b, sgu_w_s, sgu_b_s, w_att, w_sgu, w_out, mix_post_g,
              ffn2_pre_g, ffn2_post_g, ffn2_w_gate, ffn2_w_up, ffn2_w_down, final_g):
    y_prompt = _trunk(x_prompt, rel_bias, ffn1_pre_g, ffn1_post_g, ffn1_w_gate, ffn1_w_up, ffn1_w_down,
                      mix_pre_g, w_in, sgu_ln_g, sgu_ln_b, sgu_w_s, sgu_b_s, w_att, w_sgu, w_out, mix_post_g,
                      ffn2_pre_g, ffn2_post_g, ffn2_w_gate, ffn2_w_up, ffn2_w_down, final_g)
    y_sample = _trunk(x_sample, rel_bias, ffn1_pre_g, ffn1_post_g, ffn1_w_gate, ffn1_w_up, ffn1_w_down,
                      mix_pre_g, w_in, sgu_ln_g, sgu_ln_b, sgu_w_s, sgu_b_s, w_att, w_sgu, w_out, mix_post_g,
                      ffn2_pre_g, ffn2_post_g, ffn2_w_gate, ffn2_w_up, ffn2_w_down, final_g)
    return (y_prompt, y_sample)
```

```python
import numpy as np
import concourse.bass as bass
import concourse.mybir as mybir
from concourse.bass_utils import run_bass_kernel_spmd
from contextlib import ExitStack

F32 = mybir.dt.float32
BF16 = mybir.dt.bfloat16
AF = mybir.ActivationFunctionType
ALU = mybir.AluOpType
AX = mybir.AxisListType

NCORES = 8
D = 1024
DFF = 2816
NFC = 22
KC = 8
NEXT = 6144
NOWN = 4096
HALO = 1024
TT = 512
QKVW = 2316
ATTW = 264
EPS = 1e-6
NEG = -1e30
GROUPS = ((128, 1), (512, 4), (2048, 16))
WREG_ELEMS = 67584
ARENA_ELEMS = 38400

CFG = {"phases": ("A1", "A2", "ATT", "B1", "B2"), "debug": False, "n_ext_tiles": 12, "n_own_tiles": 8}


class Op:
    __slots__ = ("id", "eng", "fn", "deps", "dma", "dsem", "dval", "seq", "sig")


class Prog:
    ENGS = ("pe", "act", "dve", "pool", "sp")

    def __init__(self, nc, ndma=20):
        self.nc = nc
        self.ops = []
        self.byeng = {e: [] for e in self.ENGS}
        self.lastw = {}
        self.readers = {}
        self.ndma = ndma
        self.dma_rr = {"sp": 0, "pool": 0, "act": 0}
        self.dma_cnt = {}
        self.dma_last = {}
        self.pending = {e: set() for e in self.ENGS}

    def add(self, eng, fn, r=(), w=(), dma=False, extra=()):
        op = Op()
        op.id = len(self.ops)
        op.eng = eng
        op.fn = fn
        op.dma = dma
        op.sig = False
        op.seq = None
        deps = set(extra)
        psr = [k for k in r if isinstance(k, tuple) and k[0] in ("ps", "psT")]
        if psr:
            r = [k for k in r if k not in psr]
            w = list(w) + psr
        for k in r:
            lw = self.lastw.get(k)
            if lw is not None:
                deps.add(lw)
        for k in w:
            lw = self.lastw.get(k)
            if lw is not None:
                deps.add(lw)
            deps.update(self.readers.get(k, ()))
        if self.pending[eng]:
            deps.update(self.pending[eng])
            self.pending[eng] = set()
        if dma:
            i = self.dma_rr[eng]
            self.dma_rr[eng] = (i + 1) % self.ndma
            key = (eng, i)
            prev = self.dma_last.get(key)
            if prev is not None:
                deps.add(prev)
            self.dma_cnt[key] = self.dma_cnt.get(key, 0) + 1
            self.dma_last[key] = op.id
            op.dsem = key
            op.dval = 16 * self.dma_cnt[key]
        deps.discard(op.id)
        op.deps = deps
        for k in r:
            self.readers.setdefault(k, []).append(op.id)
        for k in w:
            self.lastw[k] = op.id
            self.readers[k] = []
        self.ops.append(op)
        self.byeng[eng].append(op)
        return op.id

    def last(self, eng):
        return self.byeng[eng][-1].id if self.byeng[eng] else None

    def barrier(self):
        ids = set()
        for e in self.ENGS:
            if self.byeng[e]:
                ids.add(self.byeng[e][-1].id)
        for key, oid in self.dma_last.items():
            ids.add(oid)
        for e in self.ENGS:
            self.pending[e] |= ids
        self.lastw = {}
        self.readers = {}

    def emit(self, block, sems, dsems):
        ops = self.ops
        for op in ops:
            for d in op.deps:
                dop = ops[d]
                if dop.dma:
                    continue
                if dop.eng == op.eng == "pe" and not op.dma:
                    continue
                dop.sig = True
        for e in self.ENGS:
            n = 0
            for op in self.byeng[e]:
                if op.sig and not op.dma:
                    n += 1
                    op.seq = n
        finals = {}
        for key, cnt in self.dma_cnt.items():
            finals[key] = 16 * cnt

        def run(eng, h):
            waited = {}
            for op in self.byeng[eng]:
                need = {}
                for d in op.deps:
                    dop = ops[d]
                    if dop.dma:
                        s, v = ("d",) + dop.dsem, dop.dval
                    else:
                        if dop.eng == eng == "pe" and not op.dma:
                            continue
                        s, v = ("e", dop.eng), dop.seq
                    if v > need.get(s, 0):
                        need[s] = v
                for s, v in need.items():
                    if waited.get(s, 0) >= v:
                        continue
                    waited[s] = v
                    sem = sems[s[1]] if s[0] == "e" else dsems[(s[1], s[2])]
                    h.wait_ge(sem, v)
                ins = op.fn(h)
                if op.dma:
                    ins.then_inc(dsems[op.dsem], 16)
                elif op.sig:
                    ins.then_inc(sems[eng], 1)
            if eng == "sp":
                for key, v in finals.items():
                    if waited.get(("d",) + key, 0) < v:
                        h.wait_ge(dsems[key], v)
                for e2 in ("pe", "act", "dve", "pool"):
                    n = max([o.seq or 0 for o in self.byeng[e2]] + [0])
                    if n:
                        h.wait_ge(sems[e2], n)

        @block.tensor
        def _(h):
            run("pe", h)

        @block.scalar
        def _(h):
            run("act", h)

        @block.vector
        def _(h):
            run("dve", h)

        @block.gpsimd
        def _(h):
            run("pool", h)

        @block.sync
        def _(h):
            run("sp", h)


class Arena:
    def __init__(self, base):
        self.base = base
        self.n = base.shape[1]
        self.off = 0

    def reset(self, off=0):
        self.off = off

    def alloc(self, free_shape, dtype, parts=128):
        n = int(np.prod(free_shape))
        ne = n * 2 if dtype == F32 else n
        self.off = (self.off + 15) // 16 * 16
        assert self.off + ne <= self.n, ("arena overflow", self.off, ne, self.n)
        ap = self.base[0:parts, self.off:self.off + ne]
        self.off += ne
        if dtype == F32:
            ap = ap.bitcast(F32)
        if len(free_shape) == 2:
            ap = ap.rearrange("p (a b) -> p a b", a=free_shape[0])
        elif len(free_shape) == 3:
            ap = ap.rearrange("p (a b c) -> p a b c", a=free_shape[0], b=free_shape[1])
        return ap


class Ctx:
    pass


def _mm(P, out, lhsT, rhs, start, stop, r, w):
    P.add("pe", lambda h, o=out, a=lhsT, b=rhs, s=start, t=stop: h.matmul(o, a, b, start=s, stop=t), r=r, w=w)


def _tr(P, out, in_, ident, r, w):
    P.add("pe", lambda h, o=out, a=in_, i=ident: h.transpose(o, a, i), r=r, w=w)


def _act(P, out, in_, func, r, w, bias=None, scale=None, accum=None):
    def fn(h, o=out, a=in_, f=func, b=bias, s=scale, acc=accum):
        kw = {}
        if b is not None:
            kw["bias"] = b
        if s is not None:
            kw["scale"] = s
        if acc is not None:
            kw["accum_out"] = acc
        return h.activation(o, a, f, **kw)
    P.add("act", fn, r=r, w=w)


def _dma(P, q, out, in_, r, w, extra=(), slow=False):
    if slow:
        return P.add(q, lambda h, o=out, a=in_: h.dma_start(out=o, in_=a, allow_slow_non_contiguous=True), r=r, w=w, dma=True, extra=extra)
    return P.add(q, lambda h, o=out, a=in_: h.dma_start(out=o, in_=a), r=r, w=w, dma=True, extra=extra)


class PsumRR:
    def __init__(self, banks, keys):
        self.banks = banks
        self.keys = keys
        self.i = 0

    def next(self):
        b, k = self.banks[self.i], self.keys[self.i]
        self.i = (self.i + 1) % len(self.banks)
        return b, k


def load_bcast(P, C, dst, vec_ap, n, key, scale=None):
    _dma(P, "sp", dst, vec_ap[0:1, :].to_broadcast([128, n]), r=(), w=(key,))
    if scale is not None:
        P.add("dve", lambda h, o=dst, s=scale: h.tensor_scalar_mul(o, o, float(s)), r=(key,), w=(key,))


def rms_rstd(P, ss, out, rk, wk, n=D):
    _act(P, ss, ss, AF.Sqrt, r=rk, w=rk, bias=C_EPS[0], scale=1.0 / n)
    P.add("dve", lambda h, o=out, a=ss: h.reciprocal(o, a), r=rk, w=wk)


C_EPS = [None]


def prologue(P, C, src_rows, tile, gain_b, tag):
    hb = tile % 2
    hT = C.hT[hb]
    for s in range(4):
        j = tile * 4 + s
        xs = j % C.nx
        X = C.X[xs]
        row0 = src_rows + j * 128
        _dma(P, "sp", X, C.src[row0:row0 + 128, :], r=(("src", j),), w=(("x", xs),))
        sc = j % 4
        _act(P, C.junk, X, AF.Square, r=(("x", xs),), w=(("ss", sc),), accum=C.ss[:, sc:sc + 1])
        rms_rstd(P, C.ss[:, sc:sc + 1], C.rs[:, sc:sc + 1], rk=(("ss", sc),), wk=(("rs", sc),))
        xb = j % 2
        P.add("dve", lambda h, o=C.xn[xb], a=X, sca=C.rs[:, sc:sc + 1], g=gain_b:
              h.scalar_tensor_tensor(o, a, sca, g, ALU.mult, ALU.mult),
              r=(("x", xs), ("rs", sc), "gains"), w=(("xn", xb),))
        pb = j % 2
        pst = C.psT[pb]
        for kc in range(KC):
            _tr(P, pst[:, kc * 128:(kc + 1) * 128], C.xn[xb][:, kc * 128:(kc + 1) * 128], C.ident,
                r=(("xn", xb), "ident"), w=(("psT", pb),))
        _act(P, hT[:, :, s * 128:(s + 1) * 128], pst.rearrange("p (k t) -> p k t", k=KC), AF.Copy,
             r=(("psT", pb),), w=(("hT", hb, s),))


def post_norm_residual(P, C, banks, gain_b, j, final_gain=None, dst_rows=None):
    X = C.Xr
    for n in range(2):
        b, k = banks[n]
        _act(P, C.junk[:, 0:512], b, AF.Square, r=(k,), w=(("ssy", n),), accum=C.ssy[:, n:n + 1])
        P.add("dve", lambda h, o=C.t[:, n * 512:(n + 1) * 512], a=b, g=gain_b[:, n * 512:(n + 1) * 512]:
              h.tensor_tensor(o, a, g, ALU.mult), r=(k, "gains"), w=(("t", n),))
    P.add("dve", lambda h, o=C.ssy[:, 2:3], a=C.ssy[:, 0:1], b2=C.ssy[:, 1:2]: h.tensor_tensor(o, a, b2, ALU.add),
          r=(("ssy", 0), ("ssy", 1)), w=(("ssy", 2),))
    rms_rstd(P, C.ssy[:, 2:3], C.ssy[:, 3:4], rk=(("ssy", 2),), wk=(("ssy", 3),))
    P.add("dve", lambda h, o=X, a=C.t, sca=C.ssy[:, 3:4], x=X: h.scalar_tensor_tensor(o, a, sca, x, ALU.mult, ALU.add),
          r=(("t", 0), ("t", 1), ("ssy", 3), "xr"), w=("xr",))
    outt = X
    outk = "xr"
    if final_gain is not None:
        _act(P, C.junk, X, AF.Square, r=("xr",), w=(("ssf", 0),), accum=C.ssy[:, 4:5])
        rms_rstd(P, C.ssy[:, 4:5], C.ssy[:, 5:6], rk=(("ssf", 0),), wk=(("ssf", 1),))
        P.add("dve", lambda h, o=C.t, a=X, sca=C.ssy[:, 5:6], g=final_gain: h.scalar_tensor_tensor(o, a, sca, g, ALU.mult, ALU.mult),
              r=("xr", ("ssf", 1), "gains", ("t", 0), ("t", 1)), w=(("t", 0), ("t", 1)))
        outt = C.t
        outk = ("t", 0)
    if CFG.get("stop", 99) < 6:
        return
    _dma(P, "sp", C.dst[dst_rows:dst_rows + 128, :], outt, r=(outk, ("t", 1)) if final_gain is not None else (outk,),
         w=(("dst", j),))


def load_ffn_weights(P, C, wg_d, wu_d, wd_d, extra):
    W = C.wreg
    C.wg = W[:, 0:KC * DFF].rearrange("p (k n) -> p k n", k=KC)
    C.wu = W[:, KC * DFF:2 * KC * DFF].rearrange("p (k n) -> p k n", k=KC)
    C.wd = W[:, 2 * KC * DFF:2 * KC * DFF + NFC * D].rearrange("p (c n) -> p c n", c=NFC)
    wgv = wg_d.rearrange("(k p) n -> p k n", p=128)
    wuv = wu_d.rearrange("(k p) n -> p k n", p=128)
    wdv = wd_d.rearrange("(c p) n -> p c n", p=128)
    NS = 4
    cs = DFF // NS
    for i in range(NS):
        _dma(P, "pool", C.wg[:, :, i * cs:(i + 1) * cs], wgv[:, :, i * cs:(i + 1) * cs], r=(), w=(("wg", i),), extra=extra)
        _dma(P, "pool", C.wu[:, :, i * cs:(i + 1) * cs], wuv[:, :, i * cs:(i + 1) * cs], r=(), w=(("wu", i),), extra=extra)
    for i in range(2):
        _dma(P, "pool", C.wd[:, i * 11:(i + 1) * 11, :], wdv[:, i * 11:(i + 1) * 11, :], r=(), w=(("wd", i),), extra=extra)
    C.wslice = cs


def ffn_phase(P, C, ntiles, gain_pre, gain_post, gain_fin):
    prs = C.psrr
    stop = CFG.get("stop", 99)
    if stop < 1:
        return
    prologue(P, C, 0, 0, gain_pre, "f")
    if stop < 2:
        return
    for i in range(ntiles):
        hT = C.hT[i % 2]
        hk = [("hT", i % 2, s) for s in range(4)]
        for c in range(NFC):
            wk = c * 128 // C.wslice
            bg, kg = prs.next()
            for kc in range(KC):
                _mm(P, bg, C.wg[:, kc, c * 128:(c + 1) * 128], hT[:, kc, :], kc == 0, kc == KC - 1,
                    r=hk + [("wg", wk)], w=(kg,))
            bu, ku = prs.next()
            for kc in range(KC):
                _mm(P, bu, C.wu[:, kc, c * 128:(c + 1) * 128], hT[:, kc, :], kc == 0, kc == KC - 1,
                    r=hk + [("wu", wk)], w=(ku,))
            sg = C.sg[c % 2]
            _act(P, sg, bg, AF.Silu, r=(kg,), w=(("sg", c % 2),))
            P.add("dve", lambda h, o=C.hid[:, c, :], a=sg, b=bu: h.tensor_tensor(o, a, b, ALU.mult),
                  r=(("sg", c % 2), ku), w=(("hid", c),))
        if stop < 3:
            return
        if i + 1 < ntiles:
            prologue(P, C, 0, i + 1, gain_pre, "f")
        for s in range(4):
            j = i * 4 + s
            banks = []
            for n in range(2):
                b, k = prs.next()
                for c in range(NFC):
                    _mm(P, b, C.hid[:, c, s * 128:(s + 1) * 128], C.wd[:, c, n * 512:(n + 1) * 512], c == 0, c == NFC - 1,
                        r=(("hid", c), ("wd", c // 11)), w=(k,))
                banks.append((b, k))
            if stop < 4:
                continue
            _dma(P, "sp", C.Xr, C.src[j * 128:(j + 1) * 128, :], r=(("src", j),), w=("xr",))
            if stop < 5:
                continue
            post_norm_residual(P, C, banks, gain_post, j, final_gain=gain_fin, dst_rows=j * 128)


QKV_BLOCKS = ((0, 512, "q"), (512, 768, "q"), (768, 1280, "k"), (1280, 1536, "k"), (1536, 2048, "v"), (2048, 2304, "v"))


def qkv_phase(P, C, ntiles, gain_pre):
    prs = C.psrr
    prologue(P, C, 0, 0, gain_pre, "q")
    ev = 0
    for i in range(ntiles):
        hT = C.hT[i % 2]
        if i + 1 < ntiles:
            prologue(P, C, 0, i + 1, gain_pre, "q")
        for s in range(4):
            j = i * 4 + s
            sb = j % 2
            stage = C.stage[sb]
            stage_v = stage[:, 1536:2316].rearrange("p (h e) -> p h e", e=65)
            for (c0, c1, kind) in QKV_BLOCKS:
                b, k = prs.next()
                n = c1 - c0
                for kc in range(KC):
                    _mm(P, b[:, 0:n], hT[:, kc, s * 128:(s + 1) * 128], C.wqkv[:, kc, c0:c1], kc == 0, kc == KC - 1,
                        r=(("hT", i % 2, s), "wqkv"), w=(k,))
                if kind == "v":
                    h0 = (c0 - 1536) // 64
                    nh = n // 64
                    out = stage_v[:, h0:h0 + nh, 0:64]
                    src = b[:, 0:n].rearrange("p (h e) -> p h e", e=64)
                else:
                    out = stage[:, c0:c1]
                    src = b[:, 0:n]
                scale = 0.125 if kind == "q" else 1.0
                if ev % 2 == 0:
                    _act(P, out, src, AF.Copy, r=(k,), w=(("stage", sb, c0),), scale=scale)
                else:
                    P.add("dve", lambda h, o=out, a=src, sc=scale: h.tensor_scalar_mul(o, a, float(sc)),
                          r=(k,), w=(("stage", sb, c0),))
                ev += 1
            _dma(P, "sp", C.dst[j * 128:(j + 1) * 128, :], stage, r=[("stage", sb, c0) for (c0, _, _) in QKV_BLOCKS],
                 w=(("dst", j),))


def build_bias(P, C, I, tscr):
    A = C.A
    rb = A.alloc((12,), F32, parts=32)
    E = A.alloc((3 * 129,), F32, parts=32)
    Tsb = A.alloc((3, 512), F32, parts=12)
    anti = A.alloc((128,), F32)
    R = [A.alloc((256,), F32) for _ in range(2)]
    _dma(P, "sp", rb, I["rel_bias"], r=(), w=("rb",))
    _dma(P, "sp", E, I["onehot"], r=(), w=("E",))
    _dma(P, "sp", anti, I["antiid"], r=(), w=("anti",))
    P.add("pool", lambda h: h.memset(Tsb, NEG), r=(), w=("Tsb",))
    b5 = C.ps[5][:, :]
    for g in range(3):
        _mm(P, b5[0:12, 0:129], rb[0:32, 0:12], E[0:32, g * 129:(g + 1) * 129], True, True, r=("rb", "E"), w=(("ps", 5),))
        P.add("dve", lambda h, o=Tsb[:, g, 127:256], a=b5[0:12, 0:129]: h.tensor_copy(o, a), r=(("ps", 5), "Tsb"), w=("Tsb",))
    for g in range(3):
        _dma(P, "sp", tscr[4 * g:4 * g + 4, :], Tsb[4 * g:4 * g + 4, g, :], r=("Tsb",), w=(("tscr", g),))
    for gh in range(12):
        Rt = R[gh % 2]
        src = bass.AP(tscr.tensor, gh * 512, [[1, 128], [1, 256]])
        _dma(P, "sp", Rt, src, r=(("tscr", gh // 4),), w=(("R", gh % 2),))
        b3, k3 = C.ps[3 + gh % 2][:, :], ("ps", 3 + gh % 2)
        _mm(P, b3[:, 0:256], anti, Rt, True, True, r=(("R", gh % 2), "anti"), w=(k3,))
        P.add("dve", lambda h, o=C.bias[:, gh, :], a=b3[:, 0:256]: h.tensor_copy(o, a), r=(k3,), w=("bias",))


def att_phase(P, C, qkv, att):
    segs = segments()
    psT_k = C.ps[5][:, :].bitcast(BF16)
    psPT = [C.ps[6][:, :].bitcast(BF16), C.ps[7][:, :].bitcast(BF16)]
    Sb = [(C.ps[i][:, 0:256], ("ps", i)) for i in range(3)]
    Ob = [(C.ps[3 + i][:, 0:260], ("ps", 3 + i)) for i in range(2)]
    qt = qkv.tensor
    ucount = 0
    bcount = 0
    for si, (g, d, r, i0, nblk) in enumerate(segs):
        sbuf = si % 2
        nt = nblk + 1
        Kraw, Vt, Qraw, KT, QT = C.Kraw[sbuf], C.Vt[sbuf], C.Qraw[sbuf], C.KT[sbuf], C.QT[sbuf]
        rowk = d * (i0 - 64) + r
        rowq = d * i0 + r
        _dma(P, "sp", Kraw[:, 0:nt, :], bass.AP(qt, rowk * QKVW + 768 + 256 * g, [[d * QKVW, 128], [128 * d * QKVW, nt], [1, 256]]),
             r=(), w=(("Kraw", sbuf),))
        _dma(P, "sp", Vt[:, 0:nt, :], bass.AP(qt, rowk * QKVW + 1536 + 260 * g, [[d * QKVW, 128], [128 * d * QKVW, nt], [1, 260]]),
             r=(), w=(("Vt", sbuf),))
        _dma(P, "sp", Qraw[:, 0:nblk, :], bass.AP(qt, rowq * QKVW + 256 * g, [[d * QKVW, 128], [128 * d * QKVW, nblk], [1, 256]]),
             r=(), w=(("Qraw", sbuf),))
        for (raw, rk, dstT, dk, ntile) in ((Kraw, ("Kraw", sbuf), KT, ("KT", sbuf), nt), (Qraw, ("Qraw", sbuf), QT, ("QT", sbuf), nblk)):
            for j0 in range(0, ntile, 4):
                nj = min(4, ntile - j0)
                for c in range(2):
                    for jj in range(nj):
                        _tr(P, psT_k[:, c * 512 + jj * 128: c * 512 + (jj + 1) * 128], raw[:, j0 + jj, c * 128:(c + 1) * 128], C.ident,
                            r=(rk, "ident"), w=(("ps", 5),))
                src = psT_k.rearrange("p (c t) -> p c t", c=2)[:, :, 0:nj * 128]
                _act(P, dstT[:, :, j0 * 128:(j0 + nj) * 128], src, AF.Copy, r=(("ps", 5),), w=(dk,))
        units = [(b, h) for b in range(nblk) for h in range(4)]

        def emit_S(u):
            b, h = units[u]
            c, hp = h // 2, h % 2
            sbk, skey = Sb[(ucount + u) % 3]
            _mm(P, sbk, QT[hp * 64:(hp + 1) * 64, c, b * 128:(b + 1) * 128], KT[hp * 64:(hp + 1) * 64, c, b * 128:b * 128 + 256],
                True, True, r=(("QT", sbuf), ("KT", sbuf)), w=(skey,))

        emit_S(0)
        if len(units) > 1:
            emit_S(1)
        for u, (b, h) in enumerate(units):
            gu = ucount + u
            sbk, skey = Sb[gu % 3]
            ob, okey = Ob[(bcount + b) % 2]
            ost = C.ostage[(bcount + b) % 2]
            ostk = ("ost", (bcount + b) % 2)
            ss_i = gu % 3
            Ssb, Pb, PTs = C.Ssb[ss_i], C.Pb[ss_i], C.PTs[ss_i]
            mcol = ost[:, 260 + h:261 + h]
            P.add("dve", lambda hh, o=Ssb, a=sbk, bb=C.bias[:, 4 * g + h, :]: hh.tensor_tensor(o, a, bb, ALU.add),
                  r=(skey, "bias"), w=(("Ssb", ss_i),))
            P.add("dve", lambda hh, o=mcol, a=Ssb: hh.reduce_max(o, a, AX.X), r=(("Ssb", ss_i),), w=(ostk + (h,),))
            P.add("dve", lambda hh, o=C.negm[:, ss_i:ss_i + 1], a=mcol: hh.tensor_scalar_mul(o, a, -1.0),
                  r=(ostk + (h,),), w=(("negm", ss_i),))
            _act(P, Pb, Ssb, AF.Exp, r=(("Ssb", ss_i), ("negm", ss_i)), w=(("Pb", ss_i),), bias=C.negm[:, ss_i:ss_i + 1])
            if u + 2 < len(units):
                emit_S(u + 2)
            pt, ptk = psPT[gu % 2], ("ps", 6 + gu % 2)
            for jj in range(2):
                _tr(P, pt[:, jj * 128:(jj + 1) * 128], Pb[:, jj * 128:(jj + 1) * 128], C.ident, r=(("Pb", ss_i), "ident"), w=(ptk,))
            for jj in range(2):
                vcol = C.vseg[:, si * 9 + b + jj: si * 9 + b + jj + 1]
                _act(P, PTs[:, jj * 128:(jj + 1) * 128], pt[:, jj * 128:(jj + 1) * 128], AF.Copy, r=(ptk, "vseg"),
                     w=(("PTs", ss_i, jj),), scale=vcol)
            for jj in range(2):
                _mm(P, ob[:, h * 65:(h + 1) * 65], PTs[:, jj * 128:(jj + 1) * 128], Vt[:, b + jj, h * 65:(h + 1) * 65], jj == 0, jj == 1,
                    r=(("PTs", ss_i, jj), ("Vt", sbuf)), w=(okey,))
            if h == 3:
                P.add("dve", lambda hh, o=ost[:, 0:260], a=ob: hh.tensor_copy(o, a), r=(okey,), w=(ostk + (9,),))
                t0 = d * (i0 + 128 * b) + r - HALO
                dst = bass.AP(att.tensor, (t0 * 3 + g) * ATTW, [[d * 3 * ATTW, 128], [1, ATTW]])
                _dma(P, "sp", dst, ost, r=[ostk + (x,) for x in (0, 1, 2, 3, 9)], w=(("att", si, b),))
        ucount += len(units)
        bcount += nblk


def b1_phase(P, C, ntiles, att):
    prs = C.psrr
    prologue(P, C, HALO, 0, C.gpre, "m")
    for i in range(ntiles):
        hT = C.hT[i % 2]
        hk = [("hT", i % 2, s) for s in range(4)]
        for s in range(4):
            j = i * 4 + s
            sl = j % 2
            bu, ku = prs.next()
            for kc in range(KC):
                _mm(P, bu, hT[:, kc, s * 128:(s + 1) * 128], C.wrest[:, kc, 0:512], kc == 0, kc == KC - 1, r=(hk[s], "wrest"), w=(ku,))
            bv, kv = prs.next()
            for kc in range(KC):
                _mm(P, bv, hT[:, kc, s * 128:(s + 1) * 128], C.wrest[:, kc, 512:1024], kc == 0, kc == KC - 1, r=(hk[s], "wrest"), w=(kv,))
            U, V = C.U[sl], C.V[sl]
            _act(P, U, bu, AF.Gelu, r=(ku,), w=(("U", sl),))
            _act(P, V, bv, AF.Gelu, r=(kv,), w=(("V", sl), ("st", sl, 0)), accum=C.st[:, sl, 0:1])
            P.add("dve", lambda h, o=C.st[:, sl, 1:2], a=C.st[:, sl, 0:1]: h.tensor_scalar_mul(o, a, -1.0 / 512.0),
                  r=(("st", sl, 0),), w=(("st", sl, 1),))
            _act(P, C.junk[:, 0:512], V, AF.Square, r=(("V", sl), ("st", sl, 1)), w=(("st", sl, 2),), bias=C.st[:, sl, 1:2],
                 accum=C.st[:, sl, 2:3])
            rms_rstd(P, C.st[:, sl, 2:3], C.st[:, sl, 3:4], rk=(("st", sl, 2),), wk=(("st", sl, 3),), n=512)
            P.add("dve", lambda h, o=C.tn, a=V, s1=C.st[:, sl, 1:2], s2=C.st[:, sl, 3:4]: h.tensor_scalar(o, a, s1, s2, ALU.add, ALU.mult),
                  r=(("V", sl), ("st", sl, 1), ("st", sl, 3)), w=("tn",))
            P.add("dve", lambda h, o=C.tn, g=C.lng: h.tensor_tensor(o, o, g, ALU.mult), r=("tn", "gains"), w=("tn",))
            P.add("dve", lambda h, o=C.vln[sl], a=C.tn, g=C.lnb: h.tensor_tensor(o, a, g, ALU.add), r=("tn", "gains"), w=(("vln", sl),))
            bm, km = prs.next()
            for gg in range(4):
                _mm(P, bm[:, gg * 128:(gg + 1) * 128], C.WsT[:, gg, :], C.vln[sl][:, gg * 128:(gg + 1) * 128], True, True,
                    r=(("vln", sl), "WsT"), w=(km,))
            for gg in range(4):
                P.add("dve", lambda h, o=C.sgu[sl][:, gg * 128:(gg + 1) * 128], a=bm[:, gg * 128:(gg + 1) * 128], sc=C.bs[:, gg:gg + 1],
                      u=U[:, gg * 128:(gg + 1) * 128]: h.scalar_tensor_tensor(o, a, sc, u, ALU.add, ALU.mult),
                      r=(km, ("U", sl), "bs"), w=(("sgu", sl),))
            pb = j % 2
            pst = C.psT[pb]
            for gg in range(4):
                _tr(P, pst[:, gg * 128:(gg + 1) * 128], C.sgu[sl][:, gg * 128:(gg + 1) * 128], C.ident, r=(("sgu", sl), "ident"), w=(("psT", pb),))
            _act(P, C.sguT[:, :, s * 128:(s + 1) * 128], pst[:, 0:512].rearrange("p (k t) -> p k t", k=4), AF.Copy,
                 r=(("psT", pb),), w=(("sguT", s),))
            AI = C.attin[sl]
            _dma(P, "sp", AI, att[j * 128:(j + 1) * 128, :].rearrange("p (g e) -> p g e", g=3), r=(), w=(("attin", sl),))
            mview = AI[:, :, 260:264]
            oview = AI[:, :, 0:260].rearrange("p g (h e) -> p g h e", e=65)
            P.add("dve", lambda h, o=C.M, a=mview.rearrange("p g h -> p h g"): h.tensor_reduce(o, a, AX.X, ALU.max),
                  r=(("attin", sl),), w=("M",))
            P.add("dve", lambda h, o=C.dd, a=mview, m=C.M.unsqueeze(1).to_broadcast([128, 3, 4]): h.tensor_tensor(o, a, m, ALU.subtract),
                  r=(("attin", sl), "M"), w=("dd",))
            _act(P, C.ww, C.dd, AF.Exp, r=("dd",), w=("ww",))
            P.add("dve", lambda h, o=C.wd_, a=C.ww, sv=oview[:, :, :, 64]: h.tensor_tensor(o, a, sv, ALU.mult),
                  r=("ww", ("attin", sl)), w=("wd",))
            P.add("dve", lambda h, o=C.den, a=C.wd_.rearrange("p g h -> p h g"): h.tensor_reduce(o, a, AX.X, ALU.add), r=("wd",), w=("den",))
            P.add("dve", lambda h, o=C.den: h.reciprocal(o, o), r=("den",), w=("den",))
            P.add("dve", lambda h, o=C.cc, a=C.ww, m=C.den.unsqueeze(1).to_broadcast([128, 3, 4]): h.tensor_tensor(o, a, m, ALU.mult),
                  r=("ww", "den"), w=("cc",))
            P.add("dve", lambda h, o=C.tmpO, a=oview[:, :, :, 0:64], m=C.cc.unsqueeze(3).to_broadcast([128, 3, 4, 64]): h.tensor_tensor(o, a, m, ALU.mult),
                  r=(("attin", sl), "cc"), w=("tmpO",))
            def _red(h, o=C.attb[sl], a=C.tmpO.rearrange("p g h e -> p h e g")):
                with C.nc.allow_low_precision(reason="3-term sum on the fp32 ALU, rounded once to bf16 for the matmul"):
                    return h.tensor_reduce(o, a, AX.X, ALU.add)
            P.add("dve", _red, r=("tmpO",), w=(("attb", sl),))
            ab = C.attb[sl].rearrange("p h e -> p (h e)")
            pb2 = (j + 1) % 2
            pst2 = C.psT[pb2]
            for cc_ in range(2):
                _tr(P, pst2[:, cc_ * 128:(cc_ + 1) * 128], ab[:, cc_ * 128:(cc_ + 1) * 128], C.ident, r=(("attb", sl), "ident"), w=(("psT", pb2),))
            _act(P, C.attT[:, :, s * 128:(s + 1) * 128], pst2[:, 0:256].rearrange("p (k t) -> p k t", k=2), AF.Copy,
                 r=(("psT", pb2),), w=(("attT", s),))
        sguk = [("sguT", s) for s in range(4)]
        attk = [("attT", s) for s in range(4)]
        for oc in range(8):
            bga, kga = prs.next()
            for kc in range(KC):
                _mm(P, bga, C.wrest[:, kc, 1024 + oc * 128:1024 + (oc + 1) * 128], hT[:, kc, :], kc == 0, kc == KC - 1, r=hk + ["wrest"], w=(kga,))
            bpa, kpa = prs.next()
            for kc in range(2):
                _mm(P, bpa, C.watt[:, kc, oc * 128:(oc + 1) * 128], C.attT[:, kc, :], kc == 0, kc == 1, r=attk + ["watt"], w=(kpa,))
            bgb, kgb = prs.next()
            for kc in range(KC):
                _mm(P, bgb, C.wrest[:, kc, 2048 + oc * 128:2048 + (oc + 1) * 128], hT[:, kc, :], kc == 0, kc == KC - 1, r=hk + ["wrest"], w=(kgb,))
            bpb, kpb = prs.next()
            for kc in range(4):
                _mm(P, bpb, C.wsgu[:, kc, oc * 128:(oc + 1) * 128], C.sguT[:, kc, :], kc == 0, kc == 3, r=sguk + ["wsgu"], w=(kpb,))
            o2 = oc % 2
            _act(P, C.sa[o2], bga, AF.Sigmoid, r=(kga,), w=(("sa", o2),))
            _act(P, C.sb_[o2], bgb, AF.Sigmoid, r=(kgb,), w=(("sb", o2),))
            P.add("dve", lambda h, o=C.sa[o2], b=bpa: h.tensor_tensor(o, o, b, ALU.mult), r=(("sa", o2), kpa), w=(("sa", o2),))
            P.add("dve", lambda h, o=C.sb_[o2], b=bpb: h.tensor_tensor(o, o, b, ALU.mult), r=(("sb", o2), kpb), w=(("sb", o2),))
            P.add("dve", lambda h, o=C.mT[:, oc, :], a=C.sa[o2], b=C.sb_[o2]: h.tensor_tensor(o, a, b, ALU.add),
                  r=(("sa", o2), ("sb", o2)), w=(("mT", oc),))
        if i + 1 < ntiles:
            prologue(P, C, HALO, i + 1, C.gpre, "m")
        mk = [("mT", oc) for oc in range(8)]
        for s in range(4):
            j = i * 4 + s
            banks = []
            for n in range(2):
                b, k = prs.next()
                for kc in range(KC):
                    _mm(P, b, C.mT[:, kc, s * 128:(s + 1) * 128], C.wout[:, kc, n * 512:(n + 1) * 512], kc == 0, kc == KC - 1,
                        r=mk + ["wout"], w=(k,))
                banks.append((b, k))
            _dma(P, "sp", C.Xr, C.src[HALO + j * 128:HALO + (j + 1) * 128, :], r=(), w=("xr",))
            post_norm_residual(P, C, banks, C.gpost, j, final_gain=None, dst_rows=j * 128)


def build_program(cfg):
    nc = bass.Bass("TRN2", target_bir_lowering=False)
    dbg = cfg["debug"]
    phases = cfg["phases"]
    n_ext = cfg["n_ext_tiles"]
    n_own = cfg["n_own_tiles"]

    def din(name, shape):
        return nc.dram_tensor(name, list(shape), F32, kind="ExternalInput").ap()

    def dscr(name, shape, dt):
        if dbg:
            return nc.dram_tensor(name, list(shape), dt, kind="ExternalOutput").ap()
        return nc.dram_tensor(name, list(shape), dt).ap()

    I = {}
    I["xe"] = din("xe", (NEXT, D))
    for nm in ("w_gate1", "w_up1", "w_gate2", "w_up2"):
        I[nm] = din(nm, (D, DFF))
    for nm in ("w_down1", "w_down2"):
        I[nm] = din(nm, (DFF, D))
    I["w_in"] = din("w_in", (D, 5376))
    I["w_att"] = din("w_att", (256, D))
    I["w_sgu"] = din("w_sgu", (512, D))
    I["w_out"] = din("w_out", (D, D))
    for nm in ("g1pre", "g1post", "gmpre", "gmpost", "g2pre", "g2post", "gfin"):
        I[nm] = din(nm, (1, D))
    I["ln_g"] = din("ln_g", (1, 512))
    I["ln_b"] = din("ln_b", (1, 512))
    I["w_s"] = din("w_s", (4, 128, 128))
    I["b_s"] = din("b_s", (4, 128))
    I["rel_bias"] = din("rel_bias", (32, 12))
    I["ident"] = din("ident", (128, 128))
    I["antiid"] = din("antiid", (128, 128))
    I["onehot"] = din("onehot", (32, 3 * 129))
    I["vseg"] = din("vseg", (128, 24 * 9))
    y = nc.dram_tensor("y", [NOWN, D], F32, kind="ExternalOutput").ap()
    x1 = dscr("x1s", (NEXT, D), F32)
    qkv = dscr("qkvs", (NEXT, QKVW), BF16)
    att = dscr("atts", (NOWN, 3 * ATTW), F32)
    x2 = dscr("x2s", (NOWN, D), F32)
    tscr = nc.dram_tensor("tscr", [12, 512], F32).ap()

    P = Prog(nc)
    with ExitStack() as es:
        wreg = es.enter_context(nc.sbuf_tensor("wreg", [128, WREG_ELEMS], BF16))
        areg = es.enter_context(nc.sbuf_tensor("areg", [128, ARENA_ELEMS], BF16))
        identt = es.enter_context(nc.sbuf_tensor("identb", [128, 128], BF16))
        ps = [es.enter_context(nc.psum_tensor("ps%d" % i, [128, 512], F32)) for i in range(8)]
        sems = {e: es.enter_context(nc.semaphore("sem_" + e)) for e in ("pe", "act", "dve", "pool")}
        dsems = {}
        for q in ("sp", "pool"):
            for i in range(P.ndma):
                dsems[(q, i)] = es.enter_context(nc.semaphore("dsem_%s%d" % (q, i)))

        C = Ctx()
        C.nc = nc
        C.wreg = wreg[:, :]
        C.ident = identt[:, :]
        A = Arena(areg[:, :])
        _dma(P, "pool", C.ident, I["ident"], r=(), w=("ident",))
        epst = es.enter_context(nc.sbuf_tensor("epst", [128, 1], F32))
        C_EPS[0] = epst[:, :]
        P.add("pool", lambda h: h.memset(epst[:, :], EPS), r=(), w=("eps",))
        P.barrier()

        def ffn_setup(src, dst, gpre, gpost, gfin):
            A.reset()
            C.src, C.dst = src, dst
            C.nx = 2
            C.X = [A.alloc((D,), F32) for _ in range(C.nx)]
            C.Xr = A.alloc((D,), F32)
            C.xn = [A.alloc((D,), BF16) for _ in range(2)]
            C.hT = [A.alloc((KC, TT), BF16) for _ in range(2)]
            C.hid = A.alloc((NFC, TT), BF16)
            C.sg = [A.alloc((TT,), BF16) for _ in range(2)]
            C.t = A.alloc((D,), F32)
            C.junk = A.alloc((D,), BF16)
            C.ss = A.alloc((4,), F32)
            C.rs = A.alloc((4,), F32)
            C.ssy = A.alloc((8,), F32)
            C.gpre = A.alloc((D,), F32)
            C.gpost = A.alloc((D,), F32)
            load_bcast(P, C, C.gpre, gpre, D, "gains")
            load_bcast(P, C, C.gpost, gpost, D, "gains", scale=0.5)
            C.gfin = None
            if gfin is not None:
                C.gfin = A.alloc((D,), F32)
                load_bcast(P, C, C.gfin, gfin, D, "gains")
            C.psT = [ps[6][:, :].bitcast(BF16), ps[7][:, :].bitcast(BF16)]
            C.psrr = PsumRR([ps[i][:, :] for i in range(6)], [("ps", i) for i in range(6)])

        last_pe = None
        if "A1" in phases:
            load_ffn_weights(P, C, I["w_gate1"], I["w_up1"], I["w_down1"], extra=())
            ffn_setup(I["xe"], x1, I["g1pre"], I["g1post"], None)
            ffn_phase(P, C, n_ext, C.gpre, C.gpost, None)
            last_pe = P.last("pe")
            P.barrier()


        def common_setup(src, dst, gpre, gpost, gpost_scale):
            A.reset()
            C.A = A
            C.ps = ps
            C.src, C.dst = src, dst
            C.nx = 2
            C.X = [A.alloc((D,), F32) for _ in range(C.nx)]
            C.xn = [A.alloc((D,), BF16) for _ in range(2)]
            C.hT = [A.alloc((KC, TT), BF16) for _ in range(2)]
            C.junk = A.alloc((D,), BF16)
            C.ss = A.alloc((4,), F32)
            C.rs = A.alloc((4,), F32)
            C.ssy = A.alloc((8,), F32)
            C.gpre = A.alloc((D,), F32)
            load_bcast(P, C, C.gpre, gpre, D, "gains")
            if gpost is not None:
                C.gpost = A.alloc((D,), F32)
                load_bcast(P, C, C.gpost, gpost, D, "gains", scale=gpost_scale)
                C.Xr = A.alloc((D,), F32)
                C.t = A.alloc((D,), F32)
            C.psT = [ps[6][:, :].bitcast(BF16), ps[7][:, :].bitcast(BF16)]
            C.psrr = PsumRR([ps[i][:, :] for i in range(6)], [("ps", i) for i in range(6)])

        if "A2" in phases:
            ex = (last_pe,) if last_pe is not None else ()
            C.wqkv = C.wreg[:, 0:KC * 2304].rearrange("p (k n) -> p k n", k=KC)
            wv = I["w_in"].rearrange("(k p) n -> p k n", p=128)
            for hh in range(2):
                _dma(P, "pool", C.wqkv[:, hh * 4:(hh + 1) * 4, :], wv[:, hh * 4:(hh + 1) * 4, 0:2304], r=(), w=("wqkv",), extra=ex)
            common_setup(x1, qkv, I["gmpre"], None, None)
            C.stage = [A.alloc((QKVW,), BF16) for _ in range(2)]
            for sb in range(2):
                sv = C.stage[sb][:, 1536:2316].rearrange("p (h e) -> p h e", e=65)
                P.add("pool", lambda h, o=sv[:, :, 64:65]: h.memset(o, 1.0), r=(), w=[("stage", sb, c0) for (c0, _, _) in QKV_BLOCKS])
            qkv_phase(P, C, n_ext, C.gpre)
            last_pe = P.last("pe")
            P.barrier()

        if "ATT" in phases:
            A.reset()
            C.A = A
            C.ps = ps
            C.Kraw = [A.alloc((9, 256), BF16) for _ in range(2)]
            C.Vt = [A.alloc((9, 260), BF16) for _ in range(2)]
            C.Qraw = [A.alloc((8, 256), BF16) for _ in range(2)]
            C.KT = [A.alloc((2, 9 * 128), BF16) for _ in range(2)]
            C.QT = [A.alloc((2, 8 * 128), BF16) for _ in range(2)]
            C.bias = A.alloc((12, 256), F32)
            C.Ssb = [A.alloc((256,), F32) for _ in range(3)]
            C.Pb = [A.alloc((256,), BF16) for _ in range(3)]
            C.PTs = [A.alloc((256,), BF16) for _ in range(3)]
            C.ostage = [A.alloc((ATTW,), F32) for _ in range(2)]
            C.vseg = A.alloc((24 * 9,), F32)
            C.negm = A.alloc((4,), F32)
            _dma(P, "sp", C.vseg, I["vseg"], r=(), w=("vseg",))
            build_bias(P, C, I, tscr)
            att_phase(P, C, qkv, att)
            last_pe = P.last("pe")
            P.barrier()

        if "B1" in phases:
            ex = (last_pe,) if last_pe is not None else ()
            W = C.wreg
            C.wrest = W[:, 0:24576].rearrange("p (k n) -> p k n", k=KC)
            C.watt = W[:, 24576:26624].rearrange("p (k n) -> p k n", k=2)
            C.wsgu = W[:, 26624:30720].rearrange("p (k n) -> p k n", k=4)
            C.wout = W[:, 30720:38912].rearrange("p (k n) -> p k n", k=KC)
            wv = I["w_in"].rearrange("(k p) n -> p k n", p=128)
            for hh in range(2):
                _dma(P, "pool", C.wrest[:, hh * 4:(hh + 1) * 4, :], wv[:, hh * 4:(hh + 1) * 4, 2304:5376], r=(), w=("wrest",), extra=ex)
            _dma(P, "pool", C.watt, I["w_att"].rearrange("(k p) n -> p k n", p=128), r=(), w=("watt",), extra=ex)
            _dma(P, "pool", C.wsgu, I["w_sgu"].rearrange("(k p) n -> p k n", p=128), r=(), w=("wsgu",), extra=ex)
            _dma(P, "pool", C.wout, I["w_out"].rearrange("(k p) n -> p k n", p=128), r=(), w=("wout",), extra=ex)
            common_setup(x1, x2, I["gmpre"], I["gmpost"], None)
            A2 = Arena(W[:, 38912:WREG_ELEMS])
            C.U = [A2.alloc((512,), BF16) for _ in range(2)]
            C.V = [A2.alloc((512,), F32) for _ in range(2)]
            C.st = A2.alloc((2, 4), F32)
            C.tn = A2.alloc((512,), F32)
            C.vln = [A2.alloc((512,), BF16) for _ in range(2)]
            C.sgu = [A2.alloc((512,), BF16) for _ in range(2)]
            C.sguT = A2.alloc((4, TT), BF16)
            C.attin = [A2.alloc((3, ATTW), F32) for _ in range(2)]
            C.M = A2.alloc((4,), F32)
            C.dd = A2.alloc((3, 4), F32)
            C.ww = A2.alloc((3, 4), F32)
            C.wd_ = A2.alloc((3, 4), F32)
            C.cc = A2.alloc((3, 4), F32)
            C.den = A2.alloc((4,), F32)
            C.tmpO = A2.alloc((3, 4, 64), F32)
            C.attb = [A2.alloc((4, 64), BF16) for _ in range(2)]
            C.attT = A2.alloc((2, TT), BF16)
            C.sa = [A2.alloc((TT,), F32) for _ in range(2)]
            C.sb_ = [A2.alloc((TT,), F32) for _ in range(2)]
            C.mT = A.alloc((KC, TT), BF16)
            C.lng = A.alloc((512,), F32)
            C.lnb = A.alloc((512,), F32)
            load_bcast(P, C, C.lng, I["ln_g"], 512, "gains")
            load_bcast(P, C, C.lnb, I["ln_b"], 512, "gains")
            C.WsT = A.alloc((4, 128), BF16)
            C.bs = A.alloc((4,), F32)
            Wn = A2.alloc((4, 128), F32)
            Wnb = A2.alloc((4, 128), BF16)
            _dma(P, "sp", Wn, I["w_s"].rearrange("g p q -> p g q"), r=(), w=("Wn",))
            _dma(P, "sp", C.bs, I["b_s"].rearrange("g p -> p g"), r=(), w=("bs",), slow=True)
            P.add("dve", lambda h: h.tensor_copy(Wnb, Wn), r=("Wn",), w=("Wnb",))
            for gg in range(4):
                _tr(P, C.psT[0][:, gg * 128:(gg + 1) * 128], Wnb[:, gg, :], C.ident, r=("Wnb", "ident"), w=(("psT", 0),))
            _act(P, C.WsT, C.psT[0][:, 0:512].rearrange("p (g q) -> p g q", g=4), AF.Copy, r=(("psT", 0),), w=("WsT",))
            b1_phase(P, C, n_own, att)
            last_pe = P.last("pe")
            P.barrier()

        if "B2" in phases:
            load_ffn_weights(P, C, I["w_gate2"], I["w_up2"], I["w_down2"], extra=(last_pe,) if last_pe is not None else ())
            ffn_setup(x2, y, I["g2pre"], I["g2post"], I["gfin"])
            ffn_phase(P, C, n_own, C.gpre, C.gpost, C.gfin)
            P.barrier()

        with nc.Block() as block:
            P.emit(block, sems, dsems)
    return nc


def _t5_bucket(rel):
    nb = 16
    max_exact = 8
    ret = (rel > 0).astype(np.int32) * nb
    n = np.abs(rel)
    large = max_exact + (np.log(np.maximum(n, 1) / max_exact) / np.log(1024 / max_exact) * (nb - max_exact)).astype(np.int32)
    large = np.minimum(large, nb - 1)
    return ret + np.where(n < max_exact, n, large)


def segments():
    segs = []
    for g, (window, d) in enumerate(GROUPS):
        i_first = HALO // d
        nown = NOWN // d
        nblk_total = nown // 128
        per = 8 if nblk_total >= 8 else nblk_total
        for r in range(d):
            for s0 in range(0, nblk_total, per):
                segs.append((g, d, r, i_first + 128 * s0, per))
    return segs


def host_constants(seq_lo, seq_hi):
    ident = np.eye(128, dtype=np.float32)
    anti = np.ascontiguousarray(ident[::-1])
    onehot = np.zeros((32, 3, 129), np.float32)
    for g, (window, d) in enumerate(GROUPS):
        half = (window // 2) // d
        off = np.arange(-half, half + 1, dtype=np.int32) * d
        bk = _t5_bucket(off)
        onehot[bk, g, np.arange(129)] = 1.0
    segs = segments()
    vseg = np.zeros((128, 24, 9), np.float32)
    for si, (g, d, r, i0, nblk) in enumerate(segs):
        for j in range(nblk + 1):
            sub = i0 - 64 + 128 * j + np.arange(128)
            pos = d * sub + r
            vseg[:, si, j] = ((pos >= seq_lo) & (pos < seq_hi)).astype(np.float32)
    return ident, anti, onehot.reshape(32, 3 * 129), vseg.reshape(128, 24 * 9)


def make_in_maps(inp):
    f = lambda a: np.ascontiguousarray(np.asarray(a, dtype=np.float32))
    xp = f(inp["x_prompt"])[0]
    xs = f(inp["x_sample"])
    shared = {
        "w_gate1": f(inp["ffn1_w_gate"])[0], "w_up1": f(inp["ffn1_w_up"])[0], "w_down1": f(inp["ffn1_w_down"])[0],
        "w_gate2": f(inp["ffn2_w_gate"])[0], "w_up2": f(inp["ffn2_w_up"])[0], "w_down2": f(inp["ffn2_w_down"])[0],
        "w_in": f(inp["w_in"])[0], "w_att": f(inp["w_att"])[0], "w_sgu": f(inp["w_sgu"])[0], "w_out": f(inp["w_out"])[0],
        "g1pre": f(inp["ffn1_pre_g"]), "g1post": f(inp["ffn1_post_g"]), "gmpre": f(inp["mix_pre_g"]),
        "gmpost": f(inp["mix_post_g"]), "g2pre": f(inp["ffn2_pre_g"]), "g2post": f(inp["ffn2_post_g"]),
        "gfin": f(inp["final_g"]), "ln_g": f(inp["sgu_ln_g"]), "ln_b": f(inp["sgu_ln_b"]),
        "w_s": f(inp["sgu_w_s"])[0], "b_s": f(inp["sgu_b_s"])[0], "rel_bias": f(inp["rel_bias"]),
    }
    maps = []
    for c in range(NCORES):
        xe = np.zeros((NEXT, D), np.float32)
        if c < 4:
            lo = c * NOWN - HALO
            hi = lo + NEXT
            a, b = max(lo, 0), min(hi, xp.shape[0])
            xe[a - lo:b - lo] = xp[a:b]
            seq_lo, seq_hi = a - lo, b - lo
        else:
            xe[HALO:HALO + NOWN] = xs[c - 4]
            seq_lo, seq_hi = HALO, HALO + NOWN
        ident, anti, onehot, vseg = host_constants(seq_lo, seq_hi)
        m = dict(shared)
        m.update({"xe": xe, "ident": ident, "antiid": anti, "onehot": onehot, "vseg": vseg})
        maps.append(m)
    return maps


_NC_CACHE = {}


def kernel(**inputs):
    key = "full"
    if key not in _NC_CACHE:
        _NC_CACHE[key] = build_program(CFG)
    nc = _NC_CACHE[key]
    maps = make_in_maps(inputs)
    res = run_bass_kernel_spmd(nc, maps, core_ids=list(range(NCORES)))
    ys = [np.asarray(r["y"], dtype=np.float32) for r in res.results]
    y_prompt = np.concatenate(ys[0:4], axis=0)[None]
    y_sample = np.stack(ys[4:8], axis=0)
    return (y_prompt, y_sample)
```

```python
import numpy as np
import concourse.bass as bass
import concourse.mybir as mybir
from concourse.bass_utils import run_bass_kernel_spmd
from contextlib import ExitStack

F32 = mybir.dt.float32
BF16 = mybir.dt.bfloat16
AF = mybir.ActivationFunctionType
ALU = mybir.AluOpType
AX = mybir.AxisListType

NCORES = 8
D = 1024
DFF = 2816
NFC = 22
KC = 8
NEXT = 6144
NOWN = 4096
HALO = 1024
TT = 512
QKVW = 2316
ATTW = 264
EPS = 1e-6
NEG = -1e30
GROUPS = ((128, 1), (512, 4), (2048, 16))
WREG_ELEMS = 67584
ARENA_ELEMS = 38400

CFG = {"phases": ("A1", "A2", "ATT", "B1", "B2"), "debug": False, "n_ext_tiles": 12, "n_own_tiles": 8}


class Op:
    __slots__ = ("id", "eng", "fn", "deps", "dma", "dsem", "dval", "seq", "sig")


class Prog:
    ENGS = ("pe", "act", "dve", "pool", "sp")

    def __init__(self, nc, ndma=20):
        self.nc = nc
        self.ops = []
        self.byeng = {e: [] for e in self.ENGS}
        self.lastw = {}
        self.readers = {}
        self.ndma = ndma
        self.dma_rr = {"sp": 0, "pool": 0, "act": 0}
        self.dma_cnt = {}
        self.dma_last = {}
        self.pending = {e: set() for e in self.ENGS}

    def add(self, eng, fn, r=(), w=(), dma=False, extra=()):
        op = Op()
        op.id = len(self.ops)
        op.eng = eng
        op.fn = fn
        op.dma = dma
        op.sig = False
        op.seq = None
        deps = set(extra)
        psr = [k for k in r if isinstance(k, tuple) and k[0] in ("ps", "psT")]
        if psr:
            r = [k for k in r if k not in psr]
            w = list(w) + psr
        for k in r:
            lw = self.lastw.get(k)
            if lw is not None:
                deps.add(lw)
        for k in w:
            lw = self.lastw.get(k)
            if lw is not None:
                deps.add(lw)
            deps.update(self.readers.get(k, ()))
        if self.pending[eng]:
            deps.update(self.pending[eng])
            self.pending[eng] = set()
        if dma:
            i = self.dma_rr[eng]
            self.dma_rr[eng] = (i + 1) % self.ndma
            key = (eng, i)
            prev = self.dma_last.get(key)
            if prev is not None:
                deps.add(prev)
            self.dma_cnt[key] = self.dma_cnt.get(key, 0) + 1
            self.dma_last[key] = op.id
            op.dsem = key
            op.dval = 16 * self.dma_cnt[key]
        deps.discard(op.id)
        op.deps = deps
        for k in r:
            self.readers.setdefault(k, []).append(op.id)
        for k in w:
            self.lastw[k] = op.id
            self.readers[k] = []
        self.ops.append(op)
        self.byeng[eng].append(op)
        return op.id

    def last(self, eng):
        return self.byeng[eng][-1].id if self.byeng[eng] else None

    def barrier(self):
        ids = set()
        for e in self.ENGS:
            if self.byeng[e]:
                ids.add(self.byeng[e][-1].id)
        for key, oid in self.dma_last.items():
            ids.add(oid)
        for e in self.ENGS:
            self.pending[e] |= ids
        self.lastw = {}
        self.readers = {}

    def emit(self, block, sems, dsems):
        ops = self.ops
        for op in ops:
            for d in op.deps:
                dop = ops[d]
                if dop.dma:
                    continue
                if dop.eng == op.eng == "pe" and not op.dma:
                    continue
                dop.sig = True
        for e in self.ENGS:
            n = 0
            for op in self.byeng[e]:
                if op.sig and not op.dma:
                    n += 1
                    op.seq = n
        finals = {}
        for key, cnt in self.dma_cnt.items():
            finals[key] = 16 * cnt

        def run(eng, h):
            waited = {}
            for op in self.byeng[eng]:
                need = {}
                for d in op.deps:
                    dop = ops[d]
                    if dop.dma:
                        s, v = ("d",) + dop.dsem, dop.dval
                    else:
                        if dop.eng == eng == "pe" and not op.dma:
                            continue
                        s, v = ("e", dop.eng), dop.seq
                    if v > need.get(s, 0):
                        need[s] = v
                for s, v in need.items():
                    if waited.get(s, 0) >= v:
                        continue
                    waited[s] = v
                    sem = sems[s[1]] if s[0] == "e" else dsems[(s[1], s[2])]
                    h.wait_ge(sem, v)
                ins = op.fn(h)
                if op.dma:
                    ins.then_inc(dsems[op.dsem], 16)
                elif op.sig:
                    ins.then_inc(sems[eng], 1)
            if eng == "sp":
                for key, v in finals.items():
                    if waited.get(("d",) + key, 0) < v:
                        h.wait_ge(dsems[key], v)
                for e2 in ("pe", "act", "dve", "pool"):
                    n = max([o.seq or 0 for o in self.byeng[e2]] + [0])
                    if n:
                        h.wait_ge(sems[e2], n)

        @block.tensor
        def _(h):
            run("pe", h)

        @block.scalar
        def _(h):
            run("act", h)

        @block.vector
        def _(h):
            run("dve", h)

        @block.gpsimd
        def _(h):
            run("pool", h)

        @block.sync
        def _(h):
            run("sp", h)


class Arena:
    def __init__(self, base):
        self.base = base
        self.n = base.shape[1]
        self.off = 0

    def reset(self, off=0):
        self.off = off

    def alloc(self, free_shape, dtype, parts=128):
        n = int(np.prod(free_shape))
        ne = n * 2 if dtype == F32 else n
        self.off = (self.off + 15) // 16 * 16
        assert self.off + ne <= self.n, ("arena overflow", self.off, ne, self.n)
        ap = self.base[0:parts, self.off:self.off + ne]
        self.off += ne
        if dtype == F32:
            ap = ap.bitcast(F32)
        if len(free_shape) == 2:
            ap = ap.rearrange("p (a b) -> p a b", a=free_shape[0])
        elif len(free_shape) == 3:
            ap = ap.rearrange("p (a b c) -> p a b c", a=free_shape[0], b=free_shape[1])
        return ap


class Ctx:
    pass


def _mm(P, out, lhsT, rhs, start, stop, r, w):
    P.add("pe", lambda h, o=out, a=lhsT, b=rhs, s=start, t=stop: h.matmul(o, a, b, start=s, stop=t), r=r, w=w)


def _tr(P, out, in_, ident, r, w):
    P.add("pe", lambda h, o=out, a=in_, i=ident: h.transpose(o, a, i), r=r, w=w)


def _act(P, out, in_, func, r, w, bias=None, scale=None, accum=None):
    def fn(h, o=out, a=in_, f=func, b=bias, s=scale, acc=accum):
        kw = {}
        if b is not None:
            kw["bias"] = b
        if s is not None:
            kw["scale"] = s
        if acc is not None:
            kw["accum_out"] = acc
        return h.activation(o, a, f, **kw)
    P.add("act", fn, r=r, w=w)


def _dma(P, q, out, in_, r, w, extra=(), slow=False):
    if slow:
        return P.add(q, lambda h, o=out, a=in_: h.dma_start(out=o, in_=a, allow_slow_non_contiguous=True), r=r, w=w, dma=True, extra=extra)
    return P.add(q, lambda h, o=out, a=in_: h.dma_start(out=o, in_=a), r=r, w=w, dma=True, extra=extra)


class PsumRR:
    def __init__(self, banks, keys):
        self.banks = banks
        self.keys = keys
        self.i = 0

    def next(self):
        b, k = self.banks[self.i], self.keys[self.i]
        self.i = (self.i + 1) % len(self.banks)
        return b, k


def load_bcast(P, C, dst, vec_ap, n, key, scale=None):
    _dma(P, "sp", dst, vec_ap[0:1, :].to_broadcast([128, n]), r=(), w=(key,))
    if scale is not None:
        P.add("dve", lambda h, o=dst, s=scale: h.tensor_scalar_mul(o, o, float(s)), r=(key,), w=(key,))


def rms_rstd(P, ss, out, rk, wk, n=D):
    _act(P, ss, ss, AF.Sqrt, r=rk, w=rk, bias=C_EPS[0], scale=1.0 / n)
    P.add("dve", lambda h, o=out, a=ss: h.reciprocal(o, a), r=rk, w=wk)


C_EPS = [None]


def prologue(P, C, src_rows, tile, gain_b, tag):
    hb = tile % 2
    hT = C.hT[hb]
    for s in range(4):
        j = tile * 4 + s
        xs = j % C.nx
        X = C.X[xs]
        row0 = src_rows + j * 128
        _dma(P, "sp", X, C.src[row0:row0 + 128, :], r=(("src", j),), w=(("x", xs),))
        sc = j % 4
        _act(P, C.junk, X, AF.Square, r=(("x", xs),), w=(("ss", sc),), accum=C.ss[:, sc:sc + 1])
        rms_rstd(P, C.ss[:, sc:sc + 1], C.rs[:, sc:sc + 1], rk=(("ss", sc),), wk=(("rs", sc),))
        xb = j % 2
        P.add("dve", lambda h, o=C.xn[xb], a=X, sca=C.rs[:, sc:sc + 1], g=gain_b:
              h.scalar_tensor_tensor(o, a, sca, g, ALU.mult, ALU.mult),
              r=(("x", xs), ("rs", sc), "gains"), w=(("xn", xb),))
        pb = j % 2
        pst = C.psT[pb]
        for kc in range(KC):
            _tr(P, pst[:, kc * 128:(kc + 1) * 128], C.xn[xb][:, kc * 128:(kc + 1) * 128], C.ident,
                r=(("xn", xb), "ident"), w=(("psT", pb),))
        _act(P, hT[:, :, s * 128:(s + 1) * 128], pst.rearrange("p (k t) -> p k t", k=KC), AF.Copy,
             r=(("psT", pb),), w=(("hT", hb, s),))


def post_norm_residual(P, C, banks, gain_b, j, final_gain=None, dst_rows=None):
    X = C.Xr
    for n in range(2):
        b, k = banks[n]
        _act(P, C.junk[:, 0:512], b, AF.Square, r=(k,), w=(("ssy", n),), accum=C.ssy[:, n:n + 1])
        P.add("dve", lambda h, o=C.t[:, n * 512:(n + 1) * 512], a=b, g=gain_b[:, n * 512:(n + 1) * 512]:
              h.tensor_tensor(o, a, g, ALU.mult), r=(k, "gains"), w=(("t", n),))
    P.add("dve", lambda h, o=C.ssy[:, 2:3], a=C.ssy[:, 0:1], b2=C.ssy[:, 1:2]: h.tensor_tensor(o, a, b2, ALU.add),
          r=(("ssy", 0), ("ssy", 1)), w=(("ssy", 2),))
    rms_rstd(P, C.ssy[:, 2:3], C.ssy[:, 3:4], rk=(("ssy", 2),), wk=(("ssy", 3),))
    P.add("dve", lambda h, o=X, a=C.t, sca=C.ssy[:, 3:4], x=X: h.scalar_tensor_tensor(o, a, sca, x, ALU.mult, ALU.add),
          r=(("t", 0), ("t", 1), ("ssy", 3), "xr"), w=("xr",))
    outt = X
    outk = "xr"
    if final_gain is not None:
        _act(P, C.junk, X, AF.Square, r=("xr",), w=(("ssf", 0),), accum=C.ssy[:, 4:5])
        rms_rstd(P, C.ssy[:, 4:5], C.ssy[:, 5:6], rk=(("ssf", 0),), wk=(("ssf", 1),))
        P.add("dve", lambda h, o=C.t, a=X, sca=C.ssy[:, 5:6], g=final_gain: h.scalar_tensor_tensor(o, a, sca, g, ALU.mult, ALU.mult),
              r=("xr", ("ssf", 1), "gains", ("t", 0), ("t", 1)), w=(("t", 0), ("t", 1)))
        outt = C.t
        outk = ("t", 0)
    if CFG.get("stop", 99) < 6:
        return
    _dma(P, "sp", C.dst[dst_rows:dst_rows + 128, :], outt, r=(outk, ("t", 1)) if final_gain is not None else (outk,),
         w=(("dst", j),))


def load_ffn_weights(P, C, wg_d, wu_d, wd_d, extra):
    W = C.wreg
    C.wg = W[:, 0:KC * DFF].rearrange("p (k n) -> p k n", k=KC)
    C.wu = W[:, KC * DFF:2 * KC * DFF].rearrange("p (k n) -> p k n", k=KC)
    C.wd = W[:, 2 * KC * DFF:2 * KC * DFF + NFC * D].rearrange("p (c n) -> p c n", c=NFC)
    wgv = wg_d.rearrange("(k p) n -> p k n", p=128)
    wuv = wu_d.rearrange("(k p) n -> p k n", p=128)
    wdv = wd_d.rearrange("(c p) n -> p c n", p=128)
    NS = 4
    cs = DFF // NS
    for i in range(NS):
        _dma(P, "pool", C.wg[:, :, i * cs:(i + 1) * cs], wgv[:, :, i * cs:(i + 1) * cs], r=(), w=(("wg", i),), extra=extra)
        _dma(P, "pool", C.wu[:, :, i * cs:(i + 1) * cs], wuv[:, :, i * cs:(i + 1) * cs], r=(), w=(("wu", i),), extra=extra)
    for i in range(2):
        _dma(P, "pool", C.wd[:, i * 11:(i + 1) * 11, :], wdv[:, i * 11:(i + 1) * 11, :], r=(), w=(("wd", i),), extra=extra)
    C.wslice = cs


def ffn_phase(P, C, ntiles, gain_pre, gain_post, gain_fin):
    prs = C.psrr
    stop = CFG.get("stop", 99)
    if stop < 1:
        return
    prologue(P, C, 0, 0, gain_pre, "f")
    if stop < 2:
        return
    for i in range(ntiles):
        hT = C.hT[i % 2]
        hk = [("hT", i % 2, s) for s in range(4)]
        for c in range(NFC):
            wk = c * 128 // C.wslice
            bg, kg = prs.next()
            for kc in range(KC):
                _mm(P, bg, C.wg[:, kc, c * 128:(c + 1) * 128], hT[:, kc, :], kc == 0, kc == KC - 1,
                    r=hk + [("wg", wk)], w=(kg,))
            bu, ku = prs.next()
            for kc in range(KC):
                _mm(P, bu, C.wu[:, kc, c * 128:(c + 1) * 128], hT[:, kc, :], kc == 0, kc == KC - 1,
                    r=hk + [("wu", wk)], w=(ku,))
            sg = C.sg[c % 2]
            _act(P, sg, bg, AF.Silu, r=(kg,), w=(("sg", c % 2),))
            P.add("dve", lambda h, o=C.hid[:, c, :], a=sg, b=bu: h.tensor_tensor(o, a, b, ALU.mult),
                  r=(("sg", c % 2), ku), w=(("hid", c),))
        if stop < 3:
            return
        if i + 1 < ntiles:
            prologue(P, C, 0, i + 1, gain_pre, "f")
        for s in range(4):
            j = i * 4 + s
            banks = []
            for n in range(2):
                b, k = prs.next()
                for c in range(NFC):
                    _mm(P, b, C.hid[:, c, s * 128:(s + 1) * 128], C.wd[:, c, n * 512:(n + 1) * 512], c == 0, c == NFC - 1,
                        r=(("hid", c), ("wd", c // 11)), w=(k,))
                banks.append((b, k))
            if stop < 4:
                continue
            _dma(P, "sp", C.Xr, C.src[j * 128:(j + 1) * 128, :], r=(("src", j),), w=("xr",))
            if stop < 5:
                continue
            post_norm_residual(P, C, banks, gain_post, j, final_gain=gain_fin, dst_rows=j * 128)


QKV_BLOCKS = ((0, 512, "q"), (512, 768, "q"), (768, 1280, "k"), (1280, 1536, "k"), (1536, 2048, "v"), (2048, 2304, "v"))


def qkv_phase(P, C, ntiles, gain_pre):
    prs = C.psrr
    prologue(P, C, 0, 0, gain_pre, "q")
    ev = 0
    for i in range(ntiles):
        hT = C.hT[i % 2]
        if i + 1 < ntiles:
            prologue(P, C, 0, i + 1, gain_pre, "q")
        for s in range(4):
            j = i * 4 + s
            sb = j % 2
            stage = C.stage[sb]
            stage_v = stage[:, 1536:2316].rearrange("p (h e) -> p h e", e=65)
            for (c0, c1, kind) in QKV_BLOCKS:
                b, k = prs.next()
                n = c1 - c0
                for kc in range(KC):
                    _mm(P, b[:, 0:n], hT[:, kc, s * 128:(s + 1) * 128], C.wqkv[:, kc, c0:c1], kc == 0, kc == KC - 1,
                        r=(("hT", i % 2, s), "wqkv"), w=(k,))
                if kind == "v":
                    h0 = (c0 - 1536) // 64
                    nh = n // 64
                    out = stage_v[:, h0:h0 + nh, 0:64]
                    src = b[:, 0:n].rearrange("p (h e) -> p h e", e=64)
                else:
                    out = stage[:, c0:c1]
                    src = b[:, 0:n]
                scale = 0.125 if kind == "q" else 1.0
                if ev % 2 == 0:
                    _act(P, out, src, AF.Copy, r=(k,), w=(("stage", sb, c0),), scale=scale)
                else:
                    P.add("dve", lambda h, o=out, a=src, sc=scale: h.tensor_scalar_mul(o, a, float(sc)),
                          r=(k,), w=(("stage", sb, c0),))
                ev += 1
            _dma(P, "sp", C.dst[j * 128:(j + 1) * 128, :], stage, r=[("stage", sb, c0) for (c0, _, _) in QKV_BLOCKS],
                 w=(("dst", j),))


def build_bias(P, C, I, tscr):
    A = C.A
    rb = A.alloc((12,), F32, parts=32)
    E = A.alloc((3 * 129,), F32, parts=32)
    Tsb = A.alloc((3, 512), F32, parts=12)
    anti = A.alloc((128,), F32)
    R = [A.alloc((256,), F32) for _ in range(2)]
    _dma(P, "sp", rb, I["rel_bias"], r=(), w=("rb",))
    _dma(P, "sp", E, I["onehot"], r=(), w=("E",))
    _dma(P, "sp", anti, I["antiid"], r=(), w=("anti",))
    P.add("pool", lambda h: h.memset(Tsb, NEG), r=(), w=("Tsb",))
    b5 = C.ps[5][:, :]
    for g in range(3):
        _mm(P, b5[0:12, 0:129], rb[0:32, 0:12], E[0:32, g * 129:(g + 1) * 129], True, True, r=("rb", "E"), w=(("ps", 5),))
        P.add("dve", lambda h, o=Tsb[:, g, 127:256], a=b5[0:12, 0:129]: h.tensor_copy(o, a), r=(("ps", 5), "Tsb"), w=("Tsb",))
    for g in range(3):
        _dma(P, "sp", tscr[4 * g:4 * g + 4, :], Tsb[4 * g:4 * g + 4, g, :], r=("Tsb",), w=(("tscr", g),))
    for gh in range(12):
        Rt = R[gh % 2]
        src = bass.AP(tscr.tensor, gh * 512, [[1, 128], [1, 256]])
        _dma(P, "sp", Rt, src, r=(("tscr", gh // 4),), w=(("R", gh % 2),))
        b3, k3 = C.ps[3 + gh % 2][:, :], ("ps", 3 + gh % 2)
        _mm(P, b3[:, 0:256], anti, Rt, True, True, r=(("R", gh % 2), "anti"), w=(k3,))
        P.add("dve", lambda h, o=C.biasb[:, gh, :], a=b3[:, 0:256]: h.tensor_copy(o, a), r=(k3,), w=("biasb",))


def att_phase(P, C, qkv, att):
    segs = segments()
    psPT = [C.ps[4][:, :].bitcast(BF16), C.ps[5][:, :].bitcast(BF16)]
    Sb = [(C.ps[i][:, :], ("ps", i)) for i in range(3)]
    Ob = [(C.ps[6 + i][:, 0:260], ("ps", 6 + i)) for i in range(2)]
    psKQ = C.ps[3][:, :].bitcast(BF16)
    qt = qkv.tensor
    tcount = [0]
    units = []
    first_unit = {}
    bc = 0
    for si, (g, d, r, i0, nblk) in enumerate(segs):
        first_unit[si] = len(units)
        for b in range(nblk):
            for c in range(2):
                units.append((si, b, c, bc + b))
        bc += nblk
    N = len(units)

    def loads(si):
        g, d, r, i0, nblk = segs[si]
        rb = si % 4
        nt = nblk + 1
        rowk = d * (i0 - 64) + r
        rowq = d * i0 + r
        _dma(P, "sp", C.Kraw[rb][:, 0:nt, :], bass.AP(qt, rowk * QKVW + 768 + 256 * g, [[d * QKVW, 128], [128 * d * QKVW, nt], [1, 256]]),
             r=(), w=(("Kraw", rb),))
        _dma(P, "sp", C.Vt[rb][:, 0:nt, :], bass.AP(qt, rowk * QKVW + 1536 + 260 * g, [[d * QKVW, 128], [128 * d * QKVW, nt], [1, 260]]),
             r=(), w=(("Vt", rb),))
        _dma(P, "sp", C.Qraw[rb][:, 0:nblk, :], bass.AP(qt, rowq * QKVW + 256 * g, [[d * QKVW, 128], [128 * d * QKVW, nblk], [1, 256]]),
             r=(), w=(("Qraw", rb),))

    def transposes(si):
        g, d, r, i0, nblk = segs[si]
        rb, tb2 = si % 4, si % 2
        nt = nblk + 1
        for (raw, rk, dstT, dk, ntile) in ((C.Kraw[rb], ("Kraw", rb), C.KT[tb2], ("KT", tb2), nt),
                                          (C.Qraw[rb], ("Qraw", rb), C.QT[tb2], ("QT", tb2), nblk)):
            for j0 in range(0, ntile, 4):
                nj = min(4, ntile - j0)
                pk, pkk = psKQ, ("ps", 3)
                for c in range(2):
                    for jj in range(nj):
                        _tr(P, pk[:, c * 512 + jj * 128: c * 512 + (jj + 1) * 128], raw[:, j0 + jj, c * 128:(c + 1) * 128], C.ident,
                            r=(rk, "ident"), w=(pkk,))
                src = pk.rearrange("p (c t) -> p c t", c=2)[:, :, 0:nj * 128]
                _act(P, dstT[:, :, j0 * 128:(j0 + nj) * 128], src, AF.Copy, r=(pkk,), w=(dk,))

    def ctx(u):
        si, b, c, gb = units[u]
        g, d, r, i0, nblk = segs[si]
        return si, b, c, gb, g, d, r, i0

    def stage_S(u):
        si, b, c, gb, g, d, r, i0 = ctx(u)
        QT, KT = C.QT[si % 2], C.KT[si % 2]
        sbk, skey = Sb[u % 3]
        for hp in range(2):
            _mm(P, sbk[:, hp * 256:(hp + 1) * 256], QT[hp * 64:(hp + 1) * 64, c, b * 128:(b + 1) * 128],
                KT[hp * 64:(hp + 1) * 64, c, b * 128:b * 128 + 256], True, False, r=(("QT", si % 2), ("KT", si % 2)), w=(skey,))
            _mm(P, sbk[:, hp * 256:(hp + 1) * 256], C.ident, C.biasb[:, 4 * g + 2 * c + hp, :], False, True,
                r=("ident", "biasb"), w=(skey,))

    def stage_1(u):
        si, b, c, gb, g, d, r, i0 = ctx(u)
        sl = u % 4
        sbk, skey = Sb[u % 3]
        ost = C.ostage[gb % 3]
        ostk = ("ost", gb % 3)
        mcols = ost[:, 260 + 2 * c:262 + 2 * c]
        P.add("dve", lambda hh, o=mcols, a=sbk.rearrange("p (h k) -> p h k", h=2): hh.reduce_max(o, a, AX.X),
              r=(skey,), w=(ostk + (c,),))
        P.add("dve", lambda hh, o=C.negm[:, sl, :], a=mcols: hh.tensor_scalar_mul(o, a, -1.0), r=(ostk + (c,),), w=(("negm", sl),))
        for hp in range(2):
            _act(P, C.Pb[sl][:, hp, :], sbk[:, hp * 256:(hp + 1) * 256], AF.Exp, r=(skey, ("negm", sl)), w=(("Pb", sl, hp),),
                 bias=C.negm[:, sl, hp:hp + 1])

    def stage_2(u):
        sl = u % 4
        pt, ptk = psPT[u % 2], ("ps", 4 + u % 2)
        for hp in range(2):
            for jj in range(2):
                _tr(P, pt[:, (hp * 2 + jj) * 128:(hp * 2 + jj + 1) * 128], C.Pb[sl][:, hp, jj * 128:(jj + 1) * 128], C.ident,
                    r=(("Pb", sl, hp), "ident"), w=(ptk,))

    def stage_3(u):
        si, b, c, gb, g, d, r, i0 = ctx(u)
        sl = u % 4
        pt, ptk = psPT[u % 2], ("ps", 4 + u % 2)
        PTs = C.PTs[sl]
        ptv = pt[:, 0:512].rearrange("p (h j q) -> p h j q", h=2, j=2)
        for jj in range(2):
            vcol = C.vseg[:, si * 9 + b + jj: si * 9 + b + jj + 1]
            if jj == 0:
                _act(P, PTs[:, :, jj, :], ptv[:, :, jj, :], AF.Copy, r=(ptk, "vseg"), w=(("PTs", sl, jj),), scale=vcol)
            else:
                P.add("dve", lambda hh, o=PTs[:, :, jj, :], a=ptv[:, :, jj, :], sc=vcol: hh.tensor_scalar_mul(o, a, sc),
                      r=(ptk, "vseg"), w=(("PTs", sl, jj),))

    def stage_4(u):
        si, b, c, gb, g, d, r, i0 = ctx(u)
        sl = u % 4
        PTs = C.PTs[sl]
        Vt = C.Vt[si % 4]
        ob, okey = Ob[gb % 2]
        ost = C.ostage[gb % 3]
        ostk = ("ost", gb % 3)
        for hp in range(2):
            h = 2 * c + hp
            for jj in range(2):
                _mm(P, ob[:, h * 65:(h + 1) * 65], PTs[:, hp, jj, :], Vt[:, b + jj, h * 65:(h + 1) * 65], jj == 0, jj == 1,
                    r=(("PTs", sl, jj), ("Vt", si % 4)), w=(okey,))
        if c == 1:
            P.add("dve", lambda hh, o=ost[:, 0:260], a=ob: hh.tensor_copy(o, a), r=(okey,), w=(ostk + (9,),))
            t0 = d * (i0 + 128 * b) + r - HALO
            dst = bass.AP(att.tensor, (t0 * 3 + g) * ATTW, [[d * 3 * ATTW, 128], [1, ATTW]])
            _dma(P, "pool", dst, ost, r=[ostk + (x,) for x in (0, 1, 9)], w=(("att", si, b),))

    nseg = len(segs)
    loads(0)
    loads(1)
    loads(2)
    loads(3)
    transposes(0)
    seg_started = set()
    for u in range(min(2, N)):
        stage_S(u)
    for t in range(N + 4):
        if t < N:
            si = units[t][0]
            if si not in seg_started and first_unit[si] == t:
                seg_started.add(si)
                if si + 1 < nseg:
                    transposes(si + 1)
        if t + 2 < N:
            stage_S(t + 2)
        if t < N:
            stage_1(t)
        if 0 <= t - 1 < N:
            stage_2(t - 1)
        if 0 <= t - 2 < N:
            stage_3(t - 2)
        if 0 <= t - 3 < N:
            stage_4(t - 3)
            si3 = units[t - 3][0]
            if t - 3 + 1 == N or units[t - 3 + 1][0] != si3:
                if si3 + 4 < nseg:
                    loads(si3 + 4)


def b1_phase(P, C, ntiles, att):
    prs = C.psrr
    prologue(P, C, HALO, 0, C.gpre, "m")
    for i in range(ntiles):
        hT = C.hT[i % 2]
        hk = [("hT", i % 2, s) for s in range(4)]
        for s in range(4):
            j = i * 4 + s
            sl = j % 2
            bu, ku = prs.next()
            for kc in range(KC):
                _mm(P, bu, hT[:, kc, s * 128:(s + 1) * 128], C.wrest[:, kc, 0:512], kc == 0, kc == KC - 1, r=(hk[s], "wrest"), w=(ku,))
            bv, kv = prs.next()
            for kc in range(KC):
                _mm(P, bv, hT[:, kc, s * 128:(s + 1) * 128], C.wrest[:, kc, 512:1024], kc == 0, kc == KC - 1, r=(hk[s], "wrest"), w=(kv,))
            U, V = C.U[sl], C.V[sl]
            _act(P, U, bu, AF.Gelu, r=(ku,), w=(("U", sl),))
            _act(P, V, bv, AF.Gelu, r=(kv,), w=(("V", sl), ("st", sl, 0)), accum=C.st[:, sl, 0:1])
            P.add("dve", lambda h, o=C.st[:, sl, 1:2], a=C.st[:, sl, 0:1]: h.tensor_scalar_mul(o, a, -1.0 / 512.0),
                  r=(("st", sl, 0),), w=(("st", sl, 1),))
            _act(P, C.junk[:, 0:512], V, AF.Square, r=(("V", sl), ("st", sl, 1)), w=(("st", sl, 2),), bias=C.st[:, sl, 1:2],
                 accum=C.st[:, sl, 2:3])
            rms_rstd(P, C.st[:, sl, 2:3], C.st[:, sl, 3:4], rk=(("st", sl, 2),), wk=(("st", sl, 3),), n=512)
            P.add("dve", lambda h, o=C.tn, a=V, s1=C.st[:, sl, 1:2], s2=C.st[:, sl, 3:4]: h.tensor_scalar(o, a, s1, s2, ALU.add, ALU.mult),
                  r=(("V", sl), ("st", sl, 1), ("st", sl, 3)), w=("tn",))
            P.add("dve", lambda h, o=C.tn, g=C.lng: h.tensor_tensor(o, o, g, ALU.mult), r=("tn", "gains"), w=("tn",))
            P.add("dve", lambda h, o=C.vln[sl], a=C.tn, g=C.lnb: h.tensor_tensor(o, a, g, ALU.add), r=("tn", "gains"), w=(("vln", sl),))
            bm, km = prs.next()
            for gg in range(4):
                _mm(P, bm[:, gg * 128:(gg + 1) * 128], C.WsT[:, gg, :], C.vln[sl][:, gg * 128:(gg + 1) * 128], True, True,
                    r=(("vln", sl), "WsT"), w=(km,))
            for gg in range(4):
                P.add("dve", lambda h, o=C.sgu[sl][:, gg * 128:(gg + 1) * 128], a=bm[:, gg * 128:(gg + 1) * 128], sc=C.bs[:, gg:gg + 1],
                      u=U[:, gg * 128:(gg + 1) * 128]: h.scalar_tensor_tensor(o, a, sc, u, ALU.add, ALU.mult),
                      r=(km, ("U", sl), "bs"), w=(("sgu", sl),))
            pb = j % 2
            pst = C.psT[pb]
            for gg in range(4):
                _tr(P, pst[:, gg * 128:(gg + 1) * 128], C.sgu[sl][:, gg * 128:(gg + 1) * 128], C.ident, r=(("sgu", sl), "ident"), w=(("psT", pb),))
            _act(P, C.sguT[:, :, s * 128:(s + 1) * 128], pst[:, 0:512].rearrange("p (k t) -> p k t", k=4), AF.Copy,
                 r=(("psT", pb),), w=(("sguT", s),))
            AI = C.attin[sl]
            _dma(P, "sp", AI, att[j * 128:(j + 1) * 128, :].rearrange("p (g e) -> p g e", g=3), r=(), w=(("attin", sl),))
            mview = AI[:, :, 260:264]
            oview = AI[:, :, 0:260].rearrange("p g (h e) -> p g h e", e=65)
            P.add("dve", lambda h, o=C.M, a=mview.rearrange("p g h -> p h g"): h.tensor_reduce(o, a, AX.X, ALU.max),
                  r=(("attin", sl),), w=("M",))
            P.add("dve", lambda h, o=C.dd, a=mview, m=C.M.unsqueeze(1).to_broadcast([128, 3, 4]): h.tensor_tensor(o, a, m, ALU.subtract),
                  r=(("attin", sl), "M"), w=("dd",))
            _act(P, C.ww, C.dd, AF.Exp, r=("dd",), w=("ww",))
            P.add("dve", lambda h, o=C.wd_, a=C.ww, sv=oview[:, :, :, 64]: h.tensor_tensor(o, a, sv, ALU.mult),
                  r=("ww", ("attin", sl)), w=("wd",))
            P.add("dve", lambda h, o=C.den, a=C.wd_.rearrange("p g h -> p h g"): h.tensor_reduce(o, a, AX.X, ALU.add), r=("wd",), w=("den",))
            P.add("dve", lambda h, o=C.den: h.reciprocal(o, o), r=("den",), w=("den",))
            P.add("dve", lambda h, o=C.cc, a=C.ww, m=C.den.unsqueeze(1).to_broadcast([128, 3, 4]): h.tensor_tensor(o, a, m, ALU.mult),
                  r=("ww", "den"), w=("cc",))
            P.add("dve", lambda h, o=C.tmpO, a=oview[:, :, :, 0:64], m=C.cc.unsqueeze(3).to_broadcast([128, 3, 4, 64]): h.tensor_tensor(o, a, m, ALU.mult),
                  r=(("attin", sl), "cc"), w=("tmpO",))
            def _red(h, o=C.attb[sl], a=C.tmpO.rearrange("p g h e -> p h e g")):
                with C.nc.allow_low_precision(reason="3-term sum on the fp32 ALU, rounded once to bf16 for the matmul"):
                    return h.tensor_reduce(o, a, AX.X, ALU.add)
            P.add("dve", _red, r=("tmpO",), w=(("attb", sl),))
            ab = C.attb[sl].rearrange("p h e -> p (h e)")
            pb2 = (j + 1) % 2
            pst2 = C.psT[pb2]
            for cc_ in range(2):
                _tr(P, pst2[:, cc_ * 128:(cc_ + 1) * 128], ab[:, cc_ * 128:(cc_ + 1) * 128], C.ident, r=(("attb", sl), "ident"), w=(("psT", pb2),))
            _act(P, C.attT[:, :, s * 128:(s + 1) * 128], pst2[:, 0:256].rearrange("p (k t) -> p k t", k=2), AF.Copy,
                 r=(("psT", pb2),), w=(("attT", s),))
        sguk = [("sguT", s) for s in range(4)]
        attk = [("attT", s) for s in range(4)]
        for oc in range(8):
            bga, kga = prs.next()
            for kc in range(KC):
                _mm(P, bga, C.wrest[:, kc, 1024 + oc * 128:1024 + (oc + 1) * 128], hT[:, kc, :], kc == 0, kc == KC - 1, r=hk + ["wrest"], w=(kga,))
            bpa, kpa = prs.next()
            for kc in range(2):
                _mm(P, bpa, C.watt[:, kc, oc * 128:(oc + 1) * 128], C.attT[:, kc, :], kc == 0, kc == 1, r=attk + ["watt"], w=(kpa,))
            bgb, kgb = prs.next()
            for kc in range(KC):
                _mm(P, bgb, C.wrest[:, kc, 2048 + oc * 128:2048 + (oc + 1) * 128], hT[:, kc, :], kc == 0, kc == KC - 1, r=hk + ["wrest"], w=(kgb,))
            bpb, kpb = prs.next()
            for kc in range(4):
                _mm(P, bpb, C.wsgu[:, kc, oc * 128:(oc + 1) * 128], C.sguT[:, kc, :], kc == 0, kc == 3, r=sguk + ["wsgu"], w=(kpb,))
            o2 = oc % 2
            _act(P, C.sa[o2], bga, AF.Sigmoid, r=(kga,), w=(("sa", o2),))
            _act(P, C.sb_[o2], bgb, AF.Sigmoid, r=(kgb,), w=(("sb", o2),))
            P.add("dve", lambda h, o=C.sa[o2], b=bpa: h.tensor_tensor(o, o, b, ALU.mult), r=(("sa", o2), kpa), w=(("sa", o2),))
            P.add("dve", lambda h, o=C.sb_[o2], b=bpb: h.tensor_tensor(o, o, b, ALU.mult), r=(("sb", o2), kpb), w=(("sb", o2),))
            P.add("dve", lambda h, o=C.mT[:, oc, :], a=C.sa[o2], b=C.sb_[o2]: h.tensor_tensor(o, a, b, ALU.add),
                  r=(("sa", o2), ("sb", o2)), w=(("mT", oc),))
        if i + 1 < ntiles:
            prologue(P, C, HALO, i + 1, C.gpre, "m")
        mk = [("mT", oc) for oc in range(8)]
        for s in range(4):
            j = i * 4 + s
            banks = []
            for n in range(2):
                b, k = prs.next()
                for kc in range(KC):
                    _mm(P, b, C.mT[:, kc, s * 128:(s + 1) * 128], C.wout[:, kc, n * 512:(n + 1) * 512], kc == 0, kc == KC - 1,
                        r=mk + ["wout"], w=(k,))
                banks.append((b, k))
            _dma(P, "sp", C.Xr, C.src[HALO + j * 128:HALO + (j + 1) * 128, :], r=(), w=("xr",))
            post_norm_residual(P, C, banks, C.gpost, j, final_gain=None, dst_rows=j * 128)


def build_program(cfg):
    nc = bass.Bass("TRN2", target_bir_lowering=False)
    dbg = cfg["debug"]
    phases = cfg["phases"]
    n_ext = cfg["n_ext_tiles"]
    n_own = cfg["n_own_tiles"]

    def din(name, shape):
        return nc.dram_tensor(name, list(shape), F32, kind="ExternalInput").ap()

    def dscr(name, shape, dt):
        if dbg:
            return nc.dram_tensor(name, list(shape), dt, kind="ExternalOutput").ap()
        return nc.dram_tensor(name, list(shape), dt).ap()

    I = {}
    I["xe"] = din("xe", (NEXT, D))
    for nm in ("w_gate1", "w_up1", "w_gate2", "w_up2"):
        I[nm] = din(nm, (D, DFF))
    for nm in ("w_down1", "w_down2"):
        I[nm] = din(nm, (DFF, D))
    I["w_in"] = din("w_in", (D, 5376))
    I["w_att"] = din("w_att", (256, D))
    I["w_sgu"] = din("w_sgu", (512, D))
    I["w_out"] = din("w_out", (D, D))
    for nm in ("g1pre", "g1post", "gmpre", "gmpost", "g2pre", "g2post", "gfin"):
        I[nm] = din(nm, (1, D))
    I["ln_g"] = din("ln_g", (1, 512))
    I["ln_b"] = din("ln_b", (1, 512))
    I["w_s"] = din("w_s", (4, 128, 128))
    I["b_s"] = din("b_s", (4, 128))
    I["rel_bias"] = din("rel_bias", (32, 12))
    I["ident"] = din("ident", (128, 128))
    I["antiid"] = din("antiid", (128, 128))
    I["onehot"] = din("onehot", (32, 3 * 129))
    I["vseg"] = din("vseg", (128, 24 * 9))
    y = nc.dram_tensor("y", [NOWN, D], F32, kind="ExternalOutput").ap()
    x1 = dscr("x1s", (NEXT, D), F32)
    qkv = dscr("qkvs", (NEXT, QKVW), BF16)
    att = dscr("atts", (NOWN, 3 * ATTW), F32)
    x2 = dscr("x2s", (NOWN, D), F32)
    tscr = nc.dram_tensor("tscr", [12, 512], F32).ap()

    P = Prog(nc)
    with ExitStack() as es:
        wreg = es.enter_context(nc.sbuf_tensor("wreg", [128, WREG_ELEMS], BF16))
        areg = es.enter_context(nc.sbuf_tensor("areg", [128, ARENA_ELEMS], BF16))
        identt = es.enter_context(nc.sbuf_tensor("identb", [128, 128], BF16))
        ps = [es.enter_context(nc.psum_tensor("ps%d" % i, [128, 512], F32)) for i in range(8)]
        sems = {e: es.enter_context(nc.semaphore("sem_" + e)) for e in ("pe", "act", "dve", "pool")}
        dsems = {}
        for q in ("sp", "pool"):
            for i in range(P.ndma):
                dsems[(q, i)] = es.enter_context(nc.semaphore("dsem_%s%d" % (q, i)))

        C = Ctx()
        C.nc = nc
        C.wreg = wreg[:, :]
        C.ident = identt[:, :]
        A = Arena(areg[:, :])
        _dma(P, "pool", C.ident, I["ident"], r=(), w=("ident",))
        epst = es.enter_context(nc.sbuf_tensor("epst", [128, 1], F32))
        C_EPS[0] = epst[:, :]
        P.add("pool", lambda h: h.memset(epst[:, :], EPS), r=(), w=("eps",))
        P.barrier()

        def ffn_setup(src, dst, gpre, gpost, gfin):
            A.reset()
            C.src, C.dst = src, dst
            C.nx = 2
            C.X = [A.alloc((D,), F32) for _ in range(C.nx)]
            C.Xr = A.alloc((D,), F32)
            C.xn = [A.alloc((D,), BF16) for _ in range(2)]
            C.hT = [A.alloc((KC, TT), BF16) for _ in range(2)]
            C.hid = A.alloc((NFC, TT), BF16)
            C.sg = [A.alloc((TT,), BF16) for _ in range(2)]
            C.t = A.alloc((D,), F32)
            C.junk = A.alloc((D,), BF16)
            C.ss = A.alloc((4,), F32)
            C.rs = A.alloc((4,), F32)
            C.ssy = A.alloc((8,), F32)
            C.gpre = A.alloc((D,), F32)
            C.gpost = A.alloc((D,), F32)
            load_bcast(P, C, C.gpre, gpre, D, "gains")
            load_bcast(P, C, C.gpost, gpost, D, "gains", scale=0.5)
            C.gfin = None
            if gfin is not None:
                C.gfin = A.alloc((D,), F32)
                load_bcast(P, C, C.gfin, gfin, D, "gains")
            C.psT = [ps[6][:, :].bitcast(BF16), ps[7][:, :].bitcast(BF16)]
            C.psrr = PsumRR([ps[i][:, :] for i in range(6)], [("ps", i) for i in range(6)])

        last_pe = None
        if "A1" in phases:
            load_ffn_weights(P, C, I["w_gate1"], I["w_up1"], I["w_down1"], extra=())
            ffn_setup(I["xe"], x1, I["g1pre"], I["g1post"], None)
            ffn_phase(P, C, n_ext, C.gpre, C.gpost, None)
            last_pe = P.last("pe")
            P.barrier()


        def common_setup(src, dst, gpre, gpost, gpost_scale):
            A.reset()
            C.A = A
            C.ps = ps
            C.src, C.dst = src, dst
            C.nx = 2
            C.X = [A.alloc((D,), F32) for _ in range(C.nx)]
            C.xn = [A.alloc((D,), BF16) for _ in range(2)]
            C.hT = [A.alloc((KC, TT), BF16) for _ in range(2)]
            C.junk = A.alloc((D,), BF16)
            C.ss = A.alloc((4,), F32)
            C.rs = A.alloc((4,), F32)
            C.ssy = A.alloc((8,), F32)
            C.gpre = A.alloc((D,), F32)
            load_bcast(P, C, C.gpre, gpre, D, "gains")
            if gpost is not None:
                C.gpost = A.alloc((D,), F32)
                load_bcast(P, C, C.gpost, gpost, D, "gains", scale=gpost_scale)
                C.Xr = A.alloc((D,), F32)
                C.t = A.alloc((D,), F32)
            C.psT = [ps[6][:, :].bitcast(BF16), ps[7][:, :].bitcast(BF16)]
            C.psrr = PsumRR([ps[i][:, :] for i in range(6)], [("ps", i) for i in range(6)])

        if "A2" in phases:
            ex = (last_pe,) if last_pe is not None else ()
            C.wqkv = C.wreg[:, 0:KC * 2304].rearrange("p (k n) -> p k n", k=KC)
            wv = I["w_in"].rearrange("(k p) n -> p k n", p=128)
            for hh in range(2):
                _dma(P, "pool", C.wqkv[:, hh * 4:(hh + 1) * 4, :], wv[:, hh * 4:(hh + 1) * 4, 0:2304], r=(), w=("wqkv",), extra=ex)
            common_setup(x1, qkv, I["gmpre"], None, None)
            C.stage = [A.alloc((QKVW,), BF16) for _ in range(2)]
            for sb in range(2):
                sv = C.stage[sb][:, 1536:2316].rearrange("p (h e) -> p h e", e=65)
                P.add("pool", lambda h, o=sv[:, :, 64:65]: h.memset(o, 1.0), r=(), w=[("stage", sb, c0) for (c0, _, _) in QKV_BLOCKS])
            qkv_phase(P, C, n_ext, C.gpre)
            last_pe = P.last("pe")
            P.barrier()

        if "ATT" in phases:
            A.reset()
            C.A = A
            C.ps = ps
            AT = Arena(C.wreg[:, 38912:WREG_ELEMS])
            C.Kraw = [AT.alloc((9, 256), BF16) for _ in range(4)]
            C.Vt = [AT.alloc((9, 260), BF16) for _ in range(4)]
            C.Qraw = [AT.alloc((8, 256), BF16) for _ in range(4)]
            C.KT = [A.alloc((2, 9 * 128), BF16) for _ in range(2)]
            C.QT = [A.alloc((2, 8 * 128), BF16) for _ in range(2)]
            C.biasb = A.alloc((12, 256), BF16)
            C.Pb = [A.alloc((2, 256), BF16) for _ in range(4)]
            C.PTs = [A.alloc((2, 2, 128), BF16) for _ in range(4)]
            C.ostage = [A.alloc((ATTW,), F32) for _ in range(3)]
            C.vseg = A.alloc((24 * 9,), F32)
            C.negm = A.alloc((4, 2), F32)
            _dma(P, "sp", C.vseg, I["vseg"], r=(), w=("vseg",))
            build_bias(P, C, I, tscr)
            att_phase(P, C, qkv, att)
            last_pe = P.last("pe")
            P.barrier()

        if "B1" in phases:
            ex = (last_pe,) if last_pe is not None else ()
            W = C.wreg
            C.wrest = W[:, 0:24576].rearrange("p (k n) -> p k n", k=KC)
            C.watt = W[:, 24576:26624].rearrange("p (k n) -> p k n", k=2)
            C.wsgu = W[:, 26624:30720].rearrange("p (k n) -> p k n", k=4)
            C.wout = W[:, 30720:38912].rearrange("p (k n) -> p k n", k=KC)
            wv = I["w_in"].rearrange("(k p) n -> p k n", p=128)
            for hh in range(2):
                _dma(P, "pool", C.wrest[:, hh * 4:(hh + 1) * 4, :], wv[:, hh * 4:(hh + 1) * 4, 2304:5376], r=(), w=("wrest",), extra=ex)
            _dma(P, "pool", C.watt, I["w_att"].rearrange("(k p) n -> p k n", p=128), r=(), w=("watt",), extra=ex)
            _dma(P, "pool", C.wsgu, I["w_sgu"].rearrange("(k p) n -> p k n", p=128), r=(), w=("wsgu",), extra=ex)
            _dma(P, "pool", C.wout, I["w_out"].rearrange("(k p) n -> p k n", p=128), r=(), w=("wout",), extra=ex)
            common_setup(x1, x2, I["gmpre"], I["gmpost"], None)
            A2 = Arena(W[:, 38912:WREG_ELEMS])
            C.U = [A2.alloc((512,), BF16) for _ in range(2)]
            C.V = [A2.alloc((512,), F32) for _ in range(2)]
            C.st = A2.alloc((2, 4), F32)
            C.tn = A2.alloc((512,), F32)
            C.vln = [A2.alloc((512,), BF16) for _ in range(2)]
            C.sgu = [A2.alloc((512,), BF16) for _ in range(2)]
            C.sguT = A2.alloc((4, TT), BF16)
            C.attin = [A2.alloc((3, ATTW), F32) for _ in range(2)]
            C.M = A2.alloc((4,), F32)
            C.dd = A2.alloc((3, 4), F32)
            C.ww = A2.alloc((3, 4), F32)
            C.wd_ = A2.alloc((3, 4), F32)
            C.cc = A2.alloc((3, 4), F32)
            C.den = A2.alloc((4,), F32)
            C.tmpO = A2.alloc((3, 4, 64), F32)
            C.attb = [A2.alloc((4, 64), BF16) for _ in range(2)]
            C.attT = A2.alloc((2, TT), BF16)
            C.sa = [A2.alloc((TT,), F32) for _ in range(2)]
            C.sb_ = [A2.alloc((TT,), F32) for _ in range(2)]
            C.mT = A.alloc((KC, TT), BF16)
            C.lng = A.alloc((512,), F32)
            C.lnb = A.alloc((512,), F32)
            load_bcast(P, C, C.lng, I["ln_g"], 512, "gains")
            load_bcast(P, C, C.lnb, I["ln_b"], 512, "gains")
            C.WsT = A.alloc((4, 128), BF16)
            C.bs = A.alloc((4,), F32)
            Wn = A2.alloc((4, 128), F32)
            Wnb = A2.alloc((4, 128), BF16)
            _dma(P, "sp", Wn, I["w_s"].rearrange("g p q -> p g q"), r=(), w=("Wn",))
            _dma(P, "sp", C.bs, I["b_s"].rearrange("g p -> p g"), r=(), w=("bs",), slow=True)
            P.add("dve", lambda h: h.tensor_copy(Wnb, Wn), r=("Wn",), w=("Wnb",))
            for gg in range(4):
                _tr(P, C.psT[0][:, gg * 128:(gg + 1) * 128], Wnb[:, gg, :], C.ident, r=("Wnb", "ident"), w=(("psT", 0),))
            _act(P, C.WsT, C.psT[0][:, 0:512].rearrange("p (g q) -> p g q", g=4), AF.Copy, r=(("psT", 0),), w=("WsT",))
            b1_phase(P, C, n_own, att)
            last_pe = P.last("pe")
            P.barrier()

        if "B2" in phases:
            load_ffn_weights(P, C, I["w_gate2"], I["w_up2"], I["w_down2"], extra=(last_pe,) if last_pe is not None else ())
            ffn_setup(x2, y, I["g2pre"], I["g2post"], I["gfin"])
            ffn_phase(P, C, n_own, C.gpre, C.gpost, C.gfin)
            P.barrier()

        with nc.Block() as block:
            P.emit(block, sems, dsems)
    return nc


def _t5_bucket(rel):
    nb = 16
    max_exact = 8
    ret = (rel > 0).astype(np.int32) * nb
    n = np.abs(rel)
    large = max_exact + (np.log(np.maximum(n, 1) / max_exact) / np.log(1024 / max_exact) * (nb - max_exact)).astype(np.int32)
    large = np.minimum(large, nb - 1)
    return ret + np.where(n < max_exact, n, large)


def segments():
    segs = []
    for g, (window, d) in enumerate(GROUPS):
        i_first = HALO // d
        nown = NOWN // d
        nblk_total = nown // 128
        per = 8 if nblk_total >= 8 else nblk_total
        for r in range(d):
            for s0 in range(0, nblk_total, per):
                segs.append((g, d, r, i_first + 128 * s0, per))
    return segs


def host_constants(seq_lo, seq_hi):
    ident = np.eye(128, dtype=np.float32)
    anti = np.ascontiguousarray(ident[::-1])
    onehot = np.zeros((32, 3, 129), np.float32)
    for g, (window, d) in enumerate(GROUPS):
        half = (window // 2) // d
        off = np.arange(-half, half + 1, dtype=np.int32) * d
        bk = _t5_bucket(off)
        onehot[bk, g, np.arange(129)] = 1.0
    segs = segments()
    vseg = np.zeros((128, 24, 9), np.float32)
    for si, (g, d, r, i0, nblk) in enumerate(segs):
        for j in range(nblk + 1):
            sub = i0 - 64 + 128 * j + np.arange(128)
            pos = d * sub + r
            vseg[:, si, j] = ((pos >= seq_lo) & (pos < seq_hi)).astype(np.float32)
    return ident, anti, onehot.reshape(32, 3 * 129), vseg.reshape(128, 24 * 9)


def make_in_maps(inp):
    f = lambda a: np.ascontiguousarray(np.asarray(a, dtype=np.float32))
    xp = f(inp["x_prompt"])[0]
    xs = f(inp["x_sample"])
    shared = {
        "w_gate1": f(inp["ffn1_w_gate"])[0], "w_up1": f(inp["ffn1_w_up"])[0], "w_down1": f(inp["ffn1_w_down"])[0],
        "w_gate2": f(inp["ffn2_w_gate"])[0], "w_up2": f(inp["ffn2_w_up"])[0], "w_down2": f(inp["ffn2_w_down"])[0],
        "w_in": f(inp["w_in"])[0], "w_att": f(inp["w_att"])[0], "w_sgu": f(inp["w_sgu"])[0], "w_out": f(inp["w_out"])[0],
        "g1pre": f(inp["ffn1_pre_g"]), "g1post": f(inp["ffn1_post_g"]), "gmpre": f(inp["mix_pre_g"]),
        "gmpost": f(inp["mix_post_g"]), "g2pre": f(inp["ffn2_pre_g"]), "g2post": f(inp["ffn2_post_g"]),
        "gfin": f(inp["final_g"]), "ln_g": f(inp["sgu_ln_g"]), "ln_b": f(inp["sgu_ln_b"]),
        "w_s": f(inp["sgu_w_s"])[0], "b_s": f(inp["sgu_b_s"])[0], "rel_bias": f(inp["rel_bias"]),
    }
    maps = []
    for c in range(NCORES):
        xe = np.zeros((NEXT, D), np.float32)
        if c < 4:
            lo = c * NOWN - HALO
            hi = lo + NEXT
            a, b = max(lo, 0), min(hi, xp.shape[0])
            xe[a - lo:b - lo] = xp[a:b]
            seq_lo, seq_hi = a - lo, b - lo
        else:
            xe[HALO:HALO + NOWN] = xs[c - 4]
            seq_lo, seq_hi = HALO, HALO + NOWN
        ident, anti, onehot, vseg = host_constants(seq_lo, seq_hi)
        m = dict(shared)
        m.update({"xe": xe, "ident": ident, "antiid": anti, "onehot": onehot, "vseg": vseg})
        maps.append(m)
    return maps


_NC_CACHE = {}


def kernel(**inputs):
    key = "full"
    if key not in _NC_CACHE:
        _NC_CACHE[key] = build_program(CFG)
    nc = _NC_CACHE[key]
    maps = make_in_maps(inputs)
    res = run_bass_kernel_spmd(nc, maps, core_ids=list(range(NCORES)))
    ys = [np.asarray(r["y"], dtype=np.float32) for r in res.results]
    y_prompt = np.concatenate(ys[0:4], axis=0)[None]
    y_sample = np.stack(ys[4:8], axis=0)
    return (y_prompt, y_sample)
```

```python
import numpy as np
import concourse.bass as bass
import concourse.mybir as mybir
from concourse.bass_utils import run_bass_kernel_spmd
from contextlib import ExitStack

F32 = mybir.dt.float32
BF16 = mybir.dt.bfloat16
AF = mybir.ActivationFunctionType
ALU = mybir.AluOpType
AX = mybir.AxisListType

NCORES = 8
D = 1024
DFF = 2816
NFC = 22
KC = 8
NEXT = 6144
NOWN = 4096
HALO = 1024
TT = 512
QKVW = 2316
ATTW = 264
EPS = 1e-6
NEG = -1e30
GROUPS = ((128, 1), (512, 4), (2048, 16))
WREG_ELEMS = 67584
ARENA_ELEMS = 38400

CFG = {"phases": ("A1", "A2", "ATT", "B1", "B2"), "debug": False, "n_ext_tiles": 12, "n_own_tiles": 8}


class Op:
    __slots__ = ("id", "eng", "fn", "deps", "dma", "dsem", "dval", "seq", "sig")


class Prog:
    ENGS = ("pe", "act", "dve", "pool", "sp")

    def __init__(self, nc, ndma=20):
        self.nc = nc
        self.ops = []
        self.byeng = {e: [] for e in self.ENGS}
        self.lastw = {}
        self.readers = {}
        self.ndma = ndma
        self.dma_rr = {"sp": 0, "pool": 0, "act": 0}
        self.dma_cnt = {}
        self.dma_last = {}
        self.pending = {e: set() for e in self.ENGS}

    def add(self, eng, fn, r=(), w=(), dma=False, extra=()):
        op = Op()
        op.id = len(self.ops)
        op.eng = eng
        op.fn = fn
        op.dma = dma
        op.sig = False
        op.seq = None
        deps = set(extra)
        psr = [k for k in r if isinstance(k, tuple) and k[0] in ("ps", "psT")]
        if psr:
            r = [k for k in r if k not in psr]
            w = list(w) + psr
        for k in r:
            lw = self.lastw.get(k)
            if lw is not None:
                deps.add(lw)
        for k in w:
            lw = self.lastw.get(k)
            if lw is not None:
                deps.add(lw)
            deps.update(self.readers.get(k, ()))
        if self.pending[eng]:
            deps.update(self.pending[eng])
            self.pending[eng] = set()
        if dma:
            i = self.dma_rr[eng]
            self.dma_rr[eng] = (i + 1) % self.ndma
            key = (eng, i)
            prev = self.dma_last.get(key)
            if prev is not None:
                deps.add(prev)
            self.dma_cnt[key] = self.dma_cnt.get(key, 0) + 1
            self.dma_last[key] = op.id
            op.dsem = key
            op.dval = 16 * self.dma_cnt[key]
        deps.discard(op.id)
        op.deps = deps
        for k in r:
            self.readers.setdefault(k, []).append(op.id)
        for k in w:
            self.lastw[k] = op.id
            self.readers[k] = []
        self.ops.append(op)
        self.byeng[eng].append(op)
        return op.id

    def last(self, eng):
        return self.byeng[eng][-1].id if self.byeng[eng] else None

    def barrier(self):
        ids = set()
        for e in self.ENGS:
            if self.byeng[e]:
                ids.add(self.byeng[e][-1].id)
        for key, oid in self.dma_last.items():
            ids.add(oid)
        for e in self.ENGS:
            self.pending[e] |= ids
        self.lastw = {}
        self.readers = {}

    def emit(self, block, sems, dsems):
        ops = self.ops
        for op in ops:
            for d in op.deps:
                dop = ops[d]
                if dop.dma:
                    continue
                if dop.eng == op.eng == "pe" and not op.dma:
                    continue
                dop.sig = True
        for e in self.ENGS:
            n = 0
            for op in self.byeng[e]:
                if op.sig and not op.dma:
                    n += 1
                    op.seq = n
        finals = {}
        for key, cnt in self.dma_cnt.items():
            finals[key] = 16 * cnt

        def run(eng, h):
            waited = {}
            for op in self.byeng[eng]:
                need = {}
                for d in op.deps:
                    dop = ops[d]
                    if dop.dma:
                        s, v = ("d",) + dop.dsem, dop.dval
                    else:
                        if dop.eng == eng == "pe" and not op.dma:
                            continue
                        s, v = ("e", dop.eng), dop.seq
                    if v > need.get(s, 0):
                        need[s] = v
                for s, v in need.items():
                    if waited.get(s, 0) >= v:
                        continue
                    waited[s] = v
                    sem = sems[s[1]] if s[0] == "e" else dsems[(s[1], s[2])]
                    h.wait_ge(sem, v)
                ins = op.fn(h)
                if op.dma:
                    ins.then_inc(dsems[op.dsem], 16)
                elif op.sig:
                    ins.then_inc(sems[eng], 1)
            if eng == "sp":
                for key, v in finals.items():
                    if waited.get(("d",) + key, 0) < v:
                        h.wait_ge(dsems[key], v)
                for e2 in ("pe", "act", "dve", "pool"):
                    n = max([o.seq or 0 for o in self.byeng[e2]] + [0])
                    if n:
                        h.wait_ge(sems[e2], n)

        @block.tensor
        def _(h):
            run("pe", h)

        @block.scalar
        def _(h):
            run("act", h)

        @block.vector
        def _(h):
            run("dve", h)

        @block.gpsimd
        def _(h):
            run("pool", h)

        @block.sync
        def _(h):
            run("sp", h)


class Arena:
    def __init__(self, base):
        self.base = base
        self.n = base.shape[1]
        self.off = 0

    def reset(self, off=0):
        self.off = off

    def alloc(self, free_shape, dtype, parts=128):
        n = int(np.prod(free_shape))
        ne = n * 2 if dtype == F32 else n
        self.off = (self.off + 15) // 16 * 16
        assert self.off + ne <= self.n, ("arena overflow", self.off, ne, self.n)
        ap = self.base[0:parts, self.off:self.off + ne]
        self.off += ne
        if dtype == F32:
            ap = ap.bitcast(F32)
        if len(free_shape) == 2:
            ap = ap.rearrange("p (a b) -> p a b", a=free_shape[0])
        elif len(free_shape) == 3:
            ap = ap.rearrange("p (a b c) -> p a b c", a=free_shape[0], b=free_shape[1])
        return ap


class Ctx:
    pass


def _mm(P, out, lhsT, rhs, start, stop, r, w):
    P.add("pe", lambda h, o=out, a=lhsT, b=rhs, s=start, t=stop: h.matmul(o, a, b, start=s, stop=t), r=r, w=w)


def _tr(P, out, in_, ident, r, w):
    P.add("pe", lambda h, o=out, a=in_, i=ident: h.transpose(o, a, i), r=r, w=w)


def _act(P, out, in_, func, r, w, bias=None, scale=None, accum=None):
    def fn(h, o=out, a=in_, f=func, b=bias, s=scale, acc=accum):
        kw = {}
        if b is not None:
            kw["bias"] = b
        if s is not None:
            kw["scale"] = s
        if acc is not None:
            kw["accum_out"] = acc
        return h.activation(o, a, f, **kw)
    P.add("act", fn, r=r, w=w)


def _dma(P, q, out, in_, r, w, extra=(), slow=False):
    if slow:
        return P.add(q, lambda h, o=out, a=in_: h.dma_start(out=o, in_=a, allow_slow_non_contiguous=True), r=r, w=w, dma=True, extra=extra)
    return P.add(q, lambda h, o=out, a=in_: h.dma_start(out=o, in_=a), r=r, w=w, dma=True, extra=extra)


class PsumRR:
    def __init__(self, banks, keys):
        self.banks = banks
        self.keys = keys
        self.i = 0

    def next(self):
        b, k = self.banks[self.i], self.keys[self.i]
        self.i = (self.i + 1) % len(self.banks)
        return b, k


def load_bcast(P, C, dst, vec_ap, n, key, scale=None):
    _dma(P, "sp", dst, vec_ap[0:1, :].to_broadcast([128, n]), r=(), w=(key,))
    if scale is not None:
        P.add("dve", lambda h, o=dst, s=scale: h.tensor_scalar_mul(o, o, float(s)), r=(key,), w=(key,))


def rms_rstd(P, ss, out, rk, wk, n=D):
    _act(P, ss, ss, AF.Sqrt, r=rk, w=rk, bias=C_EPS[0], scale=1.0 / n)
    P.add("dve", lambda h, o=out, a=ss: h.reciprocal(o, a), r=rk, w=wk)


C_EPS = [None]


def prologue(P, C, src_rows, tile, gain_b, tag):
    hb = tile % 2
    hT = C.hT[hb]
    for s in range(4):
        j = tile * 4 + s
        xs = j % C.nx
        X = C.X[xs]
        row0 = src_rows + j * 128
        _dma(P, "sp", X, C.src[row0:row0 + 128, :], r=(("src", j),), w=(("x", xs),))
        sc = j % 4
        _act(P, C.junk, X, AF.Square, r=(("x", xs),), w=(("ss", sc),), accum=C.ss[:, sc:sc + 1])
        rms_rstd(P, C.ss[:, sc:sc + 1], C.rs[:, sc:sc + 1], rk=(("ss", sc),), wk=(("rs", sc),))
        xb = j % 2
        P.add("dve", lambda h, o=C.xn[xb], a=X, sca=C.rs[:, sc:sc + 1], g=gain_b:
              h.scalar_tensor_tensor(o, a, sca, g, ALU.mult, ALU.mult),
              r=(("x", xs), ("rs", sc), "gains"), w=(("xn", xb),))
        pb = j % 2
        pst = C.psT[pb]
        for kc in range(KC):
            _tr(P, pst[:, kc * 128:(kc + 1) * 128], C.xn[xb][:, kc * 128:(kc + 1) * 128], C.ident,
                r=(("xn", xb), "ident"), w=(("psT", pb),))
        _act(P, hT[:, :, s * 128:(s + 1) * 128], pst.rearrange("p (k t) -> p k t", k=KC), AF.Copy,
             r=(("psT", pb),), w=(("hT", hb, s),))


def post_norm_residual(P, C, banks, gain_b, j, final_gain=None, dst_rows=None):
    X = C.Xr
    for n in range(2):
        b, k = banks[n]
        _act(P, C.junk[:, 0:512], b, AF.Square, r=(k,), w=(("ssy", n),), accum=C.ssy[:, n:n + 1])
        P.add("dve", lambda h, o=C.t[:, n * 512:(n + 1) * 512], a=b, g=gain_b[:, n * 512:(n + 1) * 512]:
              h.tensor_tensor(o, a, g, ALU.mult), r=(k, "gains"), w=(("t", n),))
    P.add("dve", lambda h, o=C.ssy[:, 2:3], a=C.ssy[:, 0:1], b2=C.ssy[:, 1:2]: h.tensor_tensor(o, a, b2, ALU.add),
          r=(("ssy", 0), ("ssy", 1)), w=(("ssy", 2),))
    rms_rstd(P, C.ssy[:, 2:3], C.ssy[:, 3:4], rk=(("ssy", 2),), wk=(("ssy", 3),))
    P.add("dve", lambda h, o=X, a=C.t, sca=C.ssy[:, 3:4], x=X: h.scalar_tensor_tensor(o, a, sca, x, ALU.mult, ALU.add),
          r=(("t", 0), ("t", 1), ("ssy", 3), "xr"), w=("xr",))
    outt = X
    outk = "xr"
    if final_gain is not None:
        _act(P, C.junk, X, AF.Square, r=("xr",), w=(("ssf", 0),), accum=C.ssy[:, 4:5])
        rms_rstd(P, C.ssy[:, 4:5], C.ssy[:, 5:6], rk=(("ssf", 0),), wk=(("ssf", 1),))
        P.add("dve", lambda h, o=C.t, a=X, sca=C.ssy[:, 5:6], g=final_gain: h.scalar_tensor_tensor(o, a, sca, g, ALU.mult, ALU.mult),
              r=("xr", ("ssf", 1), "gains", ("t", 0), ("t", 1)), w=(("t", 0), ("t", 1)))
        outt = C.t
        outk = ("t", 0)
    if CFG.get("stop", 99) < 6:
        return
    _dma(P, "sp", C.dst[dst_rows:dst_rows + 128, :], outt, r=(outk, ("t", 1)) if final_gain is not None else (outk,),
         w=(("dst", j),))


def load_ffn_weights(P, C, wg_d, wu_d, wd_d, extra):
    W = C.wreg
    C.wg = W[:, 0:KC * DFF].rearrange("p (k n) -> p k n", k=KC)
    C.wu = W[:, KC * DFF:2 * KC * DFF].rearrange("p (k n) -> p k n", k=KC)
    C.wd = W[:, 2 * KC * DFF:2 * KC * DFF + NFC * D].rearrange("p (c n) -> p c n", c=NFC)
    wgv = wg_d.rearrange("(k p) n -> p k n", p=128)
    wuv = wu_d.rearrange("(k p) n -> p k n", p=128)
    wdv = wd_d.rearrange("(c p) n -> p c n", p=128)
    NS = 4
    cs = DFF // NS
    for i in range(NS):
        _dma(P, "pool", C.wg[:, :, i * cs:(i + 1) * cs], wgv[:, :, i * cs:(i + 1) * cs], r=(), w=(("wg", i),), extra=extra)
        _dma(P, "pool", C.wu[:, :, i * cs:(i + 1) * cs], wuv[:, :, i * cs:(i + 1) * cs], r=(), w=(("wu", i),), extra=extra)
    for i in range(2):
        _dma(P, "pool", C.wd[:, i * 11:(i + 1) * 11, :], wdv[:, i * 11:(i + 1) * 11, :], r=(), w=(("wd", i),), extra=extra)
    C.wslice = cs


def ffn_phase(P, C, ntiles, gain_pre, gain_post, gain_fin):
    prs = C.psrr
    stop = CFG.get("stop", 99)
    if stop < 1:
        return
    prologue(P, C, 0, 0, gain_pre, "f")
    if stop < 2:
        return
    for i in range(ntiles):
        hT = C.hT[i % 2]
        hk = [("hT", i % 2, s) for s in range(4)]
        for c in range(NFC):
            wk = c * 128 // C.wslice
            bg, kg = prs.next()
            for kc in range(KC):
                _mm(P, bg, C.wg[:, kc, c * 128:(c + 1) * 128], hT[:, kc, :], kc == 0, kc == KC - 1,
                    r=hk + [("wg", wk)], w=(kg,))
            bu, ku = prs.next()
            for kc in range(KC):
                _mm(P, bu, C.wu[:, kc, c * 128:(c + 1) * 128], hT[:, kc, :], kc == 0, kc == KC - 1,
                    r=hk + [("wu", wk)], w=(ku,))
            sg = C.sg[c % 2]
            _act(P, sg, bg, AF.Silu, r=(kg,), w=(("sg", c % 2),))
            P.add("dve", lambda h, o=C.hid[:, c, :], a=sg, b=bu: h.tensor_tensor(o, a, b, ALU.mult),
                  r=(("sg", c % 2), ku), w=(("hid", c),))
        if stop < 3:
            return
        if i + 1 < ntiles:
            prologue(P, C, 0, i + 1, gain_pre, "f")
        for s in range(4):
            j = i * 4 + s
            banks = []
            for n in range(2):
                b, k = prs.next()
                for c in range(NFC):
                    _mm(P, b, C.hid[:, c, s * 128:(s + 1) * 128], C.wd[:, c, n * 512:(n + 1) * 512], c == 0, c == NFC - 1,
                        r=(("hid", c), ("wd", c // 11)), w=(k,))
                banks.append((b, k))
            if stop < 4:
                continue
            _dma(P, "sp", C.Xr, C.src[j * 128:(j + 1) * 128, :], r=(("src", j),), w=("xr",))
            if stop < 5:
                continue
            post_norm_residual(P, C, banks, gain_post, j, final_gain=gain_fin, dst_rows=j * 128)


QKV_BLOCKS = ((0, 512, "q"), (512, 768, "q"), (768, 1280, "k"), (1280, 1536, "k"), (1536, 2048, "v"), (2048, 2304, "v"))


def qkv_phase(P, C, ntiles, gain_pre):
    prs = C.psrr
    prologue(P, C, 0, 0, gain_pre, "q")
    ev = 0
    for i in range(ntiles):
        hT = C.hT[i % 2]
        if i + 1 < ntiles:
            prologue(P, C, 0, i + 1, gain_pre, "q")
        for s in range(4):
            j = i * 4 + s
            sb = j % 2
            stage = C.stage[sb]
            stage_v = stage[:, 1536:2316].rearrange("p (h e) -> p h e", e=65)
            for (c0, c1, kind) in QKV_BLOCKS:
                b, k = prs.next()
                n = c1 - c0
                for kc in range(KC):
                    _mm(P, b[:, 0:n], hT[:, kc, s * 128:(s + 1) * 128], C.wqkv[:, kc, c0:c1], kc == 0, kc == KC - 1,
                        r=(("hT", i % 2, s), "wqkv"), w=(k,))
                if kind == "v":
                    h0 = (c0 - 1536) // 64
                    nh = n // 64
                    out = stage_v[:, h0:h0 + nh, 0:64]
                    src = b[:, 0:n].rearrange("p (h e) -> p h e", e=64)
                else:
                    out = stage[:, c0:c1]
                    src = b[:, 0:n]
                scale = 0.125 if kind == "q" else 1.0
                if ev % 2 == 0:
                    _act(P, out, src, AF.Copy, r=(k,), w=(("stage", sb, c0),), scale=scale)
                else:
                    P.add("dve", lambda h, o=out, a=src, sc=scale: h.tensor_scalar_mul(o, a, float(sc)),
                          r=(k,), w=(("stage", sb, c0),))
                ev += 1
            _dma(P, "sp", C.dst[j * 128:(j + 1) * 128, :], stage, r=[("stage", sb, c0) for (c0, _, _) in QKV_BLOCKS],
                 w=(("dst", j),))


def build_bias(P, C, I, tscr):
    A = C.A
    rb = A.alloc((12,), F32, parts=32)
    E = A.alloc((3 * 129,), F32, parts=32)
    Tsb = A.alloc((3, 512), F32, parts=12)
    anti = A.alloc((128,), F32)
    R = [A.alloc((256,), F32) for _ in range(2)]
    _dma(P, "sp", rb, I["rel_bias"], r=(), w=("rb",))
    _dma(P, "sp", E, I["onehot"], r=(), w=("E",))
    _dma(P, "sp", anti, I["antiid"], r=(), w=("anti",))
    P.add("pool", lambda h: h.memset(Tsb, NEG), r=(), w=("Tsb",))
    b5 = C.ps[5][:, :]
    for g in range(3):
        _mm(P, b5[0:12, 0:129], rb[0:32, 0:12], E[0:32, g * 129:(g + 1) * 129], True, True, r=("rb", "E"), w=(("ps", 5),))
        P.add("dve", lambda h, o=Tsb[:, g, 127:256], a=b5[0:12, 0:129]: h.tensor_copy(o, a), r=(("ps", 5), "Tsb"), w=("Tsb",))
    for g in range(3):
        _dma(P, "sp", tscr[4 * g:4 * g + 4, :], Tsb[4 * g:4 * g + 4, g, :], r=("Tsb",), w=(("tscr", g),))
    for gh in range(12):
        Rt = R[gh % 2]
        src = bass.AP(tscr.tensor, gh * 512, [[1, 128], [1, 256]])
        _dma(P, "sp", Rt, src, r=(("tscr", gh // 4),), w=(("R", gh % 2),))
        b3, k3 = C.ps[3 + gh % 2][:, :], ("ps", 3 + gh % 2)
        _mm(P, b3[:, 0:256], anti, Rt, True, True, r=(("R", gh % 2), "anti"), w=(k3,))
        P.add("dve", lambda h, o=C.biasb[:, gh, :], a=b3[:, 0:256]: h.tensor_copy(o, a), r=(k3,), w=("biasb",))


def att_phase(P, C, qkv, att):
    segs = segments()
    psPT = [C.ps[4][:, :].bitcast(BF16), C.ps[5][:, :].bitcast(BF16)]
    Sb = [(C.ps[i][:, :], ("ps", i)) for i in range(3)]
    Ob = [(C.ps[6 + i][:, 0:260], ("ps", 6 + i)) for i in range(2)]
    psKQ = C.ps[3][:, :].bitcast(BF16)
    qt = qkv.tensor
    tcount = [0]
    units = []
    first_unit = {}
    bc = 0
    for si, (g, d, r, i0, nblk) in enumerate(segs):
        first_unit[si] = len(units)
        for b in range(nblk):
            for c in range(2):
                units.append((si, b, c, bc + b))
        bc += nblk
    N = len(units)

    def loads(si):
        g, d, r, i0, nblk = segs[si]
        rb = si % 4
        nt = nblk + 1
        rowk = d * (i0 - 64) + r
        rowq = d * i0 + r
        _dma(P, "sp", C.Kraw[rb][:, 0:nt, :], bass.AP(qt, rowk * QKVW + 768 + 256 * g, [[d * QKVW, 128], [128 * d * QKVW, nt], [1, 256]]),
             r=(), w=(("Kraw", rb),))
        _dma(P, "sp", C.Vt[rb][:, 0:nt, :], bass.AP(qt, rowk * QKVW + 1536 + 260 * g, [[d * QKVW, 128], [128 * d * QKVW, nt], [1, 260]]),
             r=(), w=(("Vt", rb),))
        _dma(P, "sp", C.Qraw[rb][:, 0:nblk, :], bass.AP(qt, rowq * QKVW + 256 * g, [[d * QKVW, 128], [128 * d * QKVW, nblk], [1, 256]]),
             r=(), w=(("Qraw", rb),))

    def transposes(si):
        g, d, r, i0, nblk = segs[si]
        rb, tb2 = si % 4, si % 2
        nt = nblk + 1
        for (raw, rk, dstT, dk, ntile) in ((C.Kraw[rb], ("Kraw", rb), C.KT[tb2], ("KT", tb2), nt),
                                          (C.Qraw[rb], ("Qraw", rb), C.QT[tb2], ("QT", tb2), nblk)):
            for j0 in range(0, ntile, 4):
                nj = min(4, ntile - j0)
                pk, pkk = psKQ, ("ps", 3)
                for c in range(2):
                    for jj in range(nj):
                        _tr(P, pk[:, c * 512 + jj * 128: c * 512 + (jj + 1) * 128], raw[:, j0 + jj, c * 128:(c + 1) * 128], C.ident,
                            r=(rk, "ident"), w=(pkk,))
                src = pk.rearrange("p (c t) -> p c t", c=2)[:, :, 0:nj * 128]
                _act(P, dstT[:, :, j0 * 128:(j0 + nj) * 128], src, AF.Copy, r=(pkk,), w=(dk,))

    def ctx(u):
        si, b, c, gb = units[u]
        g, d, r, i0, nblk = segs[si]
        return si, b, c, gb, g, d, r, i0

    def stage_S(u):
        si, b, c, gb, g, d, r, i0 = ctx(u)
        QT, KT = C.QT[si % 2], C.KT[si % 2]
        sbk, skey = Sb[u % 3]
        for hp in range(2):
            _mm(P, sbk[:, hp * 256:(hp + 1) * 256], QT[hp * 64:(hp + 1) * 64, c, b * 128:(b + 1) * 128],
                KT[hp * 64:(hp + 1) * 64, c, b * 128:b * 128 + 256], True, False, r=(("QT", si % 2), ("KT", si % 2)), w=(skey,))
            _mm(P, sbk[:, hp * 256:(hp + 1) * 256], C.ident, C.biasb[:, 4 * g + 2 * c + hp, :], False, True,
                r=("ident", "biasb"), w=(skey,))

    def stage_1(u):
        si, b, c, gb, g, d, r, i0 = ctx(u)
        sl = u % 4
        sbk, skey = Sb[u % 3]
        ost = C.ostage[gb % 3]
        ostk = ("ost", gb % 3)
        mcols = ost[:, 260 + 2 * c:262 + 2 * c]
        P.add("dve", lambda hh, o=mcols, a=sbk.rearrange("p (h k) -> p h k", h=2): hh.reduce_max(o, a, AX.X),
              r=(skey,), w=(ostk + (c,),))
        P.add("dve", lambda hh, o=C.negm[:, sl, :], a=mcols: hh.tensor_scalar_mul(o, a, -1.0), r=(ostk + (c,),), w=(("negm", sl),))
        for hp in range(2):
            _act(P, C.Pb[sl][:, hp, :], sbk[:, hp * 256:(hp + 1) * 256], AF.Exp, r=(skey, ("negm", sl)), w=(("Pb", sl, hp),),
                 bias=C.negm[:, sl, hp:hp + 1])

    def stage_2(u):
        sl = u % 4
        pt, ptk = psPT[u % 2], ("ps", 4 + u % 2)
        for hp in range(2):
            for jj in range(2):
                _tr(P, pt[:, (hp * 2 + jj) * 128:(hp * 2 + jj + 1) * 128], C.Pb[sl][:, hp, jj * 128:(jj + 1) * 128], C.ident,
                    r=(("Pb", sl, hp), "ident"), w=(ptk,))

    def stage_3(u):
        si, b, c, gb, g, d, r, i0 = ctx(u)
        sl = u % 4
        pt, ptk = psPT[u % 2], ("ps", 4 + u % 2)
        PTs = C.PTs[sl]
        ptv = pt[:, 0:512].rearrange("p (h j q) -> p h j q", h=2, j=2)
        for jj in range(2):
            vcol = C.vseg[:, si * 9 + b + jj: si * 9 + b + jj + 1]
            if jj == 0:
                _act(P, PTs[:, :, jj, :], ptv[:, :, jj, :], AF.Copy, r=(ptk, "vseg"), w=(("PTs", sl, jj),), scale=vcol)
            else:
                P.add("dve", lambda hh, o=PTs[:, :, jj, :], a=ptv[:, :, jj, :], sc=vcol: hh.tensor_scalar_mul(o, a, sc),
                      r=(ptk, "vseg"), w=(("PTs", sl, jj),))

    def stage_4(u):
        si, b, c, gb, g, d, r, i0 = ctx(u)
        sl = u % 4
        PTs = C.PTs[sl]
        Vt = C.Vt[si % 4]
        ob, okey = Ob[gb % 2]
        ost = C.ostage[gb % 3]
        ostk = ("ost", gb % 3)
        for hp in range(2):
            h = 2 * c + hp
            for jj in range(2):
                _mm(P, ob[:, h * 65:(h + 1) * 65], PTs[:, hp, jj, :], Vt[:, b + jj, h * 65:(h + 1) * 65], jj == 0, jj == 1,
                    r=(("PTs", sl, jj), ("Vt", si % 4)), w=(okey,))
        if c == 1:
            P.add("dve", lambda hh, o=ost[:, 0:260], a=ob: hh.tensor_copy(o, a), r=(okey,), w=(ostk + (9,),))
            t0 = d * (i0 + 128 * b) + r - HALO
            dst = bass.AP(att.tensor, (t0 * 3 + g) * ATTW, [[d * 3 * ATTW, 128], [1, ATTW]])
            _dma(P, "pool", dst, ost, r=[ostk + (x,) for x in (0, 1, 9)], w=(("att", si, b),))

    nseg = len(segs)
    loads(0)
    loads(1)
    loads(2)
    loads(3)
    transposes(0)
    seg_started = set()
    for u in range(min(2, N)):
        stage_S(u)
    for t in range(N + 4):
        if t < N:
            si = units[t][0]
            if si not in seg_started and first_unit[si] == t:
                seg_started.add(si)
                if si + 1 < nseg:
                    transposes(si + 1)
        if t + 2 < N:
            stage_S(t + 2)
        if t < N:
            stage_1(t)
        if 0 <= t - 1 < N:
            stage_2(t - 1)
        if 0 <= t - 2 < N:
            stage_3(t - 2)
        if 0 <= t - 3 < N:
            stage_4(t - 3)
            si3 = units[t - 3][0]
            if t - 3 + 1 == N or units[t - 3 + 1][0] != si3:
                if si3 + 4 < nseg:
                    loads(si3 + 4)


def b1_phase(P, C, ntiles, att):
    prs = C.psrr
    prologue(P, C, HALO, 0, C.gpre, "m")
    for i in range(ntiles):
        hT = C.hT[i % 2]
        hk = [("hT", i % 2, s) for s in range(4)]
        AI = C.attin
        _dma(P, "sp", AI, att[i * 512:(i + 1) * 512, :].rearrange("(s p) (g e) -> p s g e", p=128, g=3), r=(), w=("attin",))
        zb = []
        for s in range(4):
            bu, ku = prs.next()
            for kc in range(KC):
                _mm(P, bu, hT[:, kc, s * 128:(s + 1) * 128], C.wrest[:, kc, 0:512], kc == 0, kc == KC - 1, r=(hk[s], "wrest"), w=(ku,))
            _act(P, C.U[s], bu, AF.Gelu, r=(ku,), w=(("U", s),))
            bv, kv = prs.next()
            for kc in range(KC):
                _mm(P, bv, hT[:, kc, s * 128:(s + 1) * 128], C.wrest[:, kc, 512:1024], kc == 0, kc == KC - 1, r=(hk[s], "wrest"), w=(kv,))
            _act(P, C.V[s], bv, AF.Gelu, r=(kv,), w=(("V", s), ("st0", s)), accum=C.st[:, 0, s:s + 1])
        mview = AI[:, :, :, 260:264]
        P.add("dve", lambda h, o=C.M, a=mview.rearrange("p s g h -> p s h g"): h.tensor_reduce(o, a, AX.X, ALU.max),
              r=("attin",), w=("M",))
        P.add("dve", lambda h, o=C.dd, a=mview, m=C.M.unsqueeze(2).to_broadcast([128, 4, 3, 4]): h.tensor_tensor(o, a, m, ALU.subtract),
              r=("attin", "M"), w=("dd",))
        _act(P, C.ww, C.dd, AF.Exp, r=("dd",), w=("ww",))
        sview = AI[:, :, :, 0:260].rearrange("p s g (h e) -> p s g h e", e=65)
        for s in range(4):
            P.add("dve", lambda h, o=C.wd_[:, s], a=C.ww[:, s], sv=sview[:, s, :, :, 64]: h.tensor_tensor(o, a, sv, ALU.mult),
                  r=("ww", "attin"), w=(("wd", s),))
        P.add("dve", lambda h, o=C.den, a=C.wd_.rearrange("p s g h -> p s h g"): h.tensor_reduce(o, a, AX.X, ALU.add),
              r=[("wd", s) for s in range(4)], w=("den",))
        P.add("dve", lambda h, o=C.den: h.reciprocal(o, o), r=("den",), w=("den",))
        P.add("dve", lambda h, o=C.cc, a=C.ww, m=C.den.unsqueeze(2).to_broadcast([128, 4, 3, 4]): h.tensor_tensor(o, a, m, ALU.mult),
              r=("ww", "den"), w=("cc",))
        P.add("dve", lambda h, o=C.st[:, 1, :], a=C.st[:, 0, :]: h.tensor_scalar_mul(o, a, -1.0 / 512.0),
              r=[("st0", s) for s in range(4)], w=("st1",))
        for s in range(4):
            _act(P, C.junk[:, 0:512], C.V[s], AF.Square, r=(("V", s), "st1"), w=(("st2", s),), bias=C.st[:, 1, s:s + 1],
                 accum=C.st[:, 2, s:s + 1])
        _act(P, C.st[:, 2, :], C.st[:, 2, :], AF.Sqrt, r=[("st2", s) for s in range(4)], w=[("st2", s) for s in range(4)],
             bias=C_EPS[0], scale=1.0 / 512.0)
        P.add("dve", lambda h, o=C.st[:, 3, :], a=C.st[:, 2, :]: h.reciprocal(o, a), r=[("st2", s) for s in range(4)], w=("st3",))
        for s in range(4):
            V = C.V[s]
            P.add("dve", lambda h, o=V, s1=C.st[:, 1, s:s + 1], s2=C.st[:, 3, s:s + 1]: h.tensor_scalar(o, o, s1, s2, ALU.add, ALU.mult),
                  r=(("V", s), "st1", "st3", ("st2", s)), w=(("V", s),))
        for s in range(4):
            V = C.V[s]
            P.add("dve", lambda h, o=V, g=C.lng: h.tensor_tensor(o, o, g, ALU.mult), r=(("V", s), "gains"), w=(("V", s),))
        for s in range(4):
            P.add("dve", lambda h, o=C.vln[s], a=C.V[s], g=C.lnb: h.tensor_tensor(o, a, g, ALU.add), r=(("V", s), "gains"), w=(("vln", s),))
        for s in range(4):
            ts = s % 2
            oview = sview[:, s, :, :, 0:64]
            P.add("dve", lambda h, o=C.tmpO[ts], a=oview, m=C.cc[:, s].unsqueeze(3).to_broadcast([128, 3, 4, 64]): h.tensor_tensor(o, a, m, ALU.mult),
                  r=("attin", "cc"), w=(("tmpO", ts),))

            def _red(h, o=C.attb[s], a=C.tmpO[ts].rearrange("p g h e -> p h e g")):
                with C.nc.allow_low_precision(reason="3-term sum on the fp32 ALU, rounded once to bf16 for the matmul"):
                    return h.tensor_reduce(o, a, AX.X, ALU.add)
            P.add("dve", _red, r=(("tmpO", ts),), w=(("attb", s),))
        mb = []
        for s in range(4):
            bm, km = prs.next()
            for gg in range(4):
                _mm(P, bm[:, gg * 128:(gg + 1) * 128], C.WsT[:, gg, :], C.vln[s][:, gg * 128:(gg + 1) * 128], True, True,
                    r=(("vln", s), "WsT"), w=(km,))
            for gg in range(4):
                P.add("dve", lambda h, o=C.sgu[s][:, gg * 128:(gg + 1) * 128], a=bm[:, gg * 128:(gg + 1) * 128], sc=C.bs[:, gg:gg + 1],
                      u=C.U[s][:, gg * 128:(gg + 1) * 128]: h.scalar_tensor_tensor(o, a, sc, u, ALU.add, ALU.mult),
                      r=(km, ("U", s), "bs"), w=(("sgu", s),))
        tp = 0
        for s in range(4):
            pb = tp % 2
            tp += 1
            pst = C.psT[pb]
            for gg in range(4):
                _tr(P, pst[:, gg * 128:(gg + 1) * 128], C.sgu[s][:, gg * 128:(gg + 1) * 128], C.ident, r=(("sgu", s), "ident"), w=(("psT", pb),))
            ab = C.attb[s].rearrange("p h e -> p (h e)")
            for cc_ in range(2):
                _tr(P, pst[:, 512 + cc_ * 128:512 + (cc_ + 1) * 128], ab[:, cc_ * 128:(cc_ + 1) * 128], C.ident, r=(("attb", s), "ident"), w=(("psT", pb),))
            _act(P, C.sguT[:, :, s * 128:(s + 1) * 128], pst[:, 0:512].rearrange("p (k t) -> p k t", k=4), AF.Copy,
                 r=(("psT", pb),), w=(("sguT", s),))
            _act(P, C.attT[:, :, s * 128:(s + 1) * 128], pst[:, 512:768].rearrange("p (k t) -> p k t", k=2), AF.Copy,
                 r=(("psT", pb),), w=(("attT", s),))
        sguk = [("sguT", s) for s in range(4)]
        attk = [("attT", s) for s in range(4)]
        for oc in range(8):
            bga, kga = prs.next()
            for kc in range(KC):
                _mm(P, bga, C.wrest[:, kc, 1024 + oc * 128:1024 + (oc + 1) * 128], hT[:, kc, :], kc == 0, kc == KC - 1, r=hk + ["wrest"], w=(kga,))
            bgb, kgb = prs.next()
            for kc in range(KC):
                _mm(P, bgb, C.wrest[:, kc, 2048 + oc * 128:2048 + (oc + 1) * 128], hT[:, kc, :], kc == 0, kc == KC - 1, r=hk + ["wrest"], w=(kgb,))
            bpa, kpa = prs.next()
            for kc in range(2):
                _mm(P, bpa, C.watt[:, kc, oc * 128:(oc + 1) * 128], C.attT[:, kc, :], kc == 0, kc == 1, r=attk + ["watt"], w=(kpa,))
            bpb, kpb = prs.next()
            for kc in range(4):
                _mm(P, bpb, C.wsgu[:, kc, oc * 128:(oc + 1) * 128], C.sguT[:, kc, :], kc == 0, kc == 3, r=sguk + ["wsgu"], w=(kpb,))
            o2 = oc % 2
            _act(P, C.sa[o2], bga, AF.Sigmoid, r=(kga,), w=(("sa", o2),))
            _act(P, C.sb_[o2], bgb, AF.Sigmoid, r=(kgb,), w=(("sb", o2),))
            P.add("dve", lambda h, o=C.sa[o2], b=bpa: h.tensor_tensor(o, o, b, ALU.mult), r=(("sa", o2), kpa), w=(("sa", o2),))
            P.add("dve", lambda h, o=C.sb_[o2], b=bpb: h.tensor_tensor(o, o, b, ALU.mult), r=(("sb", o2), kpb), w=(("sb", o2),))
            P.add("dve", lambda h, o=C.mT[:, oc, :], a=C.sa[o2], b=C.sb_[o2]: h.tensor_tensor(o, a, b, ALU.add),
                  r=(("sa", o2), ("sb", o2)), w=(("mT", oc),))
        if i + 1 < ntiles:
            prologue(P, C, HALO, i + 1, C.gpre, "m")
        mk = [("mT", oc) for oc in range(8)]
        for s in range(4):
            j = i * 4 + s
            banks = []
            for n in range(2):
                b, k = prs.next()
                for kc in range(KC):
                    _mm(P, b, C.mT[:, kc, s * 128:(s + 1) * 128], C.wout[:, kc, n * 512:(n + 1) * 512], kc == 0, kc == KC - 1,
                        r=mk + ["wout"], w=(k,))
                banks.append((b, k))
            _dma(P, "sp", C.Xr, C.src[HALO + j * 128:HALO + (j + 1) * 128, :], r=(), w=("xr",))
            post_norm_residual(P, C, banks, C.gpost, j, final_gain=None, dst_rows=j * 128)


def build_program(cfg):
    nc = bass.Bass("TRN2", target_bir_lowering=False)
    dbg = cfg["debug"]
    phases = cfg["phases"]
    n_ext = cfg["n_ext_tiles"]
    n_own = cfg["n_own_tiles"]

    def din(name, shape):
        return nc.dram_tensor(name, list(shape), F32, kind="ExternalInput").ap()

    def dscr(name, shape, dt):
        if dbg:
            return nc.dram_tensor(name, list(shape), dt, kind="ExternalOutput").ap()
        return nc.dram_tensor(name, list(shape), dt).ap()

    I = {}
    I["xe"] = din("xe", (NEXT, D))
    for nm in ("w_gate1", "w_up1", "w_gate2", "w_up2"):
        I[nm] = din(nm, (D, DFF))
    for nm in ("w_down1", "w_down2"):
        I[nm] = din(nm, (DFF, D))
    I["w_in"] = din("w_in", (D, 5376))
    I["w_att"] = din("w_att", (256, D))
    I["w_sgu"] = din("w_sgu", (512, D))
    I["w_out"] = din("w_out", (D, D))
    for nm in ("g1pre", "g1post", "gmpre", "gmpost", "g2pre", "g2post", "gfin"):
        I[nm] = din(nm, (1, D))
    I["ln_g"] = din("ln_g", (1, 512))
    I["ln_b"] = din("ln_b", (1, 512))
    I["w_s"] = din("w_s", (4, 128, 128))
    I["b_s"] = din("b_s", (4, 128))
    I["rel_bias"] = din("rel_bias", (32, 12))
    I["ident"] = din("ident", (128, 128))
    I["antiid"] = din("antiid", (128, 128))
    I["onehot"] = din("onehot", (32, 3 * 129))
    I["vseg"] = din("vseg", (128, 24 * 9))
    y = nc.dram_tensor("y", [NOWN, D], F32, kind="ExternalOutput").ap()
    x1 = dscr("x1s", (NEXT, D), F32)
    qkv = dscr("qkvs", (NEXT, QKVW), BF16)
    att = dscr("atts", (NOWN, 3 * ATTW), F32)
    x2 = dscr("x2s", (NOWN, D), F32)
    tscr = nc.dram_tensor("tscr", [12, 512], F32).ap()

    P = Prog(nc)
    with ExitStack() as es:
        wreg = es.enter_context(nc.sbuf_tensor("wreg", [128, WREG_ELEMS], BF16))
        areg = es.enter_context(nc.sbuf_tensor("areg", [128, ARENA_ELEMS], BF16))
        identt = es.enter_context(nc.sbuf_tensor("identb", [128, 128], BF16))
        ps = [es.enter_context(nc.psum_tensor("ps%d" % i, [128, 512], F32)) for i in range(8)]
        sems = {e: es.enter_context(nc.semaphore("sem_" + e)) for e in ("pe", "act", "dve", "pool")}
        dsems = {}
        for q in ("sp", "pool"):
            for i in range(P.ndma):
                dsems[(q, i)] = es.enter_context(nc.semaphore("dsem_%s%d" % (q, i)))

        C = Ctx()
        C.nc = nc
        C.wreg = wreg[:, :]
        C.ident = identt[:, :]
        A = Arena(areg[:, :])
        _dma(P, "pool", C.ident, I["ident"], r=(), w=("ident",))
        epst = es.enter_context(nc.sbuf_tensor("epst", [128, 1], F32))
        C_EPS[0] = epst[:, :]
        P.add("pool", lambda h: h.memset(epst[:, :], EPS), r=(), w=("eps",))
        P.barrier()

        def ffn_setup(src, dst, gpre, gpost, gfin):
            A.reset()
            C.src, C.dst = src, dst
            C.nx = 2
            C.X = [A.alloc((D,), F32) for _ in range(C.nx)]
            C.Xr = A.alloc((D,), F32)
            C.xn = [A.alloc((D,), BF16) for _ in range(2)]
            C.hT = [A.alloc((KC, TT), BF16) for _ in range(2)]
            C.hid = A.alloc((NFC, TT), BF16)
            C.sg = [A.alloc((TT,), BF16) for _ in range(2)]
            C.t = A.alloc((D,), F32)
            C.junk = A.alloc((D,), BF16)
            C.ss = A.alloc((4,), F32)
            C.rs = A.alloc((4,), F32)
            C.ssy = A.alloc((8,), F32)
            C.gpre = A.alloc((D,), F32)
            C.gpost = A.alloc((D,), F32)
            load_bcast(P, C, C.gpre, gpre, D, "gains")
            load_bcast(P, C, C.gpost, gpost, D, "gains", scale=0.5)
            C.gfin = None
            if gfin is not None:
                C.gfin = A.alloc((D,), F32)
                load_bcast(P, C, C.gfin, gfin, D, "gains")
            C.psT = [ps[6][:, :].bitcast(BF16), ps[7][:, :].bitcast(BF16)]
            C.psrr = PsumRR([ps[i][:, :] for i in range(6)], [("ps", i) for i in range(6)])

        last_pe = None
        if "A1" in phases:
            load_ffn_weights(P, C, I["w_gate1"], I["w_up1"], I["w_down1"], extra=())
            ffn_setup(I["xe"], x1, I["g1pre"], I["g1post"], None)
            ffn_phase(P, C, n_ext, C.gpre, C.gpost, None)
            last_pe = P.last("pe")
            P.barrier()


        def common_setup(src, dst, gpre, gpost, gpost_scale):
            A.reset()
            C.A = A
            C.ps = ps
            C.src, C.dst = src, dst
            C.nx = 2
            C.X = [A.alloc((D,), F32) for _ in range(C.nx)]
            C.xn = [A.alloc((D,), BF16) for _ in range(2)]
            C.hT = [A.alloc((KC, TT), BF16) for _ in range(2)]
            C.junk = A.alloc((D,), BF16)
            C.ss = A.alloc((4,), F32)
            C.rs = A.alloc((4,), F32)
            C.ssy = A.alloc((8,), F32)
            C.gpre = A.alloc((D,), F32)
            load_bcast(P, C, C.gpre, gpre, D, "gains")
            if gpost is not None:
                C.gpost = A.alloc((D,), F32)
                load_bcast(P, C, C.gpost, gpost, D, "gains", scale=gpost_scale)
                C.Xr = A.alloc((D,), F32)
                C.t = A.alloc((D,), F32)
            C.psT = [ps[6][:, :].bitcast(BF16), ps[7][:, :].bitcast(BF16)]
            C.psrr = PsumRR([ps[i][:, :] for i in range(6)], [("ps", i) for i in range(6)])

        if "A2" in phases:
            ex = (last_pe,) if last_pe is not None else ()
            C.wqkv = C.wreg[:, 0:KC * 2304].rearrange("p (k n) -> p k n", k=KC)
            wv = I["w_in"].rearrange("(k p) n -> p k n", p=128)
            for hh in range(2):
                _dma(P, "pool", C.wqkv[:, hh * 4:(hh + 1) * 4, :], wv[:, hh * 4:(hh + 1) * 4, 0:2304], r=(), w=("wqkv",), extra=ex)
            common_setup(x1, qkv, I["gmpre"], None, None)
            C.stage = [A.alloc((QKVW,), BF16) for _ in range(2)]
            for sb in range(2):
                sv = C.stage[sb][:, 1536:2316].rearrange("p (h e) -> p h e", e=65)
                P.add("pool", lambda h, o=sv[:, :, 64:65]: h.memset(o, 1.0), r=(), w=[("stage", sb, c0) for (c0, _, _) in QKV_BLOCKS])
            qkv_phase(P, C, n_ext, C.gpre)
            last_pe = P.last("pe")
            P.barrier()

        if "ATT" in phases:
            A.reset()
            C.A = A
            C.ps = ps
            AT = Arena(C.wreg[:, 38912:WREG_ELEMS])
            C.Kraw = [AT.alloc((9, 256), BF16) for _ in range(4)]
            C.Vt = [AT.alloc((9, 260), BF16) for _ in range(4)]
            C.Qraw = [AT.alloc((8, 256), BF16) for _ in range(4)]
            C.KT = [A.alloc((2, 9 * 128), BF16) for _ in range(2)]
            C.QT = [A.alloc((2, 8 * 128), BF16) for _ in range(2)]
            C.biasb = A.alloc((12, 256), BF16)
            C.Pb = [A.alloc((2, 256), BF16) for _ in range(4)]
            C.PTs = [A.alloc((2, 2, 128), BF16) for _ in range(4)]
            C.ostage = [A.alloc((ATTW,), F32) for _ in range(3)]
            C.vseg = A.alloc((24 * 9,), F32)
            C.negm = A.alloc((4, 2), F32)
            _dma(P, "sp", C.vseg, I["vseg"], r=(), w=("vseg",))
            build_bias(P, C, I, tscr)
            att_phase(P, C, qkv, att)
            last_pe = P.last("pe")
            P.barrier()

        if "B1" in phases:
            ex = (last_pe,) if last_pe is not None else ()
            W = C.wreg
            C.wrest = W[:, 0:24576].rearrange("p (k n) -> p k n", k=KC)
            C.watt = W[:, 24576:26624].rearrange("p (k n) -> p k n", k=2)
            C.wsgu = W[:, 26624:30720].rearrange("p (k n) -> p k n", k=4)
            C.wout = W[:, 30720:38912].rearrange("p (k n) -> p k n", k=KC)
            wv = I["w_in"].rearrange("(k p) n -> p k n", p=128)
            for hh in range(2):
                _dma(P, "pool", C.wrest[:, hh * 4:(hh + 1) * 4, :], wv[:, hh * 4:(hh + 1) * 4, 2304:5376], r=(), w=("wrest",), extra=ex)
            _dma(P, "pool", C.watt, I["w_att"].rearrange("(k p) n -> p k n", p=128), r=(), w=("watt",), extra=ex)
            _dma(P, "pool", C.wsgu, I["w_sgu"].rearrange("(k p) n -> p k n", p=128), r=(), w=("wsgu",), extra=ex)
            _dma(P, "pool", C.wout, I["w_out"].rearrange("(k p) n -> p k n", p=128), r=(), w=("wout",), extra=ex)
            common_setup(x1, x2, I["gmpre"], I["gmpost"], None)
            A2 = Arena(W[:, 38912:WREG_ELEMS])
            C.U = [A2.alloc((512,), BF16) for _ in range(4)]
            C.V = [A2.alloc((512,), F32) for _ in range(4)]
            C.st = A2.alloc((4, 4), F32)
            C.vln = [A2.alloc((512,), BF16) for _ in range(4)]
            C.sgu = [A2.alloc((512,), BF16) for _ in range(4)]
            C.sguT = A2.alloc((4, TT), BF16)
            C.attin = A2.alloc((4, 3, ATTW), F32)
            C.M = A2.alloc((4, 4), F32)
            C.dd = A2.alloc((4, 3, 4), F32)
            C.ww = A2.alloc((4, 3, 4), F32)
            C.wd_ = A2.alloc((4, 3, 4), F32)
            C.cc = A2.alloc((4, 3, 4), F32)
            C.den = A2.alloc((4, 4), F32)
            C.tmpO = [A2.alloc((3, 4, 64), F32) for _ in range(2)]
            C.attb = [A2.alloc((4, 64), BF16) for _ in range(4)]
            C.attT = A2.alloc((2, TT), BF16)
            C.sa = [A.alloc((TT,), F32) for _ in range(2)]
            C.sb_ = [A.alloc((TT,), F32) for _ in range(2)]
            C.mT = A.alloc((KC, TT), BF16)
            C.lng = A.alloc((512,), F32)
            C.lnb = A.alloc((512,), F32)
            load_bcast(P, C, C.lng, I["ln_g"], 512, "gains")
            load_bcast(P, C, C.lnb, I["ln_b"], 512, "gains")
            C.WsT = A.alloc((4, 128), BF16)
            C.bs = A.alloc((4,), F32)
            Wn = A2.alloc((4, 128), F32)
            Wnb = A2.alloc((4, 128), BF16)
            _dma(P, "sp", Wn, I["w_s"].rearrange("g p q -> p g q"), r=(), w=("Wn",))
            _dma(P, "sp", C.bs, I["b_s"].rearrange("g p -> p g"), r=(), w=("bs",), slow=True)
            P.add("dve", lambda h: h.tensor_copy(Wnb, Wn), r=("Wn",), w=("Wnb",))
            for gg in range(4):
                _tr(P, C.psT[0][:, gg * 128:(gg + 1) * 128], Wnb[:, gg, :], C.ident, r=("Wnb", "ident"), w=(("psT", 0),))
            _act(P, C.WsT, C.psT[0][:, 0:512].rearrange("p (g q) -> p g q", g=4), AF.Copy, r=(("psT", 0),), w=("WsT",))
            b1_phase(P, C, n_own, att)
            last_pe = P.last("pe")
            P.barrier()

        if "B2" in phases:
            load_ffn_weights(P, C, I["w_gate2"], I["w_up2"], I["w_down2"], extra=(last_pe,) if last_pe is not None else ())
            ffn_setup(x2, y, I["g2pre"], I["g2post"], I["gfin"])
            ffn_phase(P, C, n_own, C.gpre, C.gpost, C.gfin)
            P.barrier()

        with nc.Block() as block:
            P.emit(block, sems, dsems)
    return nc


def _t5_bucket(rel):
    nb = 16
    max_exact = 8
    ret = (rel > 0).astype(np.int32) * nb
    n = np.abs(rel)
    large = max_exact + (np.log(np.maximum(n, 1) / max_exact) / np.log(1024 / max_exact) * (nb - max_exact)).astype(np.int32)
    large = np.minimum(large, nb - 1)
    return ret + np.where(n < max_exact, n, large)


def segments():
    segs = []
    for g, (window, d) in enumerate(GROUPS):
        i_first = HALO // d
        nown = NOWN // d
        nblk_total = nown // 128
        per = 8 if nblk_total >= 8 else nblk_total
        for r in range(d):
            for s0 in range(0, nblk_total, per):
                segs.append((g, d, r, i_first + 128 * s0, per))
    return segs


def host_constants(seq_lo, seq_hi):
    ident = np.eye(128, dtype=np.float32)
    anti = np.ascontiguousarray(ident[::-1])
    onehot = np.zeros((32, 3, 129), np.float32)
    for g, (window, d) in enumerate(GROUPS):
        half = (window // 2) // d
        off = np.arange(-half, half + 1, dtype=np.int32) * d
        bk = _t5_bucket(off)
        onehot[bk, g, np.arange(129)] = 1.0
    segs = segments()
    vseg = np.zeros((128, 24, 9), np.float32)
    for si, (g, d, r, i0, nblk) in enumerate(segs):
        for j in range(nblk + 1):
            sub = i0 - 64 + 128 * j + np.arange(128)
            pos = d * sub + r
            vseg[:, si, j] = ((pos >= seq_lo) & (pos < seq_hi)).astype(np.float32)
    return ident, anti, onehot.reshape(32, 3 * 129), vseg.reshape(128, 24 * 9)


def make_in_maps(inp):
    f = lambda a: np.ascontiguousarray(np.asarray(a, dtype=np.float32))
    xp = f(inp["x_prompt"])[0]
    xs = f(inp["x_sample"])
    shared = {
        "w_gate1": f(inp["ffn1_w_gate"])[0], "w_up1": f(inp["ffn1_w_up"])[0], "w_down1": f(inp["ffn1_w_down"])[0],
        "w_gate2": f(inp["ffn2_w_gate"])[0], "w_up2": f(inp["ffn2_w_up"])[0], "w_down2": f(inp["ffn2_w_down"])[0],
        "w_in": f(inp["w_in"])[0], "w_att": f(inp["w_att"])[0], "w_sgu": f(inp["w_sgu"])[0], "w_out": f(inp["w_out"])[0],
        "g1pre": f(inp["ffn1_pre_g"]), "g1post": f(inp["ffn1_post_g"]), "gmpre": f(inp["mix_pre_g"]),
        "gmpost": f(inp["mix_post_g"]), "g2pre": f(inp["ffn2_pre_g"]), "g2post": f(inp["ffn2_post_g"]),
        "gfin": f(inp["final_g"]), "ln_g": f(inp["sgu_ln_g"]), "ln_b": f(inp["sgu_ln_b"]),
        "w_s": f(inp["sgu_w_s"])[0], "b_s": f(inp["sgu_b_s"])[0], "rel_bias": f(inp["rel_bias"]),
    }
    maps = []
    for c in range(NCORES):
        xe = np.zeros((NEXT, D), np.float32)
        if c < 4:
            lo = c * NOWN - HALO
            hi = lo + NEXT
            a, b = max(lo, 0), min(hi, xp.shape[0])
            xe[a - lo:b - lo] = xp[a:b]
            seq_lo, seq_hi = a - lo, b - lo
        else:
            xe[HALO:HALO + NOWN] = xs[c - 4]
            seq_lo, seq_hi = HALO, HALO + NOWN
        ident, anti, onehot, vseg = host_constants(seq_lo, seq_hi)
        m = dict(shared)
        m.update({"xe": xe, "ident": ident, "antiid": anti, "onehot": onehot, "vseg": vseg})
        maps.append(m)
    return maps


_NC_CACHE = {}


def kernel(**inputs):
    key = "full"
    if key not in _NC_CACHE:
        _NC_CACHE[key] = build_program(CFG)
    nc = _NC_CACHE[key]
    maps = make_in_maps(inputs)
    res = run_bass_kernel_spmd(nc, maps, core_ids=list(range(NCORES)))
    ys = [np.asarray(r["y"], dtype=np.float32) for r in res.results]
    y_prompt = np.concatenate(ys[0:4], axis=0)[None]
    y_sample = np.stack(ys[4:8], axis=0)
    return (y_prompt, y_sample)
```

```python
import numpy as np
import concourse.bass as bass
import concourse.mybir as mybir
from concourse.bass_utils import run_bass_kernel_spmd
from contextlib import ExitStack

F32 = mybir.dt.float32
BF16 = mybir.dt.bfloat16
AF = mybir.ActivationFunctionType
ALU = mybir.AluOpType
AX = mybir.AxisListType

NCORES = 8
D = 1024
DFF = 2816
NFC = 22
KC = 8
NEXT = 6144
NOWN = 4096
HALO = 1024
TT = 512
QKVW = 2316
ATTW = 264
EPS = 1e-6
NEG = -1e30
GROUPS = ((128, 1), (512, 4), (2048, 16))
WREG_ELEMS = 67584
ARENA_ELEMS = 38400

CFG = {"phases": ("A1", "A2", "ATT", "B1", "B2"), "debug": False, "n_ext_tiles": 12, "n_own_tiles": 8}


class Op:
    __slots__ = ("id", "eng", "fn", "deps", "dma", "dsem", "dval", "seq", "sig")


class Prog:
    ENGS = ("pe", "act", "dve", "pool", "sp")

    def __init__(self, nc, ndma=20):
        self.nc = nc
        self.ops = []
        self.byeng = {e: [] for e in self.ENGS}
        self.lastw = {}
        self.readers = {}
        self.ndma = ndma
        self.dma_rr = {"sp": 0, "pool": 0, "act": 0}
        self.dma_cnt = {}
        self.dma_last = {}
        self.pending = {e: set() for e in self.ENGS}

    def add(self, eng, fn, r=(), w=(), dma=False, extra=()):
        op = Op()
        op.id = len(self.ops)
        op.eng = eng
        op.fn = fn
        op.dma = dma
        op.sig = False
        op.seq = None
        deps = set(extra)
        psr = [k for k in r if isinstance(k, tuple) and k[0] in ("ps", "psT")]
        if psr:
            r = [k for k in r if k not in psr]
            w = list(w) + psr
        for k in r:
            lw = self.lastw.get(k)
            if lw is not None:
                deps.add(lw)
        for k in w:
            lw = self.lastw.get(k)
            if lw is not None:
                deps.add(lw)
            deps.update(self.readers.get(k, ()))
        if self.pending[eng]:
            deps.update(self.pending[eng])
            self.pending[eng] = set()
        if dma:
            i = self.dma_rr[eng]
            self.dma_rr[eng] = (i + 1) % self.ndma
            key = (eng, i)
            prev = self.dma_last.get(key)
            if prev is not None:
                deps.add(prev)
            self.dma_cnt[key] = self.dma_cnt.get(key, 0) + 1
            self.dma_last[key] = op.id
            op.dsem = key
            op.dval = 16 * self.dma_cnt[key]
        deps.discard(op.id)
        op.deps = deps
        for k in r:
            self.readers.setdefault(k, []).append(op.id)
        for k in w:
            self.lastw[k] = op.id
            self.readers[k] = []
        self.ops.append(op)
        self.byeng[eng].append(op)
        return op.id

    def last(self, eng):
        return self.byeng[eng][-1].id if self.byeng[eng] else None

    def barrier(self):
        ids = set()
        for e in self.ENGS:
            if self.byeng[e]:
                ids.add(self.byeng[e][-1].id)
        for key, oid in self.dma_last.items():
            ids.add(oid)
        for e in self.ENGS:
            self.pending[e] |= ids
        self.lastw = {}
        self.readers = {}

    def emit(self, block, sems, dsems):
        ops = self.ops
        for op in ops:
            for d in op.deps:
                dop = ops[d]
                if dop.dma:
                    continue
                if dop.eng == op.eng == "pe" and not op.dma:
                    continue
                dop.sig = True
        for e in self.ENGS:
            n = 0
            for op in self.byeng[e]:
                if op.sig and not op.dma:
                    n += 1
                    op.seq = n
        finals = {}
        for key, cnt in self.dma_cnt.items():
            finals[key] = 16 * cnt

        def run(eng, h):
            waited = {}
            for op in self.byeng[eng]:
                need = {}
                for d in op.deps:
                    dop = ops[d]
                    if dop.dma:
                        s, v = ("d",) + dop.dsem, dop.dval
                    else:
                        if dop.eng == eng == "pe" and not op.dma:
                            continue
                        s, v = ("e", dop.eng), dop.seq
                    if v > need.get(s, 0):
                        need[s] = v
                for s, v in need.items():
                    if waited.get(s, 0) >= v:
                        continue
                    waited[s] = v
                    sem = sems[s[1]] if s[0] == "e" else dsems[(s[1], s[2])]
                    h.wait_ge(sem, v)
                ins = op.fn(h)
                if op.dma:
                    ins.then_inc(dsems[op.dsem], 16)
                elif op.sig:
                    ins.then_inc(sems[eng], 1)
            if eng == "sp":
                for key, v in finals.items():
                    if waited.get(("d",) + key, 0) < v:
                        h.wait_ge(dsems[key], v)
                for e2 in ("pe", "act", "dve", "pool"):
                    n = max([o.seq or 0 for o in self.byeng[e2]] + [0])
                    if n:
                        h.wait_ge(sems[e2], n)

        @block.tensor
        def _(h):
            run("pe", h)

        @block.scalar
        def _(h):
            run("act", h)

        @block.vector
        def _(h):
            run("dve", h)

        @block.gpsimd
        def _(h):
            run("pool", h)

        @block.sync
        def _(h):
            run("sp", h)


class Arena:
    def __init__(self, base):
        self.base = base
        self.n = base.shape[1]
        self.off = 0

    def reset(self, off=0):
        self.off = off

    def alloc(self, free_shape, dtype, parts=128):
        n = int(np.prod(free_shape))
        ne = n * 2 if dtype == F32 else n
        self.off = (self.off + 15) // 16 * 16
        assert self.off + ne <= self.n, ("arena overflow", self.off, ne, self.n)
        ap = self.base[0:parts, self.off:self.off + ne]
        self.off += ne
        if dtype == F32:
            ap = ap.bitcast(F32)
        if len(free_shape) == 2:
            ap = ap.rearrange("p (a b) -> p a b", a=free_shape[0])
        elif len(free_shape) == 3:
            ap = ap.rearrange("p (a b c) -> p a b c", a=free_shape[0], b=free_shape[1])
        return ap


class Ctx:
    pass


def _mm(P, out, lhsT, rhs, start, stop, r, w):
    P.add("pe", lambda h, o=out, a=lhsT, b=rhs, s=start, t=stop: h.matmul(o, a, b, start=s, stop=t), r=r, w=w)


def _tr(P, out, in_, ident, r, w):
    P.add("pe", lambda h, o=out, a=in_, i=ident: h.transpose(o, a, i), r=r, w=w)


def _act(P, out, in_, func, r, w, bias=None, scale=None, accum=None):
    def fn(h, o=out, a=in_, f=func, b=bias, s=scale, acc=accum):
        kw = {}
        if b is not None:
            kw["bias"] = b
        if s is not None:
            kw["scale"] = s
        if acc is not None:
            kw["accum_out"] = acc
        return h.activation(o, a, f, **kw)
    P.add("act", fn, r=r, w=w)


def _dma(P, q, out, in_, r, w, extra=(), slow=False):
    if slow:
        return P.add(q, lambda h, o=out, a=in_: h.dma_start(out=o, in_=a, allow_slow_non_contiguous=True), r=r, w=w, dma=True, extra=extra)
    return P.add(q, lambda h, o=out, a=in_: h.dma_start(out=o, in_=a), r=r, w=w, dma=True, extra=extra)


class PsumRR:
    def __init__(self, banks, keys):
        self.banks = banks
        self.keys = keys
        self.i = 0

    def next(self):
        b, k = self.banks[self.i], self.keys[self.i]
        self.i = (self.i + 1) % len(self.banks)
        return b, k


def load_bcast(P, C, dst, vec_ap, n, key, scale=None):
    _dma(P, "sp", dst, vec_ap[0:1, :].to_broadcast([128, n]), r=(), w=(key,))
    if scale is not None:
        P.add("dve", lambda h, o=dst, s=scale: h.tensor_scalar_mul(o, o, float(s)), r=(key,), w=(key,))


def rms_rstd(P, ss, out, rk, wk, n=D):
    _act(P, ss, ss, AF.Sqrt, r=rk, w=rk, bias=C_EPS[0], scale=1.0 / n)
    P.add("dve", lambda h, o=out, a=ss: h.reciprocal(o, a), r=rk, w=wk)


C_EPS = [None]


def prologue_steps(P, C, src_rows, tile, gain_b):
    hb = tile % 2
    hT = C.hT[hb]

    def load(s):
        j = tile * 4 + s
        xs = j % C.nx
        row0 = src_rows + j * 128
        _dma(P, "sp", C.X[xs], C.src[row0:row0 + 128, :], r=(("src", j),), w=(("x", xs),))

    def comp(s):
        j = tile * 4 + s
        xs = j % C.nx
        X = C.X[xs]
        sc = j % 4
        _act(P, C.junk, X, AF.Square, r=(("x", xs),), w=(("ss", sc),), accum=C.ss[:, sc:sc + 1])
        rms_rstd(P, C.ss[:, sc:sc + 1], C.rs[:, sc:sc + 1], rk=(("ss", sc),), wk=(("rs", sc),))
        xb = j % 2
        P.add("dve", lambda h, o=C.xn[xb], a=X, sca=C.rs[:, sc:sc + 1], g=gain_b:
              h.scalar_tensor_tensor(o, a, sca, g, ALU.mult, ALU.mult),
              r=(("x", xs), ("rs", sc), "gains"), w=(("xn", xb),))
        pb = j % 2
        pst = C.psT[pb]
        for kc in range(KC):
            _tr(P, pst[:, kc * 128:(kc + 1) * 128], C.xn[xb][:, kc * 128:(kc + 1) * 128], C.ident,
                r=(("xn", xb), "ident"), w=(("psT", pb),))
        _act(P, hT[:, :, s * 128:(s + 1) * 128], pst.rearrange("p (k t) -> p k t", k=KC), AF.Copy,
             r=(("psT", pb),), w=(("hT", hb, s),))

    L = lambda s: (lambda: load(s))
    Cm = lambda s: (lambda: comp(s))
    return [L(0), L(1), Cm(0), L(2), Cm(1), L(3), Cm(2), Cm(3)]


def prologue(P, C, src_rows, tile, gain_b, tag):
    for f in prologue_steps(P, C, src_rows, tile, gain_b):
        f()


class Filler:
    def __init__(self, fns, nsteps, start=0):
        self.fns = list(fns)
        self.at = {}
        n = len(self.fns)
        for k in range(n):
            step = start + (k * max(nsteps - start, 1)) // n
            self.at.setdefault(step, []).append(self.fns[k])

    def step(self, k):
        for f in self.at.pop(k, []):
            f()

    def flush(self):
        for k in sorted(self.at):
            for f in self.at[k]:
                f()
        self.at = {}


def post_norm_residual(P, C, banks, gain_b, j, final_gain=None, dst_rows=None):
    X = C.Xr
    for n in range(2):
        b, k = banks[n]
        _act(P, C.junk[:, 0:512], b, AF.Square, r=(k,), w=(("ssy", n),), accum=C.ssy[:, n:n + 1])
        P.add("dve", lambda h, o=C.t[:, n * 512:(n + 1) * 512], a=b, g=gain_b[:, n * 512:(n + 1) * 512]:
              h.tensor_tensor(o, a, g, ALU.mult), r=(k, "gains"), w=(("t", n),))
    P.add("dve", lambda h, o=C.ssy[:, 2:3], a=C.ssy[:, 0:1], b2=C.ssy[:, 1:2]: h.tensor_tensor(o, a, b2, ALU.add),
          r=(("ssy", 0), ("ssy", 1)), w=(("ssy", 2),))
    rms_rstd(P, C.ssy[:, 2:3], C.ssy[:, 3:4], rk=(("ssy", 2),), wk=(("ssy", 3),))
    P.add("dve", lambda h, o=X, a=C.t, sca=C.ssy[:, 3:4], x=X: h.scalar_tensor_tensor(o, a, sca, x, ALU.mult, ALU.add),
          r=(("t", 0), ("t", 1), ("ssy", 3), "xr"), w=("xr",))
    outt = X
    outk = "xr"
    if final_gain is not None:
        _act(P, C.junk, X, AF.Square, r=("xr",), w=(("ssf", 0),), accum=C.ssy[:, 4:5])
        rms_rstd(P, C.ssy[:, 4:5], C.ssy[:, 5:6], rk=(("ssf", 0),), wk=(("ssf", 1),))
        P.add("dve", lambda h, o=C.t, a=X, sca=C.ssy[:, 5:6], g=final_gain: h.scalar_tensor_tensor(o, a, sca, g, ALU.mult, ALU.mult),
              r=("xr", ("ssf", 1), "gains", ("t", 0), ("t", 1)), w=(("t", 0), ("t", 1)))
        outt = C.t
        outk = ("t", 0)
    if CFG.get("stop", 99) < 6:
        return
    _dma(P, "sp", C.dst[dst_rows:dst_rows + 128, :], outt, r=(outk, ("t", 1)) if final_gain is not None else (outk,),
         w=(("dst", j),))


def load_ffn_weights(P, C, wg_d, wu_d, wd_d, extra):
    W = C.wreg
    C.wg = W[:, 0:KC * DFF].rearrange("p (k n) -> p k n", k=KC)
    C.wu = W[:, KC * DFF:2 * KC * DFF].rearrange("p (k n) -> p k n", k=KC)
    C.wd = W[:, 2 * KC * DFF:2 * KC * DFF + NFC * D].rearrange("p (c n) -> p c n", c=NFC)
    wgv = wg_d.rearrange("(k p) n -> p k n", p=128)
    wuv = wu_d.rearrange("(k p) n -> p k n", p=128)
    wdv = wd_d.rearrange("(c p) n -> p c n", p=128)
    NS = 4
    cs = DFF // NS
    for i in range(NS):
        _dma(P, "pool", C.wg[:, :, i * cs:(i + 1) * cs], wgv[:, :, i * cs:(i + 1) * cs], r=(), w=(("wg", i),), extra=extra)
        _dma(P, "pool", C.wu[:, :, i * cs:(i + 1) * cs], wuv[:, :, i * cs:(i + 1) * cs], r=(), w=(("wu", i),), extra=extra)
    for i in range(2):
        _dma(P, "pool", C.wd[:, i * 11:(i + 1) * 11, :], wdv[:, i * 11:(i + 1) * 11, :], r=(), w=(("wd", i),), extra=extra)
    C.wslice = cs


def ffn_phase(P, C, ntiles, gain_pre, gain_post, gain_fin):
    prs = C.psrr
    stop = CFG.get("stop", 99)
    if stop < 1:
        return
    prologue(P, C, 0, 0, gain_pre, "f")
    if stop < 2:
        return
    for i in range(ntiles):
        hT = C.hT[i % 2]
        hk = [("hT", i % 2, s) for s in range(4)]
        fill = Filler(prologue_steps(P, C, 0, i + 1, gain_pre) if i + 1 < ntiles else [], NFC, start=2)
        for c in range(NFC):
            fill.step(c)
            wk = c * 128 // C.wslice
            bg, kg = prs.next()
            for kc in range(KC):
                _mm(P, bg, C.wg[:, kc, c * 128:(c + 1) * 128], hT[:, kc, :], kc == 0, kc == KC - 1,
                    r=hk + [("wg", wk)], w=(kg,))
            bu, ku = prs.next()
            for kc in range(KC):
                _mm(P, bu, C.wu[:, kc, c * 128:(c + 1) * 128], hT[:, kc, :], kc == 0, kc == KC - 1,
                    r=hk + [("wu", wk)], w=(ku,))
            sg = C.sg[c % 2]
            _act(P, sg, bg, AF.Silu, r=(kg,), w=(("sg", c % 2),))
            P.add("dve", lambda h, o=C.hid[:, c, :], a=sg, b=bu: h.tensor_tensor(o, a, b, ALU.mult),
                  r=(("sg", c % 2), ku), w=(("hid", c),))
        fill.flush()
        if stop < 3:
            return
        for s in range(4):
            j = i * 4 + s
            banks = []
            for n in range(2):
                b, k = prs.next()
                for c in range(NFC):
                    _mm(P, b, C.hid[:, c, s * 128:(s + 1) * 128], C.wd[:, c, n * 512:(n + 1) * 512], c == 0, c == NFC - 1,
                        r=(("hid", c), ("wd", c // 11)), w=(k,))
                banks.append((b, k))
            if stop < 4:
                continue
            _dma(P, "sp", C.Xr, C.src[j * 128:(j + 1) * 128, :], r=(("src", j),), w=("xr",))
            if stop < 5:
                continue
            post_norm_residual(P, C, banks, gain_post, j, final_gain=gain_fin, dst_rows=j * 128)


QKV_BLOCKS = ((0, 512, "q"), (512, 768, "q"), (768, 1280, "k"), (1280, 1536, "k"), (1536, 2048, "v"), (2048, 2304, "v"))


def qkv_blocks(tile, ntiles):
    if ntiles != 12:
        return QKV_BLOCKS
    if 2 <= tile <= 9:
        return QKV_BLOCKS
    if tile in (1, 10):
        return QKV_BLOCKS[2:]
    return (QKV_BLOCKS[3], QKV_BLOCKS[5])


def qkv_phase(P, C, ntiles, gain_pre):
    prs = C.psrr
    prologue(P, C, 0, 0, gain_pre, "q")
    ev = 0
    for i in range(ntiles):
        hT = C.hT[i % 2]
        blocks = qkv_blocks(i, ntiles)
        fill = Filler(prologue_steps(P, C, 0, i + 1, gain_pre) if i + 1 < ntiles else [], 4 * len(blocks))
        step = 0
        for s in range(4):
            j = i * 4 + s
            sb = j % 2
            stage = C.stage[sb]
            stage_v = stage[:, 1536:2316].rearrange("p (h e) -> p h e", e=65)
            for (c0, c1, kind) in blocks:
                fill.step(step)
                step += 1
                b, k = prs.next()
                n = c1 - c0
                for kc in range(KC):
                    _mm(P, b[:, 0:n], hT[:, kc, s * 128:(s + 1) * 128], C.wqkv[:, kc, c0:c1], kc == 0, kc == KC - 1,
                        r=(("hT", i % 2, s), "wqkv"), w=(k,))
                if kind == "v":
                    h0 = (c0 - 1536) // 64
                    nh = n // 64
                    out = stage_v[:, h0:h0 + nh, 0:64]
                    src = b[:, 0:n].rearrange("p (h e) -> p h e", e=64)
                else:
                    out = stage[:, c0:c1]
                    src = b[:, 0:n]
                scale = 0.125 if kind == "q" else 1.0
                if ev % 2 == 0:
                    _act(P, out, src, AF.Copy, r=(k,), w=(("stage", sb, c0),), scale=scale)
                else:
                    P.add("dve", lambda h, o=out, a=src, sc=scale: h.tensor_scalar_mul(o, a, float(sc)),
                          r=(k,), w=(("stage", sb, c0),))
                ev += 1
            _dma(P, "sp", C.dst[j * 128:(j + 1) * 128, :], stage, r=[("stage", sb, c0) for (c0, _, _) in QKV_BLOCKS],
                 w=(("dst", j),))
        fill.flush()


def build_bias(P, C, I, tscr):
    A = C.A
    rb = A.alloc((12,), F32, parts=32)
    E = A.alloc((3 * 129,), F32, parts=32)
    Tsb = A.alloc((3, 512), F32, parts=12)
    anti = A.alloc((128,), F32)
    R = [A.alloc((256,), F32) for _ in range(2)]
    _dma(P, "sp", rb, I["rel_bias"], r=(), w=("rb",))
    _dma(P, "sp", E, I["onehot"], r=(), w=("E",))
    _dma(P, "sp", anti, I["antiid"], r=(), w=("anti",))
    P.add("pool", lambda h: h.memset(Tsb, NEG), r=(), w=("Tsb",))
    b5 = C.ps[5][:, :]
    for g in range(3):
        _mm(P, b5[0:12, 0:129], rb[0:32, 0:12], E[0:32, g * 129:(g + 1) * 129], True, True, r=("rb", "E"), w=(("ps", 5),))
        P.add("dve", lambda h, o=Tsb[:, g, 127:256], a=b5[0:12, 0:129]: h.tensor_copy(o, a), r=(("ps", 5), "Tsb"), w=("Tsb",))
    for g in range(3):
        _dma(P, "sp", tscr[4 * g:4 * g + 4, :], Tsb[4 * g:4 * g + 4, g, :], r=("Tsb",), w=(("tscr", g),))
    for gh in range(12):
        Rt = R[gh % 2]
        src = bass.AP(tscr.tensor, gh * 512, [[1, 128], [1, 256]])
        _dma(P, "sp", Rt, src, r=(("tscr", gh // 4),), w=(("R", gh % 2),))
        b3, k3 = C.ps[3 + gh % 2][:, :], ("ps", 3 + gh % 2)
        _mm(P, b3[:, 0:256], anti, Rt, True, True, r=(("R", gh % 2), "anti"), w=(k3,))
        P.add("dve", lambda h, o=C.biasb[:, gh, :], a=b3[:, 0:256]: h.tensor_copy(o, a), r=(k3,), w=("biasb",))


def att_phase(P, C, qkv, att):
    segs = segments()
    psPT = [C.ps[4][:, :].bitcast(BF16), C.ps[5][:, :].bitcast(BF16)]
    Sb = [(C.ps[i][:, :], ("ps", i)) for i in range(3)]
    Ob = [(C.ps[6 + i][:, 0:260], ("ps", 6 + i)) for i in range(2)]
    psKQ = C.ps[3][:, :].bitcast(BF16)
    qt = qkv.tensor
    tcount = [0]
    units = []
    first_unit = {}
    bc = 0
    for si, (g, d, r, i0, nblk) in enumerate(segs):
        first_unit[si] = len(units)
        for b in range(nblk):
            for c in range(2):
                units.append((si, b, c, bc + b))
        bc += nblk
    N = len(units)

    def loads(si):
        g, d, r, i0, nblk = segs[si]
        rb = si % 4
        nt = nblk + 1
        rowk = d * (i0 - 64) + r
        rowq = d * i0 + r
        _dma(P, "sp", C.Kraw[rb][:, 0:nt, :], bass.AP(qt, rowk * QKVW + 768 + 256 * g, [[d * QKVW, 128], [128 * d * QKVW, nt], [1, 256]]),
             r=(), w=(("Kraw", rb),))
        _dma(P, "sp", C.Vt[rb][:, 0:nt, :], bass.AP(qt, rowk * QKVW + 1536 + 260 * g, [[d * QKVW, 128], [128 * d * QKVW, nt], [1, 260]]),
             r=(), w=(("Vt", rb),))
        _dma(P, "sp", C.Qraw[rb][:, 0:nblk, :], bass.AP(qt, rowq * QKVW + 256 * g, [[d * QKVW, 128], [128 * d * QKVW, nblk], [1, 256]]),
             r=(), w=(("Qraw", rb),))

    def transposes(si):
        g, d, r, i0, nblk = segs[si]
        rb, tb2 = si % 4, si % 2
        nt = nblk + 1
        for (raw, rk, dstT, dk, ntile) in ((C.Kraw[rb], ("Kraw", rb), C.KT[tb2], ("KT", tb2), nt),
                                          (C.Qraw[rb], ("Qraw", rb), C.QT[tb2], ("QT", tb2), nblk)):
            for j0 in range(0, ntile, 4):
                nj = min(4, ntile - j0)
                pk, pkk = psKQ, ("ps", 3)
                for c in range(2):
                    for jj in range(nj):
                        _tr(P, pk[:, c * 512 + jj * 128: c * 512 + (jj + 1) * 128], raw[:, j0 + jj, c * 128:(c + 1) * 128], C.ident,
                            r=(rk, "ident"), w=(pkk,))
                src = pk.rearrange("p (c t) -> p c t", c=2)[:, :, 0:nj * 128]
                if dk[0] == "KT":
                    P.add("dve", lambda h, o=dstT[:, :, j0 * 128:(j0 + nj) * 128], a=src: h.tensor_copy(o, a), r=(pkk,), w=(dk,))
                else:
                    _act(P, dstT[:, :, j0 * 128:(j0 + nj) * 128], src, AF.Copy, r=(pkk,), w=(dk,))

    def ctx(u):
        si, b, c, gb = units[u]
        g, d, r, i0, nblk = segs[si]
        return si, b, c, gb, g, d, r, i0

    def stage_S(u):
        si, b, c, gb, g, d, r, i0 = ctx(u)
        QT, KT = C.QT[si % 2], C.KT[si % 2]
        sbk, skey = Sb[u % 3]
        for hp in range(2):
            _mm(P, sbk[:, hp * 256:(hp + 1) * 256], QT[hp * 64:(hp + 1) * 64, c, b * 128:(b + 1) * 128],
                KT[hp * 64:(hp + 1) * 64, c, b * 128:b * 128 + 256], True, False, r=(("QT", si % 2), ("KT", si % 2)), w=(skey,))
            _mm(P, sbk[:, hp * 256:(hp + 1) * 256], C.ident, C.biasb[:, 4 * g + 2 * c + hp, :], False, True,
                r=("ident", "biasb"), w=(skey,))

    def stage_1(u):
        si, b, c, gb, g, d, r, i0 = ctx(u)
        sl = u % 4
        sbk, skey = Sb[u % 3]
        ost = C.ostage[gb % 3]
        ostk = ("ost", gb % 3)
        mcols = ost[:, 260 + 2 * c:262 + 2 * c]
        P.add("dve", lambda hh, o=mcols, a=sbk.rearrange("p (h k) -> p h k", h=2): hh.reduce_max(o, a, AX.X),
              r=(skey,), w=(ostk + (c,),))
        P.add("dve", lambda hh, o=C.negm[:, sl, :], a=mcols: hh.tensor_scalar_mul(o, a, -1.0), r=(ostk + (c,),), w=(("negm", sl),))
        for hp in range(2):
            _act(P, C.Pb[sl][:, hp, :], sbk[:, hp * 256:(hp + 1) * 256], AF.Exp, r=(skey, ("negm", sl)), w=(("Pb", sl, hp),),
                 bias=C.negm[:, sl, hp:hp + 1])

    def stage_2(u):
        sl = u % 4
        pt, ptk = psPT[u % 2], ("ps", 4 + u % 2)
        for hp in range(2):
            for jj in range(2):
                _tr(P, pt[:, (hp * 2 + jj) * 128:(hp * 2 + jj + 1) * 128], C.Pb[sl][:, hp, jj * 128:(jj + 1) * 128], C.ident,
                    r=(("Pb", sl, hp), "ident"), w=(ptk,))

    def stage_3(u):
        si, b, c, gb, g, d, r, i0 = ctx(u)
        sl = u % 4
        pt, ptk = psPT[u % 2], ("ps", 4 + u % 2)
        PTs = C.PTs[sl]
        ptv = pt[:, 0:512].rearrange("p (h j q) -> p h j q", h=2, j=2)
        for jj in range(2):
            vcol = C.vseg[:, si * 9 + b + jj: si * 9 + b + jj + 1]
            if jj == 0:
                _act(P, PTs[:, :, jj, :], ptv[:, :, jj, :], AF.Copy, r=(ptk, "vseg"), w=(("PTs", sl, jj),), scale=vcol)
            else:
                P.add("dve", lambda hh, o=PTs[:, :, jj, :], a=ptv[:, :, jj, :], sc=vcol: hh.tensor_scalar_mul(o, a, sc),
                      r=(ptk, "vseg"), w=(("PTs", sl, jj),))

    def stage_4(u):
        si, b, c, gb, g, d, r, i0 = ctx(u)
        sl = u % 4
        PTs = C.PTs[sl]
        Vt = C.Vt[si % 4]
        ob, okey = Ob[gb % 2]
        ost = C.ostage[gb % 3]
        ostk = ("ost", gb % 3)
        for hp in range(2):
            h = 2 * c + hp
            for jj in range(2):
                _mm(P, ob[:, h * 65:(h + 1) * 65], PTs[:, hp, jj, :], Vt[:, b + jj, h * 65:(h + 1) * 65], jj == 0, jj == 1,
                    r=(("PTs", sl, jj), ("Vt", si % 4)), w=(okey,))
        if c == 1:
            P.add("dve", lambda hh, o=ost[:, 0:260], a=ob: hh.tensor_copy(o, a), r=(okey,), w=(ostk + (9,),))
            t0 = d * (i0 + 128 * b) + r - HALO
            dst = bass.AP(att.tensor, (t0 * 3 + g) * ATTW, [[d * 3 * ATTW, 128], [1, ATTW]])
            _dma(P, "pool", dst, ost, r=[ostk + (x,) for x in (0, 1, 9)], w=(("att", si, b),))

    nseg = len(segs)
    loads(0)
    loads(1)
    loads(2)
    loads(3)
    transposes(0)
    seg_started = set()
    for u in range(min(2, N)):
        stage_S(u)
    for t in range(N + 4):
        if t < N:
            si = units[t][0]
            if si not in seg_started and first_unit[si] == t:
                seg_started.add(si)
                if si + 1 < nseg:
                    transposes(si + 1)
        if t + 2 < N:
            stage_S(t + 2)
        if t < N:
            stage_1(t)
        if 0 <= t - 1 < N:
            stage_2(t - 1)
        if 0 <= t - 2 < N:
            stage_3(t - 2)
        if 0 <= t - 3 < N:
            stage_4(t - 3)
            si3 = units[t - 3][0]
            if t - 3 + 1 == N or units[t - 3 + 1][0] != si3:
                if si3 + 4 < nseg:
                    loads(si3 + 4)


def b1_phase(P, C, ntiles, att):
    prs = C.psrr
    prologue(P, C, HALO, 0, C.gpre, "m")
    for i in range(ntiles):
        hT = C.hT[i % 2]
        hk = [("hT", i % 2, s) for s in range(4)]
        AI = C.attin
        _dma(P, "sp", AI, att[i * 512:(i + 1) * 512, :].rearrange("(s p) (g e) -> p s g e", p=128, g=3), r=(), w=("attin",))
        zb = []
        for s in range(4):
            bu, ku = prs.next()
            for kc in range(KC):
                _mm(P, bu, hT[:, kc, s * 128:(s + 1) * 128], C.wrest[:, kc, 0:512], kc == 0, kc == KC - 1, r=(hk[s], "wrest"), w=(ku,))
            _act(P, C.U[s], bu, AF.Gelu, r=(ku,), w=(("U", s),))
            bv, kv = prs.next()
            for kc in range(KC):
                _mm(P, bv, hT[:, kc, s * 128:(s + 1) * 128], C.wrest[:, kc, 512:1024], kc == 0, kc == KC - 1, r=(hk[s], "wrest"), w=(kv,))
            _act(P, C.V[s], bv, AF.Gelu, r=(kv,), w=(("V", s), ("st0", s)), accum=C.st[:, 0, s:s + 1])
        mview = AI[:, :, :, 260:264]
        P.add("dve", lambda h, o=C.M, a=mview.rearrange("p s g h -> p s h g"): h.tensor_reduce(o, a, AX.X, ALU.max),
              r=("attin",), w=("M",))
        P.add("dve", lambda h, o=C.dd, a=mview, m=C.M.unsqueeze(2).to_broadcast([128, 4, 3, 4]): h.tensor_tensor(o, a, m, ALU.subtract),
              r=("attin", "M"), w=("dd",))
        _act(P, C.ww, C.dd, AF.Exp, r=("dd",), w=("ww",))
        sview = AI[:, :, :, 0:260].rearrange("p s g (h e) -> p s g h e", e=65)
        for s in range(4):
            P.add("dve", lambda h, o=C.wd_[:, s], a=C.ww[:, s], sv=sview[:, s, :, :, 64]: h.tensor_tensor(o, a, sv, ALU.mult),
                  r=("ww", "attin"), w=(("wd", s),))
        P.add("dve", lambda h, o=C.den, a=C.wd_.rearrange("p s g h -> p s h g"): h.tensor_reduce(o, a, AX.X, ALU.add),
              r=[("wd", s) for s in range(4)], w=("den",))
        P.add("dve", lambda h, o=C.den: h.reciprocal(o, o), r=("den",), w=("den",))
        P.add("dve", lambda h, o=C.cc, a=C.ww, m=C.den.unsqueeze(2).to_broadcast([128, 4, 3, 4]): h.tensor_tensor(o, a, m, ALU.mult),
              r=("ww", "den"), w=("cc",))
        P.add("dve", lambda h, o=C.st[:, 1, :], a=C.st[:, 0, :]: h.tensor_scalar_mul(o, a, -1.0 / 512.0),
              r=[("st0", s) for s in range(4)], w=("st1",))
        for s in range(4):
            _act(P, C.junk[:, 0:512], C.V[s], AF.Square, r=(("V", s), "st1"), w=(("st2", s),), bias=C.st[:, 1, s:s + 1],
                 accum=C.st[:, 2, s:s + 1])
        _act(P, C.st[:, 2, :], C.st[:, 2, :], AF.Sqrt, r=[("st2", s) for s in range(4)], w=[("st2", s) for s in range(4)],
             bias=C_EPS[0], scale=1.0 / 512.0)
        P.add("dve", lambda h, o=C.st[:, 3, :], a=C.st[:, 2, :]: h.reciprocal(o, a), r=[("st2", s) for s in range(4)], w=("st3",))
        for s in range(4):
            V = C.V[s]
            P.add("dve", lambda h, o=V, s1=C.st[:, 1, s:s + 1], s2=C.st[:, 3, s:s + 1]: h.tensor_scalar(o, o, s1, s2, ALU.add, ALU.mult),
                  r=(("V", s), "st1", "st3", ("st2", s)), w=(("V", s),))
        for s in range(4):
            V = C.V[s]
            P.add("dve", lambda h, o=V, g=C.lng: h.tensor_tensor(o, o, g, ALU.mult), r=(("V", s), "gains"), w=(("V", s),))
        for s in range(4):
            P.add("dve", lambda h, o=C.vln[s], a=C.V[s], g=C.lnb: h.tensor_tensor(o, a, g, ALU.add), r=(("V", s), "gains"), w=(("vln", s),))
        for s in range(4):
            ts = s % 2
            oview = sview[:, s, :, :, 0:64]
            P.add("dve", lambda h, o=C.tmpO[ts], a=oview, m=C.cc[:, s].unsqueeze(3).to_broadcast([128, 3, 4, 64]): h.tensor_tensor(o, a, m, ALU.mult),
                  r=("attin", "cc"), w=(("tmpO", ts),))

            def _red(h, o=C.attb[s], a=C.tmpO[ts].rearrange("p g h e -> p h e g")):
                with C.nc.allow_low_precision(reason="3-term sum on the fp32 ALU, rounded once to bf16 for the matmul"):
                    return h.tensor_reduce(o, a, AX.X, ALU.add)
            P.add("dve", _red, r=(("tmpO", ts),), w=(("attb", s),))
        mb = []
        for s in range(4):
            bm, km = prs.next()
            for gg in range(4):
                _mm(P, bm[:, gg * 128:(gg + 1) * 128], C.WsT[:, gg, :], C.vln[s][:, gg * 128:(gg + 1) * 128], True, True,
                    r=(("vln", s), "WsT"), w=(km,))
            for gg in range(4):
                P.add("dve", lambda h, o=C.sgu[s][:, gg * 128:(gg + 1) * 128], a=bm[:, gg * 128:(gg + 1) * 128], sc=C.bs[:, gg:gg + 1],
                      u=C.U[s][:, gg * 128:(gg + 1) * 128]: h.scalar_tensor_tensor(o, a, sc, u, ALU.add, ALU.mult),
                      r=(km, ("U", s), "bs"), w=(("sgu", s),))
        tp = 0
        for s in range(4):
            pb = tp % 2
            tp += 1
            pst = C.psT[pb]
            for gg in range(4):
                _tr(P, pst[:, gg * 128:(gg + 1) * 128], C.sgu[s][:, gg * 128:(gg + 1) * 128], C.ident, r=(("sgu", s), "ident"), w=(("psT", pb),))
            ab = C.attb[s].rearrange("p h e -> p (h e)")
            for cc_ in range(2):
                _tr(P, pst[:, 512 + cc_ * 128:512 + (cc_ + 1) * 128], ab[:, cc_ * 128:(cc_ + 1) * 128], C.ident, r=(("attb", s), "ident"), w=(("psT", pb),))
            _act(P, C.sguT[:, :, s * 128:(s + 1) * 128], pst[:, 0:512].rearrange("p (k t) -> p k t", k=4), AF.Copy,
                 r=(("psT", pb),), w=(("sguT", s),))
            _act(P, C.attT[:, :, s * 128:(s + 1) * 128], pst[:, 512:768].rearrange("p (k t) -> p k t", k=2), AF.Copy,
                 r=(("psT", pb),), w=(("attT", s),))
        sguk = [("sguT", s) for s in range(4)]
        attk = [("attT", s) for s in range(4)]
        fill = Filler(prologue_steps(P, C, HALO, i + 1, C.gpre) if i + 1 < ntiles else [], 8)
        for oc in range(8):
            fill.step(oc)
            bga, kga = prs.next()
            for kc in range(KC):
                _mm(P, bga, C.wrest[:, kc, 1024 + oc * 128:1024 + (oc + 1) * 128], hT[:, kc, :], kc == 0, kc == KC - 1, r=hk + ["wrest"], w=(kga,))
            bgb, kgb = prs.next()
            for kc in range(KC):
                _mm(P, bgb, C.wrest[:, kc, 2048 + oc * 128:2048 + (oc + 1) * 128], hT[:, kc, :], kc == 0, kc == KC - 1, r=hk + ["wrest"], w=(kgb,))
            bpa, kpa = prs.next()
            for kc in range(2):
                _mm(P, bpa, C.watt[:, kc, oc * 128:(oc + 1) * 128], C.attT[:, kc, :], kc == 0, kc == 1, r=attk + ["watt"], w=(kpa,))
            bpb, kpb = prs.next()
            for kc in range(4):
                _mm(P, bpb, C.wsgu[:, kc, oc * 128:(oc + 1) * 128], C.sguT[:, kc, :], kc == 0, kc == 3, r=sguk + ["wsgu"], w=(kpb,))
            o2 = oc % 2
            _act(P, C.sa[o2], bga, AF.Sigmoid, r=(kga,), w=(("sa", o2),))
            _act(P, C.sb_[o2], bgb, AF.Sigmoid, r=(kgb,), w=(("sb", o2),))
            P.add("dve", lambda h, o=C.sa[o2], b=bpa: h.tensor_tensor(o, o, b, ALU.mult), r=(("sa", o2), kpa), w=(("sa", o2),))
            P.add("dve", lambda h, o=C.sb_[o2], b=bpb: h.tensor_tensor(o, o, b, ALU.mult), r=(("sb", o2), kpb), w=(("sb", o2),))
            P.add("dve", lambda h, o=C.mT[:, oc, :], a=C.sa[o2], b=C.sb_[o2]: h.tensor_tensor(o, a, b, ALU.add),
                  r=(("sa", o2), ("sb", o2)), w=(("mT", oc),))
        fill.flush()
        mk = [("mT", oc) for oc in range(8)]
        for s in range(4):
            j = i * 4 + s
            banks = []
            for n in range(2):
                b, k = prs.next()
                for kc in range(KC):
                    _mm(P, b, C.mT[:, kc, s * 128:(s + 1) * 128], C.wout[:, kc, n * 512:(n + 1) * 512], kc == 0, kc == KC - 1,
                        r=mk + ["wout"], w=(k,))
                banks.append((b, k))
            _dma(P, "sp", C.Xr, C.src[HALO + j * 128:HALO + (j + 1) * 128, :], r=(), w=("xr",))
            post_norm_residual(P, C, banks, C.gpost, j, final_gain=None, dst_rows=j * 128)


def build_program(cfg):
    nc = bass.Bass("TRN2", target_bir_lowering=False)
    dbg = cfg["debug"]
    phases = cfg["phases"]
    n_ext = cfg["n_ext_tiles"]
    n_own = cfg["n_own_tiles"]

    def din(name, shape):
        return nc.dram_tensor(name, list(shape), F32, kind="ExternalInput").ap()

    def dscr(name, shape, dt):
        if dbg:
            return nc.dram_tensor(name, list(shape), dt, kind="ExternalOutput").ap()
        return nc.dram_tensor(name, list(shape), dt).ap()

    I = {}
    I["xe"] = din("xe", (NEXT, D))
    for nm in ("w_gate1", "w_up1", "w_gate2", "w_up2"):
        I[nm] = din(nm, (D, DFF))
    for nm in ("w_down1", "w_down2"):
        I[nm] = din(nm, (DFF, D))
    I["w_in"] = din("w_in", (D, 5376))
    I["w_att"] = din("w_att", (256, D))
    I["w_sgu"] = din("w_sgu", (512, D))
    I["w_out"] = din("w_out", (D, D))
    for nm in ("g1pre", "g1post", "gmpre", "gmpost", "g2pre", "g2post", "gfin"):
        I[nm] = din(nm, (1, D))
    I["ln_g"] = din("ln_g", (1, 512))
    I["ln_b"] = din("ln_b", (1, 512))
    I["w_s"] = din("w_s", (4, 128, 128))
    I["b_s"] = din("b_s", (4, 128))
    I["rel_bias"] = din("rel_bias", (32, 12))
    I["ident"] = din("ident", (128, 128))
    I["antiid"] = din("antiid", (128, 128))
    I["onehot"] = din("onehot", (32, 3 * 129))
    I["vseg"] = din("vseg", (128, 24 * 9))
    y = nc.dram_tensor("y", [NOWN, D], F32, kind="ExternalOutput").ap()
    x1 = dscr("x1s", (NEXT, D), F32)
    qkv = dscr("qkvs", (NEXT, QKVW), BF16)
    att = dscr("atts", (NOWN, 3 * ATTW), F32)
    x2 = dscr("x2s", (NOWN, D), F32)
    tscr = nc.dram_tensor("tscr", [12, 512], F32).ap()

    P = Prog(nc)
    with ExitStack() as es:
        wreg = es.enter_context(nc.sbuf_tensor("wreg", [128, WREG_ELEMS], BF16))
        areg = es.enter_context(nc.sbuf_tensor("areg", [128, ARENA_ELEMS], BF16))
        identt = es.enter_context(nc.sbuf_tensor("identb", [128, 128], BF16))
        ps = [es.enter_context(nc.psum_tensor("ps%d" % i, [128, 512], F32)) for i in range(8)]
        sems = {e: es.enter_context(nc.semaphore("sem_" + e)) for e in ("pe", "act", "dve", "pool")}
        dsems = {}
        for q in ("sp", "pool"):
            for i in range(P.ndma):
                dsems[(q, i)] = es.enter_context(nc.semaphore("dsem_%s%d" % (q, i)))

        C = Ctx()
        C.nc = nc
        C.wreg = wreg[:, :]
        C.ident = identt[:, :]
        A = Arena(areg[:, :])
        _dma(P, "pool", C.ident, I["ident"], r=(), w=("ident",))
        epst = es.enter_context(nc.sbuf_tensor("epst", [128, 1], F32))
        C_EPS[0] = epst[:, :]
        P.add("pool", lambda h: h.memset(epst[:, :], EPS), r=(), w=("eps",))
        P.barrier()

        def ffn_setup(src, dst, gpre, gpost, gfin):
            A.reset()
            C.src, C.dst = src, dst
            C.nx = 2
            C.X = [A.alloc((D,), F32) for _ in range(C.nx)]
            C.Xr = A.alloc((D,), F32)
            C.xn = [A.alloc((D,), BF16) for _ in range(2)]
            C.hT = [A.alloc((KC, TT), BF16) for _ in range(2)]
            C.hid = A.alloc((NFC, TT), BF16)
            C.sg = [A.alloc((TT,), BF16) for _ in range(2)]
            C.t = A.alloc((D,), F32)
            C.junk = A.alloc((D,), BF16)
            C.ss = A.alloc((4,), F32)
            C.rs = A.alloc((4,), F32)
            C.ssy = A.alloc((8,), F32)
            C.gpre = A.alloc((D,), F32)
            C.gpost = A.alloc((D,), F32)
            load_bcast(P, C, C.gpre, gpre, D, "gains")
            load_bcast(P, C, C.gpost, gpost, D, "gains", scale=0.5)
            C.gfin = None
            if gfin is not None:
                C.gfin = A.alloc((D,), F32)
                load_bcast(P, C, C.gfin, gfin, D, "gains")
            C.psT = [ps[6][:, :].bitcast(BF16), ps[7][:, :].bitcast(BF16)]
            C.psrr = PsumRR([ps[i][:, :] for i in range(6)], [("ps", i) for i in range(6)])

        last_pe = None
        if "A1" in phases:
            load_ffn_weights(P, C, I["w_gate1"], I["w_up1"], I["w_down1"], extra=())
            ffn_setup(I["xe"], x1, I["g1pre"], I["g1post"], None)
            ffn_phase(P, C, n_ext, C.gpre, C.gpost, None)
            last_pe = P.last("pe")
            P.barrier()


        def common_setup(src, dst, gpre, gpost, gpost_scale):
            A.reset()
            C.A = A
            C.ps = ps
            C.src, C.dst = src, dst
            C.nx = 2
            C.X = [A.alloc((D,), F32) for _ in range(C.nx)]
            C.xn = [A.alloc((D,), BF16) for _ in range(2)]
            C.hT = [A.alloc((KC, TT), BF16) for _ in range(2)]
            C.junk = A.alloc((D,), BF16)
            C.ss = A.alloc((4,), F32)
            C.rs = A.alloc((4,), F32)
            C.ssy = A.alloc((8,), F32)
            C.gpre = A.alloc((D,), F32)
            load_bcast(P, C, C.gpre, gpre, D, "gains")
            if gpost is not None:
                C.gpost = A.alloc((D,), F32)
                load_bcast(P, C, C.gpost, gpost, D, "gains", scale=gpost_scale)
                C.Xr = A.alloc((D,), F32)
                C.t = A.alloc((D,), F32)
            C.psT = [ps[6][:, :].bitcast(BF16), ps[7][:, :].bitcast(BF16)]
            C.psrr = PsumRR([ps[i][:, :] for i in range(6)], [("ps", i) for i in range(6)])

        if "A2" in phases:
            ex = (last_pe,) if last_pe is not None else ()
            C.wqkv = C.wreg[:, 0:KC * 2304].rearrange("p (k n) -> p k n", k=KC)
            wv = I["w_in"].rearrange("(k p) n -> p k n", p=128)
            for hh in range(2):
                _dma(P, "pool", C.wqkv[:, hh * 4:(hh + 1) * 4, :], wv[:, hh * 4:(hh + 1) * 4, 0:2304], r=(), w=("wqkv",), extra=ex)
            common_setup(x1, qkv, I["gmpre"], None, None)
            C.stage = [A.alloc((QKVW,), BF16) for _ in range(2)]
            for sb in range(2):
                sv = C.stage[sb][:, 1536:2316].rearrange("p (h e) -> p h e", e=65)
                P.add("pool", lambda h, o=sv[:, :, 64:65]: h.memset(o, 1.0), r=(), w=[("stage", sb, c0) for (c0, _, _) in QKV_BLOCKS])
            qkv_phase(P, C, n_ext, C.gpre)
            last_pe = P.last("pe")
            P.barrier()

        if "ATT" in phases:
            A.reset()
            C.A = A
            C.ps = ps
            AT = Arena(C.wreg[:, 38912:WREG_ELEMS])
            C.Kraw = [AT.alloc((9, 256), BF16) for _ in range(4)]
            C.Vt = [AT.alloc((9, 260), BF16) for _ in range(4)]
            C.Qraw = [AT.alloc((8, 256), BF16) for _ in range(4)]
            C.KT = [A.alloc((2, 9 * 128), BF16) for _ in range(2)]
            C.QT = [A.alloc((2, 8 * 128), BF16) for _ in range(2)]
            C.biasb = A.alloc((12, 256), BF16)
            C.Pb = [A.alloc((2, 256), BF16) for _ in range(4)]
            C.PTs = [A.alloc((2, 2, 128), BF16) for _ in range(4)]
            C.ostage = [A.alloc((ATTW,), F32) for _ in range(3)]
            C.vseg = A.alloc((24 * 9,), F32)
            C.negm = A.alloc((4, 2), F32)
            _dma(P, "sp", C.vseg, I["vseg"], r=(), w=("vseg",))
            build_bias(P, C, I, tscr)
            att_phase(P, C, qkv, att)
            last_pe = P.last("pe")
            P.barrier()

        if "B1" in phases:
            ex = (last_pe,) if last_pe is not None else ()
            W = C.wreg
            C.wrest = W[:, 0:24576].rearrange("p (k n) -> p k n", k=KC)
            C.watt = W[:, 24576:26624].rearrange("p (k n) -> p k n", k=2)
            C.wsgu = W[:, 26624:30720].rearrange("p (k n) -> p k n", k=4)
            C.wout = W[:, 30720:38912].rearrange("p (k n) -> p k n", k=KC)
            wv = I["w_in"].rearrange("(k p) n -> p k n", p=128)
            for hh in range(2):
                _dma(P, "pool", C.wrest[:, hh * 4:(hh + 1) * 4, :], wv[:, hh * 4:(hh + 1) * 4, 2304:5376], r=(), w=("wrest",), extra=ex)
            _dma(P, "pool", C.watt, I["w_att"].rearrange("(k p) n -> p k n", p=128), r=(), w=("watt",), extra=ex)
            _dma(P, "pool", C.wsgu, I["w_sgu"].rearrange("(k p) n -> p k n", p=128), r=(), w=("wsgu",), extra=ex)
            _dma(P, "pool", C.wout, I["w_out"].rearrange("(k p) n -> p k n", p=128), r=(), w=("wout",), extra=ex)
            common_setup(x1, x2, I["gmpre"], I["gmpost"], None)
            A2 = Arena(W[:, 38912:WREG_ELEMS])
            C.U = [A2.alloc((512,), BF16) for _ in range(4)]
            C.V = [A2.alloc((512,), F32) for _ in range(4)]
            C.st = A2.alloc((4, 4), F32)
            C.vln = [A2.alloc((512,), BF16) for _ in range(4)]
            C.sgu = [A2.alloc((512,), BF16) for _ in range(4)]
            C.sguT = A2.alloc((4, TT), BF16)
            C.attin = A2.alloc((4, 3, ATTW), F32)
            C.M = A2.alloc((4, 4), F32)
            C.dd = A2.alloc((4, 3, 4), F32)
            C.ww = A2.alloc((4, 3, 4), F32)
            C.wd_ = A2.alloc((4, 3, 4), F32)
            C.cc = A2.alloc((4, 3, 4), F32)
            C.den = A2.alloc((4, 4), F32)
            C.tmpO = [A2.alloc((3, 4, 64), F32) for _ in range(2)]
            C.attb = [A2.alloc((4, 64), BF16) for _ in range(4)]
            C.attT = A2.alloc((2, TT), BF16)
            C.sa = [A.alloc((TT,), F32) for _ in range(2)]
            C.sb_ = [A.alloc((TT,), F32) for _ in range(2)]
            C.mT = A.alloc((KC, TT), BF16)
            C.lng = A.alloc((512,), F32)
            C.lnb = A.alloc((512,), F32)
            load_bcast(P, C, C.lng, I["ln_g"], 512, "gains")
            load_bcast(P, C, C.lnb, I["ln_b"], 512, "gains")
            C.WsT = A.alloc((4, 128), BF16)
            C.bs = A.alloc((4,), F32)
            Wn = A2.alloc((4, 128), F32)
            Wnb = A2.alloc((4, 128), BF16)
            _dma(P, "sp", Wn, I["w_s"].rearrange("g p q -> p g q"), r=(), w=("Wn",))
            _dma(P, "sp", C.bs, I["b_s"].rearrange("g p -> p g"), r=(), w=("bs",), slow=True)
            P.add("dve", lambda h: h.tensor_copy(Wnb, Wn), r=("Wn",), w=("Wnb",))
            for gg in range(4):
                _tr(P, C.psT[0][:, gg * 128:(gg + 1) * 128], Wnb[:, gg, :], C.ident, r=("Wnb", "ident"), w=(("psT", 0),))
            _act(P, C.WsT, C.psT[0][:, 0:512].rearrange("p (g q) -> p g q", g=4), AF.Copy, r=(("psT", 0),), w=("WsT",))
            b1_phase(P, C, n_own, att)
            last_pe = P.last("pe")
            P.barrier()

        if "B2" in phases:
            load_ffn_weights(P, C, I["w_gate2"], I["w_up2"], I["w_down2"], extra=(last_pe,) if last_pe is not None else ())
            ffn_setup(x2, y, I["g2pre"], I["g2post"], I["gfin"])
            ffn_phase(P, C, n_own, C.gpre, C.gpost, C.gfin)
            P.barrier()

        with nc.Block() as block:
            P.emit(block, sems, dsems)
    return nc


def _t5_bucket(rel):
    nb = 16
    max_exact = 8
    ret = (rel > 0).astype(np.int32) * nb
    n = np.abs(rel)
    large = max_exact + (np.log(np.maximum(n, 1) / max_exact) / np.log(1024 / max_exact) * (nb - max_exact)).astype(np.int32)
    large = np.minimum(large, nb - 1)
    return ret + np.where(n < max_exact, n, large)


def segments():
    segs = []
    for g, (window, d) in enumerate(GROUPS):
        i_first = HALO // d
        nown = NOWN // d
        nblk_total = nown // 128
        per = 8 if nblk_total >= 8 else nblk_total
        for r in range(d):
            for s0 in range(0, nblk_total, per):
                segs.append((g, d, r, i_first + 128 * s0, per))
    return segs


def host_constants(seq_lo, seq_hi):
    ident = np.eye(128, dtype=np.float32)
    anti = np.ascontiguousarray(ident[::-1])
    onehot = np.zeros((32, 3, 129), np.float32)
    for g, (window, d) in enumerate(GROUPS):
        half = (window // 2) // d
        off = np.arange(-half, half + 1, dtype=np.int32) * d
        bk = _t5_bucket(off)
        onehot[bk, g, np.arange(129)] = 1.0
    segs = segments()
    vseg = np.zeros((128, 24, 9), np.float32)
    for si, (g, d, r, i0, nblk) in enumerate(segs):
        for j in range(nblk + 1):
            sub = i0 - 64 + 128 * j + np.arange(128)
            pos = d * sub + r
            vseg[:, si, j] = ((pos >= seq_lo) & (pos < seq_hi)).astype(np.float32)
    return ident, anti, onehot.reshape(32, 3 * 129), vseg.reshape(128, 24 * 9)


def make_in_maps(inp):
    f = lambda a: np.ascontiguousarray(np.asarray(a, dtype=np.float32))
    xp = f(inp["x_prompt"])[0]
    xs = f(inp["x_sample"])
    shared = {
        "w_gate1": f(inp["ffn1_w_gate"])[0], "w_up1": f(inp["ffn1_w_up"])[0], "w_down1": f(inp["ffn1_w_down"])[0],
        "w_gate2": f(inp["ffn2_w_gate"])[0], "w_up2": f(inp["ffn2_w_up"])[0], "w_down2": f(inp["ffn2_w_down"])[0],
        "w_in": f(inp["w_in"])[0], "w_att": f(inp["w_att"])[0], "w_sgu": f(inp["w_sgu"])[0], "w_out": f(inp["w_out"])[0],
        "g1pre": f(inp["ffn1_pre_g"]), "g1post": f(inp["ffn1_post_g"]), "gmpre": f(inp["mix_pre_g"]),
        "gmpost": f(inp["mix_post_g"]), "g2pre": f(inp["ffn2_pre_g"]), "g2post": f(inp["ffn2_post_g"]),
        "gfin": f(inp["final_g"]), "ln_g": f(inp["sgu_ln_g"]), "ln_b": f(inp["sgu_ln_b"]),
        "w_s": f(inp["sgu_w_s"])[0], "b_s": f(inp["sgu_b_s"])[0], "rel_bias": f(inp["rel_bias"]),
    }
    maps = []
    for c in range(NCORES):
        xe = np.zeros((NEXT, D), np.float32)
        if c < 4:
            lo = c * NOWN - HALO
            hi = lo + NEXT
            a, b = max(lo, 0), min(hi, xp.shape[0])
            xe[a - lo:b - lo] = xp[a:b]
            seq_lo, seq_hi = a - lo, b - lo
        else:
            xe[HALO:HALO + NOWN] = xs[c - 4]
            seq_lo, seq_hi = HALO, HALO + NOWN
        ident, anti, onehot, vseg = host_constants(seq_lo, seq_hi)
        m = dict(shared)
        m.update({"xe": xe, "ident": ident, "antiid": anti, "onehot": onehot, "vseg": vseg})
        maps.append(m)
    return maps


_NC_CACHE = {}


def kernel(**inputs):
    key = "full"
    if key not in _NC_CACHE:
        _NC_CACHE[key] = build_program(CFG)
    nc = _NC_CACHE[key]
    maps = make_in_maps(inputs)
    res = run_bass_kernel_spmd(nc, maps, core_ids=list(range(NCORES)))
    ys = [np.asarray(r["y"], dtype=np.float32) for r in res.results]
    y_prompt = np.concatenate(ys[0:4], axis=0)[None]
    y_sample = np.stack(ys[4:8], axis=0)
    return (y_prompt, y_sample)
```

```python
import numpy as np
import concourse.bass as bass
import concourse.mybir as mybir
from concourse.bass_utils import run_bass_kernel_spmd
from contextlib import ExitStack

F32 = mybir.dt.float32
BF16 = mybir.dt.bfloat16
AF = mybir.ActivationFunctionType
ALU = mybir.AluOpType
AX = mybir.AxisListType

NCORES = 8
D = 1024
DFF = 2816
NFC = 22
KC = 8
NEXT = 6144
NOWN = 4096
HALO = 1024
TT = 512
QKVW = 2316
ATTW = 264
EPS = 1e-6
NEG = -1e30
GROUPS = ((128, 1), (512, 4), (2048, 16))
WREG_ELEMS = 67584
ARENA_ELEMS = 38400

CFG = {"phases": ("A1", "A2", "ATT", "B1", "B2"), "debug": False, "n_ext_tiles": 12, "n_own_tiles": 8}


class Op:
    __slots__ = ("id", "eng", "fn", "deps", "dma", "dsem", "dval", "seq", "sig")


LAST_PROG = [None]
STAGE_LOG = []


class Prog:
    ENGS = ("pe", "act", "dve", "pool", "sp")

    def __init__(self, nc, ndma=20):
        self.nc = nc
        self.ops = []
        self.byeng = {e: [] for e in self.ENGS}
        self.lastw = {}
        self.readers = {}
        self.ndma = ndma
        self.dma_rr = {"sp": 0, "pool": 0, "act": 0}
        self.dma_cnt = {}
        self.dma_last = {}
        self.pending = {e: set() for e in self.ENGS}

    def add(self, eng, fn, r=(), w=(), dma=False, extra=()):
        op = Op()
        op.id = len(self.ops)
        op.eng = eng
        op.fn = fn
        op.dma = dma
        op.sig = False
        op.seq = None
        deps = set(extra)
        psr = [k for k in r if isinstance(k, tuple) and k[0] in ("ps", "psT")]
        if psr:
            r = [k for k in r if k not in psr]
            w = list(w) + psr
        for k in r:
            lw = self.lastw.get(k)
            if lw is not None:
                deps.add(lw)
        for k in w:
            lw = self.lastw.get(k)
            if lw is not None:
                deps.add(lw)
            deps.update(self.readers.get(k, ()))
        if self.pending[eng]:
            deps.update(self.pending[eng])
            self.pending[eng] = set()
        if dma:
            i = self.dma_rr[eng]
            self.dma_rr[eng] = (i + 1) % self.ndma
            key = (eng, i)
            prev = self.dma_last.get(key)
            if prev is not None:
                deps.add(prev)
            self.dma_cnt[key] = self.dma_cnt.get(key, 0) + 1
            self.dma_last[key] = op.id
            op.dsem = key
            op.dval = 16 * self.dma_cnt[key]
        deps.discard(op.id)
        op.deps = deps
        for k in r:
            self.readers.setdefault(k, []).append(op.id)
        for k in w:
            self.lastw[k] = op.id
            self.readers[k] = []
        self.ops.append(op)
        self.byeng[eng].append(op)
        return op.id

    def last(self, eng):
        return self.byeng[eng][-1].id if self.byeng[eng] else None

    def barrier(self):
        ids = set()
        for e in self.ENGS:
            if self.byeng[e]:
                ids.add(self.byeng[e][-1].id)
        for key, oid in self.dma_last.items():
            ids.add(oid)
        for e in self.ENGS:
            self.pending[e] |= ids
        self.lastw = {}
        self.readers = {}

    def emit(self, block, sems, dsems):
        ops = self.ops
        for op in ops:
            for d in op.deps:
                dop = ops[d]
                if dop.dma:
                    continue
                if dop.eng == op.eng == "pe" and not op.dma:
                    continue
                dop.sig = True
        for e in self.ENGS:
            n = 0
            for op in self.byeng[e]:
                if op.sig and not op.dma:
                    n += 1
                    op.seq = n
        finals = {}
        for key, cnt in self.dma_cnt.items():
            finals[key] = 16 * cnt

        def run(eng, h):
            waited = {}
            for op in self.byeng[eng]:
                need = {}
                for d in op.deps:
                    dop = ops[d]
                    if dop.dma:
                        s, v = ("d",) + dop.dsem, dop.dval
                    else:
                        if dop.eng == eng == "pe" and not op.dma:
                            continue
                        s, v = ("e", dop.eng), dop.seq
                    if v > need.get(s, 0):
                        need[s] = v
                for s, v in need.items():
                    if waited.get(s, 0) >= v:
                        continue
                    waited[s] = v
                    sem = sems[s[1]] if s[0] == "e" else dsems[(s[1], s[2])]
                    h.wait_ge(sem, v)
                ins = op.fn(h)
                if op.dma:
                    ins.then_inc(dsems[op.dsem], 16)
                elif op.sig:
                    ins.then_inc(sems[eng], 1)
            if eng == "sp":
                for key, v in finals.items():
                    if waited.get(("d",) + key, 0) < v:
                        h.wait_ge(dsems[key], v)
                for e2 in ("pe", "act", "dve", "pool"):
                    n = max([o.seq or 0 for o in self.byeng[e2]] + [0])
                    if n:
                        h.wait_ge(sems[e2], n)

        @block.tensor
        def _(h):
            run("pe", h)

        @block.scalar
        def _(h):
            run("act", h)

        @block.vector
        def _(h):
            run("dve", h)

        @block.gpsimd
        def _(h):
            run("pool", h)

        @block.sync
        def _(h):
            run("sp", h)


class Arena:
    def __init__(self, base):
        self.base = base
        self.n = base.shape[1]
        self.off = 0

    def reset(self, off=0):
        self.off = off

    def alloc(self, free_shape, dtype, parts=128):
        n = int(np.prod(free_shape))
        ne = n * 2 if dtype == F32 else n
        self.off = (self.off + 15) // 16 * 16
        assert self.off + ne <= self.n, ("arena overflow", self.off, ne, self.n)
        ap = self.base[0:parts, self.off:self.off + ne]
        self.off += ne
        if dtype == F32:
            ap = ap.bitcast(F32)
        if len(free_shape) == 2:
            ap = ap.rearrange("p (a b) -> p a b", a=free_shape[0])
        elif len(free_shape) == 3:
            ap = ap.rearrange("p (a b c) -> p a b c", a=free_shape[0], b=free_shape[1])
        return ap


class Ctx:
    pass


def _mm(P, out, lhsT, rhs, start, stop, r, w):
    P.add("pe", lambda h, o=out, a=lhsT, b=rhs, s=start, t=stop: h.matmul(o, a, b, start=s, stop=t), r=r, w=w)


def _tr(P, out, in_, ident, r, w):
    P.add("pe", lambda h, o=out, a=in_, i=ident: h.transpose(o, a, i), r=r, w=w)


def _act(P, out, in_, func, r, w, bias=None, scale=None, accum=None):
    def fn(h, o=out, a=in_, f=func, b=bias, s=scale, acc=accum):
        kw = {}
        if b is not None:
            kw["bias"] = b
        if s is not None:
            kw["scale"] = s
        if acc is not None:
            kw["accum_out"] = acc
        return h.activation(o, a, f, **kw)
    P.add("act", fn, r=r, w=w)


def _dma(P, q, out, in_, r, w, extra=(), slow=False):
    if slow:
        return P.add(q, lambda h, o=out, a=in_: h.dma_start(out=o, in_=a, allow_slow_non_contiguous=True), r=r, w=w, dma=True, extra=extra)
    return P.add(q, lambda h, o=out, a=in_: h.dma_start(out=o, in_=a), r=r, w=w, dma=True, extra=extra)


class PsumRR:
    def __init__(self, banks, keys):
        self.banks = banks
        self.keys = keys
        self.i = 0

    def next(self):
        b, k = self.banks[self.i], self.keys[self.i]
        self.i = (self.i + 1) % len(self.banks)
        return b, k


def load_bcast(P, C, dst, vec_ap, n, key, scale=None):
    _dma(P, "sp", dst, vec_ap[0:1, :].to_broadcast([128, n]), r=(), w=(key,))
    if scale is not None:
        P.add("dve", lambda h, o=dst, s=scale: h.tensor_scalar_mul(o, o, float(s)), r=(key,), w=(key,))


def rms_rstd(P, ss, out, rk, wk, n=D):
    _act(P, ss, ss, AF.Sqrt, r=rk, w=rk, bias=C_EPS[0], scale=1.0 / n)
    P.add("dve", lambda h, o=out, a=ss: h.reciprocal(o, a), r=rk, w=wk)


C_EPS = [None]


def prologue_steps(P, C, src_rows, tile, gain_b):
    hb = tile % 2
    hT = C.hT[hb]

    def load(s):
        j = tile * 4 + s
        xs = j % C.nx
        row0 = src_rows + j * 128
        _dma(P, "sp", C.X[xs], C.src[row0:row0 + 128, :], r=(("src", j),), w=(("x", xs),))

    def comp_a(s):
        j = tile * 4 + s
        xs = j % C.nx
        X = C.X[xs]
        sc = j % 4
        _act(P, C.junk, X, AF.Square, r=(("x", xs),), w=(("ss", sc),), accum=C.ss[:, sc:sc + 1])
        rms_rstd(P, C.ss[:, sc:sc + 1], C.rs[:, sc:sc + 1], rk=(("ss", sc),), wk=(("rs", sc),))
        xb = j % 2
        P.add("dve", lambda h, o=C.xn[xb], a=X, sca=C.rs[:, sc:sc + 1], g=gain_b:
              h.scalar_tensor_tensor(o, a, sca, g, ALU.mult, ALU.mult),
              r=(("x", xs), ("rs", sc), "gains"), w=(("xn", xb),))

    def comp_b(s):
        j = tile * 4 + s
        xb = j % 2
        pb = j % 2
        pst = C.psT[pb]
        for kc in range(KC):
            _tr(P, pst[:, kc * 128:(kc + 1) * 128], C.xn[xb][:, kc * 128:(kc + 1) * 128], C.ident,
                r=(("xn", xb), "ident"), w=(("psT", pb),))
        _act(P, hT[:, :, s * 128:(s + 1) * 128], pst.rearrange("p (k t) -> p k t", k=KC), AF.Copy,
             r=(("psT", pb),), w=(("hT", hb, s),))

    L = lambda s: (lambda: load(s))
    Ca = lambda s: (lambda: comp_a(s))
    Cb = lambda s: (lambda: comp_b(s))
    return [L(0), L(1), Ca(0), L(2), Ca(1), Cb(0), L(3), Ca(2), Cb(1), Ca(3), Cb(2), Cb(3)]


def prologue(P, C, src_rows, tile, gain_b, tag):
    for f in prologue_steps(P, C, src_rows, tile, gain_b):
        f()


class Filler:
    def __init__(self, fns, nsteps, start=0):
        self.fns = list(fns)
        self.at = {}
        n = len(self.fns)
        for k in range(n):
            step = start + (k * max(nsteps - start, 1)) // n
            self.at.setdefault(step, []).append(self.fns[k])

    def step(self, k):
        for f in self.at.pop(k, []):
            f()

    def flush(self):
        for k in sorted(self.at):
            for f in self.at[k]:
                f()
        self.at = {}


def post_norm_residual(P, C, banks, gain_b, j, final_gain=None, dst_rows=None):
    X = C.Xr
    for n in range(2):
        b, k = banks[n]
        _act(P, C.junk[:, 0:512], b, AF.Square, r=(k,), w=(("ssy", n),), accum=C.ssy[:, n:n + 1])
        P.add("dve", lambda h, o=C.t[:, n * 512:(n + 1) * 512], a=b, g=gain_b[:, n * 512:(n + 1) * 512]:
              h.tensor_tensor(o, a, g, ALU.mult), r=(k, "gains"), w=(("t", n),))
    P.add("dve", lambda h, o=C.ssy[:, 2:3], a=C.ssy[:, 0:1], b2=C.ssy[:, 1:2]: h.tensor_tensor(o, a, b2, ALU.add),
          r=(("ssy", 0), ("ssy", 1)), w=(("ssy", 2),))
    rms_rstd(P, C.ssy[:, 2:3], C.ssy[:, 3:4], rk=(("ssy", 2),), wk=(("ssy", 3),))
    P.add("dve", lambda h, o=X, a=C.t, sca=C.ssy[:, 3:4], x=X: h.scalar_tensor_tensor(o, a, sca, x, ALU.mult, ALU.add),
          r=(("t", 0), ("t", 1), ("ssy", 3), "xr"), w=("xr",))
    outt = X
    outk = "xr"
    if final_gain is not None:
        _act(P, C.junk, X, AF.Square, r=("xr",), w=(("ssf", 0),), accum=C.ssy[:, 4:5])
        rms_rstd(P, C.ssy[:, 4:5], C.ssy[:, 5:6], rk=(("ssf", 0),), wk=(("ssf", 1),))
        P.add("dve", lambda h, o=C.t, a=X, sca=C.ssy[:, 5:6], g=final_gain: h.scalar_tensor_tensor(o, a, sca, g, ALU.mult, ALU.mult),
              r=("xr", ("ssf", 1), "gains", ("t", 0), ("t", 1)), w=(("t", 0), ("t", 1)))
        outt = C.t
        outk = ("t", 0)
    if CFG.get("stop", 99) < 6:
        return
    _dma(P, "sp", C.dst[dst_rows:dst_rows + 128, :], outt, r=(outk, ("t", 1)) if final_gain is not None else (outk,),
         w=(("dst", j),))


def load_ffn_weights(P, C, wg_d, wu_d, wd_d, extra):
    W = C.wreg
    C.wg = W[:, 0:KC * DFF].rearrange("p (k n) -> p k n", k=KC)
    C.wu = W[:, KC * DFF:2 * KC * DFF].rearrange("p (k n) -> p k n", k=KC)
    C.wd = W[:, 2 * KC * DFF:2 * KC * DFF + NFC * D].rearrange("p (c n) -> p c n", c=NFC)
    wgv = wg_d.rearrange("(k p) n -> p k n", p=128)
    wuv = wu_d.rearrange("(k p) n -> p k n", p=128)
    wdv = wd_d.rearrange("(c p) n -> p c n", p=128)
    NS = 4
    cs = DFF // NS
    for i in range(NS):
        _dma(P, "pool", C.wg[:, :, i * cs:(i + 1) * cs], wgv[:, :, i * cs:(i + 1) * cs], r=(), w=(("wg", i),), extra=extra)
        _dma(P, "pool", C.wu[:, :, i * cs:(i + 1) * cs], wuv[:, :, i * cs:(i + 1) * cs], r=(), w=(("wu", i),), extra=extra)
    for i in range(2):
        _dma(P, "pool", C.wd[:, i * 11:(i + 1) * 11, :], wdv[:, i * 11:(i + 1) * 11, :], r=(), w=(("wd", i),), extra=extra)
    C.wslice = cs


def ffn_phase(P, C, ntiles, gain_pre, gain_post, gain_fin):
    prs = C.psrr
    stop = CFG.get("stop", 99)
    if stop < 1:
        return
    prologue(P, C, 0, 0, gain_pre, "f")
    if stop < 2:
        return
    for i in range(ntiles):
        hT = C.hT[i % 2]
        hk = [("hT", i % 2, s) for s in range(4)]
        fill = Filler(prologue_steps(P, C, 0, i + 1, gain_pre) if i + 1 < ntiles else [], NFC, start=2)
        for c in range(NFC):
            fill.step(c)
            wk = c * 128 // C.wslice
            bg, kg = prs.next()
            for kc in range(KC):
                _mm(P, bg, C.wg[:, kc, c * 128:(c + 1) * 128], hT[:, kc, :], kc == 0, kc == KC - 1,
                    r=hk + [("wg", wk)], w=(kg,))
            bu, ku = prs.next()
            for kc in range(KC):
                _mm(P, bu, C.wu[:, kc, c * 128:(c + 1) * 128], hT[:, kc, :], kc == 0, kc == KC - 1,
                    r=hk + [("wu", wk)], w=(ku,))
            sg = C.sg[c % 2]
            _act(P, sg, bg, AF.Silu, r=(kg,), w=(("sg", c % 2),))
            P.add("dve", lambda h, o=C.hid[:, c, :], a=sg, b=bu: h.tensor_tensor(o, a, b, ALU.mult),
                  r=(("sg", c % 2), ku), w=(("hid", c),))
        fill.flush()
        if stop < 3:
            return
        for s in range(4):
            j = i * 4 + s
            banks = []
            for n in range(2):
                b, k = prs.next()
                for c in range(NFC):
                    _mm(P, b, C.hid[:, c, s * 128:(s + 1) * 128], C.wd[:, c, n * 512:(n + 1) * 512], c == 0, c == NFC - 1,
                        r=(("hid", c), ("wd", c // 11)), w=(k,))
                banks.append((b, k))
            if stop < 4:
                continue
            _dma(P, "sp", C.Xr, C.src[j * 128:(j + 1) * 128, :], r=(("src", j),), w=("xr",))
            if stop < 5:
                continue
            post_norm_residual(P, C, banks, gain_post, j, final_gain=gain_fin, dst_rows=j * 128)


QKV_BLOCKS = ((0, 512, "q"), (512, 768, "q"), (768, 1280, "k"), (1280, 1536, "k"), (1536, 2048, "v"), (2048, 2304, "v"))


def qkv_blocks(tile, ntiles):
    if ntiles != 12:
        return QKV_BLOCKS
    if 2 <= tile <= 9:
        return QKV_BLOCKS
    if tile in (1, 10):
        return QKV_BLOCKS[2:]
    return (QKV_BLOCKS[3], QKV_BLOCKS[5])


def qkv_phase(P, C, ntiles, gain_pre):
    prs = C.psrr
    prologue(P, C, 0, 0, gain_pre, "q")
    ev = 0
    for i in range(ntiles):
        hT = C.hT[i % 2]
        blocks = qkv_blocks(i, ntiles)
        fill = Filler(prologue_steps(P, C, 0, i + 1, gain_pre) if i + 1 < ntiles else [], 4 * len(blocks))
        step = 0
        for s in range(4):
            j = i * 4 + s
            sb = j % 2
            stage = C.stage[sb]
            stage_v = stage[:, 1536:2316].rearrange("p (h e) -> p h e", e=65)
            for (c0, c1, kind) in blocks:
                fill.step(step)
                step += 1
                b, k = prs.next()
                n = c1 - c0
                for kc in range(KC):
                    _mm(P, b[:, 0:n], hT[:, kc, s * 128:(s + 1) * 128], C.wqkv[:, kc, c0:c1], kc == 0, kc == KC - 1,
                        r=(("hT", i % 2, s), "wqkv"), w=(k,))
                if kind == "v":
                    h0 = (c0 - 1536) // 64
                    nh = n // 64
                    out = stage_v[:, h0:h0 + nh, 0:64]
                    src = b[:, 0:n].rearrange("p (h e) -> p h e", e=64)
                else:
                    out = stage[:, c0:c1]
                    src = b[:, 0:n]
                scale = 0.125 if kind == "q" else 1.0
                if ev % 2 == 0:
                    _act(P, out, src, AF.Copy, r=(k,), w=(("stage", sb, c0),), scale=scale)
                else:
                    P.add("dve", lambda h, o=out, a=src, sc=scale: h.tensor_scalar_mul(o, a, float(sc)),
                          r=(k,), w=(("stage", sb, c0),))
                ev += 1
            _dma(P, "sp", C.dst[j * 128:(j + 1) * 128, :], stage, r=[("stage", sb, c0) for (c0, _, _) in QKV_BLOCKS],
                 w=(("dst", j),))
        fill.flush()


def build_bias(P, C, I, tscr):
    A = C.A
    rb = A.alloc((12,), F32, parts=32)
    E = A.alloc((3 * 129,), F32, parts=32)
    Tsb = A.alloc((3, 512), F32, parts=12)
    anti = A.alloc((128,), F32)
    R = [A.alloc((256,), F32) for _ in range(2)]
    _dma(P, "sp", rb, I["rel_bias"], r=(), w=("rb",))
    _dma(P, "sp", E, I["onehot"], r=(), w=("E",))
    _dma(P, "sp", anti, I["antiid"], r=(), w=("anti",))
    P.add("pool", lambda h: h.memset(Tsb, NEG), r=(), w=("Tsb",))
    b5 = C.ps[5][:, :]
    for g in range(3):
        _mm(P, b5[0:12, 0:129], rb[0:32, 0:12], E[0:32, g * 129:(g + 1) * 129], True, True, r=("rb", "E"), w=(("ps", 5),))
        P.add("dve", lambda h, o=Tsb[:, g, 127:256], a=b5[0:12, 0:129]: h.tensor_copy(o, a), r=(("ps", 5), "Tsb"), w=("Tsb",))
    for g in range(3):
        _dma(P, "sp", tscr[4 * g:4 * g + 4, :], Tsb[4 * g:4 * g + 4, g, :], r=("Tsb",), w=(("tscr", g),))
    for gh in range(12):
        Rt = R[gh % 2]
        src = bass.AP(tscr.tensor, gh * 512, [[1, 128], [1, 256]])
        _dma(P, "sp", Rt, src, r=(("tscr", gh // 4),), w=(("R", gh % 2),))
        b3, k3 = C.ps[3 + gh % 2][:, :], ("ps", 3 + gh % 2)
        _mm(P, b3[:, 0:256], anti, Rt, True, True, r=(("R", gh % 2), "anti"), w=(k3,))
        P.add("dve", lambda h, o=C.biasb[:, gh, :], a=b3[:, 0:256]: h.tensor_copy(o, a), r=(k3,), w=("biasb",))


def att_phase(P, C, qkv, att):
    segs = segments()
    psPT = [C.ps[4][:, :].bitcast(BF16), C.ps[5][:, :].bitcast(BF16)]
    Sb = [(C.ps[i][:, :], ("ps", i)) for i in range(4)]
    Ob = [(C.ps[6 + i][:, 0:260], ("ps", 6 + i)) for i in range(2)]
    psKQ = [C.ps[6][:, :].bitcast(BF16), C.ps[7][:, :].bitcast(BF16)]
    qt = qkv.tensor
    tcount = [0]
    units = []
    first_unit = {}
    bc = 0
    for si, (g, d, r, i0, nblk) in enumerate(segs):
        first_unit[si] = len(units)
        for b in range(nblk):
            for c in range(2):
                units.append((si, b, c, bc + b))
        bc += nblk
    N = len(units)

    def loads(si):
        g, d, r, i0, nblk = segs[si]
        rb = si % 4
        nt = nblk + 1
        rowk = d * (i0 - 64) + r
        rowq = d * i0 + r
        _dma(P, "sp", C.Kraw[rb][:, 0:nt, :], bass.AP(qt, rowk * QKVW + 768 + 256 * g, [[d * QKVW, 128], [128 * d * QKVW, nt], [1, 256]]),
             r=(), w=(("Kraw", rb),))
        _dma(P, "sp", C.Vt[rb][:, 0:nt, :], bass.AP(qt, rowk * QKVW + 1536 + 260 * g, [[d * QKVW, 128], [128 * d * QKVW, nt], [1, 260]]),
             r=(), w=(("Vt", rb),))
        _dma(P, "sp", C.Qraw[rb][:, 0:nblk, :], bass.AP(qt, rowq * QKVW + 256 * g, [[d * QKVW, 128], [128 * d * QKVW, nblk], [1, 256]]),
             r=(), w=(("Qraw", rb),))

    def transposes(si, kqb):
        g, d, r, i0, nblk = segs[si]
        rb, tb2 = si % 4, si % 2
        nt = nblk + 1
        for (raw, rk, dstT, dk, ntile) in ((C.Kraw[rb], ("Kraw", rb), C.KT[tb2], ("KT", tb2), nt),
                                          (C.Qraw[rb], ("Qraw", rb), C.QT[tb2], ("QT", tb2), nblk)):
            for j0 in range(0, ntile, 4):
                nj = min(4, ntile - j0)
                pk, pkk = psKQ[kqb], ("ps", 6 + kqb)
                for c in range(2):
                    for jj in range(nj):
                        _tr(P, pk[:, c * 512 + jj * 128: c * 512 + (jj + 1) * 128], raw[:, j0 + jj, c * 128:(c + 1) * 128], C.ident,
                            r=(rk, "ident"), w=(pkk,))
                src = pk.rearrange("p (c t) -> p c t", c=2)[:, :, 0:nj * 128]
                if dk[0] == "KT":
                    P.add("dve", lambda h, o=dstT[:, :, j0 * 128:(j0 + nj) * 128], a=src: h.tensor_copy(o, a), r=(pkk,), w=(dk,))
                else:
                    _act(P, dstT[:, :, j0 * 128:(j0 + nj) * 128], src, AF.Copy, r=(pkk,), w=(dk,))

    def ctx(u):
        si, b, c, gb = units[u]
        g, d, r, i0, nblk = segs[si]
        return si, b, c, gb, g, d, r, i0

    def stage_S(u):
        si, b, c, gb, g, d, r, i0 = ctx(u)
        QT, KT = C.QT[si % 2], C.KT[si % 2]
        sbk, skey = Sb[u % 4]
        hb0 = 4 * g + 2 * c
        for hp in range(2):
            _mm(P, sbk[:, hp * 256:(hp + 1) * 256], QT[hp * 64:(hp + 1) * 64, c, b * 128:(b + 1) * 128],
                KT[hp * 64:(hp + 1) * 64, c, b * 128:b * 128 + 256], True, False, r=(("QT", si % 2), ("KT", si % 2)), w=(skey,))
            _mm(P, sbk[:, hp * 256:(hp + 1) * 256], C.ident, C.biasb[:, hb0 + hp, :], False, True,
                r=("ident", "biasb"), w=(skey,))

    def stage_1(u):
        si, b, c, gb, g, d, r, i0 = ctx(u)
        sl = u % 4
        sbk, skey = Sb[u % 4]
        ost = C.ostage[gb % 3]
        ostk = ("ost", gb % 3)
        mcols = ost[:, 260 + 2 * c:262 + 2 * c]
        P.add("dve", lambda hh, o=mcols, a=sbk.rearrange("p (h k) -> p h k", h=2): hh.reduce_max(o, a, AX.X),
              r=(skey,), w=(ostk + (c,),))
        P.add("dve", lambda hh, o=C.negm[:, sl, :], a=mcols: hh.tensor_scalar_mul(o, a, -1.0), r=(ostk + (c,),), w=(("negm", sl),))
        for hp in range(2):
            _act(P, C.Pb[sl][:, hp, :], sbk[:, hp * 256:(hp + 1) * 256], AF.Exp, r=(skey, ("negm", sl)), w=(("Pb", sl, hp),),
                 bias=C.negm[:, sl, hp:hp + 1])

    def stage_2(u):
        sl = u % 4
        pt, ptk = psPT[u % 2], ("ps", 4 + u % 2)
        for hp in range(2):
            for jj in range(2):
                _tr(P, pt[:, (hp * 2 + jj) * 128:(hp * 2 + jj + 1) * 128], C.Pb[sl][:, hp, jj * 128:(jj + 1) * 128], C.ident,
                    r=(("Pb", sl, hp), "ident"), w=(ptk,))

    def stage_3(u):
        si, b, c, gb, g, d, r, i0 = ctx(u)
        sl = u % 4
        pt, ptk = psPT[u % 2], ("ps", 4 + u % 2)
        PTs = C.PTs[sl]
        ptv = pt[:, 0:512].rearrange("p (h j q) -> p h j q", h=2, j=2)
        for jj in range(2):
            vcol = C.vseg[:, si * 9 + b + jj: si * 9 + b + jj + 1]
            P.add("dve", lambda hh, o=PTs[:, :, jj, :], a=ptv[:, :, jj, :], sc=vcol: hh.tensor_scalar_mul(o, a, sc),
                  r=(ptk, "vseg"), w=(("PTs", sl, jj),))

    def stage_4(u):
        si, b, c, gb, g, d, r, i0 = ctx(u)
        sl = u % 4
        PTs = C.PTs[sl]
        Vt = C.Vt[si % 4]
        ob, okey = Ob[gb % 2]
        ost = C.ostage[gb % 3]
        ostk = ("ost", gb % 3)
        for hp in range(2):
            h = 2 * c + hp
            for jj in range(2):
                _mm(P, ob[:, h * 65:(h + 1) * 65], PTs[:, hp, jj, :], Vt[:, b + jj, h * 65:(h + 1) * 65], jj == 0, jj == 1,
                    r=(("PTs", sl, jj), ("Vt", si % 4)), w=(okey,))
        if c == 1:
            _act(P, ost[:, 0:260], ob, AF.Copy, r=(okey,), w=(ostk + (9,),))
            t0 = d * (i0 + 128 * b) + r - HALO
            dst = bass.AP(att.tensor, (t0 * 3 + g) * ATTW, [[d * 3 * ATTW, 128], [1, ATTW]])
            _dma(P, "pool", dst, ost, r=[ostk + (x,) for x in (0, 1, 9)], w=(("att", si, b),))

    nseg = len(segs)
    loads(0)
    loads(1)
    loads(2)
    loads(3)
    transposes(0, 0)
    seg_started = set()
    for u in range(min(2, N)):
        stage_S(u)
    for t in range(N + 4):
        if t < N:
            si = units[t][0]
            if si not in seg_started and first_unit[si] == t:
                seg_started.add(si)
                if si + 1 < nseg:
                    transposes(si + 1, (units[t][3] - 1) % 2)
        n0 = len(P.ops)
        if t + 2 < N:
            stage_S(t + 2)
        n1 = len(P.ops)
        if t < N:
            stage_1(t)
        n2 = len(P.ops)
        if 0 <= t - 1 < N:
            stage_2(t - 1)
        n3 = len(P.ops)
        if 0 <= t - 2 < N:
            stage_3(t - 2)
        n4 = len(P.ops)
        if 0 <= t - 3 < N:
            stage_4(t - 3)
        STAGE_LOG.append((t, n0, n1, n2, n3, n4, len(P.ops)))
        if 0 <= t - 3 < N:
            si3 = units[t - 3][0]
            if t - 3 + 1 == N or units[t - 3 + 1][0] != si3:
                if si3 + 4 < nseg:
                    loads(si3 + 4)


def b1_phase(P, C, ntiles, att):
    prs = C.psrr

    def z_stages(i):
        hT = C.hT[i % 2]
        hk = [("hT", i % 2, s) for s in range(4)]
        AI = C.attin
        mview = AI[:, :, :, 260:264]
        sview = AI[:, :, :, 0:260].rearrange("p s g (h e) -> p s g h e", e=65)

        def z0():
            _dma(P, "sp", AI, att[i * 512:(i + 1) * 512, :].rearrange("(s p) (g e) -> p s g e", p=128, g=3), r=(), w=("attin",))

        def z1():
            for s in range(4):
                bu, ku = prs.next()
                for kc in range(KC):
                    _mm(P, bu, hT[:, kc, s * 128:(s + 1) * 128], C.wrest[:, kc, 0:512], kc == 0, kc == KC - 1, r=(hk[s], "wrest"), w=(ku,))
                _act(P, C.U[s], bu, AF.Gelu, r=(ku,), w=(("U", s),))
                bv, kv = prs.next()
                for kc in range(KC):
                    _mm(P, bv, hT[:, kc, s * 128:(s + 1) * 128], C.wrest[:, kc, 512:1024], kc == 0, kc == KC - 1, r=(hk[s], "wrest"), w=(kv,))
                _act(P, C.V[s], bv, AF.Gelu, r=(kv,), w=(("V", s), ("st0", s)), accum=C.st[:, 0, s:s + 1])

        def z2():
            P.add("dve", lambda h, o=C.M, a=mview.rearrange("p s g h -> p s h g"): h.tensor_reduce(o, a, AX.X, ALU.max),
                  r=("attin",), w=("M",))
            P.add("dve", lambda h, o=C.dd, a=mview, m=C.M.unsqueeze(2).to_broadcast([128, 4, 3, 4]): h.tensor_tensor(o, a, m, ALU.subtract),
                  r=("attin", "M"), w=("dd",))
            _act(P, C.ww, C.dd, AF.Exp, r=("dd",), w=("ww",))
            for s in range(4):
                P.add("dve", lambda h, o=C.wd_[:, s], a=C.ww[:, s], sv=sview[:, s, :, :, 64]: h.tensor_tensor(o, a, sv, ALU.mult),
                      r=("ww", "attin"), w=(("wd", s),))
            P.add("dve", lambda h, o=C.den, a=C.wd_.rearrange("p s g h -> p s h g"): h.tensor_reduce(o, a, AX.X, ALU.add),
                  r=[("wd", s) for s in range(4)], w=("den",))
            P.add("dve", lambda h, o=C.den: h.reciprocal(o, o), r=("den",), w=("den",))
            P.add("dve", lambda h, o=C.cc, a=C.ww, m=C.den.unsqueeze(2).to_broadcast([128, 4, 3, 4]): h.tensor_tensor(o, a, m, ALU.mult),
                  r=("ww", "den"), w=("cc",))

        def z3():
            P.add("dve", lambda h, o=C.st[:, 1, :], a=C.st[:, 0, :]: h.tensor_scalar_mul(o, a, -1.0 / 512.0),
                  r=[("st0", s) for s in range(4)], w=("st1",))
            for s in range(4):
                _act(P, C.junk[:, 0:512], C.V[s], AF.Square, r=(("V", s), "st1"), w=(("st2", s),), bias=C.st[:, 1, s:s + 1],
                     accum=C.st[:, 2, s:s + 1])
            _act(P, C.st[:, 2, :], C.st[:, 2, :], AF.Sqrt, r=[("st2", s) for s in range(4)], w=[("st2", s) for s in range(4)],
                 bias=C_EPS[0], scale=1.0 / 512.0)
            P.add("dve", lambda h, o=C.st[:, 3, :], a=C.st[:, 2, :]: h.reciprocal(o, a), r=[("st2", s) for s in range(4)], w=("st3",))

        def z4():
            for s in range(4):
                V = C.V[s]
                P.add("dve", lambda h, o=V, s1=C.st[:, 1, s:s + 1], s2=C.st[:, 3, s:s + 1]: h.tensor_scalar(o, o, s1, s2, ALU.add, ALU.mult),
                      r=(("V", s), "st1", "st3", ("st2", s)), w=(("V", s),))
            for s in range(4):
                V = C.V[s]
                P.add("dve", lambda h, o=V, g=C.lng: h.tensor_tensor(o, o, g, ALU.mult), r=(("V", s), "gains"), w=(("V", s),))
            for s in range(4):
                P.add("dve", lambda h, o=C.vln[s], a=C.V[s], g=C.lnb: h.tensor_tensor(o, a, g, ALU.add), r=(("V", s), "gains"), w=(("vln", s),))

        def z5():
            for s in range(4):
                ts = s % 2
                oview = sview[:, s, :, :, 0:64]
                P.add("dve", lambda h, o=C.tmpO[ts], a=oview, m=C.cc[:, s].unsqueeze(3).to_broadcast([128, 3, 4, 64]): h.tensor_tensor(o, a, m, ALU.mult),
                      r=("attin", "cc"), w=(("tmpO", ts),))

                def _red(h, o=C.attb[s], a=C.tmpO[ts].rearrange("p g h e -> p h e g")):
                    with C.nc.allow_low_precision(reason="3-term sum on the fp32 ALU, rounded once to bf16 for the matmul"):
                        return h.tensor_reduce(o, a, AX.X, ALU.add)
                P.add("dve", _red, r=(("tmpO", ts),), w=(("attb", s),))

        def z6():
            for s in range(4):
                bm, km = prs.next()
                for gg in range(4):
                    _mm(P, bm[:, gg * 128:(gg + 1) * 128], C.WsT[:, gg, :], C.vln[s][:, gg * 128:(gg + 1) * 128], True, True,
                        r=(("vln", s), "WsT"), w=(km,))
                for gg in range(4):
                    P.add("dve", lambda h, o=C.sgu[s][:, gg * 128:(gg + 1) * 128], a=bm[:, gg * 128:(gg + 1) * 128], sc=C.bs[:, gg:gg + 1],
                          u=C.U[s][:, gg * 128:(gg + 1) * 128]: h.scalar_tensor_tensor(o, a, sc, u, ALU.add, ALU.mult),
                          r=(km, ("U", s), "bs"), w=(("sgu", s),))

        def z7():
            for s in range(4):
                pb = s % 2
                pst = C.psT[pb]
                for gg in range(4):
                    _tr(P, pst[:, gg * 128:(gg + 1) * 128], C.sgu[s][:, gg * 128:(gg + 1) * 128], C.ident, r=(("sgu", s), "ident"), w=(("psT", pb),))
                ab = C.attb[s].rearrange("p h e -> p (h e)")
                for cc_ in range(2):
                    _tr(P, pst[:, 512 + cc_ * 128:512 + (cc_ + 1) * 128], ab[:, cc_ * 128:(cc_ + 1) * 128], C.ident, r=(("attb", s), "ident"), w=(("psT", pb),))
                _act(P, C.sguT[:, :, s * 128:(s + 1) * 128], pst[:, 0:512].rearrange("p (k t) -> p k t", k=4), AF.Copy,
                     r=(("psT", pb),), w=(("sguT", s),))
                _act(P, C.attT[:, :, s * 128:(s + 1) * 128], pst[:, 512:768].rearrange("p (k t) -> p k t", k=2), AF.Copy,
                     r=(("psT", pb),), w=(("attT", s),))

        return {0: [z0], 2: [z2], 4: [z5], 6: [z1], 7: [z3], 8: [z4], 9: [z6], 10: [z7]}

    def g_step(i, oc):
        hT = C.hT[i % 2]
        hk = [("hT", i % 2, s) for s in range(4)]
        sguk = [("sguT", s) for s in range(4)]
        attk = [("attT", s) for s in range(4)]
        bga, kga = prs.next()
        for kc in range(KC):
            _mm(P, bga, C.wrest[:, kc, 1024 + oc * 128:1024 + (oc + 1) * 128], hT[:, kc, :], kc == 0, kc == KC - 1, r=hk + ["wrest"], w=(kga,))
        bgb, kgb = prs.next()
        for kc in range(KC):
            _mm(P, bgb, C.wrest[:, kc, 2048 + oc * 128:2048 + (oc + 1) * 128], hT[:, kc, :], kc == 0, kc == KC - 1, r=hk + ["wrest"], w=(kgb,))
        bpa, kpa = prs.next()
        for kc in range(2):
            _mm(P, bpa, C.watt[:, kc, oc * 128:(oc + 1) * 128], C.attT[:, kc, :], kc == 0, kc == 1, r=attk + ["watt"], w=(kpa,))
        bpb, kpb = prs.next()
        for kc in range(4):
            _mm(P, bpb, C.wsgu[:, kc, oc * 128:(oc + 1) * 128], C.sguT[:, kc, :], kc == 0, kc == 3, r=sguk + ["wsgu"], w=(kpb,))
        o2 = oc % 2
        _act(P, C.sa[o2], bga, AF.Sigmoid, r=(kga,), w=(("sa", o2),))
        _act(P, C.sb_[o2], bgb, AF.Sigmoid, r=(kgb,), w=(("sb", o2),))
        P.add("dve", lambda h, o=C.sa[o2], b=bpa: h.tensor_tensor(o, o, b, ALU.mult), r=(("sa", o2), kpa), w=(("sa", o2),))
        P.add("dve", lambda h, o=C.sb_[o2], b=bpb: h.tensor_tensor(o, o, b, ALU.mult), r=(("sb", o2), kpb), w=(("sb", o2),))
        P.add("dve", lambda h, o=C.mT[:, oc, :], a=C.sa[o2], b=C.sb_[o2]: h.tensor_tensor(o, a, b, ALU.add),
              r=(("sa", o2), ("sb", o2)), w=(("mT", oc),))

    def w_step(i, s):
        mk = [("mT", oc) for oc in range(8)]
        j = i * 4 + s
        banks = []
        for n in range(2):
            b, k = prs.next()
            for kc in range(KC):
                _mm(P, b, C.mT[:, kc, s * 128:(s + 1) * 128], C.wout[:, kc, n * 512:(n + 1) * 512], kc == 0, kc == KC - 1,
                    r=mk + ["wout"], w=(k,))
            banks.append((b, k))
        _dma(P, "sp", C.Xr, C.src[HALO + j * 128:HALO + (j + 1) * 128, :], r=(), w=("xr",))
        post_norm_residual(P, C, banks, C.gpost, j, final_gain=None, dst_rows=j * 128)

    prologue(P, C, HALO, 0, C.gpre, "m")
    z = z_stages(0)
    for k in (0, 6, 2, 7, 8, 4, 9, 10):
        for f in z[k]:
            f()
    for i in range(ntiles):
        nxt = i + 1 < ntiles
        fill = Filler(prologue_steps(P, C, HALO, i + 1, C.gpre) if nxt else [], 6)
        z = z_stages(i + 1) if nxt else {}
        for step in range(12):
            if step < 8:
                g_step(i, step)
            else:
                w_step(i, step - 8)
            fill.step(step)
            for f in z.get(step, []):
                f()
        fill.flush()


def build_program(cfg):
    nc = bass.Bass("TRN2", target_bir_lowering=False)
    dbg = cfg["debug"]
    phases = cfg["phases"]
    n_ext = cfg["n_ext_tiles"]
    n_own = cfg["n_own_tiles"]

    def din(name, shape):
        return nc.dram_tensor(name, list(shape), F32, kind="ExternalInput").ap()

    def dscr(name, shape, dt):
        if dbg:
            return nc.dram_tensor(name, list(shape), dt, kind="ExternalOutput").ap()
        return nc.dram_tensor(name, list(shape), dt).ap()

    I = {}
    I["xe"] = din("xe", (NEXT, D))
    for nm in ("w_gate1", "w_up1", "w_gate2", "w_up2"):
        I[nm] = din(nm, (D, DFF))
    for nm in ("w_down1", "w_down2"):
        I[nm] = din(nm, (DFF, D))
    I["w_in"] = din("w_in", (D, 5376))
    I["w_att"] = din("w_att", (256, D))
    I["w_sgu"] = din("w_sgu", (512, D))
    I["w_out"] = din("w_out", (D, D))
    for nm in ("g1pre", "g1post", "gmpre", "gmpost", "g2pre", "g2post", "gfin"):
        I[nm] = din(nm, (1, D))
    I["ln_g"] = din("ln_g", (1, 512))
    I["ln_b"] = din("ln_b", (1, 512))
    I["w_s"] = din("w_s", (4, 128, 128))
    I["b_s"] = din("b_s", (4, 128))
    I["rel_bias"] = din("rel_bias", (32, 12))
    I["ident"] = din("ident", (128, 128))
    I["antiid"] = din("antiid", (128, 128))
    I["onehot"] = din("onehot", (32, 3 * 129))
    I["vseg"] = din("vseg", (128, 24 * 9))
    y = nc.dram_tensor("y", [NOWN, D], F32, kind="ExternalOutput").ap()
    x1 = dscr("x1s", (NEXT, D), F32)
    qkv = dscr("qkvs", (NEXT, QKVW), BF16)
    att = dscr("atts", (NOWN, 3 * ATTW), F32)
    x2 = dscr("x2s", (NOWN, D), F32)
    tscr = nc.dram_tensor("tscr", [12, 512], F32).ap()

    P = Prog(nc)
    LAST_PROG[0] = P
    with ExitStack() as es:
        wreg = es.enter_context(nc.sbuf_tensor("wreg", [128, WREG_ELEMS], BF16))
        areg = es.enter_context(nc.sbuf_tensor("areg", [128, ARENA_ELEMS], BF16))
        identt = es.enter_context(nc.sbuf_tensor("identb", [128, 128], BF16))
        ps = [es.enter_context(nc.psum_tensor("ps%d" % i, [128, 512], F32)) for i in range(8)]
        sems = {e: es.enter_context(nc.semaphore("sem_" + e)) for e in ("pe", "act", "dve", "pool")}
        dsems = {}
        for q in ("sp", "pool"):
            for i in range(P.ndma):
                dsems[(q, i)] = es.enter_context(nc.semaphore("dsem_%s%d" % (q, i)))

        C = Ctx()
        C.nc = nc
        C.wreg = wreg[:, :]
        C.ident = identt[:, :]
        A = Arena(areg[:, :])
        _dma(P, "pool", C.ident, I["ident"], r=(), w=("ident",))
        epst = es.enter_context(nc.sbuf_tensor("epst", [128, 1], F32))
        C_EPS[0] = epst[:, :]
        P.add("pool", lambda h: h.memset(epst[:, :], EPS), r=(), w=("eps",))
        P.barrier()

        def ffn_setup(src, dst, gpre, gpost, gfin):
            A.reset()
            C.src, C.dst = src, dst
            C.nx = 2
            C.X = [A.alloc((D,), F32) for _ in range(C.nx)]
            C.Xr = A.alloc((D,), F32)
            C.xn = [A.alloc((D,), BF16) for _ in range(2)]
            C.hT = [A.alloc((KC, TT), BF16) for _ in range(2)]
            C.hid = A.alloc((NFC, TT), BF16)
            C.sg = [A.alloc((TT,), BF16) for _ in range(2)]
            C.t = A.alloc((D,), F32)
            C.junk = A.alloc((D,), BF16)
            C.ss = A.alloc((4,), F32)
            C.rs = A.alloc((4,), F32)
            C.ssy = A.alloc((8,), F32)
            C.gpre = A.alloc((D,), F32)
            C.gpost = A.alloc((D,), F32)
            load_bcast(P, C, C.gpre, gpre, D, "gains")
            load_bcast(P, C, C.gpost, gpost, D, "gains", scale=0.5)
            C.gfin = None
            if gfin is not None:
                C.gfin = A.alloc((D,), F32)
                load_bcast(P, C, C.gfin, gfin, D, "gains")
            C.psT = [ps[6][:, :].bitcast(BF16), ps[7][:, :].bitcast(BF16)]
            C.psrr = PsumRR([ps[i][:, :] for i in range(6)], [("ps", i) for i in range(6)])

        last_pe = None
        if "A1" in phases:
            load_ffn_weights(P, C, I["w_gate1"], I["w_up1"], I["w_down1"], extra=())
            ffn_setup(I["xe"], x1, I["g1pre"], I["g1post"], None)
            ffn_phase(P, C, n_ext, C.gpre, C.gpost, None)
            last_pe = P.last("pe")
            P.barrier()


        def common_setup(src, dst, gpre, gpost, gpost_scale):
            A.reset()
            C.A = A
            C.ps = ps
            C.src, C.dst = src, dst
            C.nx = 2
            C.X = [A.alloc((D,), F32) for _ in range(C.nx)]
            C.xn = [A.alloc((D,), BF16) for _ in range(2)]
            C.hT = [A.alloc((KC, TT), BF16) for _ in range(2)]
            C.junk = A.alloc((D,), BF16)
            C.ss = A.alloc((4,), F32)
            C.rs = A.alloc((4,), F32)
            C.ssy = A.alloc((8,), F32)
            C.gpre = A.alloc((D,), F32)
            load_bcast(P, C, C.gpre, gpre, D, "gains")
            if gpost is not None:
                C.gpost = A.alloc((D,), F32)
                load_bcast(P, C, C.gpost, gpost, D, "gains", scale=gpost_scale)
                C.Xr = A.alloc((D,), F32)
                C.t = A.alloc((D,), F32)
            C.psT = [ps[6][:, :].bitcast(BF16), ps[7][:, :].bitcast(BF16)]
            C.psrr = PsumRR([ps[i][:, :] for i in range(6)], [("ps", i) for i in range(6)])

        if "A2" in phases:
            ex = (last_pe,) if last_pe is not None else ()
            C.wqkv = C.wreg[:, 0:KC * 2304].rearrange("p (k n) -> p k n", k=KC)
            wv = I["w_in"].rearrange("(k p) n -> p k n", p=128)
            for hh in range(2):
                _dma(P, "pool", C.wqkv[:, hh * 4:(hh + 1) * 4, :], wv[:, hh * 4:(hh + 1) * 4, 0:2304], r=(), w=("wqkv",), extra=ex)
            common_setup(x1, qkv, I["gmpre"], None, None)
            C.stage = [A.alloc((QKVW,), BF16) for _ in range(2)]
            for sb in range(2):
                sv = C.stage[sb][:, 1536:2316].rearrange("p (h e) -> p h e", e=65)
                P.add("pool", lambda h, o=sv[:, :, 64:65]: h.memset(o, 1.0), r=(), w=[("stage", sb, c0) for (c0, _, _) in QKV_BLOCKS])
            qkv_phase(P, C, n_ext, C.gpre)
            last_pe = P.last("pe")
            P.barrier()

        if "ATT" in phases:
            A.reset()
            C.A = A
            C.ps = ps
            AT = Arena(C.wreg[:, 38912:WREG_ELEMS])
            C.Kraw = [AT.alloc((9, 256), BF16) for _ in range(4)]
            C.Vt = [AT.alloc((9, 260), BF16) for _ in range(4)]
            C.Qraw = [AT.alloc((8, 256), BF16) for _ in range(4)]
            C.KT = [A.alloc((2, 9 * 128), BF16) for _ in range(2)]
            C.QT = [A.alloc((2, 8 * 128), BF16) for _ in range(2)]
            C.biasb = A.alloc((12, 256), BF16)
            C.Pb = [A.alloc((2, 256), BF16) for _ in range(4)]
            C.PTs = [A.alloc((2, 2, 128), BF16) for _ in range(4)]
            C.ostage = [A.alloc((ATTW,), F32) for _ in range(3)]
            C.vseg = A.alloc((24 * 9,), F32)
            C.negm = A.alloc((4, 2), F32)
            _dma(P, "sp", C.vseg, I["vseg"], r=(), w=("vseg",))
            build_bias(P, C, I, tscr)
            att_phase(P, C, qkv, att)
            last_pe = P.last("pe")
            P.barrier()

        if "B1" in phases:
            ex = (last_pe,) if last_pe is not None else ()
            W = C.wreg
            C.wrest = W[:, 0:24576].rearrange("p (k n) -> p k n", k=KC)
            C.watt = W[:, 24576:26624].rearrange("p (k n) -> p k n", k=2)
            C.wsgu = W[:, 26624:30720].rearrange("p (k n) -> p k n", k=4)
            C.wout = W[:, 30720:38912].rearrange("p (k n) -> p k n", k=KC)
            wv = I["w_in"].rearrange("(k p) n -> p k n", p=128)
            for hh in range(2):
                _dma(P, "pool", C.wrest[:, hh * 4:(hh + 1) * 4, :], wv[:, hh * 4:(hh + 1) * 4, 2304:5376], r=(), w=("wrest",), extra=ex)
            _dma(P, "pool", C.watt, I["w_att"].rearrange("(k p) n -> p k n", p=128), r=(), w=("watt",), extra=ex)
            _dma(P, "pool", C.wsgu, I["w_sgu"].rearrange("(k p) n -> p k n", p=128), r=(), w=("wsgu",), extra=ex)
            _dma(P, "pool", C.wout, I["w_out"].rearrange("(k p) n -> p k n", p=128), r=(), w=("wout",), extra=ex)
            common_setup(x1, x2, I["gmpre"], I["gmpost"], None)
            A2 = Arena(W[:, 38912:WREG_ELEMS])
            C.U = [A2.alloc((512,), BF16) for _ in range(4)]
            C.V = [A2.alloc((512,), F32) for _ in range(4)]
            C.st = A2.alloc((4, 4), F32)
            C.vln = [A2.alloc((512,), BF16) for _ in range(4)]
            C.sgu = [A2.alloc((512,), BF16) for _ in range(4)]
            C.sguT = A2.alloc((4, TT), BF16)
            C.attin = A2.alloc((4, 3, ATTW), F32)
            C.M = A2.alloc((4, 4), F32)
            C.dd = A2.alloc((4, 3, 4), F32)
            C.ww = A2.alloc((4, 3, 4), F32)
            C.wd_ = A2.alloc((4, 3, 4), F32)
            C.cc = A2.alloc((4, 3, 4), F32)
            C.den = A2.alloc((4, 4), F32)
            C.tmpO = [A2.alloc((3, 4, 64), F32) for _ in range(2)]
            C.attb = [A2.alloc((4, 64), BF16) for _ in range(4)]
            C.attT = A2.alloc((2, TT), BF16)
            C.sa = [A.alloc((TT,), F32) for _ in range(2)]
            C.sb_ = [A.alloc((TT,), F32) for _ in range(2)]
            C.mT = A.alloc((KC, TT), BF16)
            C.lng = A.alloc((512,), F32)
            C.lnb = A.alloc((512,), F32)
            load_bcast(P, C, C.lng, I["ln_g"], 512, "gains")
            load_bcast(P, C, C.lnb, I["ln_b"], 512, "gains")
            C.WsT = A.alloc((4, 128), BF16)
            C.bs = A.alloc((4,), F32)
            Wn = A2.alloc((4, 128), F32)
            Wnb = A2.alloc((4, 128), BF16)
            _dma(P, "sp", Wn, I["w_s"].rearrange("g p q -> p g q"), r=(), w=("Wn",))
            _dma(P, "sp", C.bs, I["b_s"].rearrange("g p -> p g"), r=(), w=("bs",), slow=True)
            P.add("dve", lambda h: h.tensor_copy(Wnb, Wn), r=("Wn",), w=("Wnb",))
            for gg in range(4):
                _tr(P, C.psT[0][:, gg * 128:(gg + 1) * 128], Wnb[:, gg, :], C.ident, r=("Wnb", "ident"), w=(("psT", 0),))
            _act(P, C.WsT, C.psT[0][:, 0:512].rearrange("p (g q) -> p g q", g=4), AF.Copy, r=(("psT", 0),), w=("WsT",))
            b1_phase(P, C, n_own, att)
            last_pe = P.last("pe")
            P.barrier()

        if "B2" in phases:
            load_ffn_weights(P, C, I["w_gate2"], I["w_up2"], I["w_down2"], extra=(last_pe,) if last_pe is not None else ())
            ffn_setup(x2, y, I["g2pre"], I["g2post"], I["gfin"])
            ffn_phase(P, C, n_own, C.gpre, C.gpost, C.gfin)
            P.barrier()

        with nc.Block() as block:
            P.emit(block, sems, dsems)
    return nc


def _t5_bucket(rel):
    nb = 16
    max_exact = 8
    ret = (rel > 0).astype(np.int32) * nb
    n = np.abs(rel)
    large = max_exact + (np.log(np.maximum(n, 1) / max_exact) / np.log(1024 / max_exact) * (nb - max_exact)).astype(np.int32)
    large = np.minimum(large, nb - 1)
    return ret + np.where(n < max_exact, n, large)


def segments():
    segs = []
    for g, (window, d) in enumerate(GROUPS):
        i_first = HALO // d
        nown = NOWN // d
        nblk_total = nown // 128
        per = 8 if nblk_total >= 8 else nblk_total
        for r in range(d):
            for s0 in range(0, nblk_total, per):
                segs.append((g, d, r, i_first + 128 * s0, per))
    return segs


def host_constants(seq_lo, seq_hi):
    ident = np.eye(128, dtype=np.float32)
    anti = np.ascontiguousarray(ident[::-1])
    onehot = np.zeros((32, 3, 129), np.float32)
    for g, (window, d) in enumerate(GROUPS):
        half = (window // 2) // d
        off = np.arange(-half, half + 1, dtype=np.int32) * d
        bk = _t5_bucket(off)
        onehot[bk, g, np.arange(129)] = 1.0
    segs = segments()
    vseg = np.zeros((128, 24, 9), np.float32)
    for si, (g, d, r, i0, nblk) in enumerate(segs):
        for j in range(nblk + 1):
            sub = i0 - 64 + 128 * j + np.arange(128)
            pos = d * sub + r
            vseg[:, si, j] = ((pos >= seq_lo) & (pos < seq_hi)).astype(np.float32)
    return ident, anti, onehot.reshape(32, 3 * 129), vseg.reshape(128, 24 * 9)


def make_in_maps(inp):
    f = lambda a: np.ascontiguousarray(np.asarray(a, dtype=np.float32))
    xp = f(inp["x_prompt"])[0]
    xs = f(inp["x_sample"])
    shared = {
        "w_gate1": f(inp["ffn1_w_gate"])[0], "w_up1": f(inp["ffn1_w_up"])[0], "w_down1": f(inp["ffn1_w_down"])[0],
        "w_gate2": f(inp["ffn2_w_gate"])[0], "w_up2": f(inp["ffn2_w_up"])[0], "w_down2": f(inp["ffn2_w_down"])[0],
        "w_in": f(inp["w_in"])[0], "w_att": f(inp["w_att"])[0], "w_sgu": f(inp["w_sgu"])[0], "w_out": f(inp["w_out"])[0],
        "g1pre": f(inp["ffn1_pre_g"]), "g1post": f(inp["ffn1_post_g"]), "gmpre": f(inp["mix_pre_g"]),
        "gmpost": f(inp["mix_post_g"]), "g2pre": f(inp["ffn2_pre_g"]), "g2post": f(inp["ffn2_post_g"]),
        "gfin": f(inp["final_g"]), "ln_g": f(inp["sgu_ln_g"]), "ln_b": f(inp["sgu_ln_b"]),
        "w_s": f(inp["sgu_w_s"])[0], "b_s": f(inp["sgu_b_s"])[0], "rel_bias": f(inp["rel_bias"]),
    }
    maps = []
    for c in range(NCORES):
        xe = np.zeros((NEXT, D), np.float32)
        if c < 4:
            lo = c * NOWN - HALO
            hi = lo + NEXT
            a, b = max(lo, 0), min(hi, xp.shape[0])
            xe[a - lo:b - lo] = xp[a:b]
            seq_lo, seq_hi = a - lo, b - lo
        else:
            xe[HALO:HALO + NOWN] = xs[c - 4]
            seq_lo, seq_hi = HALO, HALO + NOWN
        ident, anti, onehot, vseg = host_constants(seq_lo, seq_hi)
        m = dict(shared)
        m.update({"xe": xe, "ident": ident, "antiid": anti, "onehot": onehot, "vseg": vseg})
        maps.append(m)
    return maps


_NC_CACHE = {}


def kernel(**inputs):
    key = "full"
    if key not in _NC_CACHE:
        _NC_CACHE[key] = build_program(CFG)
    nc = _NC_CACHE[key]
    maps = make_in_maps(inputs)
    res = run_bass_kernel_spmd(nc, maps, core_ids=list(range(NCORES)))
    ys = [np.asarray(r["y"], dtype=np.float32) for r in res.results]
    y_prompt = np.concatenate(ys[0:4], axis=0)[None]
    y_sample = np.stack(ys[4:8], axis=0)
    return (y_prompt, y_sample)
```

```python
import numpy as np
import concourse.bass as bass
import concourse.mybir as mybir
from concourse.bass_utils import run_bass_kernel_spmd
from contextlib import ExitStack

F32 = mybir.dt.float32
BF16 = mybir.dt.bfloat16
AF = mybir.ActivationFunctionType
ALU = mybir.AluOpType
AX = mybir.AxisListType

NCORES = 8
D = 1024
DFF = 2816
NFC = 22
KC = 8
NEXT = 6144
NOWN = 4096
HALO = 1024
TT = 512
QKVW = 2316
ATTW = 264
EPS = 1e-6
NEG = -1e30
GROUPS = ((128, 1), (512, 4), (2048, 16))
WREG_ELEMS = 67584
ARENA_ELEMS = 38400

CFG = {"phases": ("A1", "A2", "ATT", "B1", "B2"), "debug": False, "n_ext_tiles": 12, "n_own_tiles": 8}


class Op:
    __slots__ = ("id", "eng", "fn", "deps", "dma", "dsem", "dval", "seq", "sig")


LAST_PROG = [None]
STAGE_LOG = []


class Prog:
    ENGS = ("pe", "act", "dve", "pool", "sp")

    def __init__(self, nc, ndma=20):
        self.nc = nc
        self.ops = []
        self.byeng = {e: [] for e in self.ENGS}
        self.lastw = {}
        self.readers = {}
        self.ndma = ndma
        self.dma_rr = {"sp": 0, "pool": 0, "act": 0}
        self.dma_cnt = {}
        self.dma_last = {}
        self.pending = {e: set() for e in self.ENGS}

    def add(self, eng, fn, r=(), w=(), dma=False, extra=()):
        op = Op()
        op.id = len(self.ops)
        op.eng = eng
        op.fn = fn
        op.dma = dma
        op.sig = False
        op.seq = None
        deps = set(extra)
        psr = [k for k in r if isinstance(k, tuple) and k[0] in ("ps", "psT")]
        if psr:
            r = [k for k in r if k not in psr]
            w = list(w) + psr
        for k in r:
            lw = self.lastw.get(k)
            if lw is not None:
                deps.add(lw)
        for k in w:
            lw = self.lastw.get(k)
            if lw is not None:
                deps.add(lw)
            deps.update(self.readers.get(k, ()))
        if self.pending[eng]:
            deps.update(self.pending[eng])
            self.pending[eng] = set()
        if dma:
            i = self.dma_rr[eng]
            self.dma_rr[eng] = (i + 1) % self.ndma
            key = (eng, i)
            prev = self.dma_last.get(key)
            if prev is not None:
                deps.add(prev)
            self.dma_cnt[key] = self.dma_cnt.get(key, 0) + 1
            self.dma_last[key] = op.id
            op.dsem = key
            op.dval = 16 * self.dma_cnt[key]
        deps.discard(op.id)
        op.deps = deps
        for k in r:
            self.readers.setdefault(k, []).append(op.id)
        for k in w:
            self.lastw[k] = op.id
            self.readers[k] = []
        self.ops.append(op)
        self.byeng[eng].append(op)
        return op.id

    def last(self, eng):
        return self.byeng[eng][-1].id if self.byeng[eng] else None

    def barrier(self):
        ids = set()
        for e in self.ENGS:
            if self.byeng[e]:
                ids.add(self.byeng[e][-1].id)
        for key, oid in self.dma_last.items():
            ids.add(oid)
        for e in self.ENGS:
            self.pending[e] |= ids
        self.lastw = {}
        self.readers = {}

    def emit(self, block, sems, dsems):
        ops = self.ops
        for op in ops:
            for d in op.deps:
                dop = ops[d]
                if dop.dma:
                    continue
                if dop.eng == op.eng == "pe" and not op.dma:
                    continue
                dop.sig = True
        for e in self.ENGS:
            n = 0
            for op in self.byeng[e]:
                if op.sig and not op.dma:
                    n += 1
                    op.seq = n
        finals = {}
        for key, cnt in self.dma_cnt.items():
            finals[key] = 16 * cnt

        def run(eng, h):
            waited = {}
            for op in self.byeng[eng]:
                need = {}
                for d in op.deps:
                    dop = ops[d]
                    if dop.dma:
                        s, v = ("d",) + dop.dsem, dop.dval
                    else:
                        if dop.eng == eng == "pe" and not op.dma:
                            continue
                        s, v = ("e", dop.eng), dop.seq
                    if v > need.get(s, 0):
                        need[s] = v
                for s, v in need.items():
                    if waited.get(s, 0) >= v:
                        continue
                    waited[s] = v
                    sem = sems[s[1]] if s[0] == "e" else dsems[(s[1], s[2])]
                    h.wait_ge(sem, v)
                ins = op.fn(h)
                if op.dma:
                    ins.then_inc(dsems[op.dsem], 16)
                elif op.sig:
                    ins.then_inc(sems[eng], 1)
            if eng == "sp":
                for key, v in finals.items():
                    if waited.get(("d",) + key, 0) < v:
                        h.wait_ge(dsems[key], v)
                for e2 in ("pe", "act", "dve", "pool"):
                    n = max([o.seq or 0 for o in self.byeng[e2]] + [0])
                    if n:
                        h.wait_ge(sems[e2], n)

        @block.tensor
        def _(h):
            run("pe", h)

        @block.scalar
        def _(h):
            run("act", h)

        @block.vector
        def _(h):
            run("dve", h)

        @block.gpsimd
        def _(h):
            run("pool", h)

        @block.sync
        def _(h):
            run("sp", h)


class Arena:
    def __init__(self, base):
        self.base = base
        self.n = base.shape[1]
        self.off = 0

    def reset(self, off=0):
        self.off = off

    def alloc(self, free_shape, dtype, parts=128):
        n = int(np.prod(free_shape))
        ne = n * 2 if dtype == F32 else n
        self.off = (self.off + 15) // 16 * 16
        assert self.off + ne <= self.n, ("arena overflow", self.off, ne, self.n)
        ap = self.base[0:parts, self.off:self.off + ne]
        self.off += ne
        if dtype == F32:
            ap = ap.bitcast(F32)
        if len(free_shape) == 2:
            ap = ap.rearrange("p (a b) -> p a b", a=free_shape[0])
        elif len(free_shape) == 3:
            ap = ap.rearrange("p (a b c) -> p a b c", a=free_shape[0], b=free_shape[1])
        return ap


class Ctx:
    pass


def _mm(P, out, lhsT, rhs, start, stop, r, w):
    P.add("pe", lambda h, o=out, a=lhsT, b=rhs, s=start, t=stop: h.matmul(o, a, b, start=s, stop=t), r=r, w=w)


def _tr(P, out, in_, ident, r, w):
    P.add("pe", lambda h, o=out, a=in_, i=ident: h.transpose(o, a, i), r=r, w=w)


def _act(P, out, in_, func, r, w, bias=None, scale=None, accum=None):
    def fn(h, o=out, a=in_, f=func, b=bias, s=scale, acc=accum):
        kw = {}
        if b is not None:
            kw["bias"] = b
        if s is not None:
            kw["scale"] = s
        if acc is not None:
            kw["accum_out"] = acc
        return h.activation(o, a, f, **kw)
    P.add("act", fn, r=r, w=w)


def _dma(P, q, out, in_, r, w, extra=(), slow=False):
    if slow:
        return P.add(q, lambda h, o=out, a=in_: h.dma_start(out=o, in_=a, allow_slow_non_contiguous=True), r=r, w=w, dma=True, extra=extra)
    return P.add(q, lambda h, o=out, a=in_: h.dma_start(out=o, in_=a), r=r, w=w, dma=True, extra=extra)


class PsumRR:
    def __init__(self, banks, keys):
        self.banks = banks
        self.keys = keys
        self.i = 0

    def next(self):
        b, k = self.banks[self.i], self.keys[self.i]
        self.i = (self.i + 1) % len(self.banks)
        return b, k


def load_bcast(P, C, dst, vec_ap, n, key, scale=None):
    _dma(P, "sp", dst, vec_ap[0:1, :].to_broadcast([128, n]), r=(), w=(key,))
    if scale is not None:
        P.add("dve", lambda h, o=dst, s=scale: h.tensor_scalar_mul(o, o, float(s)), r=(key,), w=(key,))


def rms_rstd(P, ss, out, rk, wk, n=D):
    _act(P, ss, ss, AF.Sqrt, r=rk, w=rk, bias=C_EPS[0], scale=1.0 / n)
    P.add("dve", lambda h, o=out, a=ss: h.reciprocal(o, a), r=rk, w=wk)


C_EPS = [None]


def prologue_steps(P, C, src_rows, tile, gain_b):
    hb = tile % 2
    hT = C.hT[hb]

    def load(s):
        j = tile * 4 + s
        xs = j % C.nx
        row0 = src_rows + j * 128
        _dma(P, "sp", C.X[xs], C.src[row0:row0 + 128, :], r=(("src", j),), w=(("x", xs),))

    def comp_a(s):
        j = tile * 4 + s
        xs = j % C.nx
        X = C.X[xs]
        sc = j % 4
        _act(P, C.junk, X, AF.Square, r=(("x", xs),), w=(("ss", sc),), accum=C.ss[:, sc:sc + 1])
        rms_rstd(P, C.ss[:, sc:sc + 1], C.rs[:, sc:sc + 1], rk=(("ss", sc),), wk=(("rs", sc),))
        xb = j % 2
        P.add("dve", lambda h, o=C.xn[xb], a=X, sca=C.rs[:, sc:sc + 1], g=gain_b:
              h.scalar_tensor_tensor(o, a, sca, g, ALU.mult, ALU.mult),
              r=(("x", xs), ("rs", sc), "gains"), w=(("xn", xb),))

    def comp_b(s):
        j = tile * 4 + s
        xb = j % 2
        pb = j % 2
        pst = C.psT[pb]
        for kc in range(KC):
            _tr(P, pst[:, kc * 128:(kc + 1) * 128], C.xn[xb][:, kc * 128:(kc + 1) * 128], C.ident,
                r=(("xn", xb), "ident"), w=(("psT", pb),))
        _act(P, hT[:, :, s * 128:(s + 1) * 128], pst.rearrange("p (k t) -> p k t", k=KC), AF.Copy,
             r=(("psT", pb),), w=(("hT", hb, s),))

    L = lambda s: (lambda: load(s))
    Ca = lambda s: (lambda: comp_a(s))
    Cb = lambda s: (lambda: comp_b(s))
    return [L(0), L(1), Ca(0), L(2), Ca(1), Cb(0), L(3), Ca(2), Cb(1), Ca(3), Cb(2), Cb(3)]


def prologue(P, C, src_rows, tile, gain_b, tag):
    for f in prologue_steps(P, C, src_rows, tile, gain_b):
        f()


class Filler:
    def __init__(self, fns, nsteps, start=0):
        self.fns = list(fns)
        self.at = {}
        n = len(self.fns)
        for k in range(n):
            step = start + (k * max(nsteps - start, 1)) // n
            self.at.setdefault(step, []).append(self.fns[k])

    def step(self, k):
        for f in self.at.pop(k, []):
            f()

    def flush(self):
        for k in sorted(self.at):
            for f in self.at[k]:
                f()
        self.at = {}


def post_norm_residual(P, C, banks, gain_b, j, final_gain=None, dst_rows=None):
    X = C.Xr
    for n in range(2):
        b, k = banks[n]
        _act(P, C.junk[:, 0:512], b, AF.Square, r=(k,), w=(("ssy", n),), accum=C.ssy[:, n:n + 1])
        P.add("dve", lambda h, o=C.t[:, n * 512:(n + 1) * 512], a=b, g=gain_b[:, n * 512:(n + 1) * 512]:
              h.tensor_tensor(o, a, g, ALU.mult), r=(k, "gains"), w=(("t", n),))
    P.add("dve", lambda h, o=C.ssy[:, 2:3], a=C.ssy[:, 0:1], b2=C.ssy[:, 1:2]: h.tensor_tensor(o, a, b2, ALU.add),
          r=(("ssy", 0), ("ssy", 1)), w=(("ssy", 2),))
    rms_rstd(P, C.ssy[:, 2:3], C.ssy[:, 3:4], rk=(("ssy", 2),), wk=(("ssy", 3),))
    P.add("dve", lambda h, o=X, a=C.t, sca=C.ssy[:, 3:4], x=X: h.scalar_tensor_tensor(o, a, sca, x, ALU.mult, ALU.add),
          r=(("t", 0), ("t", 1), ("ssy", 3), "xr"), w=("xr",))
    outt = X
    outk = "xr"
    if final_gain is not None:
        _act(P, C.junk, X, AF.Square, r=("xr",), w=(("ssf", 0),), accum=C.ssy[:, 4:5])
        rms_rstd(P, C.ssy[:, 4:5], C.ssy[:, 5:6], rk=(("ssf", 0),), wk=(("ssf", 1),))
        P.add("dve", lambda h, o=C.t, a=X, sca=C.ssy[:, 5:6], g=final_gain: h.scalar_tensor_tensor(o, a, sca, g, ALU.mult, ALU.mult),
              r=("xr", ("ssf", 1), "gains", ("t", 0), ("t", 1)), w=(("t", 0), ("t", 1)))
        outt = C.t
        outk = ("t", 0)
    if CFG.get("stop", 99) < 6:
        return
    _dma(P, "sp", C.dst[dst_rows:dst_rows + 128, :], outt, r=(outk, ("t", 1)) if final_gain is not None else (outk,),
         w=(("dst", j),))


def load_ffn_weights(P, C, wg_d, wu_d, wd_d, extra, parts=("wg", "wu", "wd")):
    W = C.wreg
    C.wg = W[:, 0:KC * DFF].rearrange("p (k n) -> p k n", k=KC)
    C.wu = W[:, KC * DFF:2 * KC * DFF].rearrange("p (k n) -> p k n", k=KC)
    C.wd = W[:, 2 * KC * DFF:2 * KC * DFF + NFC * D].rearrange("p (c n) -> p c n", c=NFC)
    wgv = wg_d.rearrange("(k p) n -> p k n", p=128)
    wuv = wu_d.rearrange("(k p) n -> p k n", p=128)
    wdv = wd_d.rearrange("(c p) n -> p c n", p=128)
    NS = 4
    cs = DFF // NS
    for i in range(NS):
        if "wg" in parts:
            _dma(P, "pool", C.wg[:, :, i * cs:(i + 1) * cs], wgv[:, :, i * cs:(i + 1) * cs], r=(), w=(("wg", i),), extra=extra)
        if "wu" in parts:
            _dma(P, "pool", C.wu[:, :, i * cs:(i + 1) * cs], wuv[:, :, i * cs:(i + 1) * cs], r=(), w=(("wu", i),), extra=extra)
    if "wd" in parts:
        for i in range(2):
            _dma(P, "pool", C.wd[:, i * 11:(i + 1) * 11, :], wdv[:, i * 11:(i + 1) * 11, :], r=(), w=(("wd", i),), extra=extra)
    C.wslice = cs


def ffn_phase(P, C, ntiles, gain_pre, gain_post, gain_fin):
    prs = C.psrr
    stop = CFG.get("stop", 99)
    if stop < 1:
        return
    prologue(P, C, 0, 0, gain_pre, "f")
    if stop < 2:
        return
    for i in range(ntiles):
        hT = C.hT[i % 2]
        hk = [("hT", i % 2, s) for s in range(4)]
        fill = Filler(prologue_steps(P, C, 0, i + 1, gain_pre) if i + 1 < ntiles else [], NFC, start=2)
        for c in range(NFC):
            fill.step(c)
            wk = c * 128 // C.wslice
            bg, kg = prs.next()
            for kc in range(KC):
                _mm(P, bg, C.wg[:, kc, c * 128:(c + 1) * 128], hT[:, kc, :], kc == 0, kc == KC - 1,
                    r=hk + [("wg", wk)], w=(kg,))
            bu, ku = prs.next()
            for kc in range(KC):
                _mm(P, bu, C.wu[:, kc, c * 128:(c + 1) * 128], hT[:, kc, :], kc == 0, kc == KC - 1,
                    r=hk + [("wu", wk)], w=(ku,))
            sg = C.sg[c % 2]
            _act(P, sg, bg, AF.Silu, r=(kg,), w=(("sg", c % 2),))
            P.add("dve", lambda h, o=C.hid[:, c, :], a=sg, b=bu: h.tensor_tensor(o, a, b, ALU.mult),
                  r=(("sg", c % 2), ku), w=(("hid", c),))
        fill.flush()
        if i == ntiles - 1 and getattr(C, "after_last_gateup", None):
            C.after_last_gateup()
            C.after_last_gateup = None
        if stop < 3:
            return
        for s in range(4):
            j = i * 4 + s
            banks = []
            for n in range(2):
                b, k = prs.next()
                for c in range(NFC):
                    _mm(P, b, C.hid[:, c, s * 128:(s + 1) * 128], C.wd[:, c, n * 512:(n + 1) * 512], c == 0, c == NFC - 1,
                        r=(("hid", c), ("wd", c // 11)), w=(k,))
                banks.append((b, k))
            if stop < 4:
                continue
            _dma(P, "sp", C.Xr, C.src[j * 128:(j + 1) * 128, :], r=(("src", j),), w=("xr",))
            if stop < 5:
                continue
            post_norm_residual(P, C, banks, gain_post, j, final_gain=gain_fin, dst_rows=j * 128)


QKV_BLOCKS = ((0, 512, "q"), (512, 768, "q"), (768, 1280, "k"), (1280, 1536, "k"), (1536, 2048, "v"), (2048, 2304, "v"))


def qkv_blocks(tile, ntiles):
    if ntiles != 12:
        return QKV_BLOCKS
    if 2 <= tile <= 9:
        return QKV_BLOCKS
    if tile in (1, 10):
        return QKV_BLOCKS[2:]
    return (QKV_BLOCKS[3], QKV_BLOCKS[5])


def qkv_phase(P, C, ntiles, gain_pre):
    prs = C.psrr
    prologue(P, C, 0, 0, gain_pre, "q")
    ev = 0
    for i in range(ntiles):
        hT = C.hT[i % 2]
        blocks = qkv_blocks(i, ntiles)
        fill = Filler(prologue_steps(P, C, 0, i + 1, gain_pre) if i + 1 < ntiles else [], 4 * len(blocks))
        step = 0
        for s in range(4):
            j = i * 4 + s
            sb = j % 2
            stage = C.stage[sb]
            stage_v = stage[:, 1536:2316].rearrange("p (h e) -> p h e", e=65)
            for (c0, c1, kind) in blocks:
                fill.step(step)
                step += 1
                b, k = prs.next()
                n = c1 - c0
                for kc in range(KC):
                    _mm(P, b[:, 0:n], hT[:, kc, s * 128:(s + 1) * 128], C.wqkv[:, kc, c0:c1], kc == 0, kc == KC - 1,
                        r=(("hT", i % 2, s), "wqkv"), w=(k,))
                if kind == "v":
                    h0 = (c0 - 1536) // 64
                    nh = n // 64
                    out = stage_v[:, h0:h0 + nh, 0:64]
                    src = b[:, 0:n].rearrange("p (h e) -> p h e", e=64)
                else:
                    out = stage[:, c0:c1]
                    src = b[:, 0:n]
                scale = 0.125 if kind == "q" else 1.0
                if ev % 2 == 0:
                    _act(P, out, src, AF.Copy, r=(k,), w=(("stage", sb, c0),), scale=scale)
                else:
                    P.add("dve", lambda h, o=out, a=src, sc=scale: h.tensor_scalar_mul(o, a, float(sc)),
                          r=(k,), w=(("stage", sb, c0),))
                ev += 1
            _dma(P, "sp", C.dst[j * 128:(j + 1) * 128, :], stage, r=[("stage", sb, c0) for (c0, _, _) in QKV_BLOCKS],
                 w=(("dst", j),))
        fill.flush()


def build_bias(P, C, I, tscr):
    A = C.A
    rb = A.alloc((12,), F32, parts=32)
    E = A.alloc((3 * 129,), F32, parts=32)
    Tsb = A.alloc((3, 512), F32, parts=12)
    anti = A.alloc((128,), F32)
    R = [A.alloc((256,), F32) for _ in range(2)]
    _dma(P, "sp", rb, I["rel_bias"], r=(), w=("rb",))
    _dma(P, "sp", E, I["onehot"], r=(), w=("E",))
    _dma(P, "sp", anti, I["antiid"], r=(), w=("anti",))
    P.add("pool", lambda h: h.memset(Tsb, NEG), r=(), w=("Tsb",))
    b5 = C.ps[5][:, :]
    for g in range(3):
        _mm(P, b5[0:12, 0:129], rb[0:32, 0:12], E[0:32, g * 129:(g + 1) * 129], True, True, r=("rb", "E"), w=(("ps", 5),))
        P.add("dve", lambda h, o=Tsb[:, g, 127:256], a=b5[0:12, 0:129]: h.tensor_copy(o, a), r=(("ps", 5), "Tsb"), w=("Tsb",))
    for g in range(3):
        _dma(P, "sp", tscr[4 * g:4 * g + 4, :], Tsb[4 * g:4 * g + 4, g, :], r=("Tsb",), w=(("tscr", g),))
    for gh in range(12):
        Rt = R[gh % 2]
        src = bass.AP(tscr.tensor, gh * 512, [[1, 128], [1, 256]])
        _dma(P, "sp", Rt, src, r=(("tscr", gh // 4),), w=(("R", gh % 2),))
        b3, k3 = C.ps[3 + gh % 2][:, :], ("ps", 3 + gh % 2)
        _mm(P, b3[:, 0:256], anti, Rt, True, True, r=(("R", gh % 2), "anti"), w=(k3,))
        P.add("dve", lambda h, o=C.biasb[:, gh, :], a=b3[:, 0:256]: h.tensor_copy(o, a), r=(k3,), w=("biasb",))


def att_phase(P, C, qkv, att):
    segs = segments()
    psPT = [C.ps[4][:, :].bitcast(BF16), C.ps[5][:, :].bitcast(BF16)]
    Sb = [(C.ps[i][:, :], ("ps", i)) for i in range(4)]
    Ob = [(C.ps[6 + i][:, 0:260], ("ps", 6 + i)) for i in range(2)]
    psKQ = [C.ps[6][:, :].bitcast(BF16), C.ps[7][:, :].bitcast(BF16)]
    qt = qkv.tensor
    tcount = [0]
    units = []
    first_unit = {}
    bc = 0
    for si, (g, d, r, i0, nblk) in enumerate(segs):
        first_unit[si] = len(units)
        for b in range(nblk):
            for c in range(2):
                units.append((si, b, c, bc + b))
        bc += nblk
    N = len(units)

    def loads(si):
        g, d, r, i0, nblk = segs[si]
        rb = si % 4
        nt = nblk + 1
        rowk = d * (i0 - 64) + r
        rowq = d * i0 + r
        _dma(P, "sp", C.Kraw[rb][:, 0:nt, :], bass.AP(qt, rowk * QKVW + 768 + 256 * g, [[d * QKVW, 128], [128 * d * QKVW, nt], [1, 256]]),
             r=(), w=(("Kraw", rb),))
        _dma(P, "sp", C.Vt[rb][:, 0:nt, :], bass.AP(qt, rowk * QKVW + 1536 + 260 * g, [[d * QKVW, 128], [128 * d * QKVW, nt], [1, 260]]),
             r=(), w=(("Vt", rb),))
        _dma(P, "sp", C.Qraw[rb][:, 0:nblk, :], bass.AP(qt, rowq * QKVW + 256 * g, [[d * QKVW, 128], [128 * d * QKVW, nblk], [1, 256]]),
             r=(), w=(("Qraw", rb),))

    def transposes(si, kqb):
        g, d, r, i0, nblk = segs[si]
        rb, tb2 = si % 4, si % 2
        nt = nblk + 1
        for (raw, rk, dstT, dk, ntile) in ((C.Kraw[rb], ("Kraw", rb), C.KT[tb2], ("KT", tb2), nt),
                                          (C.Qraw[rb], ("Qraw", rb), C.QT[tb2], ("QT", tb2), nblk)):
            for j0 in range(0, ntile, 4):
                nj = min(4, ntile - j0)
                pk, pkk = psKQ[kqb], ("ps", 6 + kqb)
                for c in range(2):
                    for jj in range(nj):
                        _tr(P, pk[:, c * 512 + jj * 128: c * 512 + (jj + 1) * 128], raw[:, j0 + jj, c * 128:(c + 1) * 128], C.ident,
                            r=(rk, "ident"), w=(pkk,))
                src = pk.rearrange("p (c t) -> p c t", c=2)[:, :, 0:nj * 128]
                if dk[0] == "KT":
                    P.add("dve", lambda h, o=dstT[:, :, j0 * 128:(j0 + nj) * 128], a=src: h.tensor_copy(o, a), r=(pkk,), w=(dk,))
                else:
                    _act(P, dstT[:, :, j0 * 128:(j0 + nj) * 128], src, AF.Copy, r=(pkk,), w=(dk,))

    def ctx(u):
        si, b, c, gb = units[u]
        g, d, r, i0, nblk = segs[si]
        return si, b, c, gb, g, d, r, i0

    def stage_S(u):
        si, b, c, gb, g, d, r, i0 = ctx(u)
        QT, KT = C.QT[si % 2], C.KT[si % 2]
        sbk, skey = Sb[u % 4]
        hb0 = 4 * g + 2 * c
        for hp in range(2):
            _mm(P, sbk[:, hp * 256:(hp + 1) * 256], QT[hp * 64:(hp + 1) * 64, c, b * 128:(b + 1) * 128],
                KT[hp * 64:(hp + 1) * 64, c, b * 128:b * 128 + 256], True, False, r=(("QT", si % 2), ("KT", si % 2)), w=(skey,))
            _mm(P, sbk[:, hp * 256:(hp + 1) * 256], C.ident, C.biasb[:, hb0 + hp, :], False, True,
                r=("ident", "biasb"), w=(skey,))

    def stage_1(u):
        si, b, c, gb, g, d, r, i0 = ctx(u)
        sl = u % 4
        sbk, skey = Sb[u % 4]
        ost = C.ostage[gb % 3]
        ostk = ("ost", gb % 3)
        mcols = ost[:, 260 + 2 * c:262 + 2 * c]
        P.add("dve", lambda hh, o=mcols, a=sbk.rearrange("p (h k) -> p h k", h=2): hh.reduce_max(o, a, AX.X),
              r=(skey,), w=(ostk + (c,),))
        P.add("dve", lambda hh, o=C.negm[:, sl, :], a=mcols: hh.tensor_scalar_mul(o, a, -1.0), r=(ostk + (c,),), w=(("negm", sl),))
        for hp in range(2):
            _act(P, C.Pb[sl][:, hp, :], sbk[:, hp * 256:(hp + 1) * 256], AF.Exp, r=(skey, ("negm", sl)), w=(("Pb", sl, hp),),
                 bias=C.negm[:, sl, hp:hp + 1])

    def stage_2(u):
        sl = u % 4
        pt, ptk = psPT[u % 2], ("ps", 4 + u % 2)
        for hp in range(2):
            for jj in range(2):
                _tr(P, pt[:, (hp * 2 + jj) * 128:(hp * 2 + jj + 1) * 128], C.Pb[sl][:, hp, jj * 128:(jj + 1) * 128], C.ident,
                    r=(("Pb", sl, hp), "ident"), w=(ptk,))

    def stage_3(u):
        si, b, c, gb, g, d, r, i0 = ctx(u)
        sl = u % 4
        pt, ptk = psPT[u % 2], ("ps", 4 + u % 2)
        PTs = C.PTs[sl]
        ptv = pt[:, 0:512].rearrange("p (h j q) -> p h j q", h=2, j=2)
        for jj in range(2):
            vcol = C.vseg[:, si * 9 + b + jj: si * 9 + b + jj + 1]
            P.add("dve", lambda hh, o=PTs[:, :, jj, :], a=ptv[:, :, jj, :], sc=vcol: hh.tensor_scalar_mul(o, a, sc),
                  r=(ptk, "vseg"), w=(("PTs", sl, jj),))

    def stage_4(u):
        si, b, c, gb, g, d, r, i0 = ctx(u)
        sl = u % 4
        PTs = C.PTs[sl]
        Vt = C.Vt[si % 4]
        ob, okey = Ob[gb % 2]
        ost = C.ostage[gb % 3]
        ostk = ("ost", gb % 3)
        for hp in range(2):
            h = 2 * c + hp
            for jj in range(2):
                _mm(P, ob[:, h * 65:(h + 1) * 65], PTs[:, hp, jj, :], Vt[:, b + jj, h * 65:(h + 1) * 65], jj == 0, jj == 1,
                    r=(("PTs", sl, jj), ("Vt", si % 4)), w=(okey,))
        if c == 1:
            _act(P, ost[:, 0:260], ob, AF.Copy, r=(okey,), w=(ostk + (9,),))
            t0 = d * (i0 + 128 * b) + r - HALO
            dst = bass.AP(att.tensor, (t0 * 3 + g) * ATTW, [[d * 3 * ATTW, 128], [1, ATTW]])
            _dma(P, "pool", dst, ost, r=[ostk + (x,) for x in (0, 1, 9)], w=(("att", si, b),))

    nseg = len(segs)
    loads(0)
    loads(1)
    loads(2)
    loads(3)
    transposes(0, 0)
    seg_started = set()
    for u in range(min(2, N)):
        stage_S(u)
    for t in range(N + 4):
        if t < N:
            si = units[t][0]
            if si not in seg_started and first_unit[si] == t:
                seg_started.add(si)
                if si + 1 < nseg:
                    transposes(si + 1, (units[t][3] - 1) % 2)
        n0 = len(P.ops)
        if t + 2 < N:
            stage_S(t + 2)
        n1 = len(P.ops)
        if t < N:
            stage_1(t)
        n2 = len(P.ops)
        if 0 <= t - 1 < N:
            stage_2(t - 1)
        n3 = len(P.ops)
        if 0 <= t - 2 < N:
            stage_3(t - 2)
        n4 = len(P.ops)
        if 0 <= t - 3 < N:
            stage_4(t - 3)
        STAGE_LOG.append((t, n0, n1, n2, n3, n4, len(P.ops)))
        if 0 <= t - 3 < N:
            si3 = units[t - 3][0]
            if t - 3 + 1 == N or units[t - 3 + 1][0] != si3:
                if si3 + 4 < nseg:
                    loads(si3 + 4)


def b1_phase(P, C, ntiles, att):
    prs = C.psrr

    def z_stages(i):
        hT = C.hT[i % 2]
        hk = [("hT", i % 2, s) for s in range(4)]
        AI = C.attin
        mview = AI[:, :, :, 260:264]
        sview = AI[:, :, :, 0:260].rearrange("p s g (h e) -> p s g h e", e=65)

        def z0():
            _dma(P, "sp", AI, att[i * 512:(i + 1) * 512, :].rearrange("(s p) (g e) -> p s g e", p=128, g=3), r=(), w=("attin",))

        def z1():
            for s in range(4):
                bu, ku = prs.next()
                for kc in range(KC):
                    _mm(P, bu, hT[:, kc, s * 128:(s + 1) * 128], C.wrest[:, kc, 0:512], kc == 0, kc == KC - 1, r=(hk[s], "wrest"), w=(ku,))
                _act(P, C.U[s], bu, AF.Gelu, r=(ku,), w=(("U", s),))
                bv, kv = prs.next()
                for kc in range(KC):
                    _mm(P, bv, hT[:, kc, s * 128:(s + 1) * 128], C.wrest[:, kc, 512:1024], kc == 0, kc == KC - 1, r=(hk[s], "wrest"), w=(kv,))
                _act(P, C.V[s], bv, AF.Gelu, r=(kv,), w=(("V", s), ("st0", s)), accum=C.st[:, 0, s:s + 1])

        def z2():
            P.add("dve", lambda h, o=C.M, a=mview.rearrange("p s g h -> p s h g"): h.tensor_reduce(o, a, AX.X, ALU.max),
                  r=("attin",), w=("M",))
            P.add("dve", lambda h, o=C.dd, a=mview, m=C.M.unsqueeze(2).to_broadcast([128, 4, 3, 4]): h.tensor_tensor(o, a, m, ALU.subtract),
                  r=("attin", "M"), w=("dd",))
            _act(P, C.ww, C.dd, AF.Exp, r=("dd",), w=("ww",))
            for s in range(4):
                P.add("dve", lambda h, o=C.wd_[:, s], a=C.ww[:, s], sv=sview[:, s, :, :, 64]: h.tensor_tensor(o, a, sv, ALU.mult),
                      r=("ww", "attin"), w=(("wd", s),))
            P.add("dve", lambda h, o=C.den, a=C.wd_.rearrange("p s g h -> p s h g"): h.tensor_reduce(o, a, AX.X, ALU.add),
                  r=[("wd", s) for s in range(4)], w=("den",))
            P.add("dve", lambda h, o=C.den: h.reciprocal(o, o), r=("den",), w=("den",))
            P.add("dve", lambda h, o=C.cc, a=C.ww, m=C.den.unsqueeze(2).to_broadcast([128, 4, 3, 4]): h.tensor_tensor(o, a, m, ALU.mult),
                  r=("ww", "den"), w=("cc",))

        def z3():
            P.add("dve", lambda h, o=C.st[:, 1, :], a=C.st[:, 0, :]: h.tensor_scalar_mul(o, a, -1.0 / 512.0),
                  r=[("st0", s) for s in range(4)], w=("st1",))
            for s in range(4):
                _act(P, C.junk[:, 0:512], C.V[s], AF.Square, r=(("V", s), "st1"), w=(("st2", s),), bias=C.st[:, 1, s:s + 1],
                     accum=C.st[:, 2, s:s + 1])
            _act(P, C.st[:, 2, :], C.st[:, 2, :], AF.Sqrt, r=[("st2", s) for s in range(4)], w=[("st2", s) for s in range(4)],
                 bias=C_EPS[0], scale=1.0 / 512.0)
            P.add("dve", lambda h, o=C.st[:, 3, :], a=C.st[:, 2, :]: h.reciprocal(o, a), r=[("st2", s) for s in range(4)], w=("st3",))

        def z4():
            for s in range(4):
                V = C.V[s]
                P.add("dve", lambda h, o=V, s1=C.st[:, 1, s:s + 1], s2=C.st[:, 3, s:s + 1]: h.tensor_scalar(o, o, s1, s2, ALU.add, ALU.mult),
                      r=(("V", s), "st1", "st3", ("st2", s)), w=(("V", s),))
            for s in range(4):
                V = C.V[s]
                P.add("dve", lambda h, o=V, g=C.lng: h.tensor_tensor(o, o, g, ALU.mult), r=(("V", s), "gains"), w=(("V", s),))
            for s in range(4):
                P.add("dve", lambda h, o=C.vln[s], a=C.V[s], g=C.lnb: h.tensor_tensor(o, a, g, ALU.add), r=(("V", s), "gains"), w=(("vln", s),))

        def z5():
            for s in range(4):
                ts = s % 2
                oview = sview[:, s, :, :, 0:64]
                P.add("dve", lambda h, o=C.tmpO[ts], a=oview, m=C.cc[:, s].unsqueeze(3).to_broadcast([128, 3, 4, 64]): h.tensor_tensor(o, a, m, ALU.mult),
                      r=("attin", "cc"), w=(("tmpO", ts),))

                def _red(h, o=C.attb[s], a=C.tmpO[ts].rearrange("p g h e -> p h e g")):
                    with C.nc.allow_low_precision(reason="3-term sum on the fp32 ALU, rounded once to bf16 for the matmul"):
                        return h.tensor_reduce(o, a, AX.X, ALU.add)
                P.add("dve", _red, r=(("tmpO", ts),), w=(("attb", s),))

        def z6():
            for s in range(4):
                bm, km = prs.next()
                for gg in range(4):
                    _mm(P, bm[:, gg * 128:(gg + 1) * 128], C.WsT[:, gg, :], C.vln[s][:, gg * 128:(gg + 1) * 128], True, True,
                        r=(("vln", s), "WsT"), w=(km,))
                for gg in range(4):
                    P.add("dve", lambda h, o=C.sgu[s][:, gg * 128:(gg + 1) * 128], a=bm[:, gg * 128:(gg + 1) * 128], sc=C.bs[:, gg:gg + 1],
                          u=C.U[s][:, gg * 128:(gg + 1) * 128]: h.scalar_tensor_tensor(o, a, sc, u, ALU.add, ALU.mult),
                          r=(km, ("U", s), "bs"), w=(("sgu", s),))

        def z7():
            for s in range(4):
                pb = s % 2
                pst = C.psT[pb]
                for gg in range(4):
                    _tr(P, pst[:, gg * 128:(gg + 1) * 128], C.sgu[s][:, gg * 128:(gg + 1) * 128], C.ident, r=(("sgu", s), "ident"), w=(("psT", pb),))
                ab = C.attb[s].rearrange("p h e -> p (h e)")
                for cc_ in range(2):
                    _tr(P, pst[:, 512 + cc_ * 128:512 + (cc_ + 1) * 128], ab[:, cc_ * 128:(cc_ + 1) * 128], C.ident, r=(("attb", s), "ident"), w=(("psT", pb),))
                _act(P, C.sguT[:, :, s * 128:(s + 1) * 128], pst[:, 0:512].rearrange("p (k t) -> p k t", k=4), AF.Copy,
                     r=(("psT", pb),), w=(("sguT", s),))
                _act(P, C.attT[:, :, s * 128:(s + 1) * 128], pst[:, 512:768].rearrange("p (k t) -> p k t", k=2), AF.Copy,
                     r=(("psT", pb),), w=(("attT", s),))

        return {0: [z0], 2: [z2], 4: [z5], 6: [z1], 7: [z3], 8: [z4], 9: [z6], 10: [z7]}

    def g_step(i, oc):
        hT = C.hT[i % 2]
        hk = [("hT", i % 2, s) for s in range(4)]
        sguk = [("sguT", s) for s in range(4)]
        attk = [("attT", s) for s in range(4)]
        bga, kga = prs.next()
        for kc in range(KC):
            _mm(P, bga, C.wrest[:, kc, 1024 + oc * 128:1024 + (oc + 1) * 128], hT[:, kc, :], kc == 0, kc == KC - 1, r=hk + ["wrest"], w=(kga,))
        bgb, kgb = prs.next()
        for kc in range(KC):
            _mm(P, bgb, C.wrest[:, kc, 2048 + oc * 128:2048 + (oc + 1) * 128], hT[:, kc, :], kc == 0, kc == KC - 1, r=hk + ["wrest"], w=(kgb,))
        bpa, kpa = prs.next()
        for kc in range(2):
            _mm(P, bpa, C.watt[:, kc, oc * 128:(oc + 1) * 128], C.attT[:, kc, :], kc == 0, kc == 1, r=attk + ["watt"], w=(kpa,))
        bpb, kpb = prs.next()
        for kc in range(4):
            _mm(P, bpb, C.wsgu[:, kc, oc * 128:(oc + 1) * 128], C.sguT[:, kc, :], kc == 0, kc == 3, r=sguk + ["wsgu"], w=(kpb,))
        o2 = oc % 2
        _act(P, C.sa[o2], bga, AF.Sigmoid, r=(kga,), w=(("sa", o2),))
        _act(P, C.sb_[o2], bgb, AF.Sigmoid, r=(kgb,), w=(("sb", o2),))
        P.add("dve", lambda h, o=C.sa[o2], b=bpa: h.tensor_tensor(o, o, b, ALU.mult), r=(("sa", o2), kpa), w=(("sa", o2),))
        P.add("dve", lambda h, o=C.sb_[o2], b=bpb: h.tensor_tensor(o, o, b, ALU.mult), r=(("sb", o2), kpb), w=(("sb", o2),))
        P.add("dve", lambda h, o=C.mT[:, oc, :], a=C.sa[o2], b=C.sb_[o2]: h.tensor_tensor(o, a, b, ALU.add),
              r=(("sa", o2), ("sb", o2)), w=(("mT", oc),))

    def w_step(i, s):
        mk = [("mT", oc) for oc in range(8)]
        j = i * 4 + s
        banks = []
        for n in range(2):
            b, k = prs.next()
            for kc in range(KC):
                _mm(P, b, C.mT[:, kc, s * 128:(s + 1) * 128], C.wout[:, kc, n * 512:(n + 1) * 512], kc == 0, kc == KC - 1,
                    r=mk + ["wout"], w=(k,))
            banks.append((b, k))
        _dma(P, "sp", C.Xr, C.src[HALO + j * 128:HALO + (j + 1) * 128, :], r=(), w=("xr",))
        post_norm_residual(P, C, banks, C.gpost, j, final_gain=None, dst_rows=j * 128)

    prologue(P, C, HALO, 0, C.gpre, "m")
    z = z_stages(0)
    for k in (0, 6, 2, 7, 8, 4, 9, 10):
        for f in z[k]:
            f()
    for i in range(ntiles):
        nxt = i + 1 < ntiles
        fill = Filler(prologue_steps(P, C, HALO, i + 1, C.gpre) if nxt else [], 6)
        z = z_stages(i + 1) if nxt else {}
        for step in range(12):
            if step < 8:
                g_step(i, step)
                if step == 7 and i == ntiles - 1 and getattr(C, "after_last_g", None):
                    C.after_last_g()
                    C.after_last_g = None
            else:
                w_step(i, step - 8)
            fill.step(step)
            for f in z.get(step, []):
                f()
        fill.flush()


def build_program(cfg):
    nc = bass.Bass("TRN2", target_bir_lowering=False)
    dbg = cfg["debug"]
    phases = cfg["phases"]
    n_ext = cfg["n_ext_tiles"]
    n_own = cfg["n_own_tiles"]

    def din(name, shape):
        return nc.dram_tensor(name, list(shape), F32, kind="ExternalInput").ap()

    def dscr(name, shape, dt):
        if dbg:
            return nc.dram_tensor(name, list(shape), dt, kind="ExternalOutput").ap()
        return nc.dram_tensor(name, list(shape), dt).ap()

    I = {}
    I["xe"] = din("xe", (NEXT, D))
    for nm in ("w_gate1", "w_up1", "w_gate2", "w_up2"):
        I[nm] = din(nm, (D, DFF))
    for nm in ("w_down1", "w_down2"):
        I[nm] = din(nm, (DFF, D))
    I["w_in"] = din("w_in", (D, 5376))
    I["w_att"] = din("w_att", (256, D))
    I["w_sgu"] = din("w_sgu", (512, D))
    I["w_out"] = din("w_out", (D, D))
    for nm in ("g1pre", "g1post", "gmpre", "gmpost", "g2pre", "g2post", "gfin"):
        I[nm] = din(nm, (1, D))
    I["ln_g"] = din("ln_g", (1, 512))
    I["ln_b"] = din("ln_b", (1, 512))
    I["w_s"] = din("w_s", (4, 128, 128))
    I["b_s"] = din("b_s", (4, 128))
    I["rel_bias"] = din("rel_bias", (32, 12))
    I["ident"] = din("ident", (128, 128))
    I["antiid"] = din("antiid", (128, 128))
    I["onehot"] = din("onehot", (32, 3 * 129))
    I["vseg"] = din("vseg", (128, 24 * 9))
    y = nc.dram_tensor("y", [NOWN, D], F32, kind="ExternalOutput").ap()
    x1 = dscr("x1s", (NEXT, D), F32)
    qkv = dscr("qkvs", (NEXT, QKVW), BF16)
    att = dscr("atts", (NOWN, 3 * ATTW), F32)
    x2 = dscr("x2s", (NOWN, D), F32)
    tscr = nc.dram_tensor("tscr", [12, 512], F32).ap()

    P = Prog(nc)
    LAST_PROG[0] = P
    with ExitStack() as es:
        wreg = es.enter_context(nc.sbuf_tensor("wreg", [128, WREG_ELEMS], BF16))
        areg = es.enter_context(nc.sbuf_tensor("areg", [128, ARENA_ELEMS], BF16))
        identt = es.enter_context(nc.sbuf_tensor("identb", [128, 128], BF16))
        ps = [es.enter_context(nc.psum_tensor("ps%d" % i, [128, 512], F32)) for i in range(8)]
        sems = {e: es.enter_context(nc.semaphore("sem_" + e)) for e in ("pe", "act", "dve", "pool")}
        dsems = {}
        for q in ("sp", "pool"):
            for i in range(P.ndma):
                dsems[(q, i)] = es.enter_context(nc.semaphore("dsem_%s%d" % (q, i)))

        C = Ctx()
        C.nc = nc
        C.wreg = wreg[:, :]
        C.ident = identt[:, :]
        A = Arena(areg[:, :])
        _dma(P, "pool", C.ident, I["ident"], r=(), w=("ident",))
        epst = es.enter_context(nc.sbuf_tensor("epst", [128, 1], F32))
        C_EPS[0] = epst[:, :]
        P.add("pool", lambda h: h.memset(epst[:, :], EPS), r=(), w=("eps",))
        P.barrier()

        def ffn_setup(src, dst, gpre, gpost, gfin):
            A.reset()
            C.src, C.dst = src, dst
            C.nx = 2
            C.X = [A.alloc((D,), F32) for _ in range(C.nx)]
            C.Xr = A.alloc((D,), F32)
            C.xn = [A.alloc((D,), BF16) for _ in range(2)]
            C.hT = [A.alloc((KC, TT), BF16) for _ in range(2)]
            C.hid = A.alloc((NFC, TT), BF16)
            C.sg = [A.alloc((TT,), BF16) for _ in range(2)]
            C.t = A.alloc((D,), F32)
            C.junk = A.alloc((D,), BF16)
            C.ss = A.alloc((4,), F32)
            C.rs = A.alloc((4,), F32)
            C.ssy = A.alloc((8,), F32)
            C.gpre = A.alloc((D,), F32)
            C.gpost = A.alloc((D,), F32)
            load_bcast(P, C, C.gpre, gpre, D, "gains")
            load_bcast(P, C, C.gpost, gpost, D, "gains", scale=0.5)
            C.gfin = None
            if gfin is not None:
                C.gfin = A.alloc((D,), F32)
                load_bcast(P, C, C.gfin, gfin, D, "gains")
            C.psT = [ps[6][:, :].bitcast(BF16), ps[7][:, :].bitcast(BF16)]
            C.psrr = PsumRR([ps[i][:, :] for i in range(6)], [("ps", i) for i in range(6)])

        last_pe = None
        C.wqkv = C.wreg[:, 0:KC * 2304].rearrange("p (k n) -> p k n", k=KC)
        C.pref = set()

        def load_wqkv(ex):
            wv_ = I["w_in"].rearrange("(k p) n -> p k n", p=128)
            for hh in range(2):
                _dma(P, "pool", C.wqkv[:, hh * 4:(hh + 1) * 4, :], wv_[:, hh * 4:(hh + 1) * 4, 0:2304], r=(), w=("wqkv",), extra=ex)
            C.pref.add("wqkv")

        def load_b1_weights(ex):
            W = C.wreg
            C.wrest = W[:, 0:24576].rearrange("p (k n) -> p k n", k=KC)
            C.watt = W[:, 24576:26624].rearrange("p (k n) -> p k n", k=2)
            C.wsgu = W[:, 26624:30720].rearrange("p (k n) -> p k n", k=4)
            C.wout = W[:, 30720:38912].rearrange("p (k n) -> p k n", k=KC)
            wv_ = I["w_in"].rearrange("(k p) n -> p k n", p=128)
            for hh in range(2):
                _dma(P, "pool", C.wrest[:, hh * 4:(hh + 1) * 4, :], wv_[:, hh * 4:(hh + 1) * 4, 2304:5376], r=(), w=("wrest",), extra=ex)
            _dma(P, "pool", C.watt, I["w_att"].rearrange("(k p) n -> p k n", p=128), r=(), w=("watt",), extra=ex)
            _dma(P, "pool", C.wsgu, I["w_sgu"].rearrange("(k p) n -> p k n", p=128), r=(), w=("wsgu",), extra=ex)
            _dma(P, "pool", C.wout, I["w_out"].rearrange("(k p) n -> p k n", p=128), r=(), w=("wout",), extra=ex)
            C.pref.add("b1")

        if "A1" in phases:
            load_ffn_weights(P, C, I["w_gate1"], I["w_up1"], I["w_down1"], extra=())
            ffn_setup(I["xe"], x1, I["g1pre"], I["g1post"], None)
            if "A2" in phases:
                C.after_last_gateup = lambda: load_wqkv((P.last("pe"),))
            ffn_phase(P, C, n_ext, C.gpre, C.gpost, None)
            last_pe = P.last("pe")
            P.barrier()


        def common_setup(src, dst, gpre, gpost, gpost_scale):
            A.reset()
            C.A = A
            C.ps = ps
            C.src, C.dst = src, dst
            C.nx = 2
            C.X = [A.alloc((D,), F32) for _ in range(C.nx)]
            C.xn = [A.alloc((D,), BF16) for _ in range(2)]
            C.hT = [A.alloc((KC, TT), BF16) for _ in range(2)]
            C.junk = A.alloc((D,), BF16)
            C.ss = A.alloc((4,), F32)
            C.rs = A.alloc((4,), F32)
            C.ssy = A.alloc((8,), F32)
            C.gpre = A.alloc((D,), F32)
            load_bcast(P, C, C.gpre, gpre, D, "gains")
            if gpost is not None:
                C.gpost = A.alloc((D,), F32)
                load_bcast(P, C, C.gpost, gpost, D, "gains", scale=gpost_scale)
                C.Xr = A.alloc((D,), F32)
                C.t = A.alloc((D,), F32)
            C.psT = [ps[6][:, :].bitcast(BF16), ps[7][:, :].bitcast(BF16)]
            C.psrr = PsumRR([ps[i][:, :] for i in range(6)], [("ps", i) for i in range(6)])

        if "A2" in phases:
            ex = (last_pe,) if last_pe is not None else ()
            if "wqkv" not in C.pref:
                load_wqkv(ex)
            common_setup(x1, qkv, I["gmpre"], None, None)
            C.stage = [A.alloc((QKVW,), BF16) for _ in range(2)]
            for sb in range(2):
                sv = C.stage[sb][:, 1536:2316].rearrange("p (h e) -> p h e", e=65)
                P.add("pool", lambda h, o=sv[:, :, 64:65]: h.memset(o, 1.0), r=(), w=[("stage", sb, c0) for (c0, _, _) in QKV_BLOCKS])
            qkv_phase(P, C, n_ext, C.gpre)
            last_pe = P.last("pe")
            P.barrier()

        if "ATT" in phases:
            if "B1" in phases:
                load_b1_weights((last_pe,) if last_pe is not None else ())
            A.reset()
            C.A = A
            C.ps = ps
            AT = Arena(C.wreg[:, 38912:WREG_ELEMS])
            C.Kraw = [AT.alloc((9, 256), BF16) for _ in range(4)]
            C.Vt = [AT.alloc((9, 260), BF16) for _ in range(4)]
            C.Qraw = [AT.alloc((8, 256), BF16) for _ in range(4)]
            C.KT = [A.alloc((2, 9 * 128), BF16) for _ in range(2)]
            C.QT = [A.alloc((2, 8 * 128), BF16) for _ in range(2)]
            C.biasb = A.alloc((12, 256), BF16)
            C.Pb = [A.alloc((2, 256), BF16) for _ in range(4)]
            C.PTs = [A.alloc((2, 2, 128), BF16) for _ in range(4)]
            C.ostage = [A.alloc((ATTW,), F32) for _ in range(3)]
            C.vseg = A.alloc((24 * 9,), F32)
            C.negm = A.alloc((4, 2), F32)
            _dma(P, "sp", C.vseg, I["vseg"], r=(), w=("vseg",))
            build_bias(P, C, I, tscr)
            att_phase(P, C, qkv, att)
            last_pe = P.last("pe")
            P.barrier()

        if "B1" in phases:
            ex = (last_pe,) if last_pe is not None else ()
            W = C.wreg
            if "b1" not in C.pref:
                load_b1_weights(ex)
            if "B2" in phases:
                C.after_last_g = lambda: (load_ffn_weights(P, C, I["w_gate2"], I["w_up2"], I["w_down2"], extra=(P.last("pe"),), parts=("wg",)),
                                          C.pref.add("wg2"))
            common_setup(x1, x2, I["gmpre"], I["gmpost"], None)
            A2 = Arena(W[:, 38912:WREG_ELEMS])
            C.U = [A2.alloc((512,), BF16) for _ in range(4)]
            C.V = [A2.alloc((512,), F32) for _ in range(4)]
            C.st = A2.alloc((4, 4), F32)
            C.vln = [A2.alloc((512,), BF16) for _ in range(4)]
            C.sgu = [A2.alloc((512,), BF16) for _ in range(4)]
            C.sguT = A2.alloc((4, TT), BF16)
            C.attin = A2.alloc((4, 3, ATTW), F32)
            C.M = A2.alloc((4, 4), F32)
            C.dd = A2.alloc((4, 3, 4), F32)
            C.ww = A2.alloc((4, 3, 4), F32)
            C.wd_ = A2.alloc((4, 3, 4), F32)
            C.cc = A2.alloc((4, 3, 4), F32)
            C.den = A2.alloc((4, 4), F32)
            C.tmpO = [A2.alloc((3, 4, 64), F32) for _ in range(2)]
            C.attb = [A2.alloc((4, 64), BF16) for _ in range(4)]
            C.attT = A2.alloc((2, TT), BF16)
            C.sa = [A.alloc((TT,), F32) for _ in range(2)]
            C.sb_ = [A.alloc((TT,), F32) for _ in range(2)]
            C.mT = A.alloc((KC, TT), BF16)
            C.lng = A.alloc((512,), F32)
            C.lnb = A.alloc((512,), F32)
            load_bcast(P, C, C.lng, I["ln_g"], 512, "gains")
            load_bcast(P, C, C.lnb, I["ln_b"], 512, "gains")
            C.WsT = A.alloc((4, 128), BF16)
            C.bs = A.alloc((4,), F32)
            Wn = A2.alloc((4, 128), F32)
            Wnb = A2.alloc((4, 128), BF16)
            _dma(P, "sp", Wn, I["w_s"].rearrange("g p q -> p g q"), r=(), w=("Wn",))
            _dma(P, "sp", C.bs, I["b_s"].rearrange("g p -> p g"), r=(), w=("bs",), slow=True)
            P.add("dve", lambda h: h.tensor_copy(Wnb, Wn), r=("Wn",), w=("Wnb",))
            for gg in range(4):
                _tr(P, C.psT[0][:, gg * 128:(gg + 1) * 128], Wnb[:, gg, :], C.ident, r=("Wnb", "ident"), w=(("psT", 0),))
            _act(P, C.WsT, C.psT[0][:, 0:512].rearrange("p (g q) -> p g q", g=4), AF.Copy, r=(("psT", 0),), w=("WsT",))
            b1_phase(P, C, n_own, att)
            last_pe = P.last("pe")
            P.barrier()

        if "B2" in phases:
            load_ffn_weights(P, C, I["w_gate2"], I["w_up2"], I["w_down2"], extra=(last_pe,) if last_pe is not None else (),
                             parts=("wu", "wd") if "wg2" in C.pref else ("wg", "wu", "wd"))
            ffn_setup(x2, y, I["g2pre"], I["g2post"], I["gfin"])
            ffn_phase(P, C, n_own, C.gpre, C.gpost, C.gfin)
            P.barrier()

        with nc.Block() as block:
            P.emit(block, sems, dsems)
    return nc


def _t5_bucket(rel):
    nb = 16
    max_exact = 8
    ret = (rel > 0).astype(np.int32) * nb
    n = np.abs(rel)
    large = max_exact + (np.log(np.maximum(n, 1) / max_exact) / np.log(1024 / max_exact) * (nb - max_exact)).astype(np.int32)
    large = np.minimum(large, nb - 1)
    return ret + np.where(n < max_exact, n, large)


def segments():
    segs = []
    for g, (window, d) in enumerate(GROUPS):
        i_first = HALO // d
        nown = NOWN // d
        nblk_total = nown // 128
        per = 8 if nblk_total >= 8 else nblk_total
        for r in range(d):
            for s0 in range(0, nblk_total, per):
                segs.append((g, d, r, i_first + 128 * s0, per))
    return segs


def host_constants(seq_lo, seq_hi):
    ident = np.eye(128, dtype=np.float32)
    anti = np.ascontiguousarray(ident[::-1])
    onehot = np.zeros((32, 3, 129), np.float32)
    for g, (window, d) in enumerate(GROUPS):
        half = (window // 2) // d
        off = np.arange(-half, half + 1, dtype=np.int32) * d
        bk = _t5_bucket(off)
        onehot[bk, g, np.arange(129)] = 1.0
    segs = segments()
    vseg = np.zeros((128, 24, 9), np.float32)
    for si, (g, d, r, i0, nblk) in enumerate(segs):
        for j in range(nblk + 1):
            sub = i0 - 64 + 128 * j + np.arange(128)
            pos = d * sub + r
            vseg[:, si, j] = ((pos >= seq_lo) & (pos < seq_hi)).astype(np.float32)
    return ident, anti, onehot.reshape(32, 3 * 129), vseg.reshape(128, 24 * 9)


def make_in_maps(inp):
    f = lambda a: np.ascontiguousarray(np.asarray(a, dtype=np.float32))
    xp = f(inp["x_prompt"])[0]
    xs = f(inp["x_sample"])
    shared = {
        "w_gate1": f(inp["ffn1_w_gate"])[0], "w_up1": f(inp["ffn1_w_up"])[0], "w_down1": f(inp["ffn1_w_down"])[0],
        "w_gate2": f(inp["ffn2_w_gate"])[0], "w_up2": f(inp["ffn2_w_up"])[0], "w_down2": f(inp["ffn2_w_down"])[0],
        "w_in": f(inp["w_in"])[0], "w_att": f(inp["w_att"])[0], "w_sgu": f(inp["w_sgu"])[0], "w_out": f(inp["w_out"])[0],
        "g1pre": f(inp["ffn1_pre_g"]), "g1post": f(inp["ffn1_post_g"]), "gmpre": f(inp["mix_pre_g"]),
        "gmpost": f(inp["mix_post_g"]), "g2pre": f(inp["ffn2_pre_g"]), "g2post": f(inp["ffn2_post_g"]),
        "gfin": f(inp["final_g"]), "ln_g": f(inp["sgu_ln_g"]), "ln_b": f(inp["sgu_ln_b"]),
        "w_s": f(inp["sgu_w_s"])[0], "b_s": f(inp["sgu_b_s"])[0], "rel_bias": f(inp["rel_bias"]),
    }
    maps = []
    for c in range(NCORES):
        xe = np.zeros((NEXT, D), np.float32)
        if c < 4:
            lo = c * NOWN - HALO
            hi = lo + NEXT
            a, b = max(lo, 0), min(hi, xp.shape[0])
            xe[a - lo:b - lo] = xp[a:b]
            seq_lo, seq_hi = a - lo, b - lo
        else:
            xe[HALO:HALO + NOWN] = xs[c - 4]
            seq_lo, seq_hi = HALO, HALO + NOWN
        ident, anti, onehot, vseg = host_constants(seq_lo, seq_hi)
        m = dict(shared)
        m.update({"xe": xe, "ident": ident, "antiid": anti, "onehot": onehot, "vseg": vseg})
        maps.append(m)
    return maps


_NC_CACHE = {}


def kernel(**inputs):
    key = "full"
    if key not in _NC_CACHE:
        _NC_CACHE[key] = build_program(CFG)
    nc = _NC_CACHE[key]
    maps = make_in_maps(inputs)
    res = run_bass_kernel_spmd(nc, maps, core_ids=list(range(NCORES)))
    ys = [np.asarray(r["y"], dtype=np.float32) for r in res.results]
    y_prompt = np.concatenate(ys[0:4], axis=0)[None]
    y_sample = np.stack(ys[4:8], axis=0)
    return (y_prompt, y_sample)
```

```python
import numpy as np
import concourse.bass as bass
import concourse.mybir as mybir
from concourse.bass_utils import run_bass_kernel_spmd
from contextlib import ExitStack

F32 = mybir.dt.float32
BF16 = mybir.dt.bfloat16
AF = mybir.ActivationFunctionType
ALU = mybir.AluOpType
AX = mybir.AxisListType

NCORES = 8
D = 1024
DFF = 2816
NFC = 22
KC = 8
NEXT = 6144
NOWN = 4096
HALO = 1024
TT = 512
QKVW = 2316
ATTW = 264
EPS = 1e-6
NEG = -1e30
GROUPS = ((128, 1), (512, 4), (2048, 16))
WREG_ELEMS = 67584
ARENA_ELEMS = 38400

CFG = {"phases": ("A1", "A2", "ATT", "B1", "B2"), "debug": False, "n_ext_tiles": 12, "n_own_tiles": 8}


class Op:
    __slots__ = ("id", "eng", "fn", "deps", "dma", "dsem", "dval", "seq", "sig")


LAST_PROG = [None]
STAGE_LOG = []


class Prog:
    ENGS = ("pe", "act", "dve", "pool", "sp")

    def __init__(self, nc, ndma=20):
        self.nc = nc
        self.ops = []
        self.byeng = {e: [] for e in self.ENGS}
        self.lastw = {}
        self.readers = {}
        self.psreaders = {}
        self.ndma = ndma
        self.dma_rr = {"sp": 0, "pool": 0, "act": 0}
        self.dma_cnt = {}
        self.dma_last = {}
        self.pending = {e: set() for e in self.ENGS}

    def add(self, eng, fn, r=(), w=(), dma=False, extra=()):
        op = Op()
        op.id = len(self.ops)
        op.eng = eng
        op.fn = fn
        op.dma = dma
        op.sig = False
        op.seq = None
        deps = set(extra)
        psr = [k for k in r if isinstance(k, tuple) and k[0] in ("ps", "psT")]
        if psr:
            r = [k for k in r if k not in psr]
            for k in psr:
                lw = self.lastw.get(k)
                if lw is not None:
                    deps.add(lw)
                for (rid, reng) in self.psreaders.get(k, ()):
                    if reng != eng:
                        deps.add(rid)
        for k in w:
            if k in self.psreaders:
                deps.update(rid for (rid, _) in self.psreaders[k])
        for k in r:
            lw = self.lastw.get(k)
            if lw is not None:
                deps.add(lw)
        for k in w:
            lw = self.lastw.get(k)
            if lw is not None:
                deps.add(lw)
            deps.update(self.readers.get(k, ()))
        if self.pending[eng]:
            deps.update(self.pending[eng])
            self.pending[eng] = set()
        if dma:
            i = self.dma_rr[eng]
            self.dma_rr[eng] = (i + 1) % self.ndma
            key = (eng, i)
            prev = self.dma_last.get(key)
            if prev is not None:
                deps.add(prev)
            self.dma_cnt[key] = self.dma_cnt.get(key, 0) + 1
            self.dma_last[key] = op.id
            op.dsem = key
            op.dval = 16 * self.dma_cnt[key]
        deps.discard(op.id)
        op.deps = deps
        for k in r:
            self.readers.setdefault(k, []).append(op.id)
        for k in psr:
            self.psreaders.setdefault(k, []).append((op.id, eng))
        for k in w:
            self.lastw[k] = op.id
            self.readers[k] = []
            if k in self.psreaders:
                self.psreaders[k] = []
        self.ops.append(op)
        self.byeng[eng].append(op)
        return op.id

    def last(self, eng):
        return self.byeng[eng][-1].id if self.byeng[eng] else None

    def barrier(self):
        ids = set()
        for e in self.ENGS:
            if self.byeng[e]:
                ids.add(self.byeng[e][-1].id)
        for key, oid in self.dma_last.items():
            ids.add(oid)
        for e in self.ENGS:
            self.pending[e] |= ids
        self.lastw = {}
        self.readers = {}
        self.psreaders = {}

    def emit(self, block, sems, dsems):
        ops = self.ops
        for op in ops:
            for d in op.deps:
                dop = ops[d]
                if dop.dma:
                    continue
                if dop.eng == op.eng == "pe" and not op.dma:
                    continue
                dop.sig = True
        for e in self.ENGS:
            n = 0
            for op in self.byeng[e]:
                if op.sig and not op.dma:
                    n += 1
                    op.seq = n
        finals = {}
        for key, cnt in self.dma_cnt.items():
            finals[key] = 16 * cnt

        def run(eng, h):
            waited = {}
            for op in self.byeng[eng]:
                need = {}
                for d in op.deps:
                    dop = ops[d]
                    if dop.dma:
                        s, v = ("d",) + dop.dsem, dop.dval
                    else:
                        if dop.eng == eng == "pe" and not op.dma:
                            continue
                        s, v = ("e", dop.eng), dop.seq
                    if v > need.get(s, 0):
                        need[s] = v
                for s, v in need.items():
                    if waited.get(s, 0) >= v:
                        continue
                    waited[s] = v
                    sem = sems[s[1]] if s[0] == "e" else dsems[(s[1], s[2])]
                    h.wait_ge(sem, v)
                ins = op.fn(h)
                if op.dma:
                    ins.then_inc(dsems[op.dsem], 16)
                elif op.sig:
                    ins.then_inc(sems[eng], 1)
            if eng == "sp":
                for key, v in finals.items():
                    if waited.get(("d",) + key, 0) < v:
                        h.wait_ge(dsems[key], v)
                for e2 in ("pe", "act", "dve", "pool"):
                    n = max([o.seq or 0 for o in self.byeng[e2]] + [0])
                    if n:
                        h.wait_ge(sems[e2], n)

        @block.tensor
        def _(h):
            run("pe", h)

        @block.scalar
        def _(h):
            run("act", h)

        @block.vector
        def _(h):
            run("dve", h)

        @block.gpsimd
        def _(h):
            run("pool", h)

        @block.sync
        def _(h):
            run("sp", h)


class Arena:
    def __init__(self, base):
        self.base = base
        self.n = base.shape[1]
        self.off = 0

    def reset(self, off=0):
        self.off = off

    def alloc(self, free_shape, dtype, parts=128):
        n = int(np.prod(free_shape))
        ne = n * 2 if dtype == F32 else n
        self.off = (self.off + 15) // 16 * 16
        assert self.off + ne <= self.n, ("arena overflow", self.off, ne, self.n)
        ap = self.base[0:parts, self.off:self.off + ne]
        self.off += ne
        if dtype == F32:
            ap = ap.bitcast(F32)
        if len(free_shape) == 2:
            ap = ap.rearrange("p (a b) -> p a b", a=free_shape[0])
        elif len(free_shape) == 3:
            ap = ap.rearrange("p (a b c) -> p a b c", a=free_shape[0], b=free_shape[1])
        return ap


class Ctx:
    pass


def _mm(P, out, lhsT, rhs, start, stop, r, w):
    P.add("pe", lambda h, o=out, a=lhsT, b=rhs, s=start, t=stop: h.matmul(o, a, b, start=s, stop=t), r=r, w=w)


def _tr(P, out, in_, ident, r, w):
    P.add("pe", lambda h, o=out, a=in_, i=ident: h.transpose(o, a, i), r=r, w=w)


def _act(P, out, in_, func, r, w, bias=None, scale=None, accum=None):
    def fn(h, o=out, a=in_, f=func, b=bias, s=scale, acc=accum):
        kw = {}
        if b is not None:
            kw["bias"] = b
        if s is not None:
            kw["scale"] = s
        if acc is not None:
            kw["accum_out"] = acc
        return h.activation(o, a, f, **kw)
    P.add("act", fn, r=r, w=w)


def _dma(P, q, out, in_, r, w, extra=(), slow=False):
    if slow:
        return P.add(q, lambda h, o=out, a=in_: h.dma_start(out=o, in_=a, allow_slow_non_contiguous=True), r=r, w=w, dma=True, extra=extra)
    return P.add(q, lambda h, o=out, a=in_: h.dma_start(out=o, in_=a), r=r, w=w, dma=True, extra=extra)


class PsumRR:
    def __init__(self, banks, keys):
        self.banks = banks
        self.keys = keys
        self.i = 0

    def next(self):
        b, k = self.banks[self.i], self.keys[self.i]
        self.i = (self.i + 1) % len(self.banks)
        return b, k


def load_bcast(P, C, dst, vec_ap, n, key, scale=None):
    _dma(P, "sp", dst, vec_ap[0:1, :].to_broadcast([128, n]), r=(), w=(key,))
    if scale is not None:
        P.add("dve", lambda h, o=dst, s=scale: h.tensor_scalar_mul(o, o, float(s)), r=(key,), w=(key,))


def rms_rstd(P, ss, out, rk, wk, n=D):
    _act(P, ss, ss, AF.Sqrt, r=rk, w=rk, bias=C_EPS[0], scale=1.0 / n)
    P.add("dve", lambda h, o=out, a=ss: h.reciprocal(o, a), r=rk, w=wk)


C_EPS = [None]


def prologue_steps(P, C, src_rows, tile, gain_b):
    hb = tile % 2
    hT = C.hT[hb]

    def load(s):
        j = tile * 4 + s
        xs = j % C.nx
        row0 = src_rows + j * 128
        _dma(P, "sp", C.X[xs], C.src[row0:row0 + 128, :], r=(("src", j),), w=(("x", xs),))

    def comp_a(s):
        j = tile * 4 + s
        xs = j % C.nx
        X = C.X[xs]
        sc = j % 4
        _act(P, C.junk, X, AF.Square, r=(("x", xs),), w=(("ss", sc),), accum=C.ss[:, sc:sc + 1])
        rms_rstd(P, C.ss[:, sc:sc + 1], C.rs[:, sc:sc + 1], rk=(("ss", sc),), wk=(("rs", sc),))
        xb = j % 2
        P.add("dve", lambda h, o=C.xn[xb], a=X, sca=C.rs[:, sc:sc + 1], g=gain_b:
              h.scalar_tensor_tensor(o, a, sca, g, ALU.mult, ALU.mult),
              r=(("x", xs), ("rs", sc), "gains"), w=(("xn", xb),))

    def comp_b(s):
        j = tile * 4 + s
        xb = j % 2
        pb = j % 2
        pst = C.psT[pb]
        for kc in range(KC):
            _tr(P, pst[:, kc * 128:(kc + 1) * 128], C.xn[xb][:, kc * 128:(kc + 1) * 128], C.ident,
                r=(("xn", xb), "ident"), w=(("psT", pb),))
        _act(P, hT[:, :, s * 128:(s + 1) * 128], pst.rearrange("p (k t) -> p k t", k=KC), AF.Copy,
             r=(("psT", pb),), w=(("hT", hb, s),))

    L = lambda s: (lambda: load(s))
    Ca = lambda s: (lambda: comp_a(s))
    Cb = lambda s: (lambda: comp_b(s))
    return [L(0), L(1), Ca(0), L(2), Ca(1), Cb(0), L(3), Ca(2), Cb(1), Ca(3), Cb(2), Cb(3)]


def prologue(P, C, src_rows, tile, gain_b, tag):
    for f in prologue_steps(P, C, src_rows, tile, gain_b):
        f()


class Filler:
    def __init__(self, fns, nsteps, start=0):
        self.fns = list(fns)
        self.at = {}
        n = len(self.fns)
        for k in range(n):
            step = start + (k * max(nsteps - start, 1)) // n
            self.at.setdefault(step, []).append(self.fns[k])

    def step(self, k):
        for f in self.at.pop(k, []):
            f()

    def flush(self):
        for k in sorted(self.at):
            for f in self.at[k]:
                f()
        self.at = {}


def post_norm_residual(P, C, banks, gain_b, j, final_gain=None, dst_rows=None):
    X = C.Xr
    for n in range(2):
        b, k = banks[n]
        _act(P, C.junk[:, 0:512], b, AF.Square, r=(k,), w=(("ssy", n),), accum=C.ssy[:, n:n + 1])
        P.add("dve", lambda h, o=C.t[:, n * 512:(n + 1) * 512], a=b, g=gain_b[:, n * 512:(n + 1) * 512]:
              h.tensor_tensor(o, a, g, ALU.mult), r=(k, "gains"), w=(("t", n),))
    P.add("dve", lambda h, o=C.ssy[:, 2:3], a=C.ssy[:, 0:1], b2=C.ssy[:, 1:2]: h.tensor_tensor(o, a, b2, ALU.add),
          r=(("ssy", 0), ("ssy", 1)), w=(("ssy", 2),))
    rms_rstd(P, C.ssy[:, 2:3], C.ssy[:, 3:4], rk=(("ssy", 2),), wk=(("ssy", 3),))
    P.add("dve", lambda h, o=X, a=C.t, sca=C.ssy[:, 3:4], x=X: h.scalar_tensor_tensor(o, a, sca, x, ALU.mult, ALU.add),
          r=(("t", 0), ("t", 1), ("ssy", 3), "xr"), w=("xr",))
    outt = X
    outk = "xr"
    if final_gain is not None:
        _act(P, C.junk, X, AF.Square, r=("xr",), w=(("ssf", 0),), accum=C.ssy[:, 4:5])
        rms_rstd(P, C.ssy[:, 4:5], C.ssy[:, 5:6], rk=(("ssf", 0),), wk=(("ssf", 1),))
        P.add("dve", lambda h, o=C.t, a=X, sca=C.ssy[:, 5:6], g=final_gain: h.scalar_tensor_tensor(o, a, sca, g, ALU.mult, ALU.mult),
              r=("xr", ("ssf", 1), "gains", ("t", 0), ("t", 1)), w=(("t", 0), ("t", 1)))
        outt = C.t
        outk = ("t", 0)
    if CFG.get("stop", 99) < 6:
        return
    _dma(P, "sp", C.dst[dst_rows:dst_rows + 128, :], outt, r=(outk, ("t", 1)) if final_gain is not None else (outk,),
         w=(("dst", j),))


def load_ffn_weights(P, C, wg_d, wu_d, wd_d, extra, parts=("wg", "wu", "wd")):
    W = C.wreg
    C.wg = W[:, 0:KC * DFF].rearrange("p (k n) -> p k n", k=KC)
    C.wu = W[:, KC * DFF:2 * KC * DFF].rearrange("p (k n) -> p k n", k=KC)
    C.wd = W[:, 2 * KC * DFF:2 * KC * DFF + NFC * D].rearrange("p (c n) -> p c n", c=NFC)
    wgv = wg_d.rearrange("(k p) n -> p k n", p=128)
    wuv = wu_d.rearrange("(k p) n -> p k n", p=128)
    wdv = wd_d.rearrange("(c p) n -> p c n", p=128)
    NS = 4
    cs = DFF // NS
    for i in range(NS):
        if "wg" in parts:
            _dma(P, "pool", C.wg[:, :, i * cs:(i + 1) * cs], wgv[:, :, i * cs:(i + 1) * cs], r=(), w=(("wg", i),), extra=extra)
        if "wu" in parts:
            _dma(P, "pool", C.wu[:, :, i * cs:(i + 1) * cs], wuv[:, :, i * cs:(i + 1) * cs], r=(), w=(("wu", i),), extra=extra)
    if "wd" in parts:
        for i in range(2):
            _dma(P, "pool", C.wd[:, i * 11:(i + 1) * 11, :], wdv[:, i * 11:(i + 1) * 11, :], r=(), w=(("wd", i),), extra=extra)
    C.wslice = cs


def ffn_phase(P, C, ntiles, gain_pre, gain_post, gain_fin):
    prs = C.psrr
    stop = CFG.get("stop", 99)
    if stop < 1:
        return
    prologue(P, C, 0, 0, gain_pre, "f")
    if stop < 2:
        return
    for i in range(ntiles):
        hT = C.hT[i % 2]
        hk = [("hT", i % 2, s) for s in range(4)]
        fill = Filler(prologue_steps(P, C, 0, i + 1, gain_pre) if i + 1 < ntiles else [], NFC, start=2)
        for c in range(NFC):
            fill.step(c)
            wk = c * 128 // C.wslice
            bg, kg = prs.next()
            for kc in range(KC):
                _mm(P, bg, C.wg[:, kc, c * 128:(c + 1) * 128], hT[:, kc, :], kc == 0, kc == KC - 1,
                    r=hk + [("wg", wk)], w=(kg,))
            bu, ku = prs.next()
            for kc in range(KC):
                _mm(P, bu, C.wu[:, kc, c * 128:(c + 1) * 128], hT[:, kc, :], kc == 0, kc == KC - 1,
                    r=hk + [("wu", wk)], w=(ku,))
            sg = C.sg[c % 2]
            _act(P, sg, bg, AF.Silu, r=(kg,), w=(("sg", c % 2),))
            P.add("dve", lambda h, o=C.hid[:, c, :], a=sg, b=bu: h.tensor_tensor(o, a, b, ALU.mult),
                  r=(("sg", c % 2), ku), w=(("hid", c),))
        fill.flush()
        if i == ntiles - 1 and getattr(C, "after_last_gateup", None):
            C.after_last_gateup()
            C.after_last_gateup = None
        if stop < 3:
            return
        for s in range(4):
            j = i * 4 + s
            banks = []
            for n in range(2):
                b, k = prs.next()
                for c in range(NFC):
                    _mm(P, b, C.hid[:, c, s * 128:(s + 1) * 128], C.wd[:, c, n * 512:(n + 1) * 512], c == 0, c == NFC - 1,
                        r=(("hid", c), ("wd", c // 11)), w=(k,))
                banks.append((b, k))
            if stop < 4:
                continue
            _dma(P, "sp", C.Xr, C.src[j * 128:(j + 1) * 128, :], r=(("src", j),), w=("xr",))
            if stop < 5:
                continue
            post_norm_residual(P, C, banks, gain_post, j, final_gain=gain_fin, dst_rows=j * 128)


QKV_BLOCKS = ((0, 512, "q"), (512, 768, "q"), (768, 1280, "k"), (1280, 1536, "k"), (1536, 2048, "v"), (2048, 2304, "v"))


def qkv_blocks(tile, ntiles):
    if ntiles != 12:
        return QKV_BLOCKS
    if 2 <= tile <= 9:
        return QKV_BLOCKS
    if tile in (1, 10):
        return QKV_BLOCKS[2:]
    return (QKV_BLOCKS[3], QKV_BLOCKS[5])


def qkv_phase(P, C, ntiles, gain_pre):
    prs = C.psrr
    prologue(P, C, 0, 0, gain_pre, "q")
    ev = 0
    for i in range(ntiles):
        hT = C.hT[i % 2]
        blocks = qkv_blocks(i, ntiles)
        fill = Filler(prologue_steps(P, C, 0, i + 1, gain_pre) if i + 1 < ntiles else [], 4 * len(blocks))
        step = 0
        for s in range(4):
            j = i * 4 + s
            sb = j % 2
            stage = C.stage[sb]
            stage_v = stage[:, 1536:2316].rearrange("p (h e) -> p h e", e=65)
            for (c0, c1, kind) in blocks:
                fill.step(step)
                step += 1
                b, k = prs.next()
                n = c1 - c0
                for kc in range(KC):
                    _mm(P, b[:, 0:n], hT[:, kc, s * 128:(s + 1) * 128], C.wqkv[:, kc, c0:c1], kc == 0, kc == KC - 1,
                        r=(("hT", i % 2, s), "wqkv"), w=(k,))
                if kind == "v":
                    h0 = (c0 - 1536) // 64
                    nh = n // 64
                    out = stage_v[:, h0:h0 + nh, 0:64]
                    src = b[:, 0:n].rearrange("p (h e) -> p h e", e=64)
                else:
                    out = stage[:, c0:c1]
                    src = b[:, 0:n]
                scale = 0.125 if kind == "q" else 1.0
                if ev % 2 == 0:
                    _act(P, out, src, AF.Copy, r=(k,), w=(("stage", sb, c0),), scale=scale)
                else:
                    P.add("dve", lambda h, o=out, a=src, sc=scale: h.tensor_scalar_mul(o, a, float(sc)),
                          r=(k,), w=(("stage", sb, c0),))
                ev += 1
            _dma(P, "sp", C.dst[j * 128:(j + 1) * 128, :], stage, r=[("stage", sb, c0) for (c0, _, _) in QKV_BLOCKS],
                 w=(("dst", j),))
        fill.flush()


def build_bias(P, C, I, tscr):
    A = C.A
    rb = A.alloc((12,), F32, parts=32)
    E = A.alloc((3 * 129,), F32, parts=32)
    Tsb = A.alloc((3, 512), F32, parts=12)
    anti = A.alloc((128,), F32)
    R = [A.alloc((256,), F32) for _ in range(2)]
    _dma(P, "sp", rb, I["rel_bias"], r=(), w=("rb",))
    _dma(P, "sp", E, I["onehot"], r=(), w=("E",))
    _dma(P, "sp", anti, I["antiid"], r=(), w=("anti",))
    P.add("pool", lambda h: h.memset(Tsb, NEG), r=(), w=("Tsb",))
    b5 = C.ps[5][:, :]
    for g in range(3):
        _mm(P, b5[0:12, 0:129], rb[0:32, 0:12], E[0:32, g * 129:(g + 1) * 129], True, True, r=("rb", "E"), w=(("ps", 5),))
        P.add("dve", lambda h, o=Tsb[:, g, 127:256], a=b5[0:12, 0:129]: h.tensor_copy(o, a), r=(("ps", 5), "Tsb"), w=("Tsb",))
    for g in range(3):
        _dma(P, "sp", tscr[4 * g:4 * g + 4, :], Tsb[4 * g:4 * g + 4, g, :], r=("Tsb",), w=(("tscr", g),))
    for gh in range(12):
        Rt = R[gh % 2]
        src = bass.AP(tscr.tensor, gh * 512, [[1, 128], [1, 256]])
        _dma(P, "sp", Rt, src, r=(("tscr", gh // 4),), w=(("R", gh % 2),))
        b3, k3 = C.ps[3 + gh % 2][:, :], ("ps", 3 + gh % 2)
        _mm(P, b3[:, 0:256], anti, Rt, True, True, r=(("R", gh % 2), "anti"), w=(k3,))
        P.add("dve", lambda h, o=C.biasb[:, gh, :], a=b3[:, 0:256]: h.tensor_copy(o, a), r=(k3,), w=("biasb",))


def att_phase(P, C, qkv, att):
    segs = segments()
    psPT = [C.ps[4][:, :].bitcast(BF16), C.ps[5][:, :].bitcast(BF16)]
    Sb = [(C.ps[i][:, :], ("ps", i)) for i in range(4)]
    Ob = [(C.ps[6 + i][:, 0:260], ("ps", 6 + i)) for i in range(2)]
    psKQ = [C.ps[6][:, :].bitcast(BF16), C.ps[7][:, :].bitcast(BF16)]
    qt = qkv.tensor
    tcount = [0]
    units = []
    first_unit = {}
    bc = 0
    for si, (g, d, r, i0, nblk) in enumerate(segs):
        first_unit[si] = len(units)
        for b in range(nblk):
            for c in range(2):
                units.append((si, b, c, bc + b))
        bc += nblk
    N = len(units)

    def loads(si):
        g, d, r, i0, nblk = segs[si]
        rb = si % 4
        nt = nblk + 1
        rowk = d * (i0 - 64) + r
        rowq = d * i0 + r
        _dma(P, "sp", C.Kraw[rb][:, 0:nt, :], bass.AP(qt, rowk * QKVW + 768 + 256 * g, [[d * QKVW, 128], [128 * d * QKVW, nt], [1, 256]]),
             r=(), w=(("Kraw", rb),))
        _dma(P, "sp", C.Vt[rb][:, 0:nt, :], bass.AP(qt, rowk * QKVW + 1536 + 260 * g, [[d * QKVW, 128], [128 * d * QKVW, nt], [1, 260]]),
             r=(), w=(("Vt", rb),))
        _dma(P, "sp", C.Qraw[rb][:, 0:nblk, :], bass.AP(qt, rowq * QKVW + 256 * g, [[d * QKVW, 128], [128 * d * QKVW, nblk], [1, 256]]),
             r=(), w=(("Qraw", rb),))

    def mask_v(si):
        g, d, r, i0, nblk = segs[si]
        rb = si % 4
        nt = nblk + 1
        ones = C.Vt[rb][:, 0:nt, :].rearrange("p t (h e) -> p t h e", e=65)[:, :, :, 64]
        P.add("pool", lambda h, o=ones, v=C.vseg[:, si * 9:si * 9 + nt].unsqueeze(2).to_broadcast([128, nt, 4]): h.tensor_tensor(o, o, v, ALU.mult),
              r=(("Vt", rb), "vseg"), w=(("Vt", rb),))

    def transposes(si, kqb):
        g, d, r, i0, nblk = segs[si]
        rb, tb2 = si % 4, si % 2
        nt = nblk + 1
        for (raw, rk, dstT, dk, ntile) in ((C.Kraw[rb], ("Kraw", rb), C.KT[tb2], ("KT", tb2), nt),
                                          (C.Qraw[rb], ("Qraw", rb), C.QT[tb2], ("QT", tb2), nblk)):
            for j0 in range(0, ntile, 4):
                nj = min(4, ntile - j0)
                pk, pkk = psKQ[kqb], ("ps", 6 + kqb)
                for c in range(2):
                    for jj in range(nj):
                        _tr(P, pk[:, c * 512 + jj * 128: c * 512 + (jj + 1) * 128], raw[:, j0 + jj, c * 128:(c + 1) * 128], C.ident,
                            r=(rk, "ident"), w=(pkk,))
                src = pk.rearrange("p (c t) -> p c t", c=2)[:, :, 0:nj * 128]
                if dk[0] == "KT":
                    P.add("dve", lambda h, o=dstT[:, :, j0 * 128:(j0 + nj) * 128], a=src: h.tensor_copy(o, a), r=(pkk,), w=(dk,))
                else:
                    _act(P, dstT[:, :, j0 * 128:(j0 + nj) * 128], src, AF.Copy, r=(pkk,), w=(dk,))

    def ctx(u):
        si, b, c, gb = units[u]
        g, d, r, i0, nblk = segs[si]
        return si, b, c, gb, g, d, r, i0

    def stage_S(u):
        si, b, c, gb, g, d, r, i0 = ctx(u)
        QT, KT = C.QT[si % 2], C.KT[si % 2]
        sbk, skey = Sb[u % 4]
        hb0 = 4 * g + 2 * c
        for hp in range(2):
            _mm(P, sbk[:, hp * 256:(hp + 1) * 256], QT[hp * 64:(hp + 1) * 64, c, b * 128:(b + 1) * 128],
                KT[hp * 64:(hp + 1) * 64, c, b * 128:b * 128 + 256], True, False, r=(("QT", si % 2), ("KT", si % 2)), w=(skey,))
            _mm(P, sbk[:, hp * 256:(hp + 1) * 256], C.ident, C.biasb[:, hb0 + hp, :], False, True,
                r=("ident", "biasb"), w=(skey,))

    def stage_1(u):
        si, b, c, gb, g, d, r, i0 = ctx(u)
        sl = u % 4
        sbk, skey = Sb[u % 4]
        ost = C.ostage[gb % 3]
        ostk = ("ost", gb % 3)
        mcols = ost[:, 260 + 2 * c:262 + 2 * c]
        P.add("dve", lambda hh, o=mcols, a=sbk.rearrange("p (h k) -> p h k", h=2): hh.reduce_max(o, a, AX.X, negate=True),
              r=(skey,), w=(ostk + (c,),))
        for hp in range(2):
            _act(P, C.Pb[sl][:, hp, :], sbk[:, hp * 256:(hp + 1) * 256], AF.Exp, r=(skey, ostk + (c,)), w=(("Pb", sl, hp),),
                 bias=ost[:, 260 + 2 * c + hp:261 + 2 * c + hp])

    def stage_2(u):
        sl = u % 4
        pt, ptk = psPT[u % 2], ("ps", 4 + u % 2)
        for hp in range(2):
            for jj in range(2):
                _tr(P, pt[:, (hp * 2 + jj) * 128:(hp * 2 + jj + 1) * 128], C.Pb[sl][:, hp, jj * 128:(jj + 1) * 128], C.ident,
                    r=(("Pb", sl, hp), "ident"), w=(ptk,))

    def stage_3(u):
        si, b, c, gb, g, d, r, i0 = ctx(u)
        sl = u % 4
        pt, ptk = psPT[u % 2], ("ps", 4 + u % 2)
        PTs = C.PTs[sl]
        P.add("dve", lambda hh, o=PTs.rearrange("p h j q -> p (h j q)"), a=pt[:, 0:512]: hh.tensor_copy(o, a),
              r=(ptk,), w=(("PTs", sl, 0), ("PTs", sl, 1)))

    def stage_4(u):
        si, b, c, gb, g, d, r, i0 = ctx(u)
        sl = u % 4
        PTs = C.PTs[sl]
        Vt = C.Vt[si % 4]
        ob, okey = Ob[gb % 2]
        ost = C.ostage[gb % 3]
        ostk = ("ost", gb % 3)
        for hp in range(2):
            h = 2 * c + hp
            for jj in range(2):
                _mm(P, ob[:, h * 65:(h + 1) * 65], PTs[:, hp, jj, :], Vt[:, b + jj, h * 65:(h + 1) * 65], jj == 0, jj == 1,
                    r=(("PTs", sl, jj), ("Vt", si % 4)), w=(okey,))
        if c == 1:
            _act(P, ost[:, 0:260], ob, AF.Copy, r=(okey,), w=(ostk + (9,),))
            t0 = d * (i0 + 128 * b) + r - HALO
            dst = bass.AP(att.tensor, (t0 * 3 + g) * ATTW, [[d * 3 * ATTW, 128], [1, ATTW]])
            _dma(P, "pool", dst, ost, r=[ostk + (x,) for x in (0, 1, 9)], w=(("att", si, b),))

    nseg = len(segs)
    for s0 in range(4):
        loads(s0)
        mask_v(s0)
    transposes(0, 0)
    seg_started = set()
    for u in range(min(2, N)):
        stage_S(u)
    for t in range(N + 4):
        if t < N:
            si = units[t][0]
            if si not in seg_started and first_unit[si] == t:
                seg_started.add(si)
                if si + 1 < nseg:
                    transposes(si + 1, (units[t][3] - 1) % 2)
        n0 = len(P.ops)
        if t + 2 < N:
            stage_S(t + 2)
        n1 = len(P.ops)
        if t < N:
            stage_1(t)
        n2 = len(P.ops)
        if 0 <= t - 1 < N:
            stage_2(t - 1)
        n3 = len(P.ops)
        if 0 <= t - 2 < N:
            stage_3(t - 2)
        n4 = len(P.ops)
        if 0 <= t - 3 < N:
            stage_4(t - 3)
        STAGE_LOG.append((t, n0, n1, n2, n3, n4, len(P.ops)))
        if 0 <= t - 3 < N:
            si3 = units[t - 3][0]
            if t - 3 + 1 == N or units[t - 3 + 1][0] != si3:
                if si3 + 4 < nseg:
                    loads(si3 + 4)
                    mask_v(si3 + 4)


def b1_phase(P, C, ntiles, att):
    prs = C.psrr

    def z_stages(i):
        hT = C.hT[i % 2]
        hk = [("hT", i % 2, s) for s in range(4)]
        AI = C.attin
        mview = AI[:, :, :, 260:264]
        sview = AI[:, :, :, 0:260].rearrange("p s g (h e) -> p s g h e", e=65)

        def z0():
            _dma(P, "sp", AI, att[i * 512:(i + 1) * 512, :].rearrange("(s p) (g e) -> p s g e", p=128, g=3), r=(), w=("attin",))

        def z1():
            for s in range(4):
                bu, ku = prs.next()
                for kc in range(KC):
                    _mm(P, bu, hT[:, kc, s * 128:(s + 1) * 128], C.wrest[:, kc, 0:512], kc == 0, kc == KC - 1, r=(hk[s], "wrest"), w=(ku,))
                _act(P, C.U[s], bu, AF.Gelu, r=(ku,), w=(("U", s),))
                bv, kv = prs.next()
                for kc in range(KC):
                    _mm(P, bv, hT[:, kc, s * 128:(s + 1) * 128], C.wrest[:, kc, 512:1024], kc == 0, kc == KC - 1, r=(hk[s], "wrest"), w=(kv,))
                _act(P, C.V[s], bv, AF.Gelu, r=(kv,), w=(("V", s), ("st0", s)), accum=C.st[:, 0, s:s + 1])

        def z2():
            P.add("dve", lambda h, o=C.M, a=mview.rearrange("p s g h -> p s h g"): h.tensor_reduce(o, a, AX.X, ALU.min),
                  r=("attin",), w=("M",))
            P.add("dve", lambda h, o=C.dd, a=mview, m=C.M.unsqueeze(2).to_broadcast([128, 4, 3, 4]): h.tensor_tensor(o, m, a, ALU.subtract),
                  r=("attin", "M"), w=("dd",))
            _act(P, C.ww, C.dd, AF.Exp, r=("dd",), w=("ww",))
            for s in range(4):
                P.add("dve", lambda h, o=C.wd_[:, s], a=C.ww[:, s], sv=sview[:, s, :, :, 64]: h.tensor_tensor(o, a, sv, ALU.mult),
                      r=("ww", "attin"), w=(("wd", s),))
            P.add("dve", lambda h, o=C.den, a=C.wd_.rearrange("p s g h -> p s h g"): h.tensor_reduce(o, a, AX.X, ALU.add),
                  r=[("wd", s) for s in range(4)], w=("den",))
            P.add("dve", lambda h, o=C.den: h.reciprocal(o, o), r=("den",), w=("den",))
            P.add("dve", lambda h, o=C.cc, a=C.ww, m=C.den.unsqueeze(2).to_broadcast([128, 4, 3, 4]): h.tensor_tensor(o, a, m, ALU.mult),
                  r=("ww", "den"), w=("cc",))

        def z3():
            P.add("dve", lambda h, o=C.st[:, 1, :], a=C.st[:, 0, :]: h.tensor_scalar_mul(o, a, -1.0 / 512.0),
                  r=[("st0", s) for s in range(4)], w=("st1",))
            for s in range(4):
                _act(P, C.junk[:, 0:512], C.V[s], AF.Square, r=(("V", s), "st1"), w=(("st2", s),), bias=C.st[:, 1, s:s + 1],
                     accum=C.st[:, 2, s:s + 1])
            _act(P, C.st[:, 2, :], C.st[:, 2, :], AF.Sqrt, r=[("st2", s) for s in range(4)], w=[("st2", s) for s in range(4)],
                 bias=C_EPS[0], scale=1.0 / 512.0)
            P.add("dve", lambda h, o=C.st[:, 3, :], a=C.st[:, 2, :]: h.reciprocal(o, a), r=[("st2", s) for s in range(4)], w=("st3",))

        def z4():
            for s in range(4):
                V = C.V[s]
                P.add("dve", lambda h, o=V, s1=C.st[:, 1, s:s + 1], s2=C.st[:, 3, s:s + 1]: h.tensor_scalar(o, o, s1, s2, ALU.add, ALU.mult),
                      r=(("V", s), "st1", "st3", ("st2", s)), w=(("V", s),))
            for s in range(4):
                V = C.V[s]
                P.add("dve", lambda h, o=V, g=C.lng: h.tensor_tensor(o, o, g, ALU.mult), r=(("V", s), "gains"), w=(("V", s),))
            for s in range(4):
                P.add("dve", lambda h, o=C.vln[s], a=C.V[s], g=C.lnb: h.tensor_tensor(o, a, g, ALU.add), r=(("V", s), "gains"), w=(("vln", s),))

        def z5():
            for s in range(4):
                ts = s % 2
                oview = sview[:, s, :, :, 0:64]
                P.add("dve", lambda h, o=C.tmpO[ts], a=oview, m=C.cc[:, s].unsqueeze(3).to_broadcast([128, 3, 4, 64]): h.tensor_tensor(o, a, m, ALU.mult),
                      r=("attin", "cc"), w=(("tmpO", ts),))

                def _red(h, o=C.attb[s], a=C.tmpO[ts].rearrange("p g h e -> p h e g")):
                    with C.nc.allow_low_precision(reason="3-term sum on the fp32 ALU, rounded once to bf16 for the matmul"):
                        return h.tensor_reduce(o, a, AX.X, ALU.add)
                P.add("dve", _red, r=(("tmpO", ts),), w=(("attb", s),))

        def z6():
            for s in range(4):
                bm, km = prs.next()
                for gg in range(4):
                    _mm(P, bm[:, gg * 128:(gg + 1) * 128], C.WsT[:, gg, :], C.vln[s][:, gg * 128:(gg + 1) * 128], True, True,
                        r=(("vln", s), "WsT"), w=(km,))
                for gg in range(4):
                    P.add("dve", lambda h, o=C.sgu[s][:, gg * 128:(gg + 1) * 128], a=bm[:, gg * 128:(gg + 1) * 128], sc=C.bs[:, gg:gg + 1],
                          u=C.U[s][:, gg * 128:(gg + 1) * 128]: h.scalar_tensor_tensor(o, a, sc, u, ALU.add, ALU.mult),
                          r=(km, ("U", s), "bs"), w=(("sgu", s),))

        def z7():
            for s in range(4):
                pb = s % 2
                pst = C.psT[pb]
                for gg in range(4):
                    _tr(P, pst[:, gg * 128:(gg + 1) * 128], C.sgu[s][:, gg * 128:(gg + 1) * 128], C.ident, r=(("sgu", s), "ident"), w=(("psT", pb),))
                ab = C.attb[s].rearrange("p h e -> p (h e)")
                for cc_ in range(2):
                    _tr(P, pst[:, 512 + cc_ * 128:512 + (cc_ + 1) * 128], ab[:, cc_ * 128:(cc_ + 1) * 128], C.ident, r=(("attb", s), "ident"), w=(("psT", pb),))
                _act(P, C.sguT[:, :, s * 128:(s + 1) * 128], pst[:, 0:512].rearrange("p (k t) -> p k t", k=4), AF.Copy,
                     r=(("psT", pb),), w=(("sguT", s),))
                _act(P, C.attT[:, :, s * 128:(s + 1) * 128], pst[:, 512:768].rearrange("p (k t) -> p k t", k=2), AF.Copy,
                     r=(("psT", pb),), w=(("attT", s),))

        return {0: [z0], 2: [z2], 4: [z5], 6: [z1], 7: [z3], 8: [z4], 9: [z6], 10: [z7]}

    def g_step(i, oc):
        hT = C.hT[i % 2]
        hk = [("hT", i % 2, s) for s in range(4)]
        sguk = [("sguT", s) for s in range(4)]
        attk = [("attT", s) for s in range(4)]
        bga, kga = prs.next()
        for kc in range(KC):
            _mm(P, bga, C.wrest[:, kc, 1024 + oc * 128:1024 + (oc + 1) * 128], hT[:, kc, :], kc == 0, kc == KC - 1, r=hk + ["wrest"], w=(kga,))
        bgb, kgb = prs.next()
        for kc in range(KC):
            _mm(P, bgb, C.wrest[:, kc, 2048 + oc * 128:2048 + (oc + 1) * 128], hT[:, kc, :], kc == 0, kc == KC - 1, r=hk + ["wrest"], w=(kgb,))
        bpa, kpa = prs.next()
        for kc in range(2):
            _mm(P, bpa, C.watt[:, kc, oc * 128:(oc + 1) * 128], C.attT[:, kc, :], kc == 0, kc == 1, r=attk + ["watt"], w=(kpa,))
        bpb, kpb = prs.next()
        for kc in range(4):
            _mm(P, bpb, C.wsgu[:, kc, oc * 128:(oc + 1) * 128], C.sguT[:, kc, :], kc == 0, kc == 3, r=sguk + ["wsgu"], w=(kpb,))
        o2 = oc % 2
        _act(P, C.sa[o2], bga, AF.Sigmoid, r=(kga,), w=(("sa", o2),))
        _act(P, C.sb_[o2], bgb, AF.Sigmoid, r=(kgb,), w=(("sb", o2),))
        P.add("dve", lambda h, o=C.sa[o2], b=bpa: h.tensor_tensor(o, o, b, ALU.mult), r=(("sa", o2), kpa), w=(("sa", o2),))
        P.add("dve", lambda h, o=C.sb_[o2], b=bpb: h.tensor_tensor(o, o, b, ALU.mult), r=(("sb", o2), kpb), w=(("sb", o2),))
        P.add("dve", lambda h, o=C.mT[:, oc, :], a=C.sa[o2], b=C.sb_[o2]: h.tensor_tensor(o, a, b, ALU.add),
              r=(("sa", o2), ("sb", o2)), w=(("mT", oc),))

    def w_step(i, s):
        mk = [("mT", oc) for oc in range(8)]
        j = i * 4 + s
        banks = []
        for n in range(2):
            b, k = prs.next()
            for kc in range(KC):
                _mm(P, b, C.mT[:, kc, s * 128:(s + 1) * 128], C.wout[:, kc, n * 512:(n + 1) * 512], kc == 0, kc == KC - 1,
                    r=mk + ["wout"], w=(k,))
            banks.append((b, k))
        _dma(P, "sp", C.Xr, C.src[HALO + j * 128:HALO + (j + 1) * 128, :], r=(), w=("xr",))
        post_norm_residual(P, C, banks, C.gpost, j, final_gain=None, dst_rows=j * 128)

    prologue(P, C, HALO, 0, C.gpre, "m")
    z = z_stages(0)
    for k in (0, 6, 2, 7, 8, 4, 9, 10):
        for f in z[k]:
            f()
    for i in range(ntiles):
        nxt = i + 1 < ntiles
        fill = Filler(prologue_steps(P, C, HALO, i + 1, C.gpre) if nxt else [], 6)
        z = z_stages(i + 1) if nxt else {}
        for step in range(12):
            if step < 8:
                g_step(i, step)
                if step == 7 and i == ntiles - 1 and getattr(C, "after_last_g", None):
                    C.after_last_g()
                    C.after_last_g = None
            else:
                w_step(i, step - 8)
            fill.step(step)
            for f in z.get(step, []):
                f()
        fill.flush()


def build_program(cfg):
    nc = bass.Bass("TRN2", target_bir_lowering=False)
    dbg = cfg["debug"]
    phases = cfg["phases"]
    n_ext = cfg["n_ext_tiles"]
    n_own = cfg["n_own_tiles"]

    def din(name, shape):
        return nc.dram_tensor(name, list(shape), F32, kind="ExternalInput").ap()

    def dscr(name, shape, dt):
        if dbg:
            return nc.dram_tensor(name, list(shape), dt, kind="ExternalOutput").ap()
        return nc.dram_tensor(name, list(shape), dt).ap()

    I = {}
    I["xe"] = din("xe", (NEXT, D))
    for nm in ("w_gate1", "w_up1", "w_gate2", "w_up2"):
        I[nm] = din(nm, (D, DFF))
    for nm in ("w_down1", "w_down2"):
        I[nm] = din(nm, (DFF, D))
    I["w_in"] = din("w_in", (D, 5376))
    I["w_att"] = din("w_att", (256, D))
    I["w_sgu"] = din("w_sgu", (512, D))
    I["w_out"] = din("w_out", (D, D))
    for nm in ("g1pre", "g1post", "gmpre", "gmpost", "g2pre", "g2post", "gfin"):
        I[nm] = din(nm, (1, D))
    I["ln_g"] = din("ln_g", (1, 512))
    I["ln_b"] = din("ln_b", (1, 512))
    I["w_s"] = din("w_s", (4, 128, 128))
    I["b_s"] = din("b_s", (4, 128))
    I["rel_bias"] = din("rel_bias", (32, 12))
    I["ident"] = din("ident", (128, 128))
    I["antiid"] = din("antiid", (128, 128))
    I["onehot"] = din("onehot", (32, 3 * 129))
    I["vseg"] = din("vseg", (128, 24 * 9))
    y = nc.dram_tensor("y", [NOWN, D], F32, kind="ExternalOutput").ap()
    x1 = dscr("x1s", (NEXT, D), F32)
    qkv = dscr("qkvs", (NEXT, QKVW), BF16)
    att = dscr("atts", (NOWN, 3 * ATTW), F32)
    x2 = dscr("x2s", (NOWN, D), F32)
    tscr = nc.dram_tensor("tscr", [12, 512], F32).ap()

    P = Prog(nc)
    LAST_PROG[0] = P
    with ExitStack() as es:
        wreg = es.enter_context(nc.sbuf_tensor("wreg", [128, WREG_ELEMS], BF16))
        areg = es.enter_context(nc.sbuf_tensor("areg", [128, ARENA_ELEMS], BF16))
        identt = es.enter_context(nc.sbuf_tensor("identb", [128, 128], BF16))
        ps = [es.enter_context(nc.psum_tensor("ps%d" % i, [128, 512], F32)) for i in range(8)]
        sems = {e: es.enter_context(nc.semaphore("sem_" + e)) for e in ("pe", "act", "dve", "pool")}
        dsems = {}
        for q in ("sp", "pool"):
            for i in range(P.ndma):
                dsems[(q, i)] = es.enter_context(nc.semaphore("dsem_%s%d" % (q, i)))

        C = Ctx()
        C.nc = nc
        C.wreg = wreg[:, :]
        C.ident = identt[:, :]
        A = Arena(areg[:, :])
        _dma(P, "pool", C.ident, I["ident"], r=(), w=("ident",))
        epst = es.enter_context(nc.sbuf_tensor("epst", [128, 1], F32))
        C_EPS[0] = epst[:, :]
        P.add("pool", lambda h: h.memset(epst[:, :], EPS), r=(), w=("eps",))
        P.barrier()

        def ffn_setup(src, dst, gpre, gpost, gfin):
            A.reset()
            C.src, C.dst = src, dst
            C.nx = 2
            C.X = [A.alloc((D,), F32) for _ in range(C.nx)]
            C.Xr = A.alloc((D,), F32)
            C.xn = [A.alloc((D,), BF16) for _ in range(2)]
            C.hT = [A.alloc((KC, TT), BF16) for _ in range(2)]
            C.hid = A.alloc((NFC, TT), BF16)
            C.sg = [A.alloc((TT,), BF16) for _ in range(2)]
            C.t = A.alloc((D,), F32)
            C.junk = A.alloc((D,), BF16)
            C.ss = A.alloc((4,), F32)
            C.rs = A.alloc((4,), F32)
            C.ssy = A.alloc((8,), F32)
            C.gpre = A.alloc((D,), F32)
            C.gpost = A.alloc((D,), F32)
            load_bcast(P, C, C.gpre, gpre, D, "gains")
            load_bcast(P, C, C.gpost, gpost, D, "gains", scale=0.5)
            C.gfin = None
            if gfin is not None:
                C.gfin = A.alloc((D,), F32)
                load_bcast(P, C, C.gfin, gfin, D, "gains")
            C.psT = [ps[6][:, :].bitcast(BF16), ps[7][:, :].bitcast(BF16)]
            C.psrr = PsumRR([ps[i][:, :] for i in range(6)], [("ps", i) for i in range(6)])

        last_pe = None
        C.wqkv = C.wreg[:, 0:KC * 2304].rearrange("p (k n) -> p k n", k=KC)
        C.pref = set()

        def load_wqkv(ex):
            wv_ = I["w_in"].rearrange("(k p) n -> p k n", p=128)
            for hh in range(2):
                _dma(P, "pool", C.wqkv[:, hh * 4:(hh + 1) * 4, :], wv_[:, hh * 4:(hh + 1) * 4, 0:2304], r=(), w=("wqkv",), extra=ex)
            C.pref.add("wqkv")

        def load_b1_weights(ex):
            W = C.wreg
            C.wrest = W[:, 0:24576].rearrange("p (k n) -> p k n", k=KC)
            C.watt = W[:, 24576:26624].rearrange("p (k n) -> p k n", k=2)
            C.wsgu = W[:, 26624:30720].rearrange("p (k n) -> p k n", k=4)
            C.wout = W[:, 30720:38912].rearrange("p (k n) -> p k n", k=KC)
            wv_ = I["w_in"].rearrange("(k p) n -> p k n", p=128)
            for hh in range(2):
                _dma(P, "pool", C.wrest[:, hh * 4:(hh + 1) * 4, :], wv_[:, hh * 4:(hh + 1) * 4, 2304:5376], r=(), w=("wrest",), extra=ex)
            _dma(P, "pool", C.watt, I["w_att"].rearrange("(k p) n -> p k n", p=128), r=(), w=("watt",), extra=ex)
            _dma(P, "pool", C.wsgu, I["w_sgu"].rearrange("(k p) n -> p k n", p=128), r=(), w=("wsgu",), extra=ex)
            _dma(P, "pool", C.wout, I["w_out"].rearrange("(k p) n -> p k n", p=128), r=(), w=("wout",), extra=ex)
            C.pref.add("b1")

        if "A1" in phases:
            load_ffn_weights(P, C, I["w_gate1"], I["w_up1"], I["w_down1"], extra=())
            ffn_setup(I["xe"], x1, I["g1pre"], I["g1post"], None)
            if "A2" in phases:
                C.after_last_gateup = lambda: load_wqkv((P.last("pe"),))
            ffn_phase(P, C, n_ext, C.gpre, C.gpost, None)
            last_pe = P.last("pe")
            P.barrier()


        def common_setup(src, dst, gpre, gpost, gpost_scale):
            A.reset()
            C.A = A
            C.ps = ps
            C.src, C.dst = src, dst
            C.nx = 2
            C.X = [A.alloc((D,), F32) for _ in range(C.nx)]
            C.xn = [A.alloc((D,), BF16) for _ in range(2)]
            C.hT = [A.alloc((KC, TT), BF16) for _ in range(2)]
            C.junk = A.alloc((D,), BF16)
            C.ss = A.alloc((4,), F32)
            C.rs = A.alloc((4,), F32)
            C.ssy = A.alloc((8,), F32)
            C.gpre = A.alloc((D,), F32)
            load_bcast(P, C, C.gpre, gpre, D, "gains")
            if gpost is not None:
                C.gpost = A.alloc((D,), F32)
                load_bcast(P, C, C.gpost, gpost, D, "gains", scale=gpost_scale)
                C.Xr = A.alloc((D,), F32)
                C.t = A.alloc((D,), F32)
            C.psT = [ps[6][:, :].bitcast(BF16), ps[7][:, :].bitcast(BF16)]
            C.psrr = PsumRR([ps[i][:, :] for i in range(6)], [("ps", i) for i in range(6)])

        if "A2" in phases:
            ex = (last_pe,) if last_pe is not None else ()
            if "wqkv" not in C.pref:
                load_wqkv(ex)
            common_setup(x1, qkv, I["gmpre"], None, None)
            C.stage = [A.alloc((QKVW,), BF16) for _ in range(2)]
            for sb in range(2):
                sv = C.stage[sb][:, 1536:2316].rearrange("p (h e) -> p h e", e=65)
                P.add("pool", lambda h, o=sv[:, :, 64:65]: h.memset(o, 1.0), r=(), w=[("stage", sb, c0) for (c0, _, _) in QKV_BLOCKS])
            qkv_phase(P, C, n_ext, C.gpre)
            last_pe = P.last("pe")
            P.barrier()

        if "ATT" in phases:
            if "B1" in phases:
                load_b1_weights((last_pe,) if last_pe is not None else ())
            A.reset()
            C.A = A
            C.ps = ps
            AT = Arena(C.wreg[:, 38912:WREG_ELEMS])
            C.Kraw = [AT.alloc((9, 256), BF16) for _ in range(4)]
            C.Vt = [AT.alloc((9, 260), BF16) for _ in range(4)]
            C.Qraw = [AT.alloc((8, 256), BF16) for _ in range(4)]
            C.KT = [A.alloc((2, 9 * 128), BF16) for _ in range(2)]
            C.QT = [A.alloc((2, 8 * 128), BF16) for _ in range(2)]
            C.biasb = A.alloc((12, 256), BF16)
            C.Pb = [A.alloc((2, 256), BF16) for _ in range(4)]
            C.PTs = [A.alloc((2, 2, 128), BF16) for _ in range(4)]
            C.ostage = [A.alloc((ATTW,), F32) for _ in range(3)]
            C.vseg = A.alloc((24 * 9,), F32)
            C.negm = A.alloc((4, 2), F32)
            _dma(P, "sp", C.vseg, I["vseg"], r=(), w=("vseg",))
            build_bias(P, C, I, tscr)
            att_phase(P, C, qkv, att)
            last_pe = P.last("pe")
            P.barrier()

        if "B1" in phases:
            ex = (last_pe,) if last_pe is not None else ()
            W = C.wreg
            if "b1" not in C.pref:
                load_b1_weights(ex)
            if "B2" in phases:
                C.after_last_g = lambda: (load_ffn_weights(P, C, I["w_gate2"], I["w_up2"], I["w_down2"], extra=(P.last("pe"),), parts=("wg",)),
                                          C.pref.add("wg2"))
            common_setup(x1, x2, I["gmpre"], I["gmpost"], None)
            A2 = Arena(W[:, 38912:WREG_ELEMS])
            C.U = [A2.alloc((512,), BF16) for _ in range(4)]
            C.V = [A2.alloc((512,), F32) for _ in range(4)]
            C.st = A2.alloc((4, 4), F32)
            C.vln = [A2.alloc((512,), BF16) for _ in range(4)]
            C.sgu = [A2.alloc((512,), BF16) for _ in range(4)]
            C.sguT = A2.alloc((4, TT), BF16)
            C.attin = A2.alloc((4, 3, ATTW), F32)
            C.M = A2.alloc((4, 4), F32)
            C.dd = A2.alloc((4, 3, 4), F32)
            C.ww = A2.alloc((4, 3, 4), F32)
            C.wd_ = A2.alloc((4, 3, 4), F32)
            C.cc = A2.alloc((4, 3, 4), F32)
            C.den = A2.alloc((4, 4), F32)
            C.tmpO = [A2.alloc((3, 4, 64), F32) for _ in range(2)]
            C.attb = [A2.alloc((4, 64), BF16) for _ in range(4)]
            C.attT = A2.alloc((2, TT), BF16)
            C.sa = [A.alloc((TT,), F32) for _ in range(2)]
            C.sb_ = [A.alloc((TT,), F32) for _ in range(2)]
            C.mT = A.alloc((KC, TT), BF16)
            C.lng = A.alloc((512,), F32)
            C.lnb = A.alloc((512,), F32)
            load_bcast(P, C, C.lng, I["ln_g"], 512, "gains")
            load_bcast(P, C, C.lnb, I["ln_b"], 512, "gains")
            C.WsT = A.alloc((4, 128), BF16)
            C.bs = A.alloc((4,), F32)
            Wn = A2.alloc((4, 128), F32)
            Wnb = A2.alloc((4, 128), BF16)
            _dma(P, "sp", Wn, I["w_s"].rearrange("g p q -> p g q"), r=(), w=("Wn",))
            _dma(P, "sp", C.bs, I["b_s"].rearrange("g p -> p g"), r=(), w=("bs",), slow=True)
            P.add("dve", lambda h: h.tensor_copy(Wnb, Wn), r=("Wn",), w=("Wnb",))
            for gg in range(4):
                _tr(P, C.psT[0][:, gg * 128:(gg + 1) * 128], Wnb[:, gg, :], C.ident, r=("Wnb", "ident"), w=(("psT", 0),))
            _act(P, C.WsT, C.psT[0][:, 0:512].rearrange("p (g q) -> p g q", g=4), AF.Copy, r=(("psT", 0),), w=("WsT",))
            b1_phase(P, C, n_own, att)
            last_pe = P.last("pe")
            P.barrier()

        if "B2" in phases:
            load_ffn_weights(P, C, I["w_gate2"], I["w_up2"], I["w_down2"], extra=(last_pe,) if last_pe is not None else (),
                             parts=("wu", "wd") if "wg2" in C.pref else ("wg", "wu", "wd"))
            ffn_setup(x2, y, I["g2pre"], I["g2post"], I["gfin"])
            ffn_phase(P, C, n_own, C.gpre, C.gpost, C.gfin)
            P.barrier()

        with nc.Block() as block:
            P.emit(block, sems, dsems)
    return nc


def _t5_bucket(rel):
    nb = 16
    max_exact = 8
    ret = (rel > 0).astype(np.int32) * nb
    n = np.abs(rel)
    large = max_exact + (np.log(np.maximum(n, 1) / max_exact) / np.log(1024 / max_exact) * (nb - max_exact)).astype(np.int32)
    large = np.minimum(large, nb - 1)
    return ret + np.where(n < max_exact, n, large)


def segments():
    segs = []
    for g, (window, d) in enumerate(GROUPS):
        i_first = HALO // d
        nown = NOWN // d
        nblk_total = nown // 128
        per = 8 if nblk_total >= 8 else nblk_total
        for r in range(d):
            for s0 in range(0, nblk_total, per):
                segs.append((g, d, r, i_first + 128 * s0, per))
    return segs


def host_constants(seq_lo, seq_hi):
    ident = np.eye(128, dtype=np.float32)
    anti = np.ascontiguousarray(ident[::-1])
    onehot = np.zeros((32, 3, 129), np.float32)
    for g, (window, d) in enumerate(GROUPS):
        half = (window // 2) // d
        off = np.arange(-half, half + 1, dtype=np.int32) * d
        bk = _t5_bucket(off)
        onehot[bk, g, np.arange(129)] = 1.0
    segs = segments()
    vseg = np.zeros((128, 24, 9), np.float32)
    for si, (g, d, r, i0, nblk) in enumerate(segs):
        for j in range(nblk + 1):
            sub = i0 - 64 + 128 * j + np.arange(128)
            pos = d * sub + r
            vseg[:, si, j] = ((pos >= seq_lo) & (pos < seq_hi)).astype(np.float32)
    return ident, anti, onehot.reshape(32, 3 * 129), vseg.reshape(128, 24 * 9)


def make_in_maps(inp):
    f = lambda a: np.ascontiguousarray(np.asarray(a, dtype=np.float32))
    xp = f(inp["x_prompt"])[0]
    xs = f(inp["x_sample"])
    shared = {
        "w_gate1": f(inp["ffn1_w_gate"])[0], "w_up1": f(inp["ffn1_w_up"])[0], "w_down1": f(inp["ffn1_w_down"])[0],
        "w_gate2": f(inp["ffn2_w_gate"])[0], "w_up2": f(inp["ffn2_w_up"])[0], "w_down2": f(inp["ffn2_w_down"])[0],
        "w_in": f(inp["w_in"])[0], "w_att": f(inp["w_att"])[0], "w_sgu": f(inp["w_sgu"])[0], "w_out": f(inp["w_out"])[0],
        "g1pre": f(inp["ffn1_pre_g"]), "g1post": f(inp["ffn1_post_g"]), "gmpre": f(inp["mix_pre_g"]),
        "gmpost": f(inp["mix_post_g"]), "g2pre": f(inp["ffn2_pre_g"]), "g2post": f(inp["ffn2_post_g"]),
        "gfin": f(inp["final_g"]), "ln_g": f(inp["sgu_ln_g"]), "ln_b": f(inp["sgu_ln_b"]),
        "w_s": f(inp["sgu_w_s"])[0], "b_s": f(inp["sgu_b_s"])[0], "rel_bias": f(inp["rel_bias"]),
    }
    maps = []
    for c in range(NCORES):
        xe = np.zeros((NEXT, D), np.float32)
        if c < 4:
            lo = c * NOWN - HALO
            hi = lo + NEXT
            a, b = max(lo, 0), min(hi, xp.shape[0])
            xe[a - lo:b - lo] = xp[a:b]
            seq_lo, seq_hi = a - lo, b - lo
        else:
            xe[HALO:HALO + NOWN] = xs[c - 4]
            seq_lo, seq_hi = HALO, HALO + NOWN
        ident, anti, onehot, vseg = host_constants(seq_lo, seq_hi)
        m = dict(shared)
        m.update({"xe": xe, "ident": ident, "antiid": anti, "onehot": onehot, "vseg": vseg})
        maps.append(m)
    return maps


_NC_CACHE = {}


def kernel(**inputs):
    key = "full"
    if key not in _NC_CACHE:
        _NC_CACHE[key] = build_program(CFG)
    nc = _NC_CACHE[key]
    maps = make_in_maps(inputs)
    res = run_bass_kernel_spmd(nc, maps, core_ids=list(range(NCORES)))
    ys = [np.asarray(r["y"], dtype=np.float32) for r in res.results]
    y_prompt = np.concatenate(ys[0:4], axis=0)[None]
    y_sample = np.stack(ys[4:8], axis=0)
    return (y_prompt, y_sample)
```
